# Optimizing a Trainium2 kernel written in Bass

```python
import jax, jax.numpy as jnp
from jax import lax
import numpy as np

D_MODEL = 1024
BATCH = 8
SEQ = 4096
DEPTH = 4

HEAD_DIM = 64
FOX_HEADS = 8
FOX_WIDTH = FOX_HEADS * HEAD_DIM
POOL_GROUPS = 4
POOL_WINDOWS = (2, 4, 8, 16)
POOL_WIDTH = D_MODEL - FOX_WIDTH
POOL_GROUP_DIM = POOL_WIDTH // POOL_GROUPS
Q_BLOCK = 128
IN_COLS = 4 * FOX_WIDTH + FOX_HEADS + POOL_WIDTH
SPLITS = (FOX_WIDTH, 2 * FOX_WIDTH, 3 * FOX_WIDTH, 4 * FOX_WIDTH, 4 * FOX_WIDTH + FOX_HEADS)
RWKV_HEADS = D_MODEL // HEAD_DIM
DECAY_LORA = 64
AAA_LORA = 64
MV_LORA = 32
GATE_LORA = 160
D_FF = -(-(8 * D_MODEL) // (3 * 256)) * 256
RMS_EPS = 1e-6
GN_EPS = 64e-5
N_EVEN = (DEPTH + 1) // 2
N_ODD = DEPTH // 2

kernel_name = 'fox_pool_rwkv7_hybrid_trunk'


def _rmsnorm(x, gain):
    xf = x.astype(jnp.float32)
    y = xf * lax.rsqrt(jnp.mean(xf * xf, axis=-1, keepdims=True) + RMS_EPS)
    return (y * gain.astype(jnp.float32)).astype(x.dtype)


def _swiglu(h, w_gate, w_up, w_down):
    return (jax.nn.silu(h @ w_gate) * (h @ w_up)) @ w_down


def _token_shift(x):
    return jnp.pad(x, ((0, 0), (1, 0), (0, 0)))[:, :-1]


def _forgetting_attention(q, k, v, cum):
    b, h, s, dh = q.shape
    nb = s // Q_BLOCK
    scale = dh ** -0.5
    qb = q.reshape(b, h, nb, Q_BLOCK, dh).transpose(2, 0, 1, 3, 4)
    cb = cum.reshape(b, h, nb, Q_BLOCK).transpose(2, 0, 1, 3)
    key_pos = jnp.arange(s)

    def one_block(args):
        i, q_i, c_i = args
        q_pos = i * Q_BLOCK + jnp.arange(Q_BLOCK)
        logits = (jnp.einsum('bhqd,bhkd->bhqk', q_i, k).astype(jnp.float32) * scale
                  + c_i[..., :, None] - cum[..., None, :])
        causal = key_pos[None, :] <= q_pos[:, None]
        p = jax.nn.softmax(jnp.where(causal, logits, -1e30), axis=-1)
        return jnp.einsum('bhqk,bhkd->bhqd', p.astype(v.dtype), v)

    out = lax.map(one_block, (jnp.arange(nb), qb, cb))
    return out.transpose(1, 2, 0, 3, 4).reshape(b, h, s, dh)


def _multiscale_causal_pool(u, pool_w, pool_scale):
    b, s, _ = u.shape
    uf = u.astype(jnp.float32).reshape(b, s, POOL_GROUPS, POOL_GROUP_DIM)
    csum = jnp.cumsum(uf, axis=1)
    pos = jnp.arange(s)
    groups = []
    for g, w in enumerate(POOL_WINDOWS):
        cg = csum[:, :, g]
        prev = jnp.pad(cg, ((0, 0), (w, 0), (0, 0)))[:, :s]
        count = jnp.minimum(pos + 1, w).astype(jnp.float32)[None, :, None]
        groups.append((cg - prev) / count - uf[:, :, g])
    pooled = jnp.stack(groups, axis=2).astype(u.dtype)
    mixed = jnp.einsum('bsgc,gcd->bsgd', pooled, pool_w)
    return mixed.reshape(b, s, POOL_WIDTH) * pool_scale


def _fox_pool_mixer(h, w_in, f_bias, q_gain, k_gain, pool_w, pool_scale, w_out):
    b, s, _ = h.shape
    proj = h @ w_in
    q, k, v, og, f_logit, u = jnp.split(proj, SPLITS, axis=-1)

    def heads(t):
        return t.reshape(b, s, FOX_HEADS, HEAD_DIM)

    q = _rmsnorm(heads(q), q_gain).transpose(0, 2, 1, 3)
    k = _rmsnorm(heads(k), k_gain).transpose(0, 2, 1, 3)
    v = heads(v).transpose(0, 2, 1, 3)
    log_f = jax.nn.log_sigmoid(f_logit.astype(jnp.float32) + f_bias.astype(jnp.float32))
    cum = jnp.cumsum(log_f, axis=1).transpose(0, 2, 1)
    attn = _forgetting_attention(q, k, v, cum).transpose(0, 2, 1, 3).reshape(b, s, FOX_WIDTH)
    attn = attn * jax.nn.sigmoid(og)
    pool = _multiscale_causal_pool(u, pool_w, pool_scale)
    return jnp.concatenate([attn, pool], axis=-1) @ w_out


def _rwkv7_step(state, inp):
    r_t, w_t, k_t, v_t, a_t, b_t = inp
    sa = jnp.einsum('bhvk,bhk->bhv', state, a_t)
    state = (state * w_t[:, :, None, :] + sa[..., None] * b_t[:, :, None, :]
             + v_t[..., None] * k_t[:, :, None, :])
    return state, jnp.einsum('bhvk,bhk->bhv', state, r_t)


def _rwkv7_time_mix(h, mu, w_r, w_k, w_v, w0, w1, w2, a0, a1, a2, g1, g2,
                    k_k, k_a, r_k, ln_w, ln_b, w_o, v_first, v_mix):
    b, s, d = h.shape
    f32 = jnp.float32
    xx = _token_shift(h) - h
    xr, xw, xk, xv, xa, xg = [h + xx * mu[i] for i in range(6)]
    r = (xr @ w_r).astype(f32)
    k = (xk @ w_k).astype(f32)
    v = (xv @ w_v).astype(f32)
    w = -jax.nn.softplus(-(w0 + jnp.tanh(xw @ w1) @ w2).astype(f32)) - 0.5
    a = jax.nn.sigmoid((a0 + (xa @ a1) @ a2).astype(f32))
    g = jax.nn.sigmoid(xg @ g1) @ g2
    if v_mix is None:
        v_first = v
    else:
        v0, v1, v2 = v_mix
        v = v + (v_first - v) * jax.nn.sigmoid((v0 + (xv @ v1) @ v2).astype(f32))

    def heads(t):
        return t.reshape(b, s, RWKV_HEADS, HEAD_DIM)

    kk = heads(k * k_k.astype(f32))
    kk = kk / jnp.maximum(jnp.sqrt(jnp.sum(kk * kk, axis=-1, keepdims=True)), 1e-12)
    k = k * (1.0 + (a - 1.0) * k_a.astype(f32))
    decay = jnp.exp(-jnp.exp(w))
    rh, kh, vh, ah = heads(r), heads(k), heads(v), heads(a)
    seq_first = lambda t: t.transpose(1, 0, 2, 3)
    xs = (seq_first(rh), seq_first(heads(decay)), seq_first(kh), seq_first(vh),
          seq_first(-kk), seq_first(kk * ah))
    state0 = jnp.zeros((b, RWKV_HEADS, HEAD_DIM, HEAD_DIM), f32)
    _, ys = lax.scan(_rwkv7_step, state0, xs)
    y = ys.transpose(1, 0, 2, 3)
    mean = jnp.mean(y, axis=-1, keepdims=True)
    var = jnp.mean(jnp.square(y - mean), axis=-1, keepdims=True)
    y = ((y - mean) * lax.rsqrt(var + GN_EPS)).reshape(b, s, d) * ln_w.astype(f32) + ln_b.astype(f32)
    bonus = jnp.sum(rh * kh * r_k.astype(f32), axis=-1, keepdims=True) * vh
    y = (y + bonus.reshape(b, s, d)).astype(h.dtype)
    return (y * g) @ w_o, v_first


def setup_inputs(seed: int = 0) -> dict:
    key = jax.random.key(seed)
    keys = jax.random.split(key, 40)
    counter = [0]
    f32 = jnp.float32

    def nk():
        counter[0] += 1
        return keys[counter[0] - 1]

    def nrm(shape, scale):
        return jax.random.normal(nk(), shape, f32) * scale

    def gain(shape):
        return 1.0 + 0.1 * jax.random.normal(nk(), shape, f32)

    D, F, ne, no = D_MODEL, D_FF, N_EVEN, N_ODD
    return {
        'x': nrm((BATCH, SEQ, D), 1.0),
        'mix_norm': gain((DEPTH, D)),
        'ffn_norm': gain((DEPTH, D)),
        'ffn_w_gate': nrm((DEPTH, D, F), D ** -0.5),
        'ffn_w_up': nrm((DEPTH, D, F), D ** -0.5),
        'ffn_w_down': nrm((DEPTH, F, D), F ** -0.5),
        'hy_w_in': nrm((ne, D, IN_COLS), D ** -0.5),
        'hy_f_bias': 2.0 + 0.5 * jax.random.normal(nk(), (ne, FOX_HEADS), f32),
        'hy_q_gain': gain((ne, HEAD_DIM)),
        'hy_k_gain': gain((ne, HEAD_DIM)),
        'hy_pool_w': nrm((ne, POOL_GROUPS, POOL_GROUP_DIM, POOL_GROUP_DIM), POOL_GROUP_DIM ** -0.5),
        'hy_pool_scale': gain((ne, POOL_WIDTH)),
        'hy_w_out': nrm((ne, D, D), D ** -0.5),
        'rw_mu': jax.random.uniform(nk(), (no, 6, D), f32),
        'rw_w_r': nrm((no, D, D), D ** -0.5),
        'rw_w_k': nrm((no, D, D), D ** -0.5),
        'rw_w_v': nrm((no, D, D), D ** -0.5),
        'rw_w0': nrm((no, D), 0.5),
        'rw_w1': nrm((no, D, DECAY_LORA), D ** -0.5),
        'rw_w2': nrm((no, DECAY_LORA, D), 0.5 * DECAY_LORA ** -0.5),
        'rw_a0': nrm((no, D), 0.5),
        'rw_a1': nrm((no, D, AAA_LORA), D ** -0.5),
        'rw_a2': nrm((no, AAA_LORA, D), 0.5 * AAA_LORA ** -0.5),
        'rw_g1': nrm((no, D, GATE_LORA), D ** -0.5),
        'rw_g2': nrm((no, GATE_LORA, D), GATE_LORA ** -0.5),
        'rw_k_k': gain((no, D)),
        'rw_k_a': gain((no, D)),
        'rw_r_k': nrm((no, RWKV_HEADS, HEAD_DIM), 0.1),
        'rw_ln_w': gain((no, D)),
        'rw_ln_b': nrm((no, D), 0.02),
        'rw_w_o': nrm((no, D, D), D ** -0.5),
        'rw_v0': nrm((max(no - 1, 0), D), 0.5),
        'rw_v1': nrm((max(no - 1, 0), D, MV_LORA), D ** -0.5),
        'rw_v2': nrm((max(no - 1, 0), MV_LORA, D), 0.5 * MV_LORA ** -0.5),
    }


def reference(x, mix_norm, ffn_norm, ffn_w_gate, ffn_w_up, ffn_w_down,
              hy_w_in, hy_f_bias, hy_q_gain, hy_k_gain, hy_pool_w, hy_pool_scale, hy_w_out,
              rw_mu, rw_w_r, rw_w_k, rw_w_v, rw_w0, rw_w1, rw_w2, rw_a0, rw_a1, rw_a2,
              rw_g1, rw_g2, rw_k_k, rw_k_a, rw_r_k, rw_ln_w, rw_ln_b, rw_w_o,
              rw_v0, rw_v1, rw_v2):
    v_first = None
    for layer in range(DEPTH):
        h = _rmsnorm(x, mix_norm[layer])
        if layer % 2 == 0:
            e = layer // 2
            y = _fox_pool_mixer(h, hy_w_in[e], hy_f_bias[e], hy_q_gain[e], hy_k_gain[e],
                                hy_pool_w[e], hy_pool_scale[e], hy_w_out[e])
        else:
            o = layer // 2
            v_mix = None if o == 0 else (rw_v0[o - 1], rw_v1[o - 1], rw_v2[o - 1])
            y, v_first = _rwkv7_time_mix(h, rw_mu[o], rw_w_r[o], rw_w_k[o], rw_w_v[o],
                                         rw_w0[o], rw_w1[o], rw_w2[o], rw_a0[o], rw_a1[o], rw_a2[o],
                                         rw_g1[o], rw_g2[o], rw_k_k[o], rw_k_a[o], rw_r_k[o],
                                         rw_ln_w[o], rw_ln_b[o], rw_w_o[o], v_first, v_mix)
        x = x + y
        x = x + _swiglu(_rmsnorm(x, ffn_norm[layer]), ffn_w_gate[layer], ffn_w_up[layer], ffn_w_down[layer])
    return x
```

```python
import numpy as np
from contextlib import ExitStack
import concourse.bass as bass
import concourse.mybir as mybir
from concourse.bass_utils import run_bass_kernel_spmd

F32 = mybir.dt.float32
BF16 = mybir.dt.bfloat16
AF = mybir.ActivationFunctionType
ALU = mybir.AluOpType
AX = mybir.AxisListType

D = 1024
DFF = 2816
NFC = DFF // 128
IN_COLS = 2568
RMS_EPS = 1e-6
GN_EPS = 64e-5
ENGS = ("pe", "act", "dve", "pool", "sp")
CH = 16000


class Res:
    __slots__ = ("name", "w", "r")

    def __init__(self, name=""):
        self.name = name
        self.w = None
        self.r = {}


class DmaSem:
    __slots__ = ("key", "sem", "count")

    def __init__(self, key, sem):
        self.key = key
        self.sem = sem
        self.count = 0


class Prog:
    def __init__(self, nc, stack, same_engine_sync=True):
        self.nc = nc
        self.stack = stack
        self.q = {e: [] for e in ENGS}
        self.n = {e: 0 for e in ENGS}
        self.esem = {}
        self.seen = {e: {} for e in ENGS}
        self.same = same_engine_sync
        self.dsems = []
        self.nwaits = 0

    def dmasem(self, name):
        s = self.stack.enter_context(self.nc.semaphore("d%d_%s" % (len(self.dsems), name)))
        d = DmaSem("d%d_%s" % (len(self.dsems), name), s)
        self.dsems.append(d)
        return d

    def _esem(self, e, k):
        if (e, k) not in self.esem:
            self.esem[(e, k)] = self.stack.enter_context(self.nc.semaphore("e_%s_%d" % (e, k)))
        return self.esem[(e, k)]

    def emit(self, eng, fn, reads=(), writes=(), dma=None):
        need = {}

        def want(ev):
            if ev is None:
                return
            key, val = ev[0], ev[1]
            if key == eng and (eng == "pe" or not self.same):
                return
            if need.get(key, (0,))[0] < val:
                need[key] = (val, ev[2])

        for r in reads:
            want(r.w)
        for w in writes:
            want(w.w)
            for ev in w.r.values():
                want(ev)
        waits = []
        seen = self.seen[eng]
        for key, (val, hinfo) in need.items():
            if seen.get(key, 0) >= val:
                continue
            seen[key] = val
            if key in ENGS:
                k = (val - 1) // CH
                waits.append((self._esem(key, k), val - k * CH))
            else:
                waits.append((hinfo, val))
        self.nwaits += len(waits)
        if fn is None:
            if waits:
                self.q[eng].append((waits, None, None))
            return None
        if dma is None:
            self.n[eng] += 1
            idx = self.n[eng]
            k = (idx - 1) // CH
            inc = (self._esem(eng, k), 1)
            ev = (eng, idx, None)
        else:
            dma.count += 16
            inc = (dma.sem, 16)
            ev = (dma.key, dma.count, dma.sem)
        self.q[eng].append((waits, fn, inc))
        for r in reads:
            r.r[ev[0]] = ev
        for w in writes:
            w.w = ev
            w.r = {}
        return ev

    def barrier(self):
        evs = [(e, self.n[e], None) for e in ENGS if self.n[e] > 0]
        evs += [(d.key, d.count, d.sem) for d in self.dsems if d.count > 0]
        for eng in ENGS:
            tmp = Res()
            tmp.r = {ev[0]: ev for ev in evs if ev[0] != eng}
            self.emit(eng, None, writes=[tmp])

    def finalize(self):
        nc = self.nc
        with nc.Block() as block:
            def mk(ename):
                def body(e):
                    for waits, fn, inc in self.q[ename]:
                        for sem, val in waits:
                            e.wait_ge(sem, val)
                        if fn is not None:
                            fn(e).then_inc(inc[0], inc[1])
                return body
            block.tensor(mk("pe"))
            block.scalar(mk("act"))
            block.vector(mk("dve"))
            block.gpsimd(mk("pool"))
            block.sync(mk("sp"))


class KB:
    def __init__(self, nc, S, stack):
        self.nc = nc
        self.S = S
        self.st = stack
        self.P = Prog(nc, stack)

    def sb(self, name, shape, dtype, stack=None):
        self.uid = getattr(self, "uid", 0) + 1
        return (stack or self.st).enter_context(self.nc.sbuf_tensor("%s_u%d" % (name, self.uid), list(shape), dtype))

    def ps(self, name, dtype=F32, stack=None):
        n = 512 if dtype == F32 else 1024
        self.uid = getattr(self, "uid", 0) + 1
        return (stack or self.st).enter_context(self.nc.psum_tensor("%s_u%d" % (name, self.uid), [128, n], dtype))

    def mm(self, out, lhsT, rhs, start, stop, R, W, **kw):
        return self.P.emit("pe", lambda e: e.matmul(out, lhsT=lhsT, rhs=rhs, start=start, stop=stop, **kw), R, W)

    def tr(self, out, in_, ident, R, W):
        return self.P.emit("pe", lambda e: e.transpose(out, in_, ident), R, W)

    def act(self, out, in_, func, R, W, eng="act", **kw):
        return self.P.emit(eng, lambda e: e.activation(out=out, in_=in_, func=func, **kw), R, W)

    def copy(self, eng, out, in_, R, W):
        if eng == "act":
            return self.P.emit("act", lambda e: e.copy(out=out, in_=in_), R, W)
        return self.P.emit(eng, lambda e: e.tensor_copy(out=out, in_=in_), R, W)

    def tt(self, eng, out, in0, in1, op, R, W):
        return self.P.emit(eng, lambda e: e.tensor_tensor(out=out, in0=in0, in1=in1, op=op), R, W)

    def ts(self, eng, out, in0, s1, s2, op0, op1, R, W, **kw):
        if s2 is None:
            return self.P.emit(eng, lambda e: e.tensor_scalar(out=out, in0=in0, scalar1=s1, scalar2=None, op0=op0, **kw), R, W)
        return self.P.emit(eng, lambda e: e.tensor_scalar(out=out, in0=in0, scalar1=s1, scalar2=s2, op0=op0, op1=op1, **kw), R, W)

    def stt(self, eng, out, in0, scalar, in1, op0, op1, R, W):
        return self.P.emit(eng, lambda e: e.scalar_tensor_tensor(out=out, in0=in0, scalar=scalar, in1=in1, op0=op0, op1=op1), R, W)

    def memset(self, eng, ap, val, W):
        return self.P.emit(eng, lambda e: e.memset(ap, val), (), W)

    def dma(self, eng, out, in_, sem, R, W, slow=False):
        if slow:
            return self.P.emit(eng, lambda e: e.dma_start(out=out, in_=in_, allow_slow_non_contiguous=True), R, W, dma=sem)
        return self.P.emit(eng, lambda e: e.dma_start(out=out, in_=in_), R, W, dma=sem)

    def setup_consts(self):
        nc = self.nc
        self.ident = self.sb("ident", [128, 128], BF16)
        self.Rconst = Res("const")
        W = [self.Rconst]
        self.eps_rms = self.sb("eps_rms", [128, 4], F32)
        self.memset("pool", self.eps_rms[:, 0:1], RMS_EPS, W)
        self.memset("pool", self.eps_rms[:, 1:2], GN_EPS, W)
        self.memset("pool", self.eps_rms[:, 2:3], 1.0, W)
        self.memset("pool", self.eps_rms[:, 3:4], 0.0, W)
        self.memset("pool", self.ident[:], 1.0, W)
        idt = self.ident
        self.P.emit("pool", lambda e: e.affine_select(out=idt[:], in_=idt[:], pattern=[[-1, 128]],
                                                      compare_op=ALU.is_equal, fill=0.0, base=0,
                                                      channel_multiplier=1), (), W)

    def norm_tile(self, xt, Rxt, gain_b, Rgain, hT, RhT, col0, bufs, i):
        junk, Rjunk = bufs["junk"][i % 2]
        ss, Rss = bufs["ss"][i % 2]
        hb, Rhb = bufs["hb"][i % 2]
        tp, Rtp = bufs["tp"][i % len(bufs["tp"])]
        self.act(junk[:], xt[:], AF.Square, [Rxt], [Rjunk, Rss], accum_out=ss[:, 0:1])
        self.act(ss[:, 1:2], ss[:, 0:1], AF.Ln, [Rss, self.Rconst], [Rss], scale=1.0 / D, bias=self.eps_rms[:, 0:1])
        self.act(ss[:, 2:3], ss[:, 1:2], AF.Exp, [Rss], [Rss], scale=-0.5)
        self.stt("dve", hb[:], xt[:], ss[:, 2:3], gain_b[:], ALU.mult, ALU.mult, [Rxt, Rss, Rgain], [Rhb])
        for c in range(8):
            self.tr(tp[:, c * 128:(c + 1) * 128], hb[:, c * 128:(c + 1) * 128], self.ident[:], [Rhb, self.Rconst], [Rtp])
        self.copy("act", hT[:, :, col0:col0 + 128], tp[:, :].rearrange("p (c t) -> p c t", c=8), [Rtp], [RhT])

    def ffn_phase(self, layer, xin, xout, prm):
        nc, P, S = self.nc, self.P, self.S
        TG = min(1024, S)
        NG = S // TG
        NT = TG // 128
        NH = TG // 512
        with ExitStack() as st:
            gain_b = self.sb("f_gain", [128, D], F32, st)
            hT = self.sb("f_hT", [128, 8, TG], BF16, st)
            actT = self.sb("f_actT", [128, NFC, TG], BF16, st)
            wd = self.sb("f_wd", [128, NFC, D], BF16, st)
            wg = [self.sb("f_wg%d" % i, [128, 8, 256], BF16, st) for i in range(2)]
            wu = [self.sb("f_wu%d" % i, [128, 8, 256], BF16, st) for i in range(2)]
            xts = [self.sb("f_xt%d" % i, [128, D], F32, st) for i in range(3)]
            ots = [self.sb("f_ot%d" % i, [128, 512], F32, st) for i in range(2)]
            sil = [self.sb("f_sil%d" % i, [128, 512], F32, st) for i in range(2)]
            bufs = {
                "junk": [(self.sb("f_junk%d" % i, [128, D], BF16, st), Res()) for i in range(2)],
                "ss": [(self.sb("f_ss%d" % i, [128, 4], F32, st), Res()) for i in range(2)],
                "hb": [(self.sb("f_hb%d" % i, [128, D], BF16, st), Res()) for i in range(2)],
                "tp": [(self.ps("f_tp%d" % i, BF16, st), Res()) for i in range(2)],
            }
            pg = [(self.ps("f_pg%d" % i, F32, st), Res()) for i in range(2)]
            pu = [(self.ps("f_pu%d" % i, F32, st), Res()) for i in range(2)]
            po = [(self.ps("f_po%d" % i, F32, st), Res()) for i in range(2)]
            Rgain = Res(); RhT = [Res() for _ in range(NT)]; Ract = [Res() for _ in range(NFC)]
            Rwd = Res(); Rwg = [Res(), Res()]; Rwu = [Res(), Res()]
            Rxt = [Res() for _ in range(3)]; Rot = [Res(), Res()]; Rsil = [Res(), Res()]
            d_gain = P.dmasem("fgain"); d_x = [P.dmasem("fx%d" % i) for i in range(3)]
            d_wg = [P.dmasem("fwg%d" % i) for i in range(2)]; d_wu = [P.dmasem("fwu%d" % i) for i in range(2)]
            d_wd = P.dmasem("fwd"); d_o = [P.dmasem("fo%d" % i) for i in range(2)]
            Rxdram = self.Rxdram

            self.dma("sp", gain_b[:], prm["ffn_norm"][layer:layer + 1, :].partition_broadcast(128), d_gain, [], [Rgain])
            wgv = prm["ffn_w_gate"][layer].rearrange("(c p) f -> p c f", p=128)
            wuv = prm["ffn_w_up"][layer].rearrange("(c p) f -> p c f", p=128)
            wdv = prm["ffn_w_down"][layer].rearrange("(c p) n -> p c n", p=128)
            xcnt = 0
            ocnt = 0
            step = 0
            for g in range(NG):
                t0 = g * TG
                for c0 in range(0, NFC, 2):
                    self.dma("pool", wd[:, c0:c0 + 2, :], wdv[:, c0:c0 + 2, :], d_wd, [], [Rwd])
                for i in range(NT):
                    b = xcnt % 3
                    tix = (t0 // 128) + i
                    self.dma("sp", xts[b][:], xin[t0 + i * 128:t0 + (i + 1) * 128, :], d_x[b], [Rxdram[tix]], [Rxt[b]])
                    self.norm_tile(xts[b], Rxt[b], gain_b, Rgain, hT, RhT[i], i * 128, bufs, xcnt)
                    xcnt += 1
                for fg in range(NFC // 2):
                    wb = fg % 2
                    self.dma("pool", wg[wb][:], wgv[:, :, fg * 256:(fg + 1) * 256], d_wg[wb], [], [Rwg[wb]])
                    self.dma("pool", wu[wb][:], wuv[:, :, fg * 256:(fg + 1) * 256], d_wu[wb], [], [Rwu[wb]])
                    for fc in range(2):
                        f = fg * 2 + fc
                        for th in range(NH):
                            pb = step % 2
                            pgt, Rpg = pg[pb]
                            put, Rpu = pu[pb]
                            rh = RhT[th * 4:(th + 1) * 4]
                            for c in range(8):
                                self.mm(pgt[:, :], wg[wb][:, c, fc * 128:(fc + 1) * 128], hT[:, c, th * 512:(th + 1) * 512],
                                        c == 0, c == 7, [Rwg[wb]] + rh, [Rpg])
                            for c in range(8):
                                self.mm(put[:, :], wu[wb][:, c, fc * 128:(fc + 1) * 128], hT[:, c, th * 512:(th + 1) * 512],
                                        c == 0, c == 7, [Rwu[wb]] + rh, [Rpu])
                            self.act(sil[pb][:], pgt[:, :], AF.Silu, [Rpg], [Rsil[pb]])
                            self.tt("dve", actT[:, f, th * 512:(th + 1) * 512], sil[pb][:], put[:, :], ALU.mult,
                                    [Rsil[pb], Rpu], [Ract[f]])
                            step += 1
                for i in range(NT):
                    b = xcnt % 3
                    tix = (t0 // 128) + i
                    self.dma("sp", xts[b][:], xin[t0 + i * 128:t0 + (i + 1) * 128, :], d_x[b], [Rxdram[tix]], [Rxt[b]])
                    xcnt += 1
                    for nh in range(2):
                        ob = ocnt % 2
                        pot, Rpo = po[ob]
                        for f in range(NFC):
                            self.mm(pot[:, :], actT[:, f, i * 128:(i + 1) * 128], wd[:, f, nh * 512:(nh + 1) * 512],
                                    f == 0, f == NFC - 1, [Ract[f], Rwd], [Rpo])
                        self.tt("dve", ots[ob][:], pot[:, :], xts[b][:, nh * 512:(nh + 1) * 512], ALU.add,
                                [Rpo, Rxt[b]], [Rot[ob]])
                        self.dma("sp", xout[t0 + i * 128:t0 + (i + 1) * 128, nh * 512:(nh + 1) * 512], ots[ob][:], d_o[ob],
                                 [Rot[ob]], [Rxdram[tix]])
                        ocnt += 1
            P.barrier()


    def hy_consts(self, st):
        c = {}
        W = [self.Rconst]
        sb = lambda n, shp, dt: self.sb(n, shp, dt, st)
        c["negmask"] = sb("c_negmask", [128, 128], BF16)
        c["tri"] = sb("c_tri", [128, 128], F32)
        c["nones"] = sb("c_nones", [128, 128], F32)
        c["identf"] = sb("c_identf", [128, 128], F32)
        c["bd"] = sb("c_bd", [128, 128], BF16)
        c["esel"] = sb("c_esel", [8, 8, 128], BF16)
        c["ones64"] = sb("c_ones64", [128, 64], BF16)
        c["invc"] = sb("c_invc", [128, 4, 16], F32)
        nm, tri, nones, identf, bd, esel, ones64, invc = (c[k] for k in ("negmask", "tri", "nones", "identf", "bd", "esel", "ones64", "invc"))
        self.memset("pool", nm[:], 0.0, W)
        self.P.emit("pool", lambda e: e.affine_select(out=nm[:], in_=nm[:], pattern=[[1, 128]], compare_op=ALU.is_ge,
                                                      fill=-30000.0, base=0, channel_multiplier=-1), (), W)
        self.memset("pool", tri[:], -1.0, W)
        self.P.emit("pool", lambda e: e.affine_select(out=tri[:], in_=tri[:], pattern=[[1, 128]], compare_op=ALU.is_ge,
                                                      fill=0.0, base=0, channel_multiplier=-1), (), W)
        self.memset("pool", nones[:], -1.0, W)
        self.memset("pool", identf[:], 1.0, W)
        self.P.emit("pool", lambda e: e.affine_select(out=identf[:], in_=identf[:], pattern=[[-1, 128]], compare_op=ALU.is_equal,
                                                      fill=0.0, base=0, channel_multiplier=1), (), W)
        self.memset("pool", bd[:], 0.0, W)
        self.memset("pool", bd[0:64, 0:64], 1.0, W)
        self.memset("pool", bd[64:128, 64:128], 1.0, W)
        self.memset("pool", esel[:], 8.0, W)
        self.P.emit("pool", lambda e: e.affine_select(out=esel[:], in_=esel[:], pattern=[[1, 8], [0, 128]], compare_op=ALU.is_equal,
                                                      fill=0.0, base=0, channel_multiplier=-1), (), W)
        self.memset("pool", ones64[:], 1.0, W)
        for g, w in enumerate((2, 4, 8, 16)):
            self.memset("pool", invc[:, g, :], 1.0 / w, W)
            for t in range(w - 1):
                self.memset("pool", invc[:, g, t:t + 1], 1.0 / (t + 1), W)
        return c

    def hy_phase(self, e, layer, xin, xout, prm):
        nc, P, S = self.nc, self.P, self.S
        NI = S // 512
        NB = S // 128
        Rc = self.Rconst
        with ExitStack() as st:
            cst = self.hy_consts(st)
            sb = lambda n, shp, dt: self.sb("h_" + n, shp, dt, st)
            w_in = sb("w_in", [128, 8, IN_COLS], BF16)
            w_out = sb("w_out", [128, 8, D], BF16)
            pw = sb("pw", [128, 4, 128], BF16)
            gain_b = sb("gain", [128, D], F32)
            qg = sb("qg", [128, 1], F32); kg = sb("kg", [128, 1], F32)
            pscale = sb("pscale", [128, 4], F32)
            fb = sb("fb", [128, 8], F32)
            kT = sb("kT", [128, 4, S], BF16)
            vc = sb("vc", [128, NB, 512], BF16)
            cumK = sb("cumK", [128, NB, 8], F32)
            kb = sb("kb", [128, NB, 8], F32)
            hc = sb("hc", [128, 8, 512], BF16)
            qT = sb("qT", [128, 4, 512], BF16)
            sgT = sb("sgT", [128, 4, 512], BF16)
            upad = sb("upad", [128, 4, 528], F32)
            tA = sb("tA", [128, 528], F32); tB = sb("tB", [128, 528], F32)
            pooledT = sb("pooledT", [128, 4, 512], BF16)
            xts = [sb("xt%d" % i, [128, D], F32) for i in range(2)]
            ots = [sb("ot%d" % i, [128, 512], F32) for i in range(2)]
            junk = sb("junk", [128, D], BF16)
            kf = sb("kf", [128, 512], F32); sq = sb("sq", [128, 512], BF16); rs = sb("rs", [128, 512], F32)
            pT = [sb("pT%d" % i, [128, 512], BF16) for i in range(3)]
            rden = sb("rden", [128, 512], F32); atmp = sb("atmp", [128, 512], F32)
            carry = [sb("carry%d" % i, [128, 8], F32) for i in range(2)]
            zf = sb("zf", [128, 8], F32); lf = sb("lf", [128, 8], F32)
            qctok = sb("qctok", [128, 4, 8], F32)
            qcT = sb("qcT", [8, 512], BF16)
            t16 = sb("t16", [128, 16], F32)
            bufs = {
                "junk": [(junk, Res()), (junk, Res())],
                "ss": [(sb("ss%d" % i, [128, 4], F32), Res()) for i in range(2)],
                "hb": [(sb("hb%d" % i, [128, D], BF16), Res()) for i in range(1)] * 2,
                "tp": [(self.ps("h_tp", BF16, st), Res())],
            }
            gp = [(self.ps("h_gp%d" % i, F32, st), Res()) for i in range(2)]
            sbk = [(self.ps("h_s%d" % i, F32, st), Res()) for i in range(2)]
            accN, RaccN = self.ps("h_accN", F32, st), Res()
            accD, RaccD = self.ps("h_accD", F32, st), Res()
            smp, Rsm = self.ps("h_sm", F32, st), Res()
            Rw = Res(); Rgain = Res(); Rsmall = Res()
            Rhc = Res(); RqT = Res(); RsgT = Res(); RkT = [Res() for _ in range(NI)]; Rvc = [Res() for _ in range(NB)]
            RcumK = Res(); Rkb = Res(); Rupad = Res(); RtA = Res(); RtB = Res(); Rpooled = Res()
            Rxt = [Res(), Res()]; Rot = [Res(), Res()]; Rkf = Res(); Rsq = Res(); Rrs = Res()
            RpT = [Res() for _ in range(3)]; Rrden = Res(); Ratmp = Res(); Rcarry = [Res(), Res()]
            Rzf = Res(); Rlf = Res(); Rqctok = Res(); RqcT = Res(); Rt16 = Res()
            d_w = P.dmasem("hw"); d_c = P.dmasem("hc"); d_x = [P.dmasem("hx%d" % i) for i in range(2)]
            d_o = [P.dmasem("ho%d" % i) for i in range(2)]
            Rxdram = self.Rxdram
            win_v = prm["hy_w_in"][e].rearrange("(c p) n -> p c n", p=128)
            wout_v = prm["hy_w_out"][e].rearrange("(c p) n -> p c n", p=128)
            for c in range(8):
                self.dma("pool", w_in[:, c, :], win_v[:, c, :], d_w, [], [Rw])
            for c in range(8):
                self.dma("pool", w_out[:, c, :], wout_v[:, c, :], d_w, [], [Rw])
            self.dma("pool", pw[:], prm["hy_pool_w"][e].rearrange("g c d -> c g d"), d_w, [], [Rw])
            self.dma("sp", gain_b[:], prm["mix_norm"][layer:layer + 1, :].partition_broadcast(128), d_c, [], [Rgain])
            qgv = prm["hy_q_gain"][e].rearrange("(d o) -> d o", o=1)
            kgv = prm["hy_k_gain"][e].rearrange("(d o) -> d o", o=1)
            for hp in range(2):
                self.dma("sp", qg[hp * 64:(hp + 1) * 64, :], qgv, d_c, [], [Rsmall])
                self.dma("sp", kg[hp * 64:(hp + 1) * 64, :], kgv, d_c, [], [Rsmall])
            self.dma("sp", pscale[:], prm["hy_pool_scale"][e].rearrange("(g p) -> p g", p=128), d_c, [], [Rsmall], slow=True)
            self.dma("sp", fb[:], prm["hy_f_bias"][e:e + 1, :].partition_broadcast(128), d_c, [], [Rsmall])
            self.memset("pool", carry[0][:], 0.0, [Rcarry[0]])
            self.memset("pool", upad[:, :, 0:16], 0.0, [Rupad])
            xcnt = 0; ocnt = 0; gcnt = 0; scnt = 0; pcnt = 0; ccnt = 0

            def proj_fm(col0, gi):
                pt, Rp = gp[gi % 2]
                for c in range(8):
                    self.mm(pt[:, :], w_in[:, c, col0:col0 + 128], hc[:, c, :], c == 0, c == 7, [Rw, Rhc], [Rp])
                return pt, Rp

            for I in range(NI):
                t0 = I * 512
                for i in range(4):
                    b = xcnt % 2
                    tix = I * 4 + i
                    self.dma("sp", xts[b][:], xin[t0 + i * 128:t0 + (i + 1) * 128, :], d_x[b], [Rxdram[tix]], [Rxt[b]])
                    self.norm_tile(xts[b], Rxt[b], gain_b, Rgain, hc, Rhc, i * 128, bufs, xcnt)
                    xcnt += 1
                for i in range(4):
                    blk = I * 4 + i
                    for c in range(8):
                        self.mm(smp[:, 0:8], hc[:, c, i * 128:(i + 1) * 128], w_in[:, c, 2048:2056], c == 0, c == 7, [Rw, Rhc], [Rsm])
                    self.tt("dve", zf[:], smp[:, 0:8], fb[:], ALU.add, [Rsm, Rsmall], [Rzf])
                    self.act(lf[:], zf[:], AF.Exp, [Rzf], [Rlf], scale=-1.0)
                    self.act(lf[:], lf[:], AF.Ln, [Rlf, Rc], [Rlf], bias=self.eps_rms[:, 2:3])
                    self.mm(smp[:, 8:16], cst["tri"][:], lf[:], True, True, [Rlf, Rc], [Rsm])
                    self.mm(smp[:, 16:24], cst["nones"][:], lf[:], True, True, [Rlf, Rc], [Rsm])
                    cin, cout = carry[ccnt % 2], carry[(ccnt + 1) % 2]
                    Rcin, Rcout = Rcarry[ccnt % 2], Rcarry[(ccnt + 1) % 2]
                    self.tt("dve", cumK[:, blk, :], smp[:, 8:16], cin[:], ALU.add, [Rsm, Rcin], [RcumK])
                    self.tt("dve", cout[:], smp[:, 16:24], cin[:], ALU.add, [Rsm, Rcin], [Rcout])
                    ccnt += 1
                cend, Rcend = carry[ccnt % 2], Rcarry[ccnt % 2]
                nj = 4 * I + 4
                cb4 = cend[:, :].unsqueeze(1).broadcast_to([128, 4, 8])
                self.tt("dve", qctok[:], cumK[:, 4 * I:4 * I + 4, :], cb4, ALU.subtract, [RcumK, Rcend], [Rqctok])
                cbn = cend[:, :].unsqueeze(1).broadcast_to([128, nj, 8])
                self.tt("dve", kb[:, 0:nj, :], cbn, cumK[:, 0:nj, :], ALU.subtract, [RcumK, Rcend], [Rkb])
                ptq, Rpq = gp[gcnt % 2]; gcnt += 1
                for i in range(4):
                    self.tr(ptq[0:8, i * 128:(i + 1) * 128], qctok[:, i, :], cst["identf"][:], [Rqctok, Rc], [Rpq])
                self.copy("dve", qcT[:, :], ptq[0:8, 0:512], [Rpq], [RqcT])
                for which in range(2):
                    for cc in range(4):
                        col0 = (512 if which == 0 else 0) + cc * 128
                        pt, Rp = proj_fm(col0, gcnt); gcnt += 1
                        self.copy("act", kf[:], pt[:, :], [Rp], [Rkf])
                        self.tt("pool", sq[:], kf[:], kf[:], ALU.mult, [Rkf], [Rsq])
                        pt2, Rp2 = gp[gcnt % 2]; gcnt += 1
                        self.mm(pt2[:, :], cst["bd"][:], sq[:], True, True, [Rsq, Rc], [Rp2])
                        self.act(rs[:], pt2[:, :], AF.Ln, [Rp2, Rc], [Rrs], scale=1.0 / 64, bias=self.eps_rms[:, 0:1])
                        self.act(rs[:], rs[:], AF.Exp, [Rrs], [Rrs], scale=-0.5)
                        if which == 0:
                            self.stt("dve", kT[:, cc, t0:t0 + 512], kf[:], kg[:, 0:1], rs[:], ALU.mult, ALU.mult,
                                     [Rkf, Rrs, Rsmall], [RkT[I]])
                        else:
                            self.stt("dve", qT[:, cc, :], kf[:], qg[:, 0:1], rs[:], ALU.mult, ALU.mult,
                                     [Rkf, Rrs, Rsmall], [RqT])
                for i in range(4):
                    blk = I * 4 + i
                    pt, Rp = gp[gcnt % 2]; gcnt += 1
                    for c in range(8):
                        self.mm(pt[:, :], hc[:, c, i * 128:(i + 1) * 128], w_in[:, c, 1024:1536], c == 0, c == 7, [Rw, Rhc], [Rp])
                    self.copy("act", vc[:, blk, :], pt[:, :], [Rp], [Rvc[blk]])
                for cc in range(4):
                    pt, Rp = proj_fm(1536 + cc * 128, gcnt); gcnt += 1
                    self.act(sgT[:, cc, :], pt[:, :], AF.Sigmoid, [Rp], [RsgT])
                for g in range(4):
                    pt, Rp = proj_fm(2056 + g * 128, gcnt); gcnt += 1
                    self.copy("act", upad[:, g, 16:528], pt[:, :], [Rp], [Rupad])
                for g in range(4):
                    u = upad[:, g, :]
                    cur, Rcur = u, Rupad
                    lo = 0
                    tmps = [(tA, RtA), (tB, RtB)]
                    for lvl in range(g + 1):
                        sh = 1 << lvl
                        dst, Rdst = tmps[lvl % 2]
                        nlo = lo + sh
                        self.tt("pool", dst[:, nlo:528], cur[:, nlo:528], cur[:, nlo - sh:528 - sh], ALU.add, [Rcur], [Rdst])
                        cur, Rcur, lo = dst, Rdst, nlo
                    wdt = 2 << g
                    self.stt("dve", pooledT[:, g, :], cur[:, 16:528], 1.0 / wdt, u[:, 16:528], ALU.mult, ALU.subtract,
                             [Rcur, Rupad], [Rpooled])
                    if I == 0:
                        self.tt("pool", t16[:], cur[:, 16:32], cst["invc"][:, g, :], ALU.mult, [Rcur, Rc], [Rt16])
                        self.tt("pool", pooledT[:, g, 0:16], t16[:], u[:, 16:32], ALU.subtract, [Rt16, Rupad], [Rpooled])
                self.copy("pool", upad[:, :, 0:16], upad[:, :, 512:528], [Rupad], [Rupad])
                for g in range(4):
                    pt, Rp = gp[gcnt % 2]; gcnt += 1
                    self.mm(pt[:, :], pw[:, g, :], pooledT[:, g, :], True, True, [Rw, Rpooled], [Rp])
                    self.act(hc[:, 4 + g, :], pt[:, :], AF.Copy, [Rp, Rsmall], [Rhc], scale=pscale[:, g:g + 1])
                for pr in range(4):
                    steps = []
                    for hp in range(2):
                        for j in range(nj):
                            steps.append((hp, j))

                    def qk(step, si):
                        hp, j = step
                        h = pr * 2 + hp
                        jj = j - 4 * I
                        c0 = 128 * jj if jj > 0 else 0
                        diag = jj >= 0
                        sbt, Rs = sbk[si % 2]
                        Ij = j // 4
                        self.mm(sbt[:, c0:512], kT[hp * 64:(hp + 1) * 64, pr, j * 128:(j + 1) * 128],
                                qT[hp * 64:(hp + 1) * 64, pr, c0:512], True, False, [RkT[Ij], RqT], [Rs])
                        self.mm(sbt[:, c0:512], cst["esel"][:, h, :], qcT[:, c0:512], False, not diag, [RqcT, Rc], [Rs])
                        if diag:
                            self.mm(sbt[:, c0:c0 + 128], self.ident[:], cst["negmask"][:], False, True, [Rc], [Rs])
                        return c0

                    def rest(step, si, c0):
                        hp, j = step
                        h = pr * 2 + hp
                        sbt, Rs = sbk[si % 2]
                        pt_, Rp_ = pT[si % 3], RpT[si % 3]
                        self.act(pt_[:, c0:512], sbt[:, c0:512], AF.Exp, [Rs, Rkb], [Rp_], scale=0.125, bias=kb[:, j, h:h + 1])
                        first = (j == 0)
                        last = (j == nj - 1)
                        self.mm(accN[hp * 64:(hp + 1) * 64, c0:512], vc[:, j, h * 64:(h + 1) * 64], pt_[:, c0:512],
                                first, last, [Rvc[j], Rp_], [RaccN], tile_position=(0, hp * 64))
                        self.mm(accD[hp * 64:(hp + 1) * 64, c0:512], cst["ones64"][:], pt_[:, c0:512],
                                first, last, [Rc, Rp_], [RaccD], tile_position=(0, hp * 64))

                    pend = None
                    for n, stp in enumerate(steps):
                        c0 = qk(stp, scnt + n)
                        if pend is not None:
                            rest(*pend)
                        pend = (stp, scnt + n, c0)
                    rest(*pend)
                    scnt += len(steps)
                    self.P.emit("dve", lambda e_: e_.reciprocal(out=rden[:], in_=accD[:, :]), [RaccD], [Rrden])
                    self.tt("dve", atmp[:], accN[:, :], rden[:], ALU.mult, [RaccN, Rrden], [Ratmp])
                    self.tt("pool", hc[:, pr, :], atmp[:], sgT[:, pr, :], ALU.mult, [Ratmp, RsgT], [Rhc])
                for i in range(4):
                    b = xcnt % 2
                    tix = I * 4 + i
                    self.dma("sp", xts[b][:], xin[t0 + i * 128:t0 + (i + 1) * 128, :], d_x[b], [Rxdram[tix]], [Rxt[b]])
                    xcnt += 1
                    for nh in range(2):
                        ob = ocnt % 2
                        pt, Rp = gp[gcnt % 2]; gcnt += 1
                        for c in range(8):
                            self.mm(pt[:, :], hc[:, c, i * 128:(i + 1) * 128], w_out[:, c, nh * 512:(nh + 1) * 512],
                                    c == 0, c == 7, [Rhc, Rw], [Rp])
                        self.tt("dve", ots[ob][:], pt[:, :], xts[b][:, nh * 512:(nh + 1) * 512], ALU.add, [Rp, Rxt[b]], [Rot[ob]])
                        self.dma("sp", xout[t0 + i * 128:t0 + (i + 1) * 128, nh * 512:(nh + 1) * 512], ots[ob][:], d_o[ob],
                                 [Rot[ob]], [Rxdram[tix]])
                        ocnt += 1
            P.barrier()


    def rw_phase(self, o, layer, xin, xout, prm):
        nc, P, S = self.nc, self.P, self.S
        NCH = S // 128
        Rc = self.Rconst
        first_layer = (o == 0)
        with ExitStack() as st:
            sb = lambda n, shp, dt: self.sb("r_" + n, shp, dt, st)
            Wr = sb("Wr", [128, 8, D], BF16); Wk = sb("Wk", [128, 8, D], BF16)
            Wv = sb("Wv", [128, 8, D], BF16); Wo = sb("Wo", [128, 8, D], BF16)
            l1 = sb("l1", [128, 8, 320], BF16)
            wa2 = sb("wa2", [128, D], BF16)
            g2a = sb("g2a", [128, D], BF16)
            gv2 = sb("gv2", [64, D], BF16)
            bc = sb("bc", [128, 8, D], BF16)
            gain_b = sb("gain", [128, D], F32)
            mu = sb("mu", [128, 6, 8], F32)
            IUf = sb("IUf", [128, 128], F32); SUf = sb("SUf", [128, 128], F32); SLf = sb("SLf", [128, 128], F32)
            onesf = sb("onesf", [128, 128], F32)
            tiny = sb("tiny", [128, 1], F32)
            W = [Rc]
            self.memset("pool", IUf[:], 1.0, W)
            self.P.emit("pool", lambda e: e.affine_select(out=IUf[:], in_=IUf[:], pattern=[[1, 128]], compare_op=ALU.is_ge,
                                                          fill=0.0, base=0, channel_multiplier=-1), (), W)
            self.memset("pool", SUf[:], 1.0, W)
            self.P.emit("pool", lambda e: e.affine_select(out=SUf[:], in_=SUf[:], pattern=[[1, 128]], compare_op=ALU.is_gt,
                                                          fill=0.0, base=0, channel_multiplier=-1), (), W)
            self.memset("pool", SLf[:], 1.0, W)
            self.P.emit("pool", lambda e: e.affine_select(out=SLf[:], in_=SLf[:], pattern=[[-1, 128]], compare_op=ALU.is_gt,
                                                          fill=0.0, base=0, channel_multiplier=1), (), W)
            self.memset("pool", onesf[:], 1.0, W)
            self.memset("pool", tiny[:], 1e-24, W)
            xt = sb("xt", [128, D], F32); hb = sb("hb", [128, D], BF16)
            hTe = sb("hTe", [128, 8, 129], BF16)
            xm = [sb("xm%d" % i, [128, 8, 128], BF16) for i in range(2)]
            xx = sb("xx", [128, 8, 128], BF16)
            sc = [sb("sc%d" % i, [128, D], F32) for i in range(5)]
            r_bf = sb("r_bf", [128, D], BF16); kp_bf = sb("kp_bf", [128, D], BF16); kk_bf = sb("kk_bf", [128, D], BF16)
            b_bf = sb("b_bf", [128, D], BF16); v_bf = sb("v_bf", [128, D], BF16); g_bf = sb("g_bf", [128, D], BF16)
            prod = [sb("prod%d" % i, [128, D], BF16) for i in range(2)]
            Khat = sb("Khat", [128, D], BF16); Bhat = sb("Bhat", [128, D], BF16)
            RtT = sb("RtT", [128, 8, 128], BF16); KtT = sb("KtT", [128, 8, 128], BF16)
            BtT = sb("BtT", [128, 8, 128], BF16); AtT = sb("AtT", [128, 8, 128], BF16)
            lsb = sb("lsb", [128, 512], BF16)
            Mb = [sb("Mb%d" % i, [128, 8, 128], BF16) for i in range(2)]
            Nb = [sb("Nb%d" % i, [128, 8, 128], BF16) for i in range(2)]
            Xb = [sb("Xb%d" % i, [128, 8, 128], BF16) for i in range(2)]
            XT = sb("XT", [128, 16, 128], BF16)
            AakT = sb("AakT", [128, 16, 128], BF16); ArbT = sb("ArbT", [128, 16, 128], BF16); ArkT = sb("ArkT", [128, 16, 128], BF16)
            RHS = sb("RHS", [128, D], BF16); U = sb("U", [128, D], BF16)
            ST = sb("ST", [128, 8, 64], F32); STb = sb("STb", [128, 8, 64], BF16); STt = sb("STt", [128, 8, 64], F32)
            PCc = sb("PCc", [128, 8], F32)
            st16 = sb("st16", [128, 8, 16], F32)
            yfin = sb("yfin", [128, D], BF16); yT = sb("yT", [128, 8, 128], BF16)
            ot = sb("ot", [128, D], F32)
            ss = sb("ss", [128, 4], F32)
            tp = [(self.ps("r_tp%d" % i, BF16, st), Res()) for i in range(2)]
            gpool = [(self.ps("r_gp%d" % i, F32, st), Res()) for i in range(6)]
            self._gi = 0

            def bank():
                b_ = gpool[self._gi % 6]
                self._gi += 1
                return b_

            self._ti = 0

            def tbank():
                b_ = tp[self._ti % 2]
                self._ti += 1
                return b_

            R = {k: Res(k) for k in ("w", "small", "xt", "hb", "hTe", "xx", "ss", "r_bf", "kp_bf", "kk_bf", "b_bf", "v_bf", "g_bf",
                                     "Khat", "Bhat", "RtT", "KtT", "BtT", "AtT", "lsb", "XT", "AakT", "ArbT", "ArkT", "RHS", "U",
                                     "ST", "STb", "STt", "PCc", "st16", "yfin", "yT", "ot", "gain")}
            Rsc = [Res() for _ in range(5)]; Rxm = [Res(), Res()]; Rprod = [Res(), Res()]
            RMb = [Res(), Res()]; RNb = [Res(), Res()]; RXb = [Res(), Res()]
            d_w = P.dmasem("rw"); d_c = P.dmasem("rc"); d_x = P.dmasem("rx"); d_o = P.dmasem("ro"); d_v = P.dmasem("rv")
            Rxdram = self.Rxdram
            Rvf = self.Rvf

            def wview(name):
                return prm[name][o].rearrange("(c p) n -> p c n", p=128)
            for Wt, nm in ((Wr, "rw_w_r"), (Wk, "rw_w_k"), (Wv, "rw_w_v"), (Wo, "rw_w_o")):
                v_ = wview(nm)
                for c in range(8):
                    self.dma("pool", Wt[:, c, :], v_[:, c, :], d_w, [], [R["w"]])
            self.dma("pool", l1[:, :, 0:64], wview("rw_w1"), d_w, [], [R["w"]])
            self.dma("pool", l1[:, :, 64:128], wview("rw_a1"), d_w, [], [R["w"]])
            self.dma("pool", l1[:, :, 128:288], wview("rw_g1"), d_w, [], [R["w"]])
            self.dma("pool", wa2[0:64, :], prm["rw_w2"][o], d_w, [], [R["w"]])
            self.dma("pool", wa2[64:128, :], prm["rw_a2"][o], d_w, [], [R["w"]])
            self.dma("pool", g2a[:, :], prm["rw_g2"][o][0:128, :], d_w, [], [R["w"]])
            self.dma("pool", gv2[0:32, :], prm["rw_g2"][o][128:160, :], d_w, [], [R["w"]])
            if not first_layer:
                self.dma("pool", l1[:, :, 288:320], prm["rw_v1"][o - 1].rearrange("(c p) n -> p c n", p=128), d_w, [], [R["w"]])
                self.dma("pool", gv2[32:64, :], prm["rw_v2"][o - 1], d_w, [], [R["w"]])
            rows = [prm["rw_w0"][o:o + 1, :], prm["rw_a0"][o:o + 1, :],
                    (prm["rw_v0"][o - 1:o, :] if not first_layer else prm["rw_w0"][o:o + 1, :]),
                    prm["rw_k_k"][o:o + 1, :], prm["rw_k_a"][o:o + 1, :], prm["rw_ln_w"][o:o + 1, :], prm["rw_ln_b"][o:o + 1, :],
                    prm["rw_r_k"][o:o + 1].rearrange("o h d -> o (h d)")]
            for i, rv in enumerate(rows):
                self.dma("pool", bc[:, i, :], rv.partition_broadcast(128), d_w, [], [R["w"]])
            W0, A0, V0, KK_, KA_, LNW, LNB, RK_ = (bc[:, i, :] for i in range(8))
            self.dma("sp", gain_b[:], prm["mix_norm"][layer:layer + 1, :].partition_broadcast(128), d_c, [], [R["gain"]])
            self.dma("sp", mu[:], prm["rw_mu"][o].rearrange("i (c p) -> p i c", p=128), d_c, [], [R["small"]], slow=True)
            self.memset("pool", ST[:], 0.0, [R["ST"]])
            self.memset("pool", STb[:], 0.0, [R["STb"]])
            self.memset("pool", hTe[:, :, 128:129], 0.0, [R["hTe"]])
            bufs = {"junk": [(yfin, R["yfin"])] * 2, "ss": [(ss, R["ss"])] * 2, "hb": [(hb, R["hb"])] * 2, "tp": tp}
            IUb = lambda n_: IUf[:, :].unsqueeze(1).broadcast_to([128, n_, 128])
            SUb = lambda n_: SUf[:, :].unsqueeze(1).broadcast_to([128, n_, 128])
            SLb = lambda n_: SLf[:, :].unsqueeze(1).broadcast_to([128, n_, 128])
            IDb = lambda n_: self.ident[:, :].unsqueeze(1).broadcast_to([128, n_, 128])
            h3 = lambda ap: ap.rearrange("p (h d) -> p h d", d=64)

            def proj_tok(xT, Rx, Wt, lo=0):
                bks = []
                for nh in range(2):
                    pt, Rp = bank()
                    for c in range(8):
                        self.mm(pt[:, :], xT[:, c, :], Wt[:, c, nh * 512:(nh + 1) * 512], c == 0, c == 7, [Rx, R["w"]], [Rp])
                    bks.append((pt, Rp))
                return bks

            def mix(i, j):
                for c in range(8):
                    self.stt("dve", xm[j][:, c, :], xx[:, c, :], mu[:, i, c:c + 1], hTe[:, c, 1:129], ALU.mult, ALU.add,
                             [R["xx"], R["hTe"], R["small"]], [Rxm[j]])
                return xm[j], Rxm[j]

            def evac2(bks, fn):
                for nh, (pt, Rp) in enumerate(bks):
                    fn(nh, pt, Rp, slice(nh * 512, (nh + 1) * 512))

            import os as _os
            stop = int(_os.environ.get("RW_STOP", "99"))

            def early(n_, t0_):
                self.copy("dve", ot[:], xt[:], [R["xt"]], [R["ot"]])
                self.dma("sp", xout[t0_:t0_ + 128, :], ot[:], d_o, [R["ot"]], [Rxdram[n_]])

            for n in range(NCH):
                t0 = n * 128
                self.copy("pool", hTe[:, :, 0:1], hTe[:, :, 128:129], [R["hTe"]], [R["hTe"]])
                self.dma("sp", xt[:], xin[t0:t0 + 128, :], d_x, [Rxdram[n]], [R["xt"]])
                self.norm_tile(xt, R["xt"], gain_b, R["gain"], hTe[:, :, 1:129], R["hTe"], 0, bufs, n)
                self.tt("pool", xx[:], hTe[:, :, 0:128], hTe[:, :, 1:129], ALU.subtract, [R["hTe"]], [R["xx"]])
                if stop <= 1:
                    early(n, t0)
                    continue
                xr, Rxr = mix(0, 0)
                evac2(proj_tok(xr, Rxr, Wr), lambda nh, pt, Rp, sl: self.copy("act", r_bf[:, sl], pt[:, :], [Rp], [R["r_bf"]]))
                xk, Rxk = mix(2, 1)
                evac2(proj_tok(xk, Rxk, Wk), lambda nh, pt, Rp, sl: self.copy("act", sc[0][:, sl], pt[:, :], [Rp], [Rsc[0]]))
                xv, Rxv = mix(3, 0)
                evac2(proj_tok(xv, Rxv, Wv), lambda nh, pt, Rp, sl: self.copy("act", sc[1][:, sl], pt[:, :], [Rp], [Rsc[1]]))
                lp, Rlp = bank()
                if not first_layer:
                    for c in range(8):
                        self.mm(lp[32:64, 256:384], l1[:, c, 288:320], xv[:, c, :], c == 0, c == 7, [Rxv, R["w"]], [Rlp],
                                tile_position=(0, 32))
                xw, Rxw = mix(1, 1)
                for c in range(8):
                    self.mm(lp[0:64, 0:128], l1[:, c, 0:64], xw[:, c, :], c == 0, c == 7, [Rxw, R["w"]], [Rlp])
                xa, Rxa = mix(4, 0)
                for c in range(8):
                    self.mm(lp[64:128, 0:128], l1[:, c, 64:128], xa[:, c, :], c == 0, c == 7, [Rxa, R["w"]], [Rlp],
                            tile_position=(0, 64))
                xg, Rxg = mix(5, 1)
                for c in range(8):
                    self.mm(lp[:, 128:256], l1[:, c, 128:256], xg[:, c, :], c == 0, c == 7, [Rxg, R["w"]], [Rlp])
                for c in range(8):
                    self.mm(lp[0:32, 256:384], l1[:, c, 256:288], xg[:, c, :], c == 0, c == 7, [Rxg, R["w"]], [Rlp])
                self.act(lsb[0:64, 0:128], lp[0:64, 0:128], AF.Tanh, [Rlp], [R["lsb"]])
                self.copy("act", lsb[64:128, 0:128], lp[64:128, 0:128], [Rlp], [R["lsb"]])
                self.act(lsb[:, 128:256], lp[:, 128:256], AF.Sigmoid, [Rlp], [R["lsb"]])
                self.act(lsb[0:32, 256:384], lp[0:32, 256:384], AF.Sigmoid, [Rlp], [R["lsb"]])
                if not first_layer:
                    self.copy("act", lsb[32:64, 256:384], lp[32:64, 256:384], [Rlp], [R["lsb"]])
                for nh in range(2):
                    sl = slice(nh * 512, (nh + 1) * 512)
                    pt, Rp = bank()
                    self.mm(pt[:, :], lsb[0:64, 0:128], wa2[0:64, sl], True, True, [R["lsb"], R["w"]], [Rp])
                    self.tt("dve", sc[2][:, sl], pt[:, :], W0[:, sl], ALU.add, [Rp, R["w"]], [Rsc[2]])
                self.act(sc[2][:], sc[2][:], AF.Sigmoid, [Rsc[2]], [Rsc[2]])
                self.ts("pool", sc[2][:], sc[2][:], -float(np.exp(-0.5)), None, ALU.mult, None, [Rsc[2]], [Rsc[2]])
                for nh in range(2):
                    sl = slice(nh * 512, (nh + 1) * 512)
                    pt, Rp = bank()
                    self.mm(pt[:, :], lsb[64:128, 0:128], wa2[64:128, sl], True, True, [R["lsb"], R["w"]], [Rp])
                    self.tt("dve", sc[3][:, sl], pt[:, :], A0[:, sl], ALU.add, [Rp, R["w"]], [Rsc[3]])
                self.act(sc[3][:], sc[3][:], AF.Sigmoid, [Rsc[3]], [Rsc[3]])
                for nh in range(2):
                    sl = slice(nh * 512, (nh + 1) * 512)
                    pt, Rp = bank()
                    self.mm(pt[:, :], lsb[:, 128:256], g2a[:, sl], True, False, [R["lsb"], R["w"]], [Rp])
                    self.mm(pt[:, :], lsb[0:32, 256:384], gv2[0:32, sl], False, True, [R["lsb"], R["w"]], [Rp])
                    self.copy("act", g_bf[:, sl], pt[:, :], [Rp], [R["g_bf"]])
                if first_layer:
                    self.dma("sp", self.vfirst[t0:t0 + 128, :], sc[1][:], d_v, [Rsc[1]], [Rvf[n]])
                else:
                    for nh in range(2):
                        sl = slice(nh * 512, (nh + 1) * 512)
                        pt, Rp = bank()
                        self.mm(pt[:, :], lsb[32:64, 256:384], gv2[32:64, sl], True, True, [R["lsb"], R["w"]], [Rp])
                        self.tt("dve", sc[4][:, sl], pt[:, :], V0[:, sl], ALU.add, [Rp, R["w"]], [Rsc[4]])
                    self.act(sc[4][:], sc[4][:], AF.Sigmoid, [Rsc[4]], [Rsc[4]])
                    self.dma("sp", ot[:], self.vfirst[t0:t0 + 128, :], d_v, [Rvf[n]], [R["ot"]])
                    self.tt("pool", ot[:], ot[:], sc[1][:], ALU.subtract, [R["ot"], Rsc[1]], [R["ot"]])
                    self.tt("pool", ot[:], ot[:], sc[4][:], ALU.mult, [R["ot"], Rsc[4]], [R["ot"]])
                    self.tt("pool", sc[1][:], sc[1][:], ot[:], ALU.add, [R["ot"], Rsc[1]], [Rsc[1]])
                self.copy("pool", v_bf[:], sc[1][:], [Rsc[1]], [R["v_bf"]])
                k32, a32 = sc[0], sc[3]
                self.tt("dve", sc[4][:], k32[:], KK_, ALU.mult, [Rsc[0], R["w"]], [Rsc[4]])
                self.tt("pool", sc[1][:], sc[4][:], sc[4][:], ALU.mult, [Rsc[4], R["v_bf"]], [Rsc[1]])
                self.P.emit("dve", lambda e_: e_.tensor_reduce(out=st16[:, 0, :], in_=h3(sc[1][:, :]), axis=AX.X, op=ALU.add),
                            [Rsc[1]], [R["st16"]])
                self.act(st16[:, 1, :], st16[:, 0, :], AF.Ln, [R["st16"], Rc], [R["st16"]], bias=tiny[:, 0:1])
                self.act(st16[:, 1, :], st16[:, 1, :], AF.Exp, [R["st16"]], [R["st16"]], scale=-0.5)
                rnb = st16[:, 1, :].unsqueeze(2).broadcast_to([128, 16, 64])
                self.tt("dve", h3(kk_bf[:, :]), h3(sc[4][:, :]), rnb, ALU.mult, [Rsc[4], R["st16"]], [R["kk_bf"]])
                self.stt("dve", sc[1][:], a32[:], -1.0, KA_, ALU.add, ALU.mult, [Rsc[3], R["w"]], [Rsc[1]])
                self.stt("dve", kp_bf[:], sc[1][:], 1.0, k32[:], ALU.add, ALU.mult, [Rsc[1], Rsc[0]], [R["kp_bf"]])
                self.tt("pool", b_bf[:], kk_bf[:], a32[:], ALU.mult, [R["kk_bf"], Rsc[3]], [R["b_bf"]])
                self.tt("pool", sc[4][:], r_bf[:], kp_bf[:], ALU.mult, [R["r_bf"], R["kp_bf"]], [Rsc[4]])
                self.tt("pool", sc[4][:], sc[4][:], RK_, ALU.mult, [Rsc[4], R["w"]], [Rsc[4]])
                self.P.emit("dve", lambda e_: e_.tensor_reduce(out=st16[:, 2, :], in_=h3(sc[4][:, :]), axis=AX.X, op=ALU.add),
                            [Rsc[4]], [R["st16"]])
                if stop <= 2:
                    early(n, t0)
                    continue
                ld = sc[2]
                for nh in range(2):
                    sl = slice(nh * 512, (nh + 1) * 512)
                    pt, Rp = bank()
                    self.mm(pt[:, :], IUf[:], ld[:, sl], True, True, [Rsc[2], Rc], [Rp])
                    self.act(sc[0][:, sl], pt[:, :], AF.Exp, [Rp], [Rsc[0]])
                    self.act(sc[1][:, sl], pt[:, :], AF.Exp, [Rp], [Rsc[1]], scale=-1.0)
                for nh in range(2):
                    sl = slice(nh * 512, (nh + 1) * 512)
                    pt, Rp = bank()
                    self.mm(pt[:, :], SUf[:], ld[:, sl], True, True, [Rsc[2], Rc], [Rp])
                    self.act(sc[3][:, sl], pt[:, :], AF.Exp, [Rp], [Rsc[3]])
                for nh in range(2):
                    sl = slice(nh * 512, (nh + 1) * 512)
                    pt, Rp = bank()
                    self.mm(pt[:, :], onesf[:], ld[:, sl], True, True, [Rsc[2], Rc], [Rp])
                    self.act(sc[4][:, sl], pt[:, :], AF.Exp, [Rp], [Rsc[4]])
                self.tt("pool", sc[4][:], sc[4][:], sc[1][:], ALU.mult, [Rsc[4], Rsc[1]], [Rsc[4]])
                pt, Rp = bank()
                for c in range(8):
                    self.mm(pt[:, c:c + 1], ld[:, c * 128:(c + 1) * 128], onesf[:, 0:1], True, True, [Rsc[2], Rc], [Rp])
                self.act(PCc[:], pt[:, 0:8], AF.Exp, [Rp], [R["PCc"]])
                if stop <= 3:
                    early(n, t0)
                    continue
                def prod_T(j, eng, in0, Rin0, in1, Rin1, dstT, RdstT, neg=False):
                    if neg:
                        self.stt("dve", prod[j][:], in0, -1.0, in1, ALU.mult, ALU.mult, [Rin0, Rin1], [Rprod[j]])
                    else:
                        self.tt(eng, prod[j][:], in0, in1, ALU.mult, [Rin0, Rin1], [Rprod[j]])
                    tpt, Rtp = tbank()
                    for c in range(8):
                        self.tr(tpt[:, c * 128:(c + 1) * 128], prod[j][:, c * 128:(c + 1) * 128], self.ident[:], [Rprod[j], Rc], [Rtp])
                    self.copy("act", dstT[:, :, :], tpt[:, :].rearrange("p (c t) -> p c t", c=8), [Rtp], [RdstT])
                prod_T(0, "dve", r_bf[:], R["r_bf"], sc[0][:], Rsc[0], RtT, R["RtT"])
                prod_T(1, "pool", kp_bf[:], R["kp_bf"], sc[1][:], Rsc[1], KtT, R["KtT"])
                prod_T(0, "dve", b_bf[:], R["b_bf"], sc[1][:], Rsc[1], BtT, R["BtT"])
                prod_T(1, "dve", kk_bf[:], R["kk_bf"], sc[3][:], Rsc[3], AtT, R["AtT"], neg=True)
                self.tt("pool", Khat[:], kp_bf[:], sc[4][:], ALU.mult, [R["kp_bf"], Rsc[4]], [R["Khat"]])
                self.tt("dve", Bhat[:], b_bf[:], sc[4][:], ALU.mult, [R["b_bf"], Rsc[4]], [R["Bhat"]])
                if stop <= 4:
                    early(n, t0)
                    continue
                def hv(T, h):
                    return T[(h % 2) * 64:(h % 2) * 64 + 64, h // 2, :]
                for half in range(2):
                    hs = range(half * 8, half * 8 + 8)
                    hb0 = half * 8
                    v4 = lambda pt_: pt_[:, :].rearrange("p (h t) -> p h t", h=4)

                    def amat(lhs_T, Rl, rhs_T, Rr, dst, dbase, maskb, Rdst):
                        (pe_, Rpe), (po_, Rpo) = bank(), bank()
                        for i in range(4):
                            he, ho = hb0 + 2 * i, hb0 + 2 * i + 1
                            self.mm(pe_[:, i * 128:(i + 1) * 128], hv(lhs_T, he), hv(rhs_T, he), True, True, [Rl, Rr], [Rpe])
                            self.mm(po_[:, i * 128:(i + 1) * 128], hv(lhs_T, ho), hv(rhs_T, ho), True, True, [Rl, Rr], [Rpo])
                        self.tt("dve", dst[:, dbase + 0:dbase + 8:2, :], v4(pe_), maskb, ALU.mult, [Rpe, Rc], [Rdst])
                        self.tt("dve", dst[:, dbase + 1:dbase + 8:2, :], v4(po_), maskb, ALU.mult, [Rpo, Rc], [Rdst])

                    amat(BtT, R["BtT"], AtT, R["AtT"], Mb[0], 0, SUb(4), RMb[0])
                    amat(AtT, R["AtT"], BtT, R["BtT"], Nb[0], 0, SLb(4), RNb[0])
                    amat(BtT, R["BtT"], RtT, R["RtT"], ArbT, hb0, IUb(4), R["ArbT"])
                    amat(KtT, R["KtT"], AtT, R["AtT"], AakT, hb0, SUb(4), R["AakT"])
                    amat(KtT, R["KtT"], RtT, R["RtT"], ArkT, hb0, IUb(4), R["ArkT"])
                    if stop <= 5:
                        continue
                    self.tt("pool", Xb[0][:], Mb[0][:], IDb(8), ALU.add, [RMb[0], Rc], [RXb[0]])
                    cm, cn, cx = 0, 0, 0
                    for k in range(1, 7):
                        for q4 in range(2):
                            pt, Rp = bank()
                            for i in range(4):
                                hh = q4 * 4 + i
                                self.mm(pt[:, i * 128:(i + 1) * 128], Mb[cm][:, hh, :], Nb[cn][:, hh, :], True, True, [RMb[cm], RNb[cn]], [Rp])
                            self.copy("act", Nb[1 - cn][:, q4 * 4:q4 * 4 + 4, :], pt[:, :].rearrange("p (h t) -> p h t", h=4), [Rp], [RNb[1 - cn]])
                        if k < 6:
                            for q4 in range(2):
                                pt, Rp = bank()
                                for i in range(4):
                                    hh = q4 * 4 + i
                                    self.mm(pt[:, i * 128:(i + 1) * 128], Nb[cn][:, hh, :], Mb[cm][:, hh, :], True, True, [RMb[cm], RNb[cn]], [Rp])
                                self.copy("act", Mb[1 - cm][:, q4 * 4:q4 * 4 + 4, :], pt[:, :].rearrange("p (h t) -> p h t", h=4), [Rp], [RMb[1 - cm]])
                        cn = 1 - cn
                        if k < 6:
                            cm = 1 - cm
                        for q4 in range(2):
                            pt, Rp = bank()
                            for i in range(4):
                                hh = q4 * 4 + i
                                self.mm(pt[:, i * 128:(i + 1) * 128], Nb[cn][:, hh, :], Xb[cx][:, hh, :], True, True, [RNb[cn], RXb[cx]], [Rp])
                            if k < 6:
                                dst, Rdst = Xb[1 - cx][:, q4 * 4:q4 * 4 + 4, :], RXb[1 - cx]
                            else:
                                dst, Rdst = XT[:, half * 8 + q4 * 4:half * 8 + q4 * 4 + 4, :], R["XT"]
                            self.tt("dve", dst, pt[:, :].rearrange("p (h t) -> p h t", h=4), Xb[cx][:, q4 * 4:q4 * 4 + 4, :], ALU.add,
                                    [Rp, RXb[cx]], [Rdst])
                        cx = 1 - cx
                if stop <= 6:
                    early(n, t0)
                    continue
                sthv = lambda h: STb[(h % 2) * 64:(h % 2) * 64 + 64, h // 2, :]
                hc_ = lambda T, h: T[:, h * 64:(h + 1) * 64]
                bks = [bank(), bank()]
                for h in range(16):
                    pt, Rp = bks[h // 8]
                    o_ = pt[:, (h % 8) * 64:(h % 8) * 64 + 64]
                    self.mm(o_, hv(AtT, h), sthv(h), True, False, [R["AtT"], R["STb"]], [Rp])
                    self.mm(o_, AakT[:, h, :], hc_(v_bf, h), False, True, [R["AakT"], R["v_bf"]], [Rp])
                for nh, (pt, Rp) in enumerate(bks):
                    self.copy("act", RHS[:, nh * 512:(nh + 1) * 512], pt[:, :], [Rp], [R["RHS"]])
                bks = [bank(), bank()]
                for h in range(16):
                    pt, Rp = bks[h // 8]
                    self.mm(pt[:, (h % 8) * 64:(h % 8) * 64 + 64], XT[:, h, :], hc_(RHS, h), True, True, [R["XT"], R["RHS"]], [Rp])
                for nh, (pt, Rp) in enumerate(bks):
                    self.copy("act", U[:, nh * 512:(nh + 1) * 512], pt[:, :], [Rp], [R["U"]])
                bks = [bank(), bank()]
                for h in range(16):
                    pt, Rp = bks[h // 8]
                    o_ = pt[:, (h % 8) * 64:(h % 8) * 64 + 64]
                    self.mm(o_, hv(RtT, h), sthv(h), True, False, [R["RtT"], R["STb"]], [Rp])
                    self.mm(o_, ArbT[:, h, :], hc_(U, h), False, False, [R["ArbT"], R["U"]], [Rp])
                    self.mm(o_, ArkT[:, h, :], hc_(v_bf, h), False, True, [R["ArkT"], R["v_bf"]], [Rp])
                for nh, (pt, Rp) in enumerate(bks):
                    self.copy("act", sc[0][:, nh * 512:(nh + 1) * 512], pt[:, :], [Rp], [Rsc[0]])
                pt, Rp = bank()
                for h in range(16):
                    o_ = pt[(h % 2) * 64:(h % 2) * 64 + 64, (h // 2) * 64:(h // 2) * 64 + 64]
                    self.mm(o_, hc_(Bhat, h), hc_(U, h), True, False, [R["Bhat"], R["U"]], [Rp], tile_position=(0, (h % 2) * 64))
                    self.mm(o_, hc_(Khat, h), hc_(v_bf, h), False, True, [R["Khat"], R["v_bf"]], [Rp], tile_position=(0, (h % 2) * 64))
                pcb = PCc[:, :].unsqueeze(2).broadcast_to([128, 8, 64])
                self.tt("pool", STt[:], ST[:], pcb, ALU.mult, [R["ST"], R["PCc"]], [R["STt"]])
                self.tt("dve", ST[:], STt[:], pt[:, :].rearrange("p (c v) -> p c v", c=8), ALU.add, [R["STt"], Rp], [R["ST"]])
                self.copy("pool", STb[:], ST[:], [R["ST"]], [R["STb"]])
                if stop <= 7:
                    early(n, t0)
                    continue
                y = sc[0]
                self.P.emit("dve", lambda e_: e_.tensor_reduce(out=st16[:, 3, :], in_=h3(y[:, :]), axis=AX.X, op=ALU.add),
                            [Rsc[0]], [R["st16"]])
                self.tt("pool", sc[1][:], y[:], y[:], ALU.mult, [Rsc[0]], [Rsc[1]])
                self.P.emit("dve", lambda e_: e_.tensor_reduce(out=st16[:, 4, :], in_=h3(sc[1][:, :]), axis=AX.X, op=ALU.add),
                            [Rsc[1]], [R["st16"]])
                self.ts("dve", st16[:, 3, :], st16[:, 3, :], 1.0 / 64, None, ALU.mult, None, [R["st16"]], [R["st16"]])
                self.tt("dve", st16[:, 5, :], st16[:, 3, :], st16[:, 3, :], ALU.mult, [R["st16"]], [R["st16"]])
                self.stt("dve", st16[:, 4, :], st16[:, 4, :], 1.0 / 64, st16[:, 5, :], ALU.mult, ALU.subtract, [R["st16"]], [R["st16"]])
                self.act(st16[:, 4, :], st16[:, 4, :], AF.Ln, [R["st16"], Rc], [R["st16"]], bias=self.eps_rms[:, 1:2])
                self.act(st16[:, 4, :], st16[:, 4, :], AF.Exp, [R["st16"]], [R["st16"]], scale=-0.5)
                mb_ = st16[:, 3, :].unsqueeze(2).broadcast_to([128, 16, 64])
                rb_ = st16[:, 4, :].unsqueeze(2).broadcast_to([128, 16, 64])
                bb_ = st16[:, 2, :].unsqueeze(2).broadcast_to([128, 16, 64])
                self.tt("dve", h3(sc[1][:, :]), h3(y[:, :]), mb_, ALU.subtract, [Rsc[0], R["st16"]], [Rsc[1]])
                self.tt("dve", h3(sc[1][:, :]), h3(sc[1][:, :]), rb_, ALU.mult, [Rsc[1], R["st16"]], [Rsc[1]])
                self.tt("pool", sc[1][:], sc[1][:], LNW, ALU.mult, [Rsc[1], R["w"]], [Rsc[1]])
                self.tt("pool", sc[1][:], sc[1][:], LNB, ALU.add, [Rsc[1], R["w"]], [Rsc[1]])
                self.tt("dve", h3(sc[3][:, :]), h3(v_bf[:, :]), bb_, ALU.mult, [R["v_bf"], R["st16"]], [Rsc[3]])
                self.tt("pool", sc[1][:], sc[1][:], sc[3][:], ALU.add, [Rsc[1], Rsc[3]], [Rsc[1]])
                self.tt("dve", yfin[:], sc[1][:], g_bf[:], ALU.mult, [Rsc[1], R["g_bf"]], [R["yfin"]])
                tpt, Rtp = tbank()
                for c in range(8):
                    self.tr(tpt[:, c * 128:(c + 1) * 128], yfin[:, c * 128:(c + 1) * 128], self.ident[:], [R["yfin"], Rc], [Rtp])
                self.copy("act", yT[:, :, :], tpt[:, :].rearrange("p (c t) -> p c t", c=8), [Rtp], [R["yT"]])
                for nh, (pt, Rp) in enumerate(proj_tok(yT, R["yT"], Wo)):
                    sl = slice(nh * 512, (nh + 1) * 512)
                    self.tt("dve", ot[:, sl], pt[:, :], xt[:, sl], ALU.add, [Rp, R["xt"]], [R["ot"]])
                self.dma("sp", xout[t0:t0 + 128, :], ot[:], d_o, [R["ot"]], [Rxdram[n]])
            P.barrier()


def build_program(S, sublayers, n_cores=8):
    nc = bass.Bass("TRN2", target_bir_lowering=False)
    specs = param_specs()
    prm = {}
    x = nc.dram_tensor("x", [S, D], F32, kind="ExternalInput").ap()
    for name, shp in specs.items():
        prm[name] = nc.dram_tensor(name, list(shp), F32, kind="ExternalInput").ap()
    out = nc.dram_tensor("out", [S, D], F32, kind="ExternalOutput").ap()
    with ExitStack() as st:
        kb = KB(nc, S, st)
        kb.Rxdram = [Res() for _ in range(S // 128)]
        kb.Rvf = [Res() for _ in range(S // 128)]
        kb.vfirst = nc.dram_tensor("vfirst_scratch", [S, D], F32).ap()
        kb.setup_consts()
        cur = x
        for sl in sublayers:
            if sl[0] == "ffn":
                kb.ffn_phase(sl[1], cur, out, prm)
            elif sl[0] == "hy":
                kb.hy_phase(sl[1], sl[2], cur, out, prm)
            elif sl[0] == "rw":
                kb.rw_phase(sl[1], sl[2], cur, out, prm)
            cur = out
        kb.P.barrier()
        kb.P.finalize()
        kb.stats = (dict(kb.P.n), kb.P.nwaits)
        print("instr counts", kb.P.n, "waits", kb.P.nwaits)
    return nc


def param_specs():
    return {
        "mix_norm": (4, D), "ffn_norm": (4, D),
        "ffn_w_gate": (4, D, DFF), "ffn_w_up": (4, D, DFF), "ffn_w_down": (4, DFF, D),
        "hy_w_in": (2, D, IN_COLS), "hy_f_bias": (2, 8), "hy_q_gain": (2, 64), "hy_k_gain": (2, 64),
        "hy_pool_w": (2, 4, 128, 128), "hy_pool_scale": (2, 512), "hy_w_out": (2, D, D),
        "rw_mu": (2, 6, D), "rw_w_r": (2, D, D), "rw_w_k": (2, D, D), "rw_w_v": (2, D, D),
        "rw_w0": (2, D), "rw_w1": (2, D, 64), "rw_w2": (2, 64, D), "rw_a0": (2, D), "rw_a1": (2, D, 64),
        "rw_a2": (2, 64, D), "rw_g1": (2, D, 160), "rw_g2": (2, 160, D), "rw_k_k": (2, D), "rw_k_a": (2, D),
        "rw_r_k": (2, 16, 64), "rw_ln_w": (2, D), "rw_ln_b": (2, D), "rw_w_o": (2, D, D),
        "rw_v0": (1, D), "rw_v1": (1, D, 32), "rw_v2": (1, 32, D),
    }


FULL = [("hy", 0, 0), ("ffn", 0), ("rw", 0, 1), ("ffn", 1), ("hy", 1, 2), ("ffn", 2), ("rw", 1, 3), ("ffn", 3)]


def run(inputs, S, sublayers, n_cores=8, trace=False):
    nc = build_program(S, sublayers)
    specs = param_specs()
    x = np.ascontiguousarray(np.asarray(inputs["x"], dtype=np.float32))
    shared = {k: np.ascontiguousarray(np.asarray(inputs[k], dtype=np.float32)) for k in specs}
    in_maps = []
    for c in range(n_cores):
        m = dict(shared)
        m["x"] = x[c]
        in_maps.append(m)
    res = run_bass_kernel_spmd(nc, in_maps, core_ids=list(range(n_cores)), trace=trace)
    outs = np.stack([np.asarray(r["out"]) for r in res.results], axis=0)
    return outs, res


def kernel(**inputs):
    outs, _ = run(inputs, 4096, FULL, n_cores=8)
    return outs.astype(np.float32)
```

```python
import numpy as np
from contextlib import ExitStack
import concourse.bass as bass
import concourse.mybir as mybir
from concourse.bass_utils import run_bass_kernel_spmd

F32 = mybir.dt.float32
BF16 = mybir.dt.bfloat16
AF = mybir.ActivationFunctionType
ALU = mybir.AluOpType
AX = mybir.AxisListType

D = 1024
DFF = 2816
NFC = DFF // 128
IN_COLS = 2568
RMS_EPS = 1e-6
GN_EPS = 64e-5
ENGS = ("pe", "act", "dve", "pool", "sp")
CH = 16000


class Res:
    __slots__ = ("name", "w", "r")

    def __init__(self, name=""):
        self.name = name
        self.w = None
        self.r = {}


class DmaSem:
    __slots__ = ("key", "sem", "count")

    def __init__(self, key, sem):
        self.key = key
        self.sem = sem
        self.count = 0


class Prog:
    def __init__(self, nc, stack, same_engine_sync=True):
        self.nc = nc
        self.stack = stack
        self.q = {e: [] for e in ENGS}
        self.n = {e: 0 for e in ENGS}
        self.esem = {}
        self.seen = {e: {} for e in ENGS}
        self.same = same_engine_sync
        self.dsems = []
        self.nwaits = 0

    def dmasem(self, name):
        s = self.stack.enter_context(self.nc.semaphore("d%d_%s" % (len(self.dsems), name)))
        d = DmaSem("d%d_%s" % (len(self.dsems), name), s)
        self.dsems.append(d)
        return d

    def _esem(self, e, k):
        if (e, k) not in self.esem:
            self.esem[(e, k)] = self.stack.enter_context(self.nc.semaphore("e_%s_%d" % (e, k)))
        return self.esem[(e, k)]

    def emit(self, eng, fn, reads=(), writes=(), dma=None):
        need = {}

        def want(ev):
            if ev is None:
                return
            key, val = ev[0], ev[1]
            if key == eng and (eng == "pe" or not self.same):
                return
            if ev[2] is not None:
                val = ev[2].count
            if need.get(key, (0,))[0] < val:
                need[key] = (val, ev[2])

        for r in reads:
            want(r.w)
        for w in writes:
            want(w.w)
            for ev in w.r.values():
                want(ev)
        waits = []
        seen = self.seen[eng]
        for key, (val, hinfo) in need.items():
            if seen.get(key, 0) >= val:
                continue
            seen[key] = val
            if key in ENGS:
                k = (val - 1) // CH
                waits.append((self._esem(key, k), val - k * CH))
            else:
                waits.append((hinfo.sem, val))
        self.nwaits += len(waits)
        if fn is None:
            if waits:
                self.q[eng].append((waits, None, None))
            return None
        if dma is None:
            self.n[eng] += 1
            idx = self.n[eng]
            k = (idx - 1) // CH
            inc = (self._esem(eng, k), 1)
            ev = (eng, idx, None)
        else:
            dma.count += 16
            inc = (dma.sem, 16)
            ev = (dma.key, dma.count, dma)
        self.q[eng].append((waits, fn, inc))
        for r in reads:
            r.r[ev[0]] = ev
        for w in writes:
            w.w = ev
            w.r = {}
        return ev

    def barrier(self):
        evs = [(e, self.n[e], None) for e in ENGS if self.n[e] > 0]
        evs += [(d.key, d.count, d) for d in self.dsems if d.count > 0]
        for eng in ENGS:
            tmp = Res()
            tmp.r = {ev[0]: ev for ev in evs if ev[0] != eng}
            self.emit(eng, None, writes=[tmp])

    def finalize(self):
        nc = self.nc
        with nc.Block() as block:
            def mk(ename):
                def body(e):
                    for waits, fn, inc in self.q[ename]:
                        for sem, val in waits:
                            e.wait_ge(sem, val)
                        if fn is not None:
                            fn(e).then_inc(inc[0], inc[1])
                return body
            block.tensor(mk("pe"))
            block.scalar(mk("act"))
            block.vector(mk("dve"))
            block.gpsimd(mk("pool"))
            block.sync(mk("sp"))


class KB:
    def __init__(self, nc, S, stack):
        self.nc = nc
        self.S = S
        self.st = stack
        self.P = Prog(nc, stack)

    def sb(self, name, shape, dtype, stack=None):
        self.uid = getattr(self, "uid", 0) + 1
        return (stack or self.st).enter_context(self.nc.sbuf_tensor("%s_u%d" % (name, self.uid), list(shape), dtype))

    def ps(self, name, dtype=F32, stack=None):
        n = 512 if dtype == F32 else 1024
        self.uid = getattr(self, "uid", 0) + 1
        return (stack or self.st).enter_context(self.nc.psum_tensor("%s_u%d" % (name, self.uid), [128, n], dtype))

    def mm(self, out, lhsT, rhs, start, stop, R, W, **kw):
        return self.P.emit("pe", lambda e: e.matmul(out, lhsT=lhsT, rhs=rhs, start=start, stop=stop, **kw), R, W)

    def tr(self, out, in_, ident, R, W):
        return self.P.emit("pe", lambda e: e.transpose(out, in_, ident), R, W)

    def act(self, out, in_, func, R, W, eng="act", **kw):
        return self.P.emit(eng, lambda e: e.activation(out=out, in_=in_, func=func, **kw), R, W)

    def copy(self, eng, out, in_, R, W):
        if eng == "act":
            return self.P.emit("act", lambda e: e.copy(out=out, in_=in_), R, W)
        return self.P.emit(eng, lambda e: e.tensor_copy(out=out, in_=in_), R, W)

    def tt(self, eng, out, in0, in1, op, R, W):
        return self.P.emit(eng, lambda e: e.tensor_tensor(out=out, in0=in0, in1=in1, op=op), R, W)

    def ts(self, eng, out, in0, s1, s2, op0, op1, R, W, **kw):
        if s2 is None:
            return self.P.emit(eng, lambda e: e.tensor_scalar(out=out, in0=in0, scalar1=s1, scalar2=None, op0=op0, **kw), R, W)
        return self.P.emit(eng, lambda e: e.tensor_scalar(out=out, in0=in0, scalar1=s1, scalar2=s2, op0=op0, op1=op1, **kw), R, W)

    def stt(self, eng, out, in0, scalar, in1, op0, op1, R, W):
        return self.P.emit(eng, lambda e: e.scalar_tensor_tensor(out=out, in0=in0, scalar=scalar, in1=in1, op0=op0, op1=op1), R, W)

    def memset(self, eng, ap, val, W):
        return self.P.emit(eng, lambda e: e.memset(ap, val), (), W)

    def dma(self, eng, out, in_, sem, R, W, slow=False):
        if slow:
            return self.P.emit(eng, lambda e: e.dma_start(out=out, in_=in_, allow_slow_non_contiguous=True), R, W, dma=sem)
        return self.P.emit(eng, lambda e: e.dma_start(out=out, in_=in_), R, W, dma=sem)

    def setup_consts(self):
        nc = self.nc
        self.ident = self.sb("ident", [128, 128], BF16)
        self.Rconst = Res("const")
        W = [self.Rconst]
        self.eps_rms = self.sb("eps_rms", [128, 4], F32)
        self.memset("pool", self.eps_rms[:, 0:1], RMS_EPS, W)
        self.memset("pool", self.eps_rms[:, 1:2], GN_EPS, W)
        self.memset("pool", self.eps_rms[:, 2:3], 1.0, W)
        self.memset("pool", self.eps_rms[:, 3:4], 0.0, W)
        self.memset("pool", self.ident[:], 1.0, W)
        idt = self.ident
        self.P.emit("pool", lambda e: e.affine_select(out=idt[:], in_=idt[:], pattern=[[-1, 128]],
                                                      compare_op=ALU.is_equal, fill=0.0, base=0,
                                                      channel_multiplier=1), (), W)

    def norm_tile(self, xt, Rxt, gain_b, Rgain, hT, RhT, col0, bufs, i):
        junk, Rjunk = bufs["junk"][i % 2]
        ss, Rss = bufs["ss"][i % 2]
        hb, Rhb = bufs["hb"][i % 2]
        tp, Rtp = bufs["tp"][i % len(bufs["tp"])]
        self.act(junk[:], xt[:], AF.Square, [Rxt], [Rjunk, Rss], accum_out=ss[:, 0:1])
        self.act(ss[:, 1:2], ss[:, 0:1], AF.Ln, [Rss, self.Rconst], [Rss], scale=1.0 / D, bias=self.eps_rms[:, 0:1])
        self.act(ss[:, 2:3], ss[:, 1:2], AF.Exp, [Rss], [Rss], scale=-0.5)
        self.stt("dve", hb[:], xt[:], ss[:, 2:3], gain_b[:], ALU.mult, ALU.mult, [Rxt, Rss, Rgain], [Rhb])
        for c in range(8):
            self.tr(tp[:, c * 128:(c + 1) * 128], hb[:, c * 128:(c + 1) * 128], self.ident[:], [Rhb, self.Rconst], [Rtp])
        self.copy("act", hT[:, :, col0:col0 + 128], tp[:, :].rearrange("p (c t) -> p c t", c=8), [Rtp], [RhT])

    def ffn_phase(self, layer, xin, xout, prm):
        nc, P, S = self.nc, self.P, self.S
        TG = min(1024, S)
        NG = S // TG
        NT = TG // 128
        NH = TG // 512
        with ExitStack() as st:
            gain_b = self.sb("f_gain", [128, D], F32, st)
            hT = self.sb("f_hT", [128, 8, TG], BF16, st)
            actT = self.sb("f_actT", [128, NFC, TG], BF16, st)
            wd = self.sb("f_wd", [128, NFC, D], BF16, st)
            wg = [self.sb("f_wg%d" % i, [128, 8, 256], BF16, st) for i in range(2)]
            wu = [self.sb("f_wu%d" % i, [128, 8, 256], BF16, st) for i in range(2)]
            xts = [self.sb("f_xt%d" % i, [128, D], F32, st) for i in range(3)]
            ots = [self.sb("f_ot%d" % i, [128, 512], F32, st) for i in range(2)]
            sil = [self.sb("f_sil%d" % i, [128, 512], F32, st) for i in range(2)]
            bufs = {
                "junk": [(self.sb("f_junk%d" % i, [128, D], BF16, st), Res()) for i in range(2)],
                "ss": [(self.sb("f_ss%d" % i, [128, 4], F32, st), Res()) for i in range(2)],
                "hb": [(self.sb("f_hb%d" % i, [128, D], BF16, st), Res()) for i in range(2)],
                "tp": [(self.ps("f_tp%d" % i, BF16, st), Res()) for i in range(2)],
            }
            pg = [(self.ps("f_pg%d" % i, F32, st), Res()) for i in range(2)]
            pu = [(self.ps("f_pu%d" % i, F32, st), Res()) for i in range(2)]
            po = [(self.ps("f_po%d" % i, F32, st), Res()) for i in range(2)]
            Rgain = Res(); RhT = [Res() for _ in range(NT)]; Ract = [Res() for _ in range(NFC)]
            Rwd = Res(); Rwg = [Res(), Res()]; Rwu = [Res(), Res()]
            Rxt = [Res() for _ in range(3)]; Rot = [Res(), Res()]; Rsil = [Res(), Res()]
            d_gain = P.dmasem("fgain"); d_x = [P.dmasem("fx%d" % i) for i in range(3)]
            d_wg = [P.dmasem("fwg%d" % i) for i in range(2)]; d_wu = [P.dmasem("fwu%d" % i) for i in range(2)]
            d_wd = P.dmasem("fwd"); d_o = [P.dmasem("fo%d" % i) for i in range(2)]
            Rxdram = self.Rxdram

            self.dma("sp", gain_b[:], prm["ffn_norm"][layer:layer + 1, :].partition_broadcast(128), d_gain, [], [Rgain])
            wgv = prm["ffn_w_gate"][layer].rearrange("(c p) f -> p c f", p=128)
            wuv = prm["ffn_w_up"][layer].rearrange("(c p) f -> p c f", p=128)
            wdv = prm["ffn_w_down"][layer].rearrange("(c p) n -> p c n", p=128)
            xcnt = 0
            ocnt = 0
            step = 0
            for g in range(NG):
                t0 = g * TG
                for c0 in range(0, NFC, 2):
                    self.dma("pool", wd[:, c0:c0 + 2, :], wdv[:, c0:c0 + 2, :], d_wd, [], [Rwd])
                for i in range(NT):
                    b = xcnt % 3
                    tix = (t0 // 128) + i
                    self.dma("sp", xts[b][:], xin[t0 + i * 128:t0 + (i + 1) * 128, :], d_x[b], [Rxdram[tix]], [Rxt[b]])
                    self.norm_tile(xts[b], Rxt[b], gain_b, Rgain, hT, RhT[i], i * 128, bufs, xcnt)
                    xcnt += 1
                for fg in range(NFC // 2):
                    wb = fg % 2
                    self.dma("pool", wg[wb][:], wgv[:, :, fg * 256:(fg + 1) * 256], d_wg[wb], [], [Rwg[wb]])
                    self.dma("pool", wu[wb][:], wuv[:, :, fg * 256:(fg + 1) * 256], d_wu[wb], [], [Rwu[wb]])
                    for fc in range(2):
                        f = fg * 2 + fc
                        for th in range(NH):
                            pb = step % 2
                            pgt, Rpg = pg[pb]
                            put, Rpu = pu[pb]
                            rh = RhT[th * 4:(th + 1) * 4]
                            for c in range(8):
                                self.mm(pgt[:, :], wg[wb][:, c, fc * 128:(fc + 1) * 128], hT[:, c, th * 512:(th + 1) * 512],
                                        c == 0, c == 7, [Rwg[wb]] + rh, [Rpg])
                            for c in range(8):
                                self.mm(put[:, :], wu[wb][:, c, fc * 128:(fc + 1) * 128], hT[:, c, th * 512:(th + 1) * 512],
                                        c == 0, c == 7, [Rwu[wb]] + rh, [Rpu])
                            self.act(sil[pb][:], pgt[:, :], AF.Silu, [Rpg], [Rsil[pb]])
                            self.tt("dve", actT[:, f, th * 512:(th + 1) * 512], sil[pb][:], put[:, :], ALU.mult,
                                    [Rsil[pb], Rpu], [Ract[f]])
                            step += 1
                for i in range(NT):
                    b = xcnt % 3
                    tix = (t0 // 128) + i
                    self.dma("sp", xts[b][:], xin[t0 + i * 128:t0 + (i + 1) * 128, :], d_x[b], [Rxdram[tix]], [Rxt[b]])
                    xcnt += 1
                    for nh in range(2):
                        ob = ocnt % 2
                        pot, Rpo = po[ob]
                        for f in range(NFC):
                            self.mm(pot[:, :], actT[:, f, i * 128:(i + 1) * 128], wd[:, f, nh * 512:(nh + 1) * 512],
                                    f == 0, f == NFC - 1, [Ract[f], Rwd], [Rpo])
                        self.tt("dve", ots[ob][:], pot[:, :], xts[b][:, nh * 512:(nh + 1) * 512], ALU.add,
                                [Rpo, Rxt[b]], [Rot[ob]])
                        self.dma("sp", xout[t0 + i * 128:t0 + (i + 1) * 128, nh * 512:(nh + 1) * 512], ots[ob][:], d_o[ob],
                                 [Rot[ob]], [Rxdram[tix]])
                        ocnt += 1
            P.barrier()


    def hy_consts(self, st):
        c = {}
        W = [self.Rconst]
        sb = lambda n, shp, dt: self.sb(n, shp, dt, st)
        c["negmask"] = sb("c_negmask", [128, 128], BF16)
        c["tri"] = sb("c_tri", [128, 128], F32)
        c["nones"] = sb("c_nones", [128, 128], F32)
        c["identf"] = sb("c_identf", [128, 128], F32)
        c["bd"] = sb("c_bd", [128, 128], BF16)
        c["esel"] = sb("c_esel", [8, 8, 128], BF16)
        c["ones64"] = sb("c_ones64", [128, 64], BF16)
        c["invc"] = sb("c_invc", [128, 4, 16], F32)
        nm, tri, nones, identf, bd, esel, ones64, invc = (c[k] for k in ("negmask", "tri", "nones", "identf", "bd", "esel", "ones64", "invc"))
        self.memset("pool", nm[:], 0.0, W)
        self.P.emit("pool", lambda e: e.affine_select(out=nm[:], in_=nm[:], pattern=[[1, 128]], compare_op=ALU.is_ge,
                                                      fill=-30000.0, base=0, channel_multiplier=-1), (), W)
        self.memset("pool", tri[:], -1.0, W)
        self.P.emit("pool", lambda e: e.affine_select(out=tri[:], in_=tri[:], pattern=[[1, 128]], compare_op=ALU.is_ge,
                                                      fill=0.0, base=0, channel_multiplier=-1), (), W)
        self.memset("pool", nones[:], -1.0, W)
        self.memset("pool", identf[:], 1.0, W)
        self.P.emit("pool", lambda e: e.affine_select(out=identf[:], in_=identf[:], pattern=[[-1, 128]], compare_op=ALU.is_equal,
                                                      fill=0.0, base=0, channel_multiplier=1), (), W)
        self.memset("pool", bd[:], 0.0, W)
        self.memset("pool", bd[0:64, 0:64], 1.0, W)
        self.memset("pool", bd[64:128, 64:128], 1.0, W)
        self.memset("pool", esel[:], 8.0, W)
        self.P.emit("pool", lambda e: e.affine_select(out=esel[:], in_=esel[:], pattern=[[1, 8], [0, 128]], compare_op=ALU.is_equal,
                                                      fill=0.0, base=0, channel_multiplier=-1), (), W)
        self.memset("pool", ones64[:], 1.0, W)
        for g, w in enumerate((2, 4, 8, 16)):
            self.memset("pool", invc[:, g, :], 1.0 / w, W)
            for t in range(w - 1):
                self.memset("pool", invc[:, g, t:t + 1], 1.0 / (t + 1), W)
        return c

    def hy_phase(self, e, layer, xin, xout, prm):
        nc, P, S = self.nc, self.P, self.S
        NI = S // 512
        NB = S // 128
        Rc = self.Rconst
        with ExitStack() as st:
            cst = self.hy_consts(st)
            sb = lambda n, shp, dt: self.sb("h_" + n, shp, dt, st)
            w_in = sb("w_in", [128, 8, IN_COLS], BF16)
            w_out = sb("w_out", [128, 8, D], BF16)
            pw = sb("pw", [128, 4, 128], BF16)
            gain_b = sb("gain", [128, D], F32)
            qg = sb("qg", [128, 1], F32); kg = sb("kg", [128, 1], F32)
            pscale = sb("pscale", [128, 4], F32)
            fb = sb("fb", [128, 8], F32)
            kT = sb("kT", [128, 4, S], BF16)
            vc = sb("vc", [128, NB, 512], BF16)
            cumK = sb("cumK", [128, NB, 8], F32)
            kb = sb("kb", [128, NB, 8], F32)
            hc = sb("hc", [128, 8, 512], BF16)
            qT = sb("qT", [128, 4, 512], BF16)
            sgT = sb("sgT", [128, 4, 512], BF16)
            upad = sb("upad", [128, 4, 528], F32)
            tA = sb("tA", [128, 528], F32); tB = sb("tB", [128, 528], F32)
            pooledT = sb("pooledT", [128, 4, 512], BF16)
            xts = [sb("xt%d" % i, [128, D], F32) for i in range(2)]
            ots = [sb("ot%d" % i, [128, 512], F32) for i in range(2)]
            junk = sb("junk", [128, D], BF16)
            kf = sb("kf", [128, 512], F32); sq = sb("sq", [128, 512], BF16); rs = sb("rs", [128, 512], F32)
            pT = [sb("pT%d" % i, [128, 512], BF16) for i in range(4)]
            rden = sb("rden", [128, 512], F32); atmp = sb("atmp", [128, 512], F32)
            carry = [sb("carry%d" % i, [128, 8], F32) for i in range(2)]
            zf = sb("zf", [128, 8], F32); lf = sb("lf", [128, 8], F32)
            qctok = sb("qctok", [128, 4, 8], F32)
            qcT = sb("qcT", [8, 512], BF16)
            t16 = sb("t16", [128, 16], F32)
            bufs = {
                "junk": [(junk, Res()), (junk, Res())],
                "ss": [(sb("ss%d" % i, [128, 4], F32), Res()) for i in range(2)],
                "hb": [(sb("hb%d" % i, [128, D], BF16), Res()) for i in range(1)] * 2,
                "tp": [(self.ps("h_tp", BF16, st), Res())],
            }
            gp = [(self.ps("h_gp%d" % i, F32, st), Res()) for i in range(2)]
            sbk = [(self.ps("h_s%d" % i, F32, st), Res()) for i in range(3)]
            accN, RaccN = self.ps("h_accN", F32, st), Res()
            accD, RaccD = self.ps("h_accD", F32, st), Res()
            Rw = Res(); Rgain = Res(); Rsmall = Res()
            Rhc = Res(); RqT = Res(); RsgT = Res(); RkT = [Res() for _ in range(NI)]; Rvc = [Res() for _ in range(NB)]
            RcumK = Res(); Rkb = Res(); Rupad = Res(); RtA = Res(); RtB = Res(); Rpooled = Res()
            Rxt = [Res(), Res()]; Rot = [Res(), Res()]; Rkf = Res(); Rsq = Res(); Rrs = Res()
            RpT = [Res() for _ in range(4)]; Rrden = Res(); Ratmp = Res(); Rcarry = [Res(), Res()]
            Rzf = Res(); Rlf = Res(); Rqctok = Res(); RqcT = Res(); Rt16 = Res()
            d_w = P.dmasem("hw"); d_c = P.dmasem("hc"); d_x = [P.dmasem("hx%d" % i) for i in range(2)]
            d_o = [P.dmasem("ho%d" % i) for i in range(2)]
            Rxdram = self.Rxdram
            win_v = prm["hy_w_in"][e].rearrange("(c p) n -> p c n", p=128)
            wout_v = prm["hy_w_out"][e].rearrange("(c p) n -> p c n", p=128)
            for c in range(8):
                self.dma("pool", w_in[:, c, :], win_v[:, c, :], d_w, [], [Rw])
            for c in range(8):
                self.dma("pool", w_out[:, c, :], wout_v[:, c, :], d_w, [], [Rw])
            self.dma("pool", pw[:], prm["hy_pool_w"][e].rearrange("g c d -> c g d"), d_w, [], [Rw])
            self.dma("sp", gain_b[:], prm["mix_norm"][layer:layer + 1, :].partition_broadcast(128), d_c, [], [Rgain])
            qgv = prm["hy_q_gain"][e].rearrange("(d o) -> d o", o=1)
            kgv = prm["hy_k_gain"][e].rearrange("(d o) -> d o", o=1)
            for hp in range(2):
                self.dma("sp", qg[hp * 64:(hp + 1) * 64, :], qgv, d_c, [], [Rsmall])
                self.dma("sp", kg[hp * 64:(hp + 1) * 64, :], kgv, d_c, [], [Rsmall])
            self.dma("sp", pscale[:], prm["hy_pool_scale"][e].rearrange("(g p) -> p g", p=128), d_c, [], [Rsmall], slow=True)
            self.dma("sp", fb[:], prm["hy_f_bias"][e:e + 1, :].partition_broadcast(128), d_c, [], [Rsmall])
            self.memset("pool", carry[0][:], 0.0, [Rcarry[0]])
            self.memset("pool", upad[:, :, 0:16], 0.0, [Rupad])
            xcnt = 0; ocnt = 0; gcnt = 0; scnt = 0; pcnt = 0; ccnt = 0

            def proj_fm(col0, gi):
                pt, Rp = gp[gi % 2]
                for c in range(8):
                    self.mm(pt[:, :], w_in[:, c, col0:col0 + 128], hc[:, c, :], c == 0, c == 7, [Rw, Rhc], [Rp])
                return pt, Rp

            for I in range(NI):
                t0 = I * 512
                for i in range(4):
                    b = xcnt % 2
                    tix = I * 4 + i
                    self.dma("sp", xts[b][:], xin[t0 + i * 128:t0 + (i + 1) * 128, :], d_x[b], [Rxdram[tix]], [Rxt[b]])
                    self.norm_tile(xts[b], Rxt[b], gain_b, Rgain, hc, Rhc, i * 128, bufs, xcnt)
                    xcnt += 1
                for i in range(4):
                    blk = I * 4 + i
                    smp, Rsm = gp[gcnt % 2]; gcnt += 1
                    for c in range(8):
                        self.mm(smp[:, 0:8], hc[:, c, i * 128:(i + 1) * 128], w_in[:, c, 2048:2056], c == 0, c == 7, [Rw, Rhc], [Rsm])
                    self.tt("dve", zf[:], smp[:, 0:8], fb[:], ALU.add, [Rsm, Rsmall], [Rzf])
                    self.act(lf[:], zf[:], AF.Exp, [Rzf], [Rlf], scale=-1.0)
                    self.act(lf[:], lf[:], AF.Ln, [Rlf, Rc], [Rlf], bias=self.eps_rms[:, 2:3])
                    self.mm(smp[:, 8:16], cst["tri"][:], lf[:], True, True, [Rlf, Rc], [Rsm])
                    self.mm(smp[:, 16:24], cst["nones"][:], lf[:], True, True, [Rlf, Rc], [Rsm])
                    cin, cout = carry[ccnt % 2], carry[(ccnt + 1) % 2]
                    Rcin, Rcout = Rcarry[ccnt % 2], Rcarry[(ccnt + 1) % 2]
                    self.tt("dve", cumK[:, blk, :], smp[:, 8:16], cin[:], ALU.add, [Rsm, Rcin], [RcumK])
                    self.tt("dve", cout[:], smp[:, 16:24], cin[:], ALU.add, [Rsm, Rcin], [Rcout])
                    ccnt += 1
                cend, Rcend = carry[ccnt % 2], Rcarry[ccnt % 2]
                nj = 4 * I + 4
                cb4 = cend[:, :].unsqueeze(1).broadcast_to([128, 4, 8])
                self.tt("dve", qctok[:], cumK[:, 4 * I:4 * I + 4, :], cb4, ALU.subtract, [RcumK, Rcend], [Rqctok])
                cbn = cend[:, :].unsqueeze(1).broadcast_to([128, nj, 8])
                self.tt("dve", kb[:, 0:nj, :], cbn, cumK[:, 0:nj, :], ALU.subtract, [RcumK, Rcend], [Rkb])
                ptq, Rpq = gp[gcnt % 2]; gcnt += 1
                for i in range(4):
                    self.tr(ptq[0:8, i * 128:(i + 1) * 128], qctok[:, i, :], cst["identf"][:], [Rqctok, Rc], [Rpq])
                self.copy("dve", qcT[:, :], ptq[0:8, 0:512], [Rpq], [RqcT])
                for which in range(2):
                    for cc in range(4):
                        col0 = (512 if which == 0 else 0) + cc * 128
                        pt, Rp = proj_fm(col0, gcnt); gcnt += 1
                        self.copy("act", kf[:], pt[:, :], [Rp], [Rkf])
                        self.tt("pool", sq[:], kf[:], kf[:], ALU.mult, [Rkf], [Rsq])
                        pt2, Rp2 = gp[gcnt % 2]; gcnt += 1
                        self.mm(pt2[:, :], cst["bd"][:], sq[:], True, True, [Rsq, Rc], [Rp2])
                        self.act(rs[:], pt2[:, :], AF.Ln, [Rp2, Rc], [Rrs], scale=1.0 / 64, bias=self.eps_rms[:, 0:1])
                        self.act(rs[:], rs[:], AF.Exp, [Rrs], [Rrs], scale=-0.5)
                        if which == 0:
                            self.stt("dve", kT[:, cc, t0:t0 + 512], kf[:], kg[:, 0:1], rs[:], ALU.mult, ALU.mult,
                                     [Rkf, Rrs, Rsmall], [RkT[I]])
                        else:
                            self.stt("dve", qT[:, cc, :], kf[:], qg[:, 0:1], rs[:], ALU.mult, ALU.mult,
                                     [Rkf, Rrs, Rsmall], [RqT])
                for i in range(4):
                    blk = I * 4 + i
                    pt, Rp = gp[gcnt % 2]; gcnt += 1
                    for c in range(8):
                        self.mm(pt[:, :], hc[:, c, i * 128:(i + 1) * 128], w_in[:, c, 1024:1536], c == 0, c == 7, [Rw, Rhc], [Rp])
                    self.copy("act", vc[:, blk, :], pt[:, :], [Rp], [Rvc[blk]])
                for cc in range(4):
                    pt, Rp = proj_fm(1536 + cc * 128, gcnt); gcnt += 1
                    self.act(sgT[:, cc, :], pt[:, :], AF.Sigmoid, [Rp], [RsgT])
                for g in range(4):
                    pt, Rp = proj_fm(2056 + g * 128, gcnt); gcnt += 1
                    self.copy("act", upad[:, g, 16:528], pt[:, :], [Rp], [Rupad])
                for g in range(4):
                    u = upad[:, g, :]
                    cur, Rcur = u, Rupad
                    lo = 0
                    tmps = [(tA, RtA), (tB, RtB)]
                    for lvl in range(g + 1):
                        sh = 1 << lvl
                        dst, Rdst = tmps[lvl % 2]
                        nlo = lo + sh
                        self.tt("pool", dst[:, nlo:528], cur[:, nlo:528], cur[:, nlo - sh:528 - sh], ALU.add, [Rcur], [Rdst])
                        cur, Rcur, lo = dst, Rdst, nlo
                    wdt = 2 << g
                    self.stt("dve", pooledT[:, g, :], cur[:, 16:528], 1.0 / wdt, u[:, 16:528], ALU.mult, ALU.subtract,
                             [Rcur, Rupad], [Rpooled])
                    if I == 0:
                        self.tt("pool", t16[:], cur[:, 16:32], cst["invc"][:, g, :], ALU.mult, [Rcur, Rc], [Rt16])
                        self.tt("pool", pooledT[:, g, 0:16], t16[:], u[:, 16:32], ALU.subtract, [Rt16, Rupad], [Rpooled])
                self.copy("pool", upad[:, :, 0:16], upad[:, :, 512:528], [Rupad], [Rupad])
                for g in range(4):
                    pt, Rp = gp[gcnt % 2]; gcnt += 1
                    self.mm(pt[:, :], pw[:, g, :], pooledT[:, g, :], True, True, [Rw, Rpooled], [Rp])
                    self.act(hc[:, 4 + g, :], pt[:, :], AF.Copy, [Rp, Rsmall], [Rhc], scale=pscale[:, g:g + 1])
                steps = [(pr, hp, j) for pr in range(4) for hp in range(2) for j in range(nj)]

                def qk(step, si):
                    pr, hp, j = step
                    h = pr * 2 + hp
                    jj = j - 4 * I
                    c0 = 128 * jj if jj > 0 else 0
                    diag = jj >= 0
                    sbt, Rs = sbk[si % 3]
                    Ij = j // 4
                    self.mm(sbt[:, c0:512], kT[hp * 64:(hp + 1) * 64, pr, j * 128:(j + 1) * 128],
                            qT[hp * 64:(hp + 1) * 64, pr, c0:512], True, False, [RkT[Ij], RqT], [Rs])
                    self.mm(sbt[:, c0:512], cst["esel"][:, h, :], qcT[:, c0:512], False, not diag, [RqcT, Rc], [Rs])
                    if diag:
                        self.mm(sbt[:, c0:c0 + 128], self.ident[:], cst["negmask"][:], False, True, [Rc], [Rs])
                    return c0

                def rest(step, si, c0):
                    pr, hp, j = step
                    h = pr * 2 + hp
                    sbt, Rs = sbk[si % 3]
                    pt_, Rp_ = pT[si % 4], RpT[si % 4]
                    self.act(pt_[:, c0:512], sbt[:, c0:512], AF.Exp, [Rs, Rkb], [Rp_], scale=0.125, bias=kb[:, j, h:h + 1])
                    first = (j == 0)
                    last = (j == nj - 1)
                    self.mm(accN[hp * 64:(hp + 1) * 64, c0:512], vc[:, j, h * 64:(h + 1) * 64], pt_[:, c0:512],
                            first, last, [Rvc[j], Rp_], [RaccN], tile_position=(0, hp * 64))
                    self.mm(accD[hp * 64:(hp + 1) * 64, c0:512], cst["ones64"][:], pt_[:, c0:512],
                            first, last, [Rc, Rp_], [RaccD], tile_position=(0, hp * 64))
                    if hp == 1 and last:
                        self.P.emit("dve", lambda e_: e_.reciprocal(out=rden[:], in_=accD[:, :]), [RaccD], [Rrden])
                        self.tt("dve", atmp[:], accN[:, :], rden[:], ALU.mult, [RaccN, Rrden], [Ratmp])
                        self.tt("pool", hc[:, pr, :], atmp[:], sgT[:, pr, :], ALU.mult, [Ratmp, RsgT], [Rhc])

                pend = []
                for n, stp in enumerate(steps):
                    c0 = qk(stp, scnt + n)
                    pend.append((stp, scnt + n, c0))
                    if len(pend) > 2:
                        rest(*pend.pop(0))
                while pend:
                    rest(*pend.pop(0))
                scnt += len(steps)
                for i in range(4):
                    b = xcnt % 2
                    tix = I * 4 + i
                    self.dma("sp", xts[b][:], xin[t0 + i * 128:t0 + (i + 1) * 128, :], d_x[b], [Rxdram[tix]], [Rxt[b]])
                    xcnt += 1
                    for nh in range(2):
                        ob = ocnt % 2
                        pt, Rp = gp[gcnt % 2]; gcnt += 1
                        for c in range(8):
                            self.mm(pt[:, :], hc[:, c, i * 128:(i + 1) * 128], w_out[:, c, nh * 512:(nh + 1) * 512],
                                    c == 0, c == 7, [Rhc, Rw], [Rp])
                        self.tt("dve", ots[ob][:], pt[:, :], xts[b][:, nh * 512:(nh + 1) * 512], ALU.add, [Rp, Rxt[b]], [Rot[ob]])
                        self.dma("sp", xout[t0 + i * 128:t0 + (i + 1) * 128, nh * 512:(nh + 1) * 512], ots[ob][:], d_o[ob],
                                 [Rot[ob]], [Rxdram[tix]])
                        ocnt += 1
            P.barrier()


    def rw_phase(self, o, layer, xin, xout, prm):
        nc, P, S = self.nc, self.P, self.S
        NCH = S // 128
        Rc = self.Rconst
        first_layer = (o == 0)
        with ExitStack() as st:
            sb = lambda n, shp, dt: self.sb("r_" + n, shp, dt, st)
            Wr = sb("Wr", [128, 8, D], BF16); Wk = sb("Wk", [128, 8, D], BF16)
            Wv = sb("Wv", [128, 8, D], BF16); Wo = sb("Wo", [128, 8, D], BF16)
            l1 = sb("l1", [128, 8, 320], BF16)
            wa2 = sb("wa2", [128, D], BF16)
            g2a = sb("g2a", [128, D], BF16)
            gv2 = sb("gv2", [64, D], BF16)
            bc = sb("bc", [128, 8, D], BF16)
            gain_b = sb("gain", [128, D], F32)
            mu = sb("mu", [128, 6, 8], F32)
            IUf = sb("IUf", [128, 128], F32); SUf = sb("SUf", [128, 128], F32); SLf = sb("SLf", [128, 128], F32)
            onesf = sb("onesf", [128, 128], F32)
            tiny = sb("tiny", [128, 1], F32)
            W = [Rc]
            self.memset("pool", IUf[:], 1.0, W)
            self.P.emit("pool", lambda e: e.affine_select(out=IUf[:], in_=IUf[:], pattern=[[1, 128]], compare_op=ALU.is_ge,
                                                          fill=0.0, base=0, channel_multiplier=-1), (), W)
            self.memset("pool", SUf[:], 1.0, W)
            self.P.emit("pool", lambda e: e.affine_select(out=SUf[:], in_=SUf[:], pattern=[[1, 128]], compare_op=ALU.is_gt,
                                                          fill=0.0, base=0, channel_multiplier=-1), (), W)
            self.memset("pool", SLf[:], 1.0, W)
            self.P.emit("pool", lambda e: e.affine_select(out=SLf[:], in_=SLf[:], pattern=[[-1, 128]], compare_op=ALU.is_gt,
                                                          fill=0.0, base=0, channel_multiplier=1), (), W)
            self.memset("pool", onesf[:], -float(np.exp(-0.5)), W)
            cIU = sb("cIU", [128, 128], F32); cSU = sb("cSU", [128, 128], F32)
            self.ts("pool", cIU[:], IUf[:], -float(np.exp(-0.5)), None, ALU.mult, None, [Rc], W)
            self.ts("pool", cSU[:], SUf[:], -float(np.exp(-0.5)), None, ALU.mult, None, [Rc], W)
            self.memset("pool", tiny[:], 1e-24, W)
            xt = sb("xt", [128, D], F32); hb = sb("hb", [128, D], BF16)
            hTe = sb("hTe", [128, 8, 129], BF16)
            xm = [sb("xm%d" % i, [128, 8, 128], BF16) for i in range(2)]
            xx = sb("xx", [128, 8, 128], BF16)
            sc = [sb("sc%d" % i, [128, D], F32) for i in range(5)]
            r_bf = sb("r_bf", [128, D], BF16); kp_bf = sb("kp_bf", [128, D], BF16); kk_bf = sb("kk_bf", [128, D], BF16)
            b_bf = sb("b_bf", [128, D], BF16); v_bf = sb("v_bf", [128, D], BF16); g_bf = sb("g_bf", [128, D], BF16)
            prod = [sb("prod%d" % i, [128, D], BF16) for i in range(2)]
            Khat = sb("Khat", [128, D], BF16); Bhat = sb("Bhat", [128, D], BF16)
            RtT = sb("RtT", [128, 8, 128], BF16); KtT = sb("KtT", [128, 8, 128], BF16)
            BtT = sb("BtT", [128, 8, 128], BF16); AtT = sb("AtT", [128, 8, 128], BF16)
            lsb = sb("lsb", [128, 512], BF16)
            Mb = [sb("Mb%d" % i, [128, 8, 128], BF16) for i in range(2)]
            Nb = [sb("Nb%d" % i, [128, 8, 128], BF16) for i in range(2)]
            Xb = [sb("Xb%d" % i, [128, 8, 128], BF16) for i in range(2)]
            XT = sb("XT", [128, 16, 128], BF16)
            AakT = sb("AakT", [128, 16, 128], BF16); ArbT = sb("ArbT", [128, 16, 128], BF16); ArkT = sb("ArkT", [128, 16, 128], BF16)
            RHS = sb("RHS", [128, D], BF16); U = sb("U", [128, D], BF16)
            ST = sb("ST", [128, 8, 64], F32); STb = sb("STb", [128, 8, 64], BF16); STt = sb("STt", [128, 8, 64], F32)
            PCc = sb("PCc", [128, 8], F32)
            st16 = sb("st16", [128, 8, 16], F32)
            yfin = sb("yfin", [128, D], BF16); yT = sb("yT", [128, 8, 128], BF16)
            ot = sb("ot", [128, D], F32)
            ss = sb("ss", [128, 4], F32)
            tp = [(self.ps("r_tp%d" % i, BF16, st), Res()) for i in range(2)]
            gpool = [(self.ps("r_gp%d" % i, F32, st), Res()) for i in range(6)]
            self._gi = 0

            def bank():
                b_ = gpool[self._gi % 6]
                self._gi += 1
                return b_

            self._ti = 0

            def tbank():
                b_ = tp[self._ti % 2]
                self._ti += 1
                return b_

            R = {k: Res(k) for k in ("w", "small", "xt", "hb", "hTe", "xx", "ss", "r_bf", "kp_bf", "kk_bf", "b_bf", "v_bf", "g_bf",
                                     "Khat", "Bhat", "RtT", "KtT", "BtT", "AtT", "lsb", "XT", "AakT", "ArbT", "ArkT", "RHS", "U",
                                     "ST", "STb", "STt", "PCc", "st16", "yfin", "yT", "ot", "gain")}
            Rsc = [Res() for _ in range(5)]; Rxm = [Res(), Res()]; Rprod = [Res(), Res()]
            RMb = [Res(), Res()]; RNb = [Res(), Res()]; RXb = [Res(), Res()]
            d_w = P.dmasem("rw"); d_c = P.dmasem("rc"); d_x = P.dmasem("rx"); d_o = P.dmasem("ro"); d_v = P.dmasem("rv")
            Rxdram = self.Rxdram
            Rvf = self.Rvf

            def wview(name):
                return prm[name][o].rearrange("(c p) n -> p c n", p=128)
            for Wt, nm in ((Wr, "rw_w_r"), (Wk, "rw_w_k"), (Wv, "rw_w_v"), (Wo, "rw_w_o")):
                v_ = wview(nm)
                for c in range(8):
                    self.dma("pool", Wt[:, c, :], v_[:, c, :], d_w, [], [R["w"]])
            self.dma("pool", l1[:, :, 0:64], wview("rw_w1"), d_w, [], [R["w"]])
            self.dma("pool", l1[:, :, 64:128], wview("rw_a1"), d_w, [], [R["w"]])
            self.dma("pool", l1[:, :, 128:288], wview("rw_g1"), d_w, [], [R["w"]])
            self.dma("pool", wa2[0:64, :], prm["rw_w2"][o], d_w, [], [R["w"]])
            self.dma("pool", wa2[64:128, :], prm["rw_a2"][o], d_w, [], [R["w"]])
            self.dma("pool", g2a[:, :], prm["rw_g2"][o][0:128, :], d_w, [], [R["w"]])
            self.dma("pool", gv2[0:32, :], prm["rw_g2"][o][128:160, :], d_w, [], [R["w"]])
            if not first_layer:
                self.dma("pool", l1[:, :, 288:320], prm["rw_v1"][o - 1].rearrange("(c p) n -> p c n", p=128), d_w, [], [R["w"]])
                self.dma("pool", gv2[32:64, :], prm["rw_v2"][o - 1], d_w, [], [R["w"]])
            rows = [prm["rw_w0"][o:o + 1, :], prm["rw_a0"][o:o + 1, :],
                    (prm["rw_v0"][o - 1:o, :] if not first_layer else prm["rw_w0"][o:o + 1, :]),
                    prm["rw_k_k"][o:o + 1, :], prm["rw_k_a"][o:o + 1, :], prm["rw_ln_w"][o:o + 1, :], prm["rw_ln_b"][o:o + 1, :],
                    prm["rw_r_k"][o:o + 1].rearrange("o h d -> o (h d)")]
            for i, rv in enumerate(rows):
                self.dma("pool", bc[:, i, :], rv.partition_broadcast(128), d_w, [], [R["w"]])
            W0, A0, V0, KK_, KA_, LNW, LNB, RK_ = (bc[:, i, :] for i in range(8))
            self.dma("sp", gain_b[:], prm["mix_norm"][layer:layer + 1, :].partition_broadcast(128), d_c, [], [R["gain"]])
            self.dma("sp", mu[:], prm["rw_mu"][o].rearrange("i (c p) -> p i c", p=128), d_c, [], [R["small"]], slow=True)
            self.memset("pool", ST[:], 0.0, [R["ST"]])
            self.memset("pool", STb[:], 0.0, [R["STb"]])
            self.memset("pool", hTe[:, :, 128:129], 0.0, [R["hTe"]])
            bufs = {"junk": [(yfin, R["yfin"])] * 2, "ss": [(ss, R["ss"])] * 2, "hb": [(hb, R["hb"])] * 2, "tp": tp}
            IUb = lambda n_: IUf[:, :].unsqueeze(1).broadcast_to([128, n_, 128])
            SUb = lambda n_: SUf[:, :].unsqueeze(1).broadcast_to([128, n_, 128])
            SLb = lambda n_: SLf[:, :].unsqueeze(1).broadcast_to([128, n_, 128])
            IDb = lambda n_: self.ident[:, :].unsqueeze(1).broadcast_to([128, n_, 128])
            h3 = lambda ap: ap.rearrange("p (h d) -> p h d", d=64)

            def proj_tok(xT, Rx, Wt, lo=0):
                bks = []
                for nh in range(2):
                    pt, Rp = bank()
                    for c in range(8):
                        self.mm(pt[:, :], xT[:, c, :], Wt[:, c, nh * 512:(nh + 1) * 512], c == 0, c == 7, [Rx, R["w"]], [Rp])
                    bks.append((pt, Rp))
                return bks

            def mix(i, j):
                mub = mu[:, i, :].unsqueeze(2).broadcast_to([128, 8, 128])
                self.tt("dve", xm[j][:], xx[:], mub, ALU.mult, [R["xx"], R["small"]], [Rxm[j]])
                self.tt("dve", xm[j][:], xm[j][:], hTe[:, :, 1:129], ALU.add, [Rxm[j], R["hTe"]], [Rxm[j]])
                return xm[j], Rxm[j]

            def evac2(bks, fn):
                for nh, (pt, Rp) in enumerate(bks):
                    fn(nh, pt, Rp, slice(nh * 512, (nh + 1) * 512))

            import os as _os
            stop = int(_os.environ.get("RW_STOP", "99"))

            def early(n_, t0_):
                self.copy("dve", ot[:], xt[:], [R["xt"]], [R["ot"]])
                self.dma("sp", xout[t0_:t0_ + 128, :], ot[:], d_o, [R["ot"]], [Rxdram[n_]])

            for n in range(NCH):
                t0 = n * 128
                self.copy("pool", hTe[:, :, 0:1], hTe[:, :, 128:129], [R["hTe"]], [R["hTe"]])
                self.dma("sp", xt[:], xin[t0:t0 + 128, :], d_x, [Rxdram[n]], [R["xt"]])
                self.norm_tile(xt, R["xt"], gain_b, R["gain"], hTe[:, :, 1:129], R["hTe"], 0, bufs, n)
                self.tt("pool", xx[:], hTe[:, :, 0:128], hTe[:, :, 1:129], ALU.subtract, [R["hTe"]], [R["xx"]])
                if stop <= 1:
                    early(n, t0)
                    continue
                xr, Rxr = mix(0, 0)
                evac2(proj_tok(xr, Rxr, Wr), lambda nh, pt, Rp, sl: self.copy("act", r_bf[:, sl], pt[:, :], [Rp], [R["r_bf"]]))
                xk, Rxk = mix(2, 1)
                evac2(proj_tok(xk, Rxk, Wk), lambda nh, pt, Rp, sl: self.copy("act", sc[0][:, sl], pt[:, :], [Rp], [Rsc[0]]))
                xv, Rxv = mix(3, 0)
                evac2(proj_tok(xv, Rxv, Wv), lambda nh, pt, Rp, sl: self.copy("act", sc[1][:, sl], pt[:, :], [Rp], [Rsc[1]]))
                lp, Rlp = bank()
                if not first_layer:
                    for c in range(8):
                        self.mm(lp[32:64, 256:384], l1[:, c, 288:320], xv[:, c, :], c == 0, c == 7, [Rxv, R["w"]], [Rlp],
                                tile_position=(0, 32))
                xw, Rxw = mix(1, 1)
                for c in range(8):
                    self.mm(lp[0:64, 0:128], l1[:, c, 0:64], xw[:, c, :], c == 0, c == 7, [Rxw, R["w"]], [Rlp])
                xa, Rxa = mix(4, 0)
                for c in range(8):
                    self.mm(lp[64:128, 0:128], l1[:, c, 64:128], xa[:, c, :], c == 0, c == 7, [Rxa, R["w"]], [Rlp],
                            tile_position=(0, 64))
                xg, Rxg = mix(5, 1)
                for c in range(8):
                    self.mm(lp[:, 128:256], l1[:, c, 128:256], xg[:, c, :], c == 0, c == 7, [Rxg, R["w"]], [Rlp])
                for c in range(8):
                    self.mm(lp[0:32, 256:384], l1[:, c, 256:288], xg[:, c, :], c == 0, c == 7, [Rxg, R["w"]], [Rlp])
                self.act(lsb[0:64, 0:128], lp[0:64, 0:128], AF.Tanh, [Rlp], [R["lsb"]])
                self.copy("act", lsb[64:128, 0:128], lp[64:128, 0:128], [Rlp], [R["lsb"]])
                self.act(lsb[:, 128:256], lp[:, 128:256], AF.Sigmoid, [Rlp], [R["lsb"]])
                self.act(lsb[0:32, 256:384], lp[0:32, 256:384], AF.Sigmoid, [Rlp], [R["lsb"]])
                if not first_layer:
                    self.copy("act", lsb[32:64, 256:384], lp[32:64, 256:384], [Rlp], [R["lsb"]])
                for nh in range(2):
                    sl = slice(nh * 512, (nh + 1) * 512)
                    pt, Rp = bank()
                    self.mm(pt[:, :], lsb[0:64, 0:128], wa2[0:64, sl], True, True, [R["lsb"], R["w"]], [Rp])
                    self.tt("dve", sc[2][:, sl], pt[:, :], W0[:, sl], ALU.add, [Rp, R["w"]], [Rsc[2]])
                self.act(sc[2][:], sc[2][:], AF.Sigmoid, [Rsc[2]], [Rsc[2]])
                for nh in range(2):
                    sl = slice(nh * 512, (nh + 1) * 512)
                    pt, Rp = bank()
                    self.mm(pt[:, :], lsb[64:128, 0:128], wa2[64:128, sl], True, True, [R["lsb"], R["w"]], [Rp])
                    self.tt("dve", sc[3][:, sl], pt[:, :], A0[:, sl], ALU.add, [Rp, R["w"]], [Rsc[3]])
                self.act(sc[3][:], sc[3][:], AF.Sigmoid, [Rsc[3]], [Rsc[3]])
                for nh in range(2):
                    sl = slice(nh * 512, (nh + 1) * 512)
                    pt, Rp = bank()
                    self.mm(pt[:, :], lsb[:, 128:256], g2a[:, sl], True, False, [R["lsb"], R["w"]], [Rp])
                    self.mm(pt[:, :], lsb[0:32, 256:384], gv2[0:32, sl], False, True, [R["lsb"], R["w"]], [Rp])
                    self.copy("act", g_bf[:, sl], pt[:, :], [Rp], [R["g_bf"]])
                if first_layer:
                    self.dma("sp", self.vfirst[t0:t0 + 128, :], sc[1][:], d_v, [Rsc[1]], [Rvf[n]])
                else:
                    for nh in range(2):
                        sl = slice(nh * 512, (nh + 1) * 512)
                        pt, Rp = bank()
                        self.mm(pt[:, :], lsb[32:64, 256:384], gv2[32:64, sl], True, True, [R["lsb"], R["w"]], [Rp])
                        self.tt("dve", sc[4][:, sl], pt[:, :], V0[:, sl], ALU.add, [Rp, R["w"]], [Rsc[4]])
                    self.act(sc[4][:], sc[4][:], AF.Sigmoid, [Rsc[4]], [Rsc[4]])
                    self.dma("sp", ot[:], self.vfirst[t0:t0 + 128, :], d_v, [Rvf[n]], [R["ot"]])
                    self.tt("pool", ot[:], ot[:], sc[1][:], ALU.subtract, [R["ot"], Rsc[1]], [R["ot"]])
                    self.tt("pool", ot[:], ot[:], sc[4][:], ALU.mult, [R["ot"], Rsc[4]], [R["ot"]])
                    self.tt("pool", sc[1][:], sc[1][:], ot[:], ALU.add, [R["ot"], Rsc[1]], [Rsc[1]])
                self.copy("act", v_bf[:], sc[1][:], [Rsc[1]], [R["v_bf"]])
                k32, a32 = sc[0], sc[3]
                self.tt("dve", sc[4][:], k32[:], KK_, ALU.mult, [Rsc[0], R["w"]], [Rsc[4]])
                self.act(sc[1][:], sc[4][:], AF.Square, [Rsc[4], R["v_bf"]], [Rsc[1]])
                self.P.emit("dve", lambda e_: e_.tensor_reduce(out=st16[:, 0, :], in_=h3(sc[1][:, :]), axis=AX.X, op=ALU.add),
                            [Rsc[1]], [R["st16"]])
                self.act(st16[:, 1, :], st16[:, 0, :], AF.Ln, [R["st16"], Rc], [R["st16"]], bias=tiny[:, 0:1])
                self.act(st16[:, 1, :], st16[:, 1, :], AF.Exp, [R["st16"]], [R["st16"]], scale=-0.5)
                rnb = st16[:, 1, :].unsqueeze(2).broadcast_to([128, 16, 64])
                self.tt("dve", h3(kk_bf[:, :]), h3(sc[4][:, :]), rnb, ALU.mult, [Rsc[4], R["st16"]], [R["kk_bf"]])
                self.stt("dve", sc[1][:], a32[:], -1.0, KA_, ALU.add, ALU.mult, [Rsc[3], R["w"]], [Rsc[1]])
                self.stt("dve", kp_bf[:], sc[1][:], 1.0, k32[:], ALU.add, ALU.mult, [Rsc[1], Rsc[0]], [R["kp_bf"]])
                self.tt("pool", b_bf[:], kk_bf[:], a32[:], ALU.mult, [R["kk_bf"], Rsc[3]], [R["b_bf"]])
                self.tt("pool", sc[4][:], r_bf[:], kp_bf[:], ALU.mult, [R["r_bf"], R["kp_bf"]], [Rsc[4]])
                self.tt("pool", sc[4][:], sc[4][:], RK_, ALU.mult, [Rsc[4], R["w"]], [Rsc[4]])
                self.P.emit("dve", lambda e_: e_.tensor_reduce(out=st16[:, 2, :], in_=h3(sc[4][:, :]), axis=AX.X, op=ALU.add),
                            [Rsc[4]], [R["st16"]])
                if stop <= 2:
                    early(n, t0)
                    continue
                ld = sc[2]
                for nh in range(2):
                    sl = slice(nh * 512, (nh + 1) * 512)
                    pt, Rp = bank()
                    self.mm(pt[:, :], cIU[:], ld[:, sl], True, True, [Rsc[2], Rc], [Rp])
                    self.act(sc[0][:, sl], pt[:, :], AF.Exp, [Rp], [Rsc[0]])
                    self.act(sc[1][:, sl], pt[:, :], AF.Exp, [Rp], [Rsc[1]], scale=-1.0)
                for nh in range(2):
                    sl = slice(nh * 512, (nh + 1) * 512)
                    pt, Rp = bank()
                    self.mm(pt[:, :], cSU[:], ld[:, sl], True, True, [Rsc[2], Rc], [Rp])
                    self.act(sc[3][:, sl], pt[:, :], AF.Exp, [Rp], [Rsc[3]])
                for nh in range(2):
                    sl = slice(nh * 512, (nh + 1) * 512)
                    pt, Rp = bank()
                    self.mm(pt[:, :], onesf[:], ld[:, sl], True, True, [Rsc[2], Rc], [Rp])
                    self.act(sc[4][:, sl], pt[:, :], AF.Exp, [Rp], [Rsc[4]])
                self.tt("pool", sc[4][:], sc[4][:], sc[1][:], ALU.mult, [Rsc[4], Rsc[1]], [Rsc[4]])
                pt, Rp = bank()
                for c in range(8):
                    self.mm(pt[:, c:c + 1], ld[:, c * 128:(c + 1) * 128], onesf[:, 0:1], True, True, [Rsc[2], Rc], [Rp])
                self.act(PCc[:], pt[:, 0:8], AF.Exp, [Rp], [R["PCc"]])
                if stop <= 3:
                    early(n, t0)
                    continue
                def prod_T(j, eng, in0, Rin0, in1, Rin1, dstT, RdstT, neg=False):
                    if neg:
                        self.stt("dve", prod[j][:], in0, -1.0, in1, ALU.mult, ALU.mult, [Rin0, Rin1], [Rprod[j]])
                    else:
                        self.tt(eng, prod[j][:], in0, in1, ALU.mult, [Rin0, Rin1], [Rprod[j]])
                    tpt, Rtp = tbank()
                    for c in range(8):
                        self.tr(tpt[:, c * 128:(c + 1) * 128], prod[j][:, c * 128:(c + 1) * 128], self.ident[:], [Rprod[j], Rc], [Rtp])
                    self.copy("act", dstT[:, :, :], tpt[:, :].rearrange("p (c t) -> p c t", c=8), [Rtp], [RdstT])
                prod_T(0, "dve", r_bf[:], R["r_bf"], sc[0][:], Rsc[0], RtT, R["RtT"])
                prod_T(1, "pool", kp_bf[:], R["kp_bf"], sc[1][:], Rsc[1], KtT, R["KtT"])
                prod_T(0, "dve", b_bf[:], R["b_bf"], sc[1][:], Rsc[1], BtT, R["BtT"])
                prod_T(1, "dve", kk_bf[:], R["kk_bf"], sc[3][:], Rsc[3], AtT, R["AtT"], neg=True)
                self.tt("pool", Khat[:], kp_bf[:], sc[4][:], ALU.mult, [R["kp_bf"], Rsc[4]], [R["Khat"]])
                self.tt("dve", Bhat[:], b_bf[:], sc[4][:], ALU.mult, [R["b_bf"], Rsc[4]], [R["Bhat"]])
                if stop <= 4:
                    early(n, t0)
                    continue
                def hv(T, h):
                    return T[(h % 2) * 64:(h % 2) * 64 + 64, h // 2, :]
                for half in range(2):
                    hs = range(half * 8, half * 8 + 8)
                    hb0 = half * 8
                    v4 = lambda pt_: pt_[:, :].rearrange("p (h t) -> p h t", h=4)

                    def amat(lhs_T, Rl, rhs_T, Rr, dst, dbase, maskb, Rdst):
                        (pe_, Rpe), (po_, Rpo) = bank(), bank()
                        for i in range(4):
                            he, ho = hb0 + 2 * i, hb0 + 2 * i + 1
                            self.mm(pe_[:, i * 128:(i + 1) * 128], hv(lhs_T, he), hv(rhs_T, he), True, True, [Rl, Rr], [Rpe])
                            self.mm(po_[:, i * 128:(i + 1) * 128], hv(lhs_T, ho), hv(rhs_T, ho), True, True, [Rl, Rr], [Rpo])
                        self.tt("dve", dst[:, dbase + 0:dbase + 8:2, :], v4(pe_), maskb, ALU.mult, [Rpe, Rc], [Rdst])
                        self.tt("dve", dst[:, dbase + 1:dbase + 8:2, :], v4(po_), maskb, ALU.mult, [Rpo, Rc], [Rdst])

                    amat(BtT, R["BtT"], AtT, R["AtT"], Mb[0], 0, SUb(4), RMb[0])
                    amat(AtT, R["AtT"], BtT, R["BtT"], Nb[0], 0, SLb(4), RNb[0])
                    amat(BtT, R["BtT"], RtT, R["RtT"], ArbT, hb0, IUb(4), R["ArbT"])
                    amat(KtT, R["KtT"], AtT, R["AtT"], AakT, hb0, SUb(4), R["AakT"])
                    amat(KtT, R["KtT"], RtT, R["RtT"], ArkT, hb0, IUb(4), R["ArkT"])
                    if stop <= 5:
                        continue
                    self.tt("pool", Xb[0][:], Mb[0][:], IDb(8), ALU.add, [RMb[0], Rc], [RXb[0]])
                    cm, cn, cx = 0, 0, 0
                    for k in range(1, 7):
                        for q4 in range(2):
                            pt, Rp = bank()
                            for i in range(4):
                                hh = q4 * 4 + i
                                self.mm(pt[:, i * 128:(i + 1) * 128], Mb[cm][:, hh, :], Nb[cn][:, hh, :], True, True, [RMb[cm], RNb[cn]], [Rp])
                            self.copy("act", Nb[1 - cn][:, q4 * 4:q4 * 4 + 4, :], pt[:, :].rearrange("p (h t) -> p h t", h=4), [Rp], [RNb[1 - cn]])
                        if k < 6:
                            for q4 in range(2):
                                pt, Rp = bank()
                                for i in range(4):
                                    hh = q4 * 4 + i
                                    self.mm(pt[:, i * 128:(i + 1) * 128], Nb[cn][:, hh, :], Mb[cm][:, hh, :], True, True, [RMb[cm], RNb[cn]], [Rp])
                                self.copy("act", Mb[1 - cm][:, q4 * 4:q4 * 4 + 4, :], pt[:, :].rearrange("p (h t) -> p h t", h=4), [Rp], [RMb[1 - cm]])
                        cn = 1 - cn
                        if k < 6:
                            cm = 1 - cm
                        for q4 in range(2):
                            pt, Rp = bank()
                            for i in range(4):
                                hh = q4 * 4 + i
                                self.mm(pt[:, i * 128:(i + 1) * 128], Nb[cn][:, hh, :], Xb[cx][:, hh, :], True, True, [RNb[cn], RXb[cx]], [Rp])
                            if k < 6:
                                dst, Rdst = Xb[1 - cx][:, q4 * 4:q4 * 4 + 4, :], RXb[1 - cx]
                            else:
                                dst, Rdst = XT[:, half * 8 + q4 * 4:half * 8 + q4 * 4 + 4, :], R["XT"]
                            self.tt("dve", dst, pt[:, :].rearrange("p (h t) -> p h t", h=4), Xb[cx][:, q4 * 4:q4 * 4 + 4, :], ALU.add,
                                    [Rp, RXb[cx]], [Rdst])
                        cx = 1 - cx
                if stop <= 6:
                    early(n, t0)
                    continue
                sthv = lambda h: STb[(h % 2) * 64:(h % 2) * 64 + 64, h // 2, :]
                hc_ = lambda T, h: T[:, h * 64:(h + 1) * 64]
                bks = [bank(), bank()]
                for h in range(16):
                    pt, Rp = bks[h // 8]
                    o_ = pt[:, (h % 8) * 64:(h % 8) * 64 + 64]
                    self.mm(o_, hv(AtT, h), sthv(h), True, False, [R["AtT"], R["STb"]], [Rp])
                    self.mm(o_, AakT[:, h, :], hc_(v_bf, h), False, True, [R["AakT"], R["v_bf"]], [Rp])
                for nh, (pt, Rp) in enumerate(bks):
                    self.copy("act", RHS[:, nh * 512:(nh + 1) * 512], pt[:, :], [Rp], [R["RHS"]])
                bks = [bank(), bank()]
                for h in range(16):
                    pt, Rp = bks[h // 8]
                    self.mm(pt[:, (h % 8) * 64:(h % 8) * 64 + 64], XT[:, h, :], hc_(RHS, h), True, True, [R["XT"], R["RHS"]], [Rp])
                for nh, (pt, Rp) in enumerate(bks):
                    self.copy("act", U[:, nh * 512:(nh + 1) * 512], pt[:, :], [Rp], [R["U"]])
                bks = [bank(), bank()]
                for h in range(16):
                    pt, Rp = bks[h // 8]
                    o_ = pt[:, (h % 8) * 64:(h % 8) * 64 + 64]
                    self.mm(o_, hv(RtT, h), sthv(h), True, False, [R["RtT"], R["STb"]], [Rp])
                    self.mm(o_, ArbT[:, h, :], hc_(U, h), False, False, [R["ArbT"], R["U"]], [Rp])
                    self.mm(o_, ArkT[:, h, :], hc_(v_bf, h), False, True, [R["ArkT"], R["v_bf"]], [Rp])
                for nh, (pt, Rp) in enumerate(bks):
                    self.copy("act", sc[0][:, nh * 512:(nh + 1) * 512], pt[:, :], [Rp], [Rsc[0]])
                pt, Rp = bank()
                for h in range(16):
                    o_ = pt[(h % 2) * 64:(h % 2) * 64 + 64, (h // 2) * 64:(h // 2) * 64 + 64]
                    self.mm(o_, hc_(Bhat, h), hc_(U, h), True, False, [R["Bhat"], R["U"]], [Rp], tile_position=(0, (h % 2) * 64))
                    self.mm(o_, hc_(Khat, h), hc_(v_bf, h), False, True, [R["Khat"], R["v_bf"]], [Rp], tile_position=(0, (h % 2) * 64))
                pcb = PCc[:, :].unsqueeze(2).broadcast_to([128, 8, 64])
                self.tt("pool", STt[:], ST[:], pcb, ALU.mult, [R["ST"], R["PCc"]], [R["STt"]])
                self.tt("dve", ST[:], STt[:], pt[:, :].rearrange("p (c v) -> p c v", c=8), ALU.add, [R["STt"], Rp], [R["ST"]])
                self.copy("pool", STb[:], ST[:], [R["ST"]], [R["STb"]])
                if stop <= 7:
                    early(n, t0)
                    continue
                y = sc[0]
                self.P.emit("dve", lambda e_: e_.tensor_reduce(out=st16[:, 3, :], in_=h3(y[:, :]), axis=AX.X, op=ALU.add),
                            [Rsc[0]], [R["st16"]])
                self.act(sc[1][:], y[:], AF.Square, [Rsc[0]], [Rsc[1]])
                self.P.emit("dve", lambda e_: e_.tensor_reduce(out=st16[:, 4, :], in_=h3(sc[1][:, :]), axis=AX.X, op=ALU.add),
                            [Rsc[1]], [R["st16"]])
                self.ts("dve", st16[:, 3, :], st16[:, 3, :], 1.0 / 64, None, ALU.mult, None, [R["st16"]], [R["st16"]])
                self.tt("dve", st16[:, 5, :], st16[:, 3, :], st16[:, 3, :], ALU.mult, [R["st16"]], [R["st16"]])
                self.stt("dve", st16[:, 4, :], st16[:, 4, :], 1.0 / 64, st16[:, 5, :], ALU.mult, ALU.subtract, [R["st16"]], [R["st16"]])
                self.act(st16[:, 4, :], st16[:, 4, :], AF.Ln, [R["st16"], Rc], [R["st16"]], bias=self.eps_rms[:, 1:2])
                self.act(st16[:, 4, :], st16[:, 4, :], AF.Exp, [R["st16"]], [R["st16"]], scale=-0.5)
                mb_ = st16[:, 3, :].unsqueeze(2).broadcast_to([128, 16, 64])
                rb_ = st16[:, 4, :].unsqueeze(2).broadcast_to([128, 16, 64])
                bb_ = st16[:, 2, :].unsqueeze(2).broadcast_to([128, 16, 64])
                self.tt("dve", h3(sc[1][:, :]), h3(y[:, :]), mb_, ALU.subtract, [Rsc[0], R["st16"]], [Rsc[1]])
                self.tt("dve", h3(sc[1][:, :]), h3(sc[1][:, :]), rb_, ALU.mult, [Rsc[1], R["st16"]], [Rsc[1]])
                self.tt("pool", sc[1][:], sc[1][:], LNW, ALU.mult, [Rsc[1], R["w"]], [Rsc[1]])
                self.tt("pool", sc[1][:], sc[1][:], LNB, ALU.add, [Rsc[1], R["w"]], [Rsc[1]])
                self.tt("dve", h3(sc[3][:, :]), h3(v_bf[:, :]), bb_, ALU.mult, [R["v_bf"], R["st16"]], [Rsc[3]])
                self.tt("pool", sc[1][:], sc[1][:], sc[3][:], ALU.add, [Rsc[1], Rsc[3]], [Rsc[1]])
                self.tt("dve", yfin[:], sc[1][:], g_bf[:], ALU.mult, [Rsc[1], R["g_bf"]], [R["yfin"]])
                tpt, Rtp = tbank()
                for c in range(8):
                    self.tr(tpt[:, c * 128:(c + 1) * 128], yfin[:, c * 128:(c + 1) * 128], self.ident[:], [R["yfin"], Rc], [Rtp])
                self.copy("act", yT[:, :, :], tpt[:, :].rearrange("p (c t) -> p c t", c=8), [Rtp], [R["yT"]])
                for nh, (pt, Rp) in enumerate(proj_tok(yT, R["yT"], Wo)):
                    sl = slice(nh * 512, (nh + 1) * 512)
                    self.tt("dve", ot[:, sl], pt[:, :], xt[:, sl], ALU.add, [Rp, R["xt"]], [R["ot"]])
                self.dma("sp", xout[t0:t0 + 128, :], ot[:], d_o, [R["ot"]], [Rxdram[n]])
            P.barrier()


def build_program(S, sublayers, n_cores=8):
    nc = bass.Bass("TRN2", target_bir_lowering=False)
    specs = param_specs()
    prm = {}
    x = nc.dram_tensor("x", [S, D], F32, kind="ExternalInput").ap()
    for name, shp in specs.items():
        prm[name] = nc.dram_tensor(name, list(shp), F32, kind="ExternalInput").ap()
    out = nc.dram_tensor("out", [S, D], F32, kind="ExternalOutput").ap()
    with ExitStack() as st:
        kb = KB(nc, S, st)
        kb.Rxdram = [Res() for _ in range(S // 128)]
        kb.Rvf = [Res() for _ in range(S // 128)]
        kb.vfirst = nc.dram_tensor("vfirst_scratch", [S, D], F32).ap()
        kb.setup_consts()
        cur = x
        for sl in sublayers:
            if sl[0] == "ffn":
                kb.ffn_phase(sl[1], cur, out, prm)
            elif sl[0] == "hy":
                kb.hy_phase(sl[1], sl[2], cur, out, prm)
            elif sl[0] == "rw":
                kb.rw_phase(sl[1], sl[2], cur, out, prm)
            cur = out
        kb.P.barrier()
        kb.P.finalize()
        kb.stats = (dict(kb.P.n), kb.P.nwaits)
        print("instr counts", kb.P.n, "waits", kb.P.nwaits)
    return nc


def param_specs():
    return {
        "mix_norm": (4, D), "ffn_norm": (4, D),
        "ffn_w_gate": (4, D, DFF), "ffn_w_up": (4, D, DFF), "ffn_w_down": (4, DFF, D),
        "hy_w_in": (2, D, IN_COLS), "hy_f_bias": (2, 8), "hy_q_gain": (2, 64), "hy_k_gain": (2, 64),
        "hy_pool_w": (2, 4, 128, 128), "hy_pool_scale": (2, 512), "hy_w_out": (2, D, D),
        "rw_mu": (2, 6, D), "rw_w_r": (2, D, D), "rw_w_k": (2, D, D), "rw_w_v": (2, D, D),
        "rw_w0": (2, D), "rw_w1": (2, D, 64), "rw_w2": (2, 64, D), "rw_a0": (2, D), "rw_a1": (2, D, 64),
        "rw_a2": (2, 64, D), "rw_g1": (2, D, 160), "rw_g2": (2, 160, D), "rw_k_k": (2, D), "rw_k_a": (2, D),
        "rw_r_k": (2, 16, 64), "rw_ln_w": (2, D), "rw_ln_b": (2, D), "rw_w_o": (2, D, D),
        "rw_v0": (1, D), "rw_v1": (1, D, 32), "rw_v2": (1, 32, D),
    }


FULL = [("hy", 0, 0), ("ffn", 0), ("rw", 0, 1), ("ffn", 1), ("hy", 1, 2), ("ffn", 2), ("rw", 1, 3), ("ffn", 3)]


def run(inputs, S, sublayers, n_cores=8, trace=False):
    nc = build_program(S, sublayers)
    specs = param_specs()
    x = np.ascontiguousarray(np.asarray(inputs["x"], dtype=np.float32))
    shared = {k: np.ascontiguousarray(np.asarray(inputs[k], dtype=np.float32)) for k in specs}
    in_maps = []
    for c in range(n_cores):
        m = dict(shared)
        m["x"] = x[c]
        in_maps.append(m)
    res = run_bass_kernel_spmd(nc, in_maps, core_ids=list(range(n_cores)), trace=trace)
    outs = np.stack([np.asarray(r["out"]) for r in res.results], axis=0)
    return outs, res


def kernel(**inputs):
    outs, _ = run(inputs, 4096, FULL, n_cores=8)
    return outs.astype(np.float32)
```

```python
import numpy as np
from contextlib import ExitStack
import concourse.bass as bass
import concourse.mybir as mybir
from concourse.bass_utils import run_bass_kernel_spmd

F32 = mybir.dt.float32
BF16 = mybir.dt.bfloat16
AF = mybir.ActivationFunctionType
ALU = mybir.AluOpType
AX = mybir.AxisListType

D = 1024
DFF = 2816
NFC = DFF // 128
IN_COLS = 2568
RMS_EPS = 1e-6
GN_EPS = 64e-5
ENGS = ("pe", "act", "dve", "pool", "sp")
CH = 16000


class Res:
    __slots__ = ("name", "w", "r")

    def __init__(self, name=""):
        self.name = name
        self.w = None
        self.r = {}


class DmaSem:
    __slots__ = ("key", "sem", "count")

    def __init__(self, key, sem):
        self.key = key
        self.sem = sem
        self.count = 0


class Prog:
    def __init__(self, nc, stack, same_engine_sync=True):
        self.nc = nc
        self.stack = stack
        self.q = {e: [] for e in ENGS}
        self.n = {e: 0 for e in ENGS}
        self.esem = {}
        self.seen = {e: {} for e in ENGS}
        self.same = same_engine_sync
        self.dsems = []
        self.nwaits = 0

    def dmasem(self, name):
        s = self.stack.enter_context(self.nc.semaphore("d%d_%s" % (len(self.dsems), name)))
        d = DmaSem("d%d_%s" % (len(self.dsems), name), s)
        self.dsems.append(d)
        return d

    def _esem(self, e, k):
        if (e, k) not in self.esem:
            self.esem[(e, k)] = self.stack.enter_context(self.nc.semaphore("e_%s_%d" % (e, k)))
        return self.esem[(e, k)]

    def emit(self, eng, fn, reads=(), writes=(), dma=None):
        need = {}

        def want(ev):
            if ev is None:
                return
            key, val = ev[0], ev[1]
            if key == eng and (eng == "pe" or not self.same):
                return
            if ev[2] is not None:
                val = ev[2].count
            if need.get(key, (0,))[0] < val:
                need[key] = (val, ev[2])

        for r in reads:
            want(r.w)
        for w in writes:
            want(w.w)
            for ev in w.r.values():
                want(ev)
        waits = []
        seen = self.seen[eng]
        for key, (val, hinfo) in need.items():
            if seen.get(key, 0) >= val:
                continue
            seen[key] = val
            if key in ENGS:
                k = (val - 1) // CH
                waits.append((self._esem(key, k), val - k * CH))
            else:
                waits.append((hinfo.sem, val))
        self.nwaits += len(waits)
        if fn is None:
            if waits:
                self.q[eng].append((waits, None, None))
            return None
        if dma is None:
            self.n[eng] += 1
            idx = self.n[eng]
            k = (idx - 1) // CH
            inc = (self._esem(eng, k), 1)
            ev = (eng, idx, None)
        else:
            dma.count += 16
            inc = (dma.sem, 16)
            ev = (dma.key, dma.count, dma)
        self.q[eng].append((waits, fn, inc))
        for r in reads:
            r.r[ev[0]] = ev
        for w in writes:
            w.w = ev
            w.r = {}
        return ev

    def barrier(self):
        evs = [(e, self.n[e], None) for e in ENGS if self.n[e] > 0]
        evs += [(d.key, d.count, d) for d in self.dsems if d.count > 0]
        for eng in ENGS:
            tmp = Res()
            tmp.r = {ev[0]: ev for ev in evs if ev[0] != eng}
            self.emit(eng, None, writes=[tmp])

    def finalize(self):
        nc = self.nc
        with nc.Block() as block:
            def mk(ename):
                def body(e):
                    for waits, fn, inc in self.q[ename]:
                        for sem, val in waits:
                            e.wait_ge(sem, val)
                        if fn is not None:
                            fn(e).then_inc(inc[0], inc[1])
                return body
            block.tensor(mk("pe"))
            block.scalar(mk("act"))
            block.vector(mk("dve"))
            block.gpsimd(mk("pool"))
            block.sync(mk("sp"))


class KB:
    def __init__(self, nc, S, stack):
        self.nc = nc
        self.S = S
        self.st = stack
        self.P = Prog(nc, stack)

    def sb(self, name, shape, dtype, stack=None):
        self.uid = getattr(self, "uid", 0) + 1
        return (stack or self.st).enter_context(self.nc.sbuf_tensor("%s_u%d" % (name, self.uid), list(shape), dtype))

    def ps(self, name, dtype=F32, stack=None):
        n = 512 if dtype == F32 else 1024
        self.uid = getattr(self, "uid", 0) + 1
        return (stack or self.st).enter_context(self.nc.psum_tensor("%s_u%d" % (name, self.uid), [128, n], dtype))

    def mm(self, out, lhsT, rhs, start, stop, R, W, **kw):
        return self.P.emit("pe", lambda e: e.matmul(out, lhsT=lhsT, rhs=rhs, start=start, stop=stop, **kw), R, W)

    def tr(self, out, in_, ident, R, W):
        return self.P.emit("pe", lambda e: e.transpose(out, in_, ident), R, W)

    def act(self, out, in_, func, R, W, eng="act", **kw):
        return self.P.emit(eng, lambda e: e.activation(out=out, in_=in_, func=func, **kw), R, W)

    def copy(self, eng, out, in_, R, W):
        if eng == "act":
            return self.P.emit("act", lambda e: e.copy(out=out, in_=in_), R, W)
        return self.P.emit(eng, lambda e: e.tensor_copy(out=out, in_=in_), R, W)

    def tt(self, eng, out, in0, in1, op, R, W):
        return self.P.emit(eng, lambda e: e.tensor_tensor(out=out, in0=in0, in1=in1, op=op), R, W)

    def ts(self, eng, out, in0, s1, s2, op0, op1, R, W, **kw):
        if s2 is None:
            return self.P.emit(eng, lambda e: e.tensor_scalar(out=out, in0=in0, scalar1=s1, scalar2=None, op0=op0, **kw), R, W)
        return self.P.emit(eng, lambda e: e.tensor_scalar(out=out, in0=in0, scalar1=s1, scalar2=s2, op0=op0, op1=op1, **kw), R, W)

    def stt(self, eng, out, in0, scalar, in1, op0, op1, R, W):
        return self.P.emit(eng, lambda e: e.scalar_tensor_tensor(out=out, in0=in0, scalar=scalar, in1=in1, op0=op0, op1=op1), R, W)

    def memset(self, eng, ap, val, W):
        return self.P.emit(eng, lambda e: e.memset(ap, val), (), W)

    def dma(self, eng, out, in_, sem, R, W, slow=False):
        if slow:
            return self.P.emit(eng, lambda e: e.dma_start(out=out, in_=in_, allow_slow_non_contiguous=True), R, W, dma=sem)
        return self.P.emit(eng, lambda e: e.dma_start(out=out, in_=in_), R, W, dma=sem)

    def setup_consts(self):
        nc = self.nc
        self.ident = self.sb("ident", [128, 128], BF16)
        self.Rconst = Res("const")
        W = [self.Rconst]
        self.eps_rms = self.sb("eps_rms", [128, 4], F32)
        self.memset("pool", self.eps_rms[:, 0:1], RMS_EPS, W)
        self.memset("pool", self.eps_rms[:, 1:2], GN_EPS, W)
        self.memset("pool", self.eps_rms[:, 2:3], 1.0, W)
        self.memset("pool", self.eps_rms[:, 3:4], 0.0, W)
        self.memset("pool", self.ident[:], 1.0, W)
        idt = self.ident
        self.P.emit("pool", lambda e: e.affine_select(out=idt[:], in_=idt[:], pattern=[[-1, 128]],
                                                      compare_op=ALU.is_equal, fill=0.0, base=0,
                                                      channel_multiplier=1), (), W)

    def norm_tile(self, xt, Rxt, gain_b, Rgain, hT, RhT, col0, bufs, i):
        junk, Rjunk = bufs["junk"][i % 2]
        ss, Rss = bufs["ss"][i % 2]
        hb, Rhb = bufs["hb"][i % 2]
        tp, Rtp = bufs["tp"][i % len(bufs["tp"])]
        self.act(junk[:], xt[:], AF.Square, [Rxt], [Rjunk, Rss], accum_out=ss[:, 0:1])
        self.act(ss[:, 1:2], ss[:, 0:1], AF.Ln, [Rss, self.Rconst], [Rss], scale=1.0 / D, bias=self.eps_rms[:, 0:1])
        self.act(ss[:, 2:3], ss[:, 1:2], AF.Exp, [Rss], [Rss], scale=-0.5)
        self.stt("dve", hb[:], xt[:], ss[:, 2:3], gain_b[:], ALU.mult, ALU.mult, [Rxt, Rss, Rgain], [Rhb])
        for c in range(8):
            self.tr(tp[:, c * 128:(c + 1) * 128], hb[:, c * 128:(c + 1) * 128], self.ident[:], [Rhb, self.Rconst], [Rtp])
        self.copy("act", hT[:, :, col0:col0 + 128], tp[:, :].rearrange("p (c t) -> p c t", c=8), [Rtp], [RhT])

    def ffn_phase(self, layer, xin, xout, prm):
        nc, P, S = self.nc, self.P, self.S
        TG = min(1024, S)
        NG = S // TG
        NT = TG // 128
        NH = TG // 512
        with ExitStack() as st:
            gain_b = self.sb("f_gain", [128, D], F32, st)
            hT = self.sb("f_hT", [128, 8, TG], BF16, st)
            actT = self.sb("f_actT", [128, NFC, TG], BF16, st)
            wd = self.sb("f_wd", [128, NFC, D], BF16, st)
            wg = [self.sb("f_wg%d" % i, [128, 8, 256], BF16, st) for i in range(2)]
            wu = [self.sb("f_wu%d" % i, [128, 8, 256], BF16, st) for i in range(2)]
            xts = [self.sb("f_xt%d" % i, [128, D], F32, st) for i in range(3)]
            ots = [self.sb("f_ot%d" % i, [128, 512], F32, st) for i in range(2)]
            sil = [self.sb("f_sil%d" % i, [128, 512], F32, st) for i in range(2)]
            bufs = {
                "junk": [(self.sb("f_junk%d" % i, [128, D], BF16, st), Res()) for i in range(2)],
                "ss": [(self.sb("f_ss%d" % i, [128, 4], F32, st), Res()) for i in range(2)],
                "hb": [(self.sb("f_hb%d" % i, [128, D], BF16, st), Res()) for i in range(2)],
                "tp": [(self.ps("f_tp%d" % i, BF16, st), Res()) for i in range(2)],
            }
            pg = [(self.ps("f_pg%d" % i, F32, st), Res()) for i in range(2)]
            pu = [(self.ps("f_pu%d" % i, F32, st), Res()) for i in range(2)]
            po = [(self.ps("f_po%d" % i, F32, st), Res()) for i in range(2)]
            Rgain = Res(); RhT = [Res() for _ in range(NT)]; Ract = [Res() for _ in range(NFC)]
            Rwd = Res(); Rwg = [Res(), Res()]; Rwu = [Res(), Res()]
            Rxt = [Res() for _ in range(3)]; Rot = [Res(), Res()]; Rsil = [Res(), Res()]
            d_gain = P.dmasem("fgain"); d_x = [P.dmasem("fx%d" % i) for i in range(3)]
            d_wg = [P.dmasem("fwg%d" % i) for i in range(2)]; d_wu = [P.dmasem("fwu%d" % i) for i in range(2)]
            d_wd = P.dmasem("fwd"); d_o = [P.dmasem("fo%d" % i) for i in range(2)]
            Rxdram = self.Rxdram

            self.dma("sp", gain_b[:], prm["ffn_norm"][layer:layer + 1, :].partition_broadcast(128), d_gain, [], [Rgain])
            wgv = prm["ffn_w_gate"][layer].rearrange("(c p) f -> p c f", p=128)
            wuv = prm["ffn_w_up"][layer].rearrange("(c p) f -> p c f", p=128)
            wdv = prm["ffn_w_down"][layer].rearrange("(c p) n -> p c n", p=128)
            xcnt = 0
            ocnt = 0
            step = 0
            for g in range(NG):
                t0 = g * TG
                for c0 in range(0, NFC, 2):
                    self.dma("pool", wd[:, c0:c0 + 2, :], wdv[:, c0:c0 + 2, :], d_wd, [], [Rwd])
                for i in range(NT):
                    b = xcnt % 3
                    tix = (t0 // 128) + i
                    self.dma("sp", xts[b][:], xin[t0 + i * 128:t0 + (i + 1) * 128, :], d_x[b], [Rxdram[tix]], [Rxt[b]])
                    self.norm_tile(xts[b], Rxt[b], gain_b, Rgain, hT, RhT[i], i * 128, bufs, xcnt)
                    xcnt += 1
                for fg in range(NFC // 2):
                    wb = fg % 2
                    self.dma("pool", wg[wb][:], wgv[:, :, fg * 256:(fg + 1) * 256], d_wg[wb], [], [Rwg[wb]])
                    self.dma("pool", wu[wb][:], wuv[:, :, fg * 256:(fg + 1) * 256], d_wu[wb], [], [Rwu[wb]])
                    for fc in range(2):
                        f = fg * 2 + fc
                        for th in range(NH):
                            pb = step % 2
                            pgt, Rpg = pg[pb]
                            put, Rpu = pu[pb]
                            rh = RhT[th * 4:(th + 1) * 4]
                            for c in range(8):
                                self.mm(pgt[:, :], wg[wb][:, c, fc * 128:(fc + 1) * 128], hT[:, c, th * 512:(th + 1) * 512],
                                        c == 0, c == 7, [Rwg[wb]] + rh, [Rpg])
                            for c in range(8):
                                self.mm(put[:, :], wu[wb][:, c, fc * 128:(fc + 1) * 128], hT[:, c, th * 512:(th + 1) * 512],
                                        c == 0, c == 7, [Rwu[wb]] + rh, [Rpu])
                            self.act(sil[pb][:], pgt[:, :], AF.Silu, [Rpg], [Rsil[pb]])
                            self.tt("dve", actT[:, f, th * 512:(th + 1) * 512], sil[pb][:], put[:, :], ALU.mult,
                                    [Rsil[pb], Rpu], [Ract[f]])
                            step += 1
                for i in range(NT):
                    b = xcnt % 3
                    tix = (t0 // 128) + i
                    self.dma("sp", xts[b][:], xin[t0 + i * 128:t0 + (i + 1) * 128, :], d_x[b], [Rxdram[tix]], [Rxt[b]])
                    xcnt += 1
                    for nh in range(2):
                        ob = ocnt % 2
                        pot, Rpo = po[ob]
                        for f in range(NFC):
                            self.mm(pot[:, :], actT[:, f, i * 128:(i + 1) * 128], wd[:, f, nh * 512:(nh + 1) * 512],
                                    f == 0, f == NFC - 1, [Ract[f], Rwd], [Rpo])
                        self.tt("dve", ots[ob][:], pot[:, :], xts[b][:, nh * 512:(nh + 1) * 512], ALU.add,
                                [Rpo, Rxt[b]], [Rot[ob]])
                        self.dma("sp", xout[t0 + i * 128:t0 + (i + 1) * 128, nh * 512:(nh + 1) * 512], ots[ob][:], d_o[ob],
                                 [Rot[ob]], [Rxdram[tix]])
                        ocnt += 1
            P.barrier()


    def hy_consts(self, st):
        c = {}
        W = [self.Rconst]
        sb = lambda n, shp, dt: self.sb(n, shp, dt, st)
        c["negmask"] = sb("c_negmask", [128, 128], BF16)
        c["tri"] = sb("c_tri", [128, 128], F32)
        c["nones"] = sb("c_nones", [128, 128], F32)
        c["identf"] = sb("c_identf", [128, 128], F32)
        c["bd"] = sb("c_bd", [128, 128], BF16)
        c["esel"] = sb("c_esel", [72, 8, 128], BF16)
        c["ones64"] = sb("c_ones64", [128, 64], BF16)
        c["invc"] = sb("c_invc", [128, 4, 16], F32)
        nm, tri, nones, identf, bd, esel, ones64, invc = (c[k] for k in ("negmask", "tri", "nones", "identf", "bd", "esel", "ones64", "invc"))
        self.memset("pool", nm[:], 0.0, W)
        self.P.emit("pool", lambda e: e.affine_select(out=nm[:], in_=nm[:], pattern=[[1, 128]], compare_op=ALU.is_ge,
                                                      fill=-30000.0, base=0, channel_multiplier=-1), (), W)
        self.memset("pool", tri[:], -1.0, W)
        self.P.emit("pool", lambda e: e.affine_select(out=tri[:], in_=tri[:], pattern=[[1, 128]], compare_op=ALU.is_ge,
                                                      fill=0.0, base=0, channel_multiplier=-1), (), W)
        self.memset("pool", nones[:], -1.0, W)
        self.memset("pool", identf[:], 1.0, W)
        self.P.emit("pool", lambda e: e.affine_select(out=identf[:], in_=identf[:], pattern=[[-1, 128]], compare_op=ALU.is_equal,
                                                      fill=0.0, base=0, channel_multiplier=1), (), W)
        self.memset("pool", bd[:], 0.0, W)
        self.memset("pool", bd[0:64, 0:64], 1.0, W)
        self.memset("pool", bd[64:128, 64:128], 1.0, W)
        self.memset("pool", esel[0:8], 8.0, W)
        self.P.emit("pool", lambda e: e.affine_select(out=esel[0:8], in_=esel[0:8], pattern=[[1, 8], [0, 128]], compare_op=ALU.is_equal,
                                                      fill=0.0, base=0, channel_multiplier=-1), (), W)
        d_e = self.P.dmasem("esel")
        self.dma("sp", esel[64:72], esel[0:8], d_e, [self.Rconst], [self.Rconst])
        self.memset("pool", ones64[:], 1.0, W)
        for g, w in enumerate((2, 4, 8, 16)):
            self.memset("pool", invc[:, g, :], 1.0 / w, W)
            for t in range(w - 1):
                self.memset("pool", invc[:, g, t:t + 1], 1.0 / (t + 1), W)
        return c

    def hy_phase(self, e, layer, xin, xout, prm):
        nc, P, S = self.nc, self.P, self.S
        NI = S // 512
        NB = S // 128
        Rc = self.Rconst
        with ExitStack() as st:
            cst = self.hy_consts(st)
            sb = lambda n, shp, dt: self.sb("h_" + n, shp, dt, st)
            w_in = sb("w_in", [128, 8, IN_COLS], BF16)
            w_out = sb("w_out", [128, 8, D], BF16)
            pw = sb("pw", [128, 4, 128], BF16)
            gain_b = sb("gain", [128, D], F32)
            qg = sb("qg", [128, 1], F32); kg = sb("kg", [128, 1], F32)
            pscale = sb("pscale", [128, 4], F32)
            fb = sb("fb", [128, 8], F32)
            kT = sb("kT", [128, 4, S], BF16)
            vc = sb("vc", [128, NB, 512], BF16)
            cumK = sb("cumK", [128, NB, 8], F32)
            kb = sb("kb", [128, NB, 8], F32)
            hc = sb("hc", [128, 8, 512], BF16)
            qT = sb("qT", [128, 4, 512], BF16)
            sgT = sb("sgT", [128, 4, 512], BF16)
            upad = sb("upad", [128, 4, 528], F32)
            tA = sb("tA", [128, 528], F32); tB = sb("tB", [128, 528], F32)
            pooledT = sb("pooledT", [128, 4, 512], BF16)
            xts = [sb("xt%d" % i, [128, D], F32) for i in range(2)]
            ots = [sb("ot%d" % i, [128, 512], F32) for i in range(2)]
            junk = sb("junk", [128, D], BF16)
            kf = sb("kf", [128, 512], F32); sq = sb("sq", [128, 512], BF16); rs = sb("rs", [128, 512], F32)
            pT = [sb("pT%d" % i, [128, 512], BF16) for i in range(4)]
            rden = sb("rden", [128, 512], F32); atmp = sb("atmp", [128, 512], F32)
            carry = [sb("carry%d" % i, [128, 8], F32) for i in range(2)]
            zf = sb("zf", [128, 8], F32); lf = sb("lf", [128, 8], F32)
            qctok = sb("qctok", [128, 4, 8], F32)
            qcT = sb("qcT", [72, 512], BF16)
            t16 = sb("t16", [128, 16], F32)
            bufs = {
                "junk": [(junk, Res()), (junk, Res())],
                "ss": [(sb("ss%d" % i, [128, 4], F32), Res()) for i in range(2)],
                "hb": [(sb("hb%d" % i, [128, D], BF16), Res()) for i in range(1)] * 2,
            }
            gp = [(self.ps("h_gp%d" % i, F32, st), Res()) for i in range(2)]
            bufs["tp"] = [(g_[:, :].bitcast(BF16), Rg_) for g_, Rg_ in gp]
            sbk = [(self.ps("h_s%d" % i, F32, st), Res()) for i in range(4)]
            accN, RaccN = self.ps("h_accN", F32, st), Res()
            accD, RaccD = self.ps("h_accD", F32, st), Res()
            Rw = Res(); Rgain = Res(); Rsmall = Res()
            Rhc = Res(); RqT = Res(); RsgT = Res(); RkT = [Res() for _ in range(NI)]; Rvc = [Res() for _ in range(NB)]
            RcumK = Res(); Rkb = Res(); Rupad = Res(); RtA = Res(); RtB = Res(); Rpooled = Res()
            Rxt = [Res(), Res()]; Rot = [Res(), Res()]; Rkf = Res(); Rsq = Res(); Rrs = Res()
            RpT = [Res() for _ in range(4)]; Rrden = Res(); Ratmp = Res(); Rcarry = [Res(), Res()]
            Rzf = Res(); Rlf = Res(); Rqctok = Res(); RqcT = Res(); Rt16 = Res()
            d_q = P.dmasem("hq"); d_w = P.dmasem("hw"); d_c = P.dmasem("hc"); d_x = [P.dmasem("hx%d" % i) for i in range(2)]
            d_o = [P.dmasem("ho%d" % i) for i in range(2)]
            Rxdram = self.Rxdram
            win_v = prm["hy_w_in"][e].rearrange("(c p) n -> p c n", p=128)
            wout_v = prm["hy_w_out"][e].rearrange("(c p) n -> p c n", p=128)
            for c in range(8):
                self.dma("pool", w_in[:, c, :], win_v[:, c, :], d_w, [], [Rw])
            for c in range(8):
                self.dma("pool", w_out[:, c, :], wout_v[:, c, :], d_w, [], [Rw])
            self.dma("pool", pw[:], prm["hy_pool_w"][e].rearrange("g c d -> c g d"), d_w, [], [Rw])
            self.dma("sp", gain_b[:], prm["mix_norm"][layer:layer + 1, :].partition_broadcast(128), d_c, [], [Rgain])
            qgv = prm["hy_q_gain"][e].rearrange("(d o) -> d o", o=1)
            kgv = prm["hy_k_gain"][e].rearrange("(d o) -> d o", o=1)
            for hp in range(2):
                self.dma("sp", qg[hp * 64:(hp + 1) * 64, :], qgv, d_c, [], [Rsmall])
                self.dma("sp", kg[hp * 64:(hp + 1) * 64, :], kgv, d_c, [], [Rsmall])
            self.dma("sp", pscale[:], prm["hy_pool_scale"][e].rearrange("(g p) -> p g", p=128), d_c, [], [Rsmall], slow=True)
            self.dma("sp", fb[:], prm["hy_f_bias"][e:e + 1, :].partition_broadcast(128), d_c, [], [Rsmall])
            self.memset("pool", carry[0][:], 0.0, [Rcarry[0]])
            self.memset("pool", upad[:, :, 0:16], 0.0, [Rupad])
            xcnt = 0; ocnt = 0; gcnt = 0; scnt = 0; pcnt = 0; ccnt = 0

            def proj_fm(col0, gi):
                pt, Rp = gp[gi % 2]
                for c in range(8):
                    self.mm(pt[:, :], w_in[:, c, col0:col0 + 128], hc[:, c, :], c == 0, c == 7, [Rw, Rhc], [Rp])
                return pt, Rp

            for I in range(NI):
                t0 = I * 512
                for i in range(4):
                    b = xcnt % 2
                    tix = I * 4 + i
                    self.dma("sp", xts[b][:], xin[t0 + i * 128:t0 + (i + 1) * 128, :], d_x[b], [Rxdram[tix]], [Rxt[b]])
                    self.norm_tile(xts[b], Rxt[b], gain_b, Rgain, hc, Rhc, i * 128, bufs, xcnt)
                    xcnt += 1
                for i in range(4):
                    blk = I * 4 + i
                    smp, Rsm = gp[gcnt % 2]; gcnt += 1
                    for c in range(8):
                        self.mm(smp[:, 0:8], hc[:, c, i * 128:(i + 1) * 128], w_in[:, c, 2048:2056], c == 0, c == 7, [Rw, Rhc], [Rsm])
                    self.tt("dve", zf[:], smp[:, 0:8], fb[:], ALU.add, [Rsm, Rsmall], [Rzf])
                    self.act(lf[:], zf[:], AF.Exp, [Rzf], [Rlf], scale=-1.0)
                    self.act(lf[:], lf[:], AF.Ln, [Rlf, Rc], [Rlf], bias=self.eps_rms[:, 2:3])
                    self.mm(smp[:, 8:16], cst["tri"][:], lf[:], True, True, [Rlf, Rc], [Rsm])
                    self.mm(smp[:, 16:24], cst["nones"][:], lf[:], True, True, [Rlf, Rc], [Rsm])
                    cin, cout = carry[ccnt % 2], carry[(ccnt + 1) % 2]
                    Rcin, Rcout = Rcarry[ccnt % 2], Rcarry[(ccnt + 1) % 2]
                    self.tt("dve", cumK[:, blk, :], smp[:, 8:16], cin[:], ALU.add, [Rsm, Rcin], [RcumK])
                    self.tt("dve", cout[:], smp[:, 16:24], cin[:], ALU.add, [Rsm, Rcin], [Rcout])
                    ccnt += 1
                cend, Rcend = carry[ccnt % 2], Rcarry[ccnt % 2]
                nj = 4 * I + 4
                cb4 = cend[:, :].unsqueeze(1).broadcast_to([128, 4, 8])
                self.tt("dve", qctok[:], cumK[:, 4 * I:4 * I + 4, :], cb4, ALU.subtract, [RcumK, Rcend], [Rqctok])
                cbn = cend[:, :].unsqueeze(1).broadcast_to([128, nj, 8])
                self.tt("dve", kb[:, 0:nj, :], cbn, cumK[:, 0:nj, :], ALU.subtract, [RcumK, Rcend], [Rkb])
                ptq, Rpq = gp[gcnt % 2]; gcnt += 1
                for i in range(4):
                    self.tr(ptq[0:8, i * 128:(i + 1) * 128], qctok[:, i, :], cst["identf"][:], [Rqctok, Rc], [Rpq])
                self.copy("dve", qcT[0:8, :], ptq[0:8, 0:512], [Rpq], [RqcT])
                self.dma("sp", qcT[64:72, :], qcT[0:8, :], d_q, [RqcT], [RqcT])
                for which in range(2):
                    for cc in range(4):
                        col0 = (512 if which == 0 else 0) + cc * 128
                        pt, Rp = proj_fm(col0, gcnt); gcnt += 1
                        self.copy("act", kf[:], pt[:, :], [Rp], [Rkf])
                        self.tt("pool", sq[:], kf[:], kf[:], ALU.mult, [Rkf], [Rsq])
                        pt2, Rp2 = gp[gcnt % 2]; gcnt += 1
                        self.mm(pt2[:, :], cst["bd"][:], sq[:], True, True, [Rsq, Rc], [Rp2])
                        self.act(rs[:], pt2[:, :], AF.Ln, [Rp2, Rc], [Rrs], scale=1.0 / 64, bias=self.eps_rms[:, 0:1])
                        self.act(rs[:], rs[:], AF.Exp, [Rrs], [Rrs], scale=-0.5)
                        if which == 0:
                            self.stt("dve", kT[:, cc, t0:t0 + 512], kf[:], kg[:, 0:1], rs[:], ALU.mult, ALU.mult,
                                     [Rkf, Rrs, Rsmall], [RkT[I]])
                        else:
                            self.stt("dve", qT[:, cc, :], kf[:], qg[:, 0:1], rs[:], ALU.mult, ALU.mult,
                                     [Rkf, Rrs, Rsmall], [RqT])
                for i in range(4):
                    blk = I * 4 + i
                    pt, Rp = gp[gcnt % 2]; gcnt += 1
                    for c in range(8):
                        self.mm(pt[:, :], hc[:, c, i * 128:(i + 1) * 128], w_in[:, c, 1024:1536], c == 0, c == 7, [Rw, Rhc], [Rp])
                    self.copy("act", vc[:, blk, :], pt[:, :], [Rp], [Rvc[blk]])
                for cc in range(4):
                    pt, Rp = proj_fm(1536 + cc * 128, gcnt); gcnt += 1
                    self.act(sgT[:, cc, :], pt[:, :], AF.Sigmoid, [Rp], [RsgT])
                for g in range(4):
                    pt, Rp = proj_fm(2056 + g * 128, gcnt); gcnt += 1
                    self.copy("act", upad[:, g, 16:528], pt[:, :], [Rp], [Rupad])
                for g in range(4):
                    u = upad[:, g, :]
                    cur, Rcur = u, Rupad
                    lo = 0
                    tmps = [(tA, RtA), (tB, RtB)]
                    for lvl in range(g + 1):
                        sh = 1 << lvl
                        dst, Rdst = tmps[lvl % 2]
                        nlo = lo + sh
                        self.tt("pool", dst[:, nlo:528], cur[:, nlo:528], cur[:, nlo - sh:528 - sh], ALU.add, [Rcur], [Rdst])
                        cur, Rcur, lo = dst, Rdst, nlo
                    wdt = 2 << g
                    self.stt("dve", pooledT[:, g, :], cur[:, 16:528], 1.0 / wdt, u[:, 16:528], ALU.mult, ALU.subtract,
                             [Rcur, Rupad], [Rpooled])
                    if I == 0:
                        self.tt("pool", t16[:], cur[:, 16:32], cst["invc"][:, g, :], ALU.mult, [Rcur, Rc], [Rt16])
                        self.tt("pool", pooledT[:, g, 0:16], t16[:], u[:, 16:32], ALU.subtract, [Rt16, Rupad], [Rpooled])
                self.copy("pool", upad[:, :, 0:16], upad[:, :, 512:528], [Rupad], [Rupad])
                for g in range(4):
                    pt, Rp = gp[gcnt % 2]; gcnt += 1
                    self.mm(pt[:, :], pw[:, g, :], pooledT[:, g, :], True, True, [Rw, Rpooled], [Rp])
                    self.act(hc[:, 4 + g, :], pt[:, :], AF.Copy, [Rp, Rsmall], [Rhc], scale=pscale[:, g:g + 1])
                steps = [(pr, j) for pr in range(4) for j in range(nj)]

                def qk(step, si):
                    pr, j = step
                    jj = j - 4 * I
                    c0 = 128 * jj if jj > 0 else 0
                    diag = jj >= 0
                    Ij = j // 4
                    for hp in range(2):
                        sbt, Rs = sbk[(si % 2) * 2 + hp]
                        self.mm(sbt[:, c0:512], kT[hp * 64:(hp + 1) * 64, pr, j * 128:(j + 1) * 128],
                                qT[hp * 64:(hp + 1) * 64, pr, c0:512], True, False, [RkT[Ij], RqT], [Rs])
                    for hp in range(2):
                        h = pr * 2 + hp
                        sbt, Rs = sbk[(si % 2) * 2 + hp]
                        self.mm(sbt[:, c0:512], cst["esel"][hp * 64:hp * 64 + 8, h, :], qcT[hp * 64:hp * 64 + 8, c0:512],
                                False, not diag, [RqcT, Rc], [Rs])
                    if diag:
                        for hp in range(2):
                            sbt, Rs = sbk[(si % 2) * 2 + hp]
                            self.mm(sbt[:, c0:c0 + 128], self.ident[:], cst["negmask"][:], False, True, [Rc], [Rs])
                    return c0

                def rest(step, si, c0):
                    pr, j = step
                    first = (j == 0)
                    last = (j == nj - 1)
                    pts = []
                    for hp in range(2):
                        h = pr * 2 + hp
                        sbt, Rs = sbk[(si % 2) * 2 + hp]
                        pt_, Rp_ = pT[(si % 2) * 2 + hp], RpT[(si % 2) * 2 + hp]
                        self.act(pt_[:, c0:512], sbt[:, c0:512], AF.Exp, [Rs, Rkb], [Rp_], scale=0.125, bias=kb[:, j, h:h + 1])
                        pts.append((pt_, Rp_))
                    for hp in range(2):
                        h = pr * 2 + hp
                        pt_, Rp_ = pts[hp]
                        self.mm(accN[hp * 64:(hp + 1) * 64, c0:512], vc[:, j, h * 64:(h + 1) * 64], pt_[:, c0:512],
                                first, last, [Rvc[j], Rp_], [RaccN], tile_position=(0, hp * 64))
                    for hp in range(2):
                        pt_, Rp_ = pts[hp]
                        self.mm(accD[hp * 64:(hp + 1) * 64, c0:512], cst["ones64"][:], pt_[:, c0:512],
                                first, last, [Rc, Rp_], [RaccD], tile_position=(0, hp * 64))
                    if last:
                        self.P.emit("dve", lambda e_: e_.reciprocal(out=rden[:], in_=accD[:, :]), [RaccD], [Rrden])
                        self.tt("dve", atmp[:], accN[:, :], rden[:], ALU.mult, [RaccN, Rrden], [Ratmp])
                        self.tt("pool", hc[:, pr, :], atmp[:], sgT[:, pr, :], ALU.mult, [Ratmp, RsgT], [Rhc])

                pend = []
                for n, stp in enumerate(steps):
                    c0 = qk(stp, scnt + n)
                    pend.append((stp, scnt + n, c0))
                    if len(pend) > 1:
                        rest(*pend.pop(0))
                while pend:
                    rest(*pend.pop(0))
                scnt += len(steps)
                for i in range(4):
                    b = xcnt % 2
                    tix = I * 4 + i
                    self.dma("sp", xts[b][:], xin[t0 + i * 128:t0 + (i + 1) * 128, :], d_x[b], [Rxdram[tix]], [Rxt[b]])
                    xcnt += 1
                    for nh in range(2):
                        ob = ocnt % 2
                        pt, Rp = gp[gcnt % 2]; gcnt += 1
                        for c in range(8):
                            self.mm(pt[:, :], hc[:, c, i * 128:(i + 1) * 128], w_out[:, c, nh * 512:(nh + 1) * 512],
                                    c == 0, c == 7, [Rhc, Rw], [Rp])
                        self.tt("dve", ots[ob][:], pt[:, :], xts[b][:, nh * 512:(nh + 1) * 512], ALU.add, [Rp, Rxt[b]], [Rot[ob]])
                        self.dma("sp", xout[t0 + i * 128:t0 + (i + 1) * 128, nh * 512:(nh + 1) * 512], ots[ob][:], d_o[ob],
                                 [Rot[ob]], [Rxdram[tix]])
                        ocnt += 1
            P.barrier()


    def rw_phase(self, o, layer, xin, xout, prm):
        nc, P, S = self.nc, self.P, self.S
        NCH = S // 128
        Rc = self.Rconst
        first_layer = (o == 0)
        with ExitStack() as st:
            sb = lambda n, shp, dt: self.sb("r_" + n, shp, dt, st)
            Wr = sb("Wr", [128, 8, D], BF16); Wk = sb("Wk", [128, 8, D], BF16)
            Wv = sb("Wv", [128, 8, D], BF16); Wo = sb("Wo", [128, 8, D], BF16)
            l1 = sb("l1", [128, 8, 320], BF16)
            wa2 = sb("wa2", [128, D], BF16)
            g2a = sb("g2a", [128, D], BF16)
            gv2 = sb("gv2", [64, D], BF16)
            bc = sb("bc", [128, 8, D], BF16)
            gain_b = sb("gain", [128, D], F32)
            mu = sb("mu", [128, 6, 8], F32)
            IUf = sb("IUf", [128, 128], F32); SUf = sb("SUf", [128, 128], F32); SLf = sb("SLf", [128, 128], F32)
            onesf = sb("onesf", [128, 128], F32)
            tiny = sb("tiny", [128, 1], F32)
            W = [Rc]
            self.memset("pool", IUf[:], 1.0, W)
            self.P.emit("pool", lambda e: e.affine_select(out=IUf[:], in_=IUf[:], pattern=[[1, 128]], compare_op=ALU.is_ge,
                                                          fill=0.0, base=0, channel_multiplier=-1), (), W)
            self.memset("pool", SUf[:], 1.0, W)
            self.P.emit("pool", lambda e: e.affine_select(out=SUf[:], in_=SUf[:], pattern=[[1, 128]], compare_op=ALU.is_gt,
                                                          fill=0.0, base=0, channel_multiplier=-1), (), W)
            self.memset("pool", SLf[:], 1.0, W)
            self.P.emit("pool", lambda e: e.affine_select(out=SLf[:], in_=SLf[:], pattern=[[-1, 128]], compare_op=ALU.is_gt,
                                                          fill=0.0, base=0, channel_multiplier=1), (), W)
            self.memset("pool", onesf[:], -float(np.exp(-0.5)), W)
            cIU = sb("cIU", [128, 128], F32); cSU = sb("cSU", [128, 128], F32)
            self.ts("pool", cIU[:], IUf[:], -float(np.exp(-0.5)), None, ALU.mult, None, [Rc], W)
            self.ts("pool", cSU[:], SUf[:], -float(np.exp(-0.5)), None, ALU.mult, None, [Rc], W)
            self.memset("pool", tiny[:], 1e-24, W)
            xt = sb("xt", [128, D], F32); hb = sb("hb", [128, D], BF16)
            hTe = sb("hTe", [128, 8, 129], BF16)
            xm = [sb("xm%d" % i, [128, 8, 128], BF16) for i in range(2)]
            xx = sb("xx", [128, 8, 128], BF16)
            sc = [sb("sc%d" % i, [128, D], F32) for i in range(5)]
            r_bf = sb("r_bf", [128, D], BF16); kp_bf = sb("kp_bf", [128, D], BF16); kk_bf = sb("kk_bf", [128, D], BF16)
            b_bf = sb("b_bf", [128, D], BF16); v_bf = sb("v_bf", [128, D], BF16); g_bf = sb("g_bf", [128, D], BF16)
            prod = [sb("prod%d" % i, [128, D], BF16) for i in range(2)]
            Khat = sb("Khat", [128, D], BF16); Bhat = sb("Bhat", [128, D], BF16)
            RtT = sb("RtT", [128, 8, 128], BF16); KtT = sb("KtT", [128, 8, 128], BF16)
            BtT = sb("BtT", [128, 8, 128], BF16); AtT = sb("AtT", [128, 8, 128], BF16)
            lsb = sb("lsb", [128, 512], BF16)
            Mb = [sb("Mb%d" % i, [128, 8, 128], BF16) for i in range(2)]
            Nb = [sb("Nb%d" % i, [128, 8, 128], BF16) for i in range(2)]
            Xb = [sb("Xb%d" % i, [128, 8, 128], BF16) for i in range(2)]
            XT = sb("XT", [128, 16, 128], BF16)
            AakT = sb("AakT", [128, 16, 128], BF16); ArbT = sb("ArbT", [128, 16, 128], BF16); ArkT = sb("ArkT", [128, 16, 128], BF16)
            RHS = sb("RHS", [128, D], BF16); U = sb("U", [128, D], BF16)
            ST = sb("ST", [128, 8, 64], F32); STb = sb("STb", [128, 8, 64], BF16); STt = sb("STt", [128, 8, 64], F32)
            PCc = sb("PCc", [128, 8], F32)
            st16 = sb("st16", [128, 8, 16], F32)
            yfin = sb("yfin", [128, D], BF16); yT = sb("yT", [128, 8, 128], BF16)
            ot = sb("ot", [128, D], F32)
            ss = sb("ss", [128, 4], F32)
            tp = [(self.ps("r_tp%d" % i, BF16, st), Res()) for i in range(2)]
            gpool = [(self.ps("r_gp%d" % i, F32, st), Res()) for i in range(6)]
            self._gi = 0

            def bank():
                b_ = gpool[self._gi % 6]
                self._gi += 1
                return b_

            self._ti = 0

            def tbank():
                b_ = tp[self._ti % 2]
                self._ti += 1
                return b_

            R = {k: Res(k) for k in ("w", "small", "xt", "hb", "hTe", "xx", "ss", "r_bf", "kp_bf", "kk_bf", "b_bf", "v_bf", "g_bf",
                                     "Khat", "Bhat", "RtT", "KtT", "BtT", "AtT", "lsb", "XT", "AakT", "ArbT", "ArkT", "RHS", "U",
                                     "ST", "STb", "STt", "PCc", "st16", "yfin", "yT", "ot", "gain")}
            Rsc = [Res() for _ in range(5)]; Rxm = [Res(), Res()]; Rprod = [Res(), Res()]
            RMb = [Res(), Res()]; RNb = [Res(), Res()]; RXb = [Res(), Res()]
            d_w = P.dmasem("rw"); d_c = P.dmasem("rc"); d_x = P.dmasem("rx"); d_o = P.dmasem("ro"); d_v = P.dmasem("rv")
            Rxdram = self.Rxdram
            Rvf = self.Rvf

            def wview(name):
                return prm[name][o].rearrange("(c p) n -> p c n", p=128)
            for Wt, nm in ((Wr, "rw_w_r"), (Wk, "rw_w_k"), (Wv, "rw_w_v"), (Wo, "rw_w_o")):
                v_ = wview(nm)
                for c in range(8):
                    self.dma("pool", Wt[:, c, :], v_[:, c, :], d_w, [], [R["w"]])
            self.dma("pool", l1[:, :, 0:64], wview("rw_w1"), d_w, [], [R["w"]])
            self.dma("pool", l1[:, :, 64:128], wview("rw_a1"), d_w, [], [R["w"]])
            self.dma("pool", l1[:, :, 128:288], wview("rw_g1"), d_w, [], [R["w"]])
            self.dma("pool", wa2[0:64, :], prm["rw_w2"][o], d_w, [], [R["w"]])
            self.dma("pool", wa2[64:128, :], prm["rw_a2"][o], d_w, [], [R["w"]])
            self.dma("pool", g2a[:, :], prm["rw_g2"][o][0:128, :], d_w, [], [R["w"]])
            self.dma("pool", gv2[0:32, :], prm["rw_g2"][o][128:160, :], d_w, [], [R["w"]])
            if not first_layer:
                self.dma("pool", l1[:, :, 288:320], prm["rw_v1"][o - 1].rearrange("(c p) n -> p c n", p=128), d_w, [], [R["w"]])
                self.dma("pool", gv2[32:64, :], prm["rw_v2"][o - 1], d_w, [], [R["w"]])
            rows = [prm["rw_w0"][o:o + 1, :], prm["rw_a0"][o:o + 1, :],
                    (prm["rw_v0"][o - 1:o, :] if not first_layer else prm["rw_w0"][o:o + 1, :]),
                    prm["rw_k_k"][o:o + 1, :], prm["rw_k_a"][o:o + 1, :], prm["rw_ln_w"][o:o + 1, :], prm["rw_ln_b"][o:o + 1, :],
                    prm["rw_r_k"][o:o + 1].rearrange("o h d -> o (h d)")]
            for i, rv in enumerate(rows):
                self.dma("pool", bc[:, i, :], rv.partition_broadcast(128), d_w, [], [R["w"]])
            W0, A0, V0, KK_, KA_, LNW, LNB, RK_ = (bc[:, i, :] for i in range(8))
            self.dma("sp", gain_b[:], prm["mix_norm"][layer:layer + 1, :].partition_broadcast(128), d_c, [], [R["gain"]])
            self.dma("sp", mu[:], prm["rw_mu"][o].rearrange("i (c p) -> p i c", p=128), d_c, [], [R["small"]], slow=True)
            self.memset("pool", ST[:], 0.0, [R["ST"]])
            self.memset("pool", STb[:], 0.0, [R["STb"]])
            self.memset("pool", hTe[:, :, 128:129], 0.0, [R["hTe"]])
            bufs = {"junk": [(yfin, R["yfin"])] * 2, "ss": [(ss, R["ss"])] * 2, "hb": [(hb, R["hb"])] * 2, "tp": tp}
            IUb = lambda n_: IUf[:, :].unsqueeze(1).broadcast_to([128, n_, 128])
            SUb = lambda n_: SUf[:, :].unsqueeze(1).broadcast_to([128, n_, 128])
            SLb = lambda n_: SLf[:, :].unsqueeze(1).broadcast_to([128, n_, 128])
            IDb = lambda n_: self.ident[:, :].unsqueeze(1).broadcast_to([128, n_, 128])
            h3 = lambda ap: ap.rearrange("p (h d) -> p h d", d=64)

            def proj_tok(xT, Rx, Wt, lo=0):
                bks = []
                for nh in range(2):
                    pt, Rp = bank()
                    for c in range(8):
                        self.mm(pt[:, :], xT[:, c, :], Wt[:, c, nh * 512:(nh + 1) * 512], c == 0, c == 7, [Rx, R["w"]], [Rp])
                    bks.append((pt, Rp))
                return bks

            def mix(i, j):
                mub = mu[:, i, :].unsqueeze(2).broadcast_to([128, 8, 128])
                self.tt("dve", xm[j][:], xx[:], mub, ALU.mult, [R["xx"], R["small"]], [Rxm[j]])
                self.tt("dve", xm[j][:], xm[j][:], hTe[:, :, 1:129], ALU.add, [Rxm[j], R["hTe"]], [Rxm[j]])
                return xm[j], Rxm[j]

            def evac2(bks, fn):
                for nh, (pt, Rp) in enumerate(bks):
                    fn(nh, pt, Rp, slice(nh * 512, (nh + 1) * 512))

            import os as _os
            stop = int(_os.environ.get("RW_STOP", "99"))

            def early(n_, t0_):
                self.copy("dve", ot[:], xt[:], [R["xt"]], [R["ot"]])
                self.dma("sp", xout[t0_:t0_ + 128, :], ot[:], d_o, [R["ot"]], [Rxdram[n_]])

            for n in range(NCH):
                t0 = n * 128
                self.copy("pool", hTe[:, :, 0:1], hTe[:, :, 128:129], [R["hTe"]], [R["hTe"]])
                self.dma("sp", xt[:], xin[t0:t0 + 128, :], d_x, [Rxdram[n]], [R["xt"]])
                self.norm_tile(xt, R["xt"], gain_b, R["gain"], hTe[:, :, 1:129], R["hTe"], 0, bufs, n)
                self.tt("pool", xx[:], hTe[:, :, 0:128], hTe[:, :, 1:129], ALU.subtract, [R["hTe"]], [R["xx"]])
                if stop <= 1:
                    early(n, t0)
                    continue
                xr, Rxr = mix(0, 0)
                evac2(proj_tok(xr, Rxr, Wr), lambda nh, pt, Rp, sl: self.copy("act", r_bf[:, sl], pt[:, :], [Rp], [R["r_bf"]]))
                xk, Rxk = mix(2, 1)
                evac2(proj_tok(xk, Rxk, Wk), lambda nh, pt, Rp, sl: self.copy("act", sc[0][:, sl], pt[:, :], [Rp], [Rsc[0]]))
                xv, Rxv = mix(3, 0)
                evac2(proj_tok(xv, Rxv, Wv), lambda nh, pt, Rp, sl: self.copy("act", sc[1][:, sl], pt[:, :], [Rp], [Rsc[1]]))
                lp, Rlp = bank()
                if not first_layer:
                    for c in range(8):
                        self.mm(lp[32:64, 256:384], l1[:, c, 288:320], xv[:, c, :], c == 0, c == 7, [Rxv, R["w"]], [Rlp],
                                tile_position=(0, 32))
                xw, Rxw = mix(1, 1)
                for c in range(8):
                    self.mm(lp[0:64, 0:128], l1[:, c, 0:64], xw[:, c, :], c == 0, c == 7, [Rxw, R["w"]], [Rlp])
                xa, Rxa = mix(4, 0)
                for c in range(8):
                    self.mm(lp[64:128, 0:128], l1[:, c, 64:128], xa[:, c, :], c == 0, c == 7, [Rxa, R["w"]], [Rlp],
                            tile_position=(0, 64))
                xg, Rxg = mix(5, 1)
                for c in range(8):
                    self.mm(lp[:, 128:256], l1[:, c, 128:256], xg[:, c, :], c == 0, c == 7, [Rxg, R["w"]], [Rlp])
                for c in range(8):
                    self.mm(lp[0:32, 256:384], l1[:, c, 256:288], xg[:, c, :], c == 0, c == 7, [Rxg, R["w"]], [Rlp])
                self.act(lsb[0:64, 0:128], lp[0:64, 0:128], AF.Tanh, [Rlp], [R["lsb"]])
                self.copy("act", lsb[64:128, 0:128], lp[64:128, 0:128], [Rlp], [R["lsb"]])
                self.act(lsb[:, 128:256], lp[:, 128:256], AF.Sigmoid, [Rlp], [R["lsb"]])
                self.act(lsb[0:32, 256:384], lp[0:32, 256:384], AF.Sigmoid, [Rlp], [R["lsb"]])
                if not first_layer:
                    self.copy("act", lsb[32:64, 256:384], lp[32:64, 256:384], [Rlp], [R["lsb"]])
                for nh in range(2):
                    sl = slice(nh * 512, (nh + 1) * 512)
                    pt, Rp = bank()
                    self.mm(pt[:, :], lsb[0:64, 0:128], wa2[0:64, sl], True, True, [R["lsb"], R["w"]], [Rp])
                    self.tt("dve", sc[2][:, sl], pt[:, :], W0[:, sl], ALU.add, [Rp, R["w"]], [Rsc[2]])
                self.act(sc[2][:], sc[2][:], AF.Sigmoid, [Rsc[2]], [Rsc[2]])
                for nh in range(2):
                    sl = slice(nh * 512, (nh + 1) * 512)
                    pt, Rp = bank()
                    self.mm(pt[:, :], lsb[64:128, 0:128], wa2[64:128, sl], True, True, [R["lsb"], R["w"]], [Rp])
                    self.tt("dve", sc[3][:, sl], pt[:, :], A0[:, sl], ALU.add, [Rp, R["w"]], [Rsc[3]])
                self.act(sc[3][:], sc[3][:], AF.Sigmoid, [Rsc[3]], [Rsc[3]])
                for nh in range(2):
                    sl = slice(nh * 512, (nh + 1) * 512)
                    pt, Rp = bank()
                    self.mm(pt[:, :], lsb[:, 128:256], g2a[:, sl], True, False, [R["lsb"], R["w"]], [Rp])
                    self.mm(pt[:, :], lsb[0:32, 256:384], gv2[0:32, sl], False, True, [R["lsb"], R["w"]], [Rp])
                    self.copy("act", g_bf[:, sl], pt[:, :], [Rp], [R["g_bf"]])
                if first_layer:
                    self.dma("sp", self.vfirst[t0:t0 + 128, :], sc[1][:], d_v, [Rsc[1]], [Rvf[n]])
                else:
                    for nh in range(2):
                        sl = slice(nh * 512, (nh + 1) * 512)
                        pt, Rp = bank()
                        self.mm(pt[:, :], lsb[32:64, 256:384], gv2[32:64, sl], True, True, [R["lsb"], R["w"]], [Rp])
                        self.tt("dve", sc[4][:, sl], pt[:, :], V0[:, sl], ALU.add, [Rp, R["w"]], [Rsc[4]])
                    self.act(sc[4][:], sc[4][:], AF.Sigmoid, [Rsc[4]], [Rsc[4]])
                    self.dma("sp", ot[:], self.vfirst[t0:t0 + 128, :], d_v, [Rvf[n]], [R["ot"]])
                    self.tt("pool", ot[:], ot[:], sc[1][:], ALU.subtract, [R["ot"], Rsc[1]], [R["ot"]])
                    self.tt("pool", ot[:], ot[:], sc[4][:], ALU.mult, [R["ot"], Rsc[4]], [R["ot"]])
                    self.tt("pool", sc[1][:], sc[1][:], ot[:], ALU.add, [R["ot"], Rsc[1]], [Rsc[1]])
                self.copy("act", v_bf[:], sc[1][:], [Rsc[1]], [R["v_bf"]])
                k32, a32 = sc[0], sc[3]
                self.tt("dve", sc[4][:], k32[:], KK_, ALU.mult, [Rsc[0], R["w"]], [Rsc[4]])
                self.act(sc[1][:], sc[4][:], AF.Square, [Rsc[4], R["v_bf"]], [Rsc[1]])
                self.P.emit("dve", lambda e_: e_.tensor_reduce(out=st16[:, 0, :], in_=h3(sc[1][:, :]), axis=AX.X, op=ALU.add),
                            [Rsc[1]], [R["st16"]])
                self.act(st16[:, 1, :], st16[:, 0, :], AF.Ln, [R["st16"], Rc], [R["st16"]], bias=tiny[:, 0:1])
                self.act(st16[:, 1, :], st16[:, 1, :], AF.Exp, [R["st16"]], [R["st16"]], scale=-0.5)
                rnb = st16[:, 1, :].unsqueeze(2).broadcast_to([128, 16, 64])
                self.tt("dve", h3(kk_bf[:, :]), h3(sc[4][:, :]), rnb, ALU.mult, [Rsc[4], R["st16"]], [R["kk_bf"]])
                self.stt("dve", sc[1][:], a32[:], -1.0, KA_, ALU.add, ALU.mult, [Rsc[3], R["w"]], [Rsc[1]])
                self.stt("dve", kp_bf[:], sc[1][:], 1.0, k32[:], ALU.add, ALU.mult, [Rsc[1], Rsc[0]], [R["kp_bf"]])
                self.tt("pool", b_bf[:], kk_bf[:], a32[:], ALU.mult, [R["kk_bf"], Rsc[3]], [R["b_bf"]])
                self.tt("pool", sc[4][:], r_bf[:], kp_bf[:], ALU.mult, [R["r_bf"], R["kp_bf"]], [Rsc[4]])
                self.tt("pool", sc[4][:], sc[4][:], RK_, ALU.mult, [Rsc[4], R["w"]], [Rsc[4]])
                self.P.emit("dve", lambda e_: e_.tensor_reduce(out=st16[:, 2, :], in_=h3(sc[4][:, :]), axis=AX.X, op=ALU.add),
                            [Rsc[4]], [R["st16"]])
                if stop <= 2:
                    early(n, t0)
                    continue
                ld = sc[2]
                for nh in range(2):
                    sl = slice(nh * 512, (nh + 1) * 512)
                    pt, Rp = bank()
                    self.mm(pt[:, :], cIU[:], ld[:, sl], True, True, [Rsc[2], Rc], [Rp])
                    self.act(sc[0][:, sl], pt[:, :], AF.Exp, [Rp], [Rsc[0]])
                    self.act(sc[1][:, sl], pt[:, :], AF.Exp, [Rp], [Rsc[1]], scale=-1.0)
                for nh in range(2):
                    sl = slice(nh * 512, (nh + 1) * 512)
                    pt, Rp = bank()
                    self.mm(pt[:, :], cSU[:], ld[:, sl], True, True, [Rsc[2], Rc], [Rp])
                    self.act(sc[3][:, sl], pt[:, :], AF.Exp, [Rp], [Rsc[3]])
                for nh in range(2):
                    sl = slice(nh * 512, (nh + 1) * 512)
                    pt, Rp = bank()
                    self.mm(pt[:, :], onesf[:], ld[:, sl], True, True, [Rsc[2], Rc], [Rp])
                    self.act(sc[4][:, sl], pt[:, :], AF.Exp, [Rp], [Rsc[4]])
                self.tt("pool", sc[4][:], sc[4][:], sc[1][:], ALU.mult, [Rsc[4], Rsc[1]], [Rsc[4]])
                pt, Rp = bank()
                for c in range(8):
                    self.mm(pt[:, c:c + 1], ld[:, c * 128:(c + 1) * 128], onesf[:, 0:1], True, True, [Rsc[2], Rc], [Rp])
                self.act(PCc[:], pt[:, 0:8], AF.Exp, [Rp], [R["PCc"]])
                if stop <= 3:
                    early(n, t0)
                    continue
                def prod_T(j, eng, in0, Rin0, in1, Rin1, dstT, RdstT, neg=False):
                    if neg:
                        self.stt("dve", prod[j][:], in0, -1.0, in1, ALU.mult, ALU.mult, [Rin0, Rin1], [Rprod[j]])
                    else:
                        self.tt(eng, prod[j][:], in0, in1, ALU.mult, [Rin0, Rin1], [Rprod[j]])
                    tpt, Rtp = tbank()
                    for c in range(8):
                        self.tr(tpt[:, c * 128:(c + 1) * 128], prod[j][:, c * 128:(c + 1) * 128], self.ident[:], [Rprod[j], Rc], [Rtp])
                    self.copy("act", dstT[:, :, :], tpt[:, :].rearrange("p (c t) -> p c t", c=8), [Rtp], [RdstT])
                prod_T(0, "dve", r_bf[:], R["r_bf"], sc[0][:], Rsc[0], RtT, R["RtT"])
                prod_T(1, "pool", kp_bf[:], R["kp_bf"], sc[1][:], Rsc[1], KtT, R["KtT"])
                prod_T(0, "dve", b_bf[:], R["b_bf"], sc[1][:], Rsc[1], BtT, R["BtT"])
                prod_T(1, "dve", kk_bf[:], R["kk_bf"], sc[3][:], Rsc[3], AtT, R["AtT"], neg=True)
                self.tt("pool", Khat[:], kp_bf[:], sc[4][:], ALU.mult, [R["kp_bf"], Rsc[4]], [R["Khat"]])
                self.tt("dve", Bhat[:], b_bf[:], sc[4][:], ALU.mult, [R["b_bf"], Rsc[4]], [R["Bhat"]])
                if stop <= 4:
                    early(n, t0)
                    continue
                def hv(T, h):
                    return T[(h % 2) * 64:(h % 2) * 64 + 64, h // 2, :]
                for half in range(2):
                    hs = range(half * 8, half * 8 + 8)
                    hb0 = half * 8
                    v4 = lambda pt_: pt_[:, :].rearrange("p (h t) -> p h t", h=4)

                    def amat(lhs_T, Rl, rhs_T, Rr, dst, dbase, maskb, Rdst):
                        (pe_, Rpe), (po_, Rpo) = bank(), bank()
                        for i in range(4):
                            he, ho = hb0 + 2 * i, hb0 + 2 * i + 1
                            self.mm(pe_[:, i * 128:(i + 1) * 128], hv(lhs_T, he), hv(rhs_T, he), True, True, [Rl, Rr], [Rpe])
                            self.mm(po_[:, i * 128:(i + 1) * 128], hv(lhs_T, ho), hv(rhs_T, ho), True, True, [Rl, Rr], [Rpo])
                        self.tt("dve", dst[:, dbase + 0:dbase + 8:2, :], v4(pe_), maskb, ALU.mult, [Rpe, Rc], [Rdst])
                        self.tt("dve", dst[:, dbase + 1:dbase + 8:2, :], v4(po_), maskb, ALU.mult, [Rpo, Rc], [Rdst])

                    amat(BtT, R["BtT"], AtT, R["AtT"], Mb[0], 0, SUb(4), RMb[0])
                    amat(AtT, R["AtT"], BtT, R["BtT"], Nb[0], 0, SLb(4), RNb[0])
                    amat(BtT, R["BtT"], RtT, R["RtT"], ArbT, hb0, IUb(4), R["ArbT"])
                    amat(KtT, R["KtT"], AtT, R["AtT"], AakT, hb0, SUb(4), R["AakT"])
                    amat(KtT, R["KtT"], RtT, R["RtT"], ArkT, hb0, IUb(4), R["ArkT"])
                    if stop <= 5:
                        continue
                    self.tt("pool", Xb[0][:], Mb[0][:], IDb(8), ALU.add, [RMb[0], Rc], [RXb[0]])
                    cm, cn, cx = 0, 0, 0
                    for k in range(1, 7):
                        for q4 in range(2):
                            pt, Rp = bank()
                            for i in range(4):
                                hh = q4 * 4 + i
                                self.mm(pt[:, i * 128:(i + 1) * 128], Mb[cm][:, hh, :], Nb[cn][:, hh, :], True, True, [RMb[cm], RNb[cn]], [Rp])
                            self.copy("act", Nb[1 - cn][:, q4 * 4:q4 * 4 + 4, :], pt[:, :].rearrange("p (h t) -> p h t", h=4), [Rp], [RNb[1 - cn]])
                        if k < 6:
                            for q4 in range(2):
                                pt, Rp = bank()
                                for i in range(4):
                                    hh = q4 * 4 + i
                                    self.mm(pt[:, i * 128:(i + 1) * 128], Nb[cn][:, hh, :], Mb[cm][:, hh, :], True, True, [RMb[cm], RNb[cn]], [Rp])
                                self.copy("act", Mb[1 - cm][:, q4 * 4:q4 * 4 + 4, :], pt[:, :].rearrange("p (h t) -> p h t", h=4), [Rp], [RMb[1 - cm]])
                        cn = 1 - cn
                        if k < 6:
                            cm = 1 - cm
                        for q4 in range(2):
                            pt, Rp = bank()
                            for i in range(4):
                                hh = q4 * 4 + i
                                self.mm(pt[:, i * 128:(i + 1) * 128], Nb[cn][:, hh, :], Xb[cx][:, hh, :], True, True, [RNb[cn], RXb[cx]], [Rp])
                            if k < 6:
                                dst, Rdst = Xb[1 - cx][:, q4 * 4:q4 * 4 + 4, :], RXb[1 - cx]
                            else:
                                dst, Rdst = XT[:, half * 8 + q4 * 4:half * 8 + q4 * 4 + 4, :], R["XT"]
                            self.tt("dve", dst, pt[:, :].rearrange("p (h t) -> p h t", h=4), Xb[cx][:, q4 * 4:q4 * 4 + 4, :], ALU.add,
                                    [Rp, RXb[cx]], [Rdst])
                        cx = 1 - cx
                if stop <= 6:
                    early(n, t0)
                    continue
                sthv = lambda h: STb[(h % 2) * 64:(h % 2) * 64 + 64, h // 2, :]
                hc_ = lambda T, h: T[:, h * 64:(h + 1) * 64]
                bks = [bank(), bank()]
                for h in range(16):
                    pt, Rp = bks[h // 8]
                    o_ = pt[:, (h % 8) * 64:(h % 8) * 64 + 64]
                    self.mm(o_, hv(AtT, h), sthv(h), True, False, [R["AtT"], R["STb"]], [Rp])
                    self.mm(o_, AakT[:, h, :], hc_(v_bf, h), False, True, [R["AakT"], R["v_bf"]], [Rp])
                for nh, (pt, Rp) in enumerate(bks):
                    self.copy("act", RHS[:, nh * 512:(nh + 1) * 512], pt[:, :], [Rp], [R["RHS"]])
                bks = [bank(), bank()]
                for h in range(16):
                    pt, Rp = bks[h // 8]
                    self.mm(pt[:, (h % 8) * 64:(h % 8) * 64 + 64], XT[:, h, :], hc_(RHS, h), True, True, [R["XT"], R["RHS"]], [Rp])
                for nh, (pt, Rp) in enumerate(bks):
                    self.copy("act", U[:, nh * 512:(nh + 1) * 512], pt[:, :], [Rp], [R["U"]])
                bks = [bank(), bank()]
                for h in range(16):
                    pt, Rp = bks[h // 8]
                    o_ = pt[:, (h % 8) * 64:(h % 8) * 64 + 64]
                    self.mm(o_, hv(RtT, h), sthv(h), True, False, [R["RtT"], R["STb"]], [Rp])
                    self.mm(o_, ArbT[:, h, :], hc_(U, h), False, False, [R["ArbT"], R["U"]], [Rp])
                    self.mm(o_, ArkT[:, h, :], hc_(v_bf, h), False, True, [R["ArkT"], R["v_bf"]], [Rp])
                for nh, (pt, Rp) in enumerate(bks):
                    self.copy("act", sc[0][:, nh * 512:(nh + 1) * 512], pt[:, :], [Rp], [Rsc[0]])
                pt, Rp = bank()
                for h in range(16):
                    o_ = pt[(h % 2) * 64:(h % 2) * 64 + 64, (h // 2) * 64:(h // 2) * 64 + 64]
                    self.mm(o_, hc_(Bhat, h), hc_(U, h), True, False, [R["Bhat"], R["U"]], [Rp], tile_position=(0, (h % 2) * 64))
                    self.mm(o_, hc_(Khat, h), hc_(v_bf, h), False, True, [R["Khat"], R["v_bf"]], [Rp], tile_position=(0, (h % 2) * 64))
                pcb = PCc[:, :].unsqueeze(2).broadcast_to([128, 8, 64])
                self.tt("pool", STt[:], ST[:], pcb, ALU.mult, [R["ST"], R["PCc"]], [R["STt"]])
                self.tt("dve", ST[:], STt[:], pt[:, :].rearrange("p (c v) -> p c v", c=8), ALU.add, [R["STt"], Rp], [R["ST"]])
                self.copy("pool", STb[:], ST[:], [R["ST"]], [R["STb"]])
                if stop <= 7:
                    early(n, t0)
                    continue
                y = sc[0]
                self.P.emit("dve", lambda e_: e_.tensor_reduce(out=st16[:, 3, :], in_=h3(y[:, :]), axis=AX.X, op=ALU.add),
                            [Rsc[0]], [R["st16"]])
                self.act(sc[1][:], y[:], AF.Square, [Rsc[0]], [Rsc[1]])
                self.P.emit("dve", lambda e_: e_.tensor_reduce(out=st16[:, 4, :], in_=h3(sc[1][:, :]), axis=AX.X, op=ALU.add),
                            [Rsc[1]], [R["st16"]])
                self.ts("dve", st16[:, 3, :], st16[:, 3, :], 1.0 / 64, None, ALU.mult, None, [R["st16"]], [R["st16"]])
                self.tt("dve", st16[:, 5, :], st16[:, 3, :], st16[:, 3, :], ALU.mult, [R["st16"]], [R["st16"]])
                self.stt("dve", st16[:, 4, :], st16[:, 4, :], 1.0 / 64, st16[:, 5, :], ALU.mult, ALU.subtract, [R["st16"]], [R["st16"]])
                self.act(st16[:, 4, :], st16[:, 4, :], AF.Ln, [R["st16"], Rc], [R["st16"]], bias=self.eps_rms[:, 1:2])
                self.act(st16[:, 4, :], st16[:, 4, :], AF.Exp, [R["st16"]], [R["st16"]], scale=-0.5)
                mb_ = st16[:, 3, :].unsqueeze(2).broadcast_to([128, 16, 64])
                rb_ = st16[:, 4, :].unsqueeze(2).broadcast_to([128, 16, 64])
                bb_ = st16[:, 2, :].unsqueeze(2).broadcast_to([128, 16, 64])
                self.tt("dve", h3(sc[1][:, :]), h3(y[:, :]), mb_, ALU.subtract, [Rsc[0], R["st16"]], [Rsc[1]])
                self.tt("dve", h3(sc[1][:, :]), h3(sc[1][:, :]), rb_, ALU.mult, [Rsc[1], R["st16"]], [Rsc[1]])
                self.tt("pool", sc[1][:], sc[1][:], LNW, ALU.mult, [Rsc[1], R["w"]], [Rsc[1]])
                self.tt("pool", sc[1][:], sc[1][:], LNB, ALU.add, [Rsc[1], R["w"]], [Rsc[1]])
                self.tt("dve", h3(sc[3][:, :]), h3(v_bf[:, :]), bb_, ALU.mult, [R["v_bf"], R["st16"]], [Rsc[3]])
                self.tt("pool", sc[1][:], sc[1][:], sc[3][:], ALU.add, [Rsc[1], Rsc[3]], [Rsc[1]])
                self.tt("dve", yfin[:], sc[1][:], g_bf[:], ALU.mult, [Rsc[1], R["g_bf"]], [R["yfin"]])
                tpt, Rtp = tbank()
                for c in range(8):
                    self.tr(tpt[:, c * 128:(c + 1) * 128], yfin[:, c * 128:(c + 1) * 128], self.ident[:], [R["yfin"], Rc], [Rtp])
                self.copy("act", yT[:, :, :], tpt[:, :].rearrange("p (c t) -> p c t", c=8), [Rtp], [R["yT"]])
                for nh, (pt, Rp) in enumerate(proj_tok(yT, R["yT"], Wo)):
                    sl = slice(nh * 512, (nh + 1) * 512)
                    self.tt("dve", ot[:, sl], pt[:, :], xt[:, sl], ALU.add, [Rp, R["xt"]], [R["ot"]])
                self.dma("sp", xout[t0:t0 + 128, :], ot[:], d_o, [R["ot"]], [Rxdram[n]])
            P.barrier()


def build_program(S, sublayers, n_cores=8):
    nc = bass.Bass("TRN2", target_bir_lowering=False)
    specs = param_specs()
    prm = {}
    x = nc.dram_tensor("x", [S, D], F32, kind="ExternalInput").ap()
    for name, shp in specs.items():
        prm[name] = nc.dram_tensor(name, list(shp), F32, kind="ExternalInput").ap()
    out = nc.dram_tensor("out", [S, D], F32, kind="ExternalOutput").ap()
    with ExitStack() as st:
        kb = KB(nc, S, st)
        kb.Rxdram = [Res() for _ in range(S // 128)]
        kb.Rvf = [Res() for _ in range(S // 128)]
        kb.vfirst = nc.dram_tensor("vfirst_scratch", [S, D], F32).ap()
        kb.setup_consts()
        cur = x
        for sl in sublayers:
            if sl[0] == "ffn":
                kb.ffn_phase(sl[1], cur, out, prm)
            elif sl[0] == "hy":
                kb.hy_phase(sl[1], sl[2], cur, out, prm)
            elif sl[0] == "rw":
                kb.rw_phase(sl[1], sl[2], cur, out, prm)
            cur = out
        kb.P.barrier()
        kb.P.finalize()
        kb.stats = (dict(kb.P.n), kb.P.nwaits)
        print("instr counts", kb.P.n, "waits", kb.P.nwaits)
    return nc


def param_specs():
    return {
        "mix_norm": (4, D), "ffn_norm": (4, D),
        "ffn_w_gate": (4, D, DFF), "ffn_w_up": (4, D, DFF), "ffn_w_down": (4, DFF, D),
        "hy_w_in": (2, D, IN_COLS), "hy_f_bias": (2, 8), "hy_q_gain": (2, 64), "hy_k_gain": (2, 64),
        "hy_pool_w": (2, 4, 128, 128), "hy_pool_scale": (2, 512), "hy_w_out": (2, D, D),
        "rw_mu": (2, 6, D), "rw_w_r": (2, D, D), "rw_w_k": (2, D, D), "rw_w_v": (2, D, D),
        "rw_w0": (2, D), "rw_w1": (2, D, 64), "rw_w2": (2, 64, D), "rw_a0": (2, D), "rw_a1": (2, D, 64),
        "rw_a2": (2, 64, D), "rw_g1": (2, D, 160), "rw_g2": (2, 160, D), "rw_k_k": (2, D), "rw_k_a": (2, D),
        "rw_r_k": (2, 16, 64), "rw_ln_w": (2, D), "rw_ln_b": (2, D), "rw_w_o": (2, D, D),
        "rw_v0": (1, D), "rw_v1": (1, D, 32), "rw_v2": (1, 32, D),
    }


FULL = [("hy", 0, 0), ("ffn", 0), ("rw", 0, 1), ("ffn", 1), ("hy", 1, 2), ("ffn", 2), ("rw", 1, 3), ("ffn", 3)]


def run(inputs, S, sublayers, n_cores=8, trace=False):
    nc = build_program(S, sublayers)
    specs = param_specs()
    x = np.ascontiguousarray(np.asarray(inputs["x"], dtype=np.float32))
    shared = {k: np.ascontiguousarray(np.asarray(inputs[k], dtype=np.float32)) for k in specs}
    in_maps = []
    for c in range(n_cores):
        m = dict(shared)
        m["x"] = x[c]
        in_maps.append(m)
    res = run_bass_kernel_spmd(nc, in_maps, core_ids=list(range(n_cores)), trace=trace)
    outs = np.stack([np.asarray(r["out"]) for r in res.results], axis=0)
    return outs, res


def kernel(**inputs):
    outs, _ = run(inputs, 4096, FULL, n_cores=8)
    return outs.astype(np.float32)
```

```python
import numpy as np
from contextlib import ExitStack
import concourse.bass as bass
import concourse.mybir as mybir
from concourse.bass_utils import run_bass_kernel_spmd

F32 = mybir.dt.float32
BF16 = mybir.dt.bfloat16
AF = mybir.ActivationFunctionType
ALU = mybir.AluOpType
AX = mybir.AxisListType

D = 1024
DFF = 2816
NFC = DFF // 128
IN_COLS = 2568
RMS_EPS = 1e-6
GN_EPS = 64e-5
ENGS = ("pe", "act", "dve", "pool", "sp")
CH = 16000


class Res:
    __slots__ = ("name", "w", "r")

    def __init__(self, name=""):
        self.name = name
        self.w = None
        self.r = {}


class DmaSem:
    __slots__ = ("key", "sem", "count")

    def __init__(self, key, sem):
        self.key = key
        self.sem = sem
        self.count = 0


class Prog:
    def __init__(self, nc, stack, same_engine_sync=True):
        self.nc = nc
        self.stack = stack
        self.q = {e: [] for e in ENGS}
        self.n = {e: 0 for e in ENGS}
        self.esem = {}
        self.seen = {e: {} for e in ENGS}
        self.same = same_engine_sync
        self.dsems = []
        self.nwaits = 0
        self.tag = None

    def dmasem(self, name):
        s = self.stack.enter_context(self.nc.semaphore("d%d_%s" % (len(self.dsems), name)))
        d = DmaSem("d%d_%s" % (len(self.dsems), name), s)
        self.dsems.append(d)
        return d

    def _esem(self, e, k):
        if (e, k) not in self.esem:
            self.esem[(e, k)] = self.stack.enter_context(self.nc.semaphore("e_%s_%d" % (e, k)))
        return self.esem[(e, k)]

    def emit(self, eng, fn, reads=(), writes=(), dma=None):
        need = {}

        def want(ev):
            if ev is None:
                return
            key, val = ev[0], ev[1]
            if key == eng and (eng == "pe" or not self.same):
                return
            if ev[2] is not None:
                val = ev[2].count
            if need.get(key, (0,))[0] < val:
                need[key] = (val, ev[2])

        for r in reads:
            want(r.w)
        for w in writes:
            want(w.w)
            for ev in w.r.values():
                want(ev)
        waits = []
        seen = self.seen[eng]
        for key, (val, hinfo) in need.items():
            if seen.get(key, 0) >= val:
                continue
            seen[key] = val
            if key in ENGS:
                k = (val - 1) // CH
                waits.append((self._esem(key, k), val - k * CH))
            else:
                waits.append((hinfo.sem, val))
        self.nwaits += len(waits)
        if fn is None:
            if waits:
                self.q[eng].append((waits, None, None, None))
            return None
        if dma is None:
            self.n[eng] += 1
            idx = self.n[eng]
            k = (idx - 1) // CH
            inc = (self._esem(eng, k), 1)
            ev = (eng, idx, None)
        else:
            dma.count += 16
            inc = (dma.sem, 16)
            ev = (dma.key, dma.count, dma)
        self.q[eng].append((waits, fn, inc, self.tag))
        for r in reads:
            r.r[ev[0]] = ev
        for w in writes:
            w.w = ev
            w.r = {}
        return ev

    def barrier(self):
        evs = [(e, self.n[e], None) for e in ENGS if self.n[e] > 0]
        evs += [(d.key, d.count, d) for d in self.dsems if d.count > 0]
        for eng in ENGS:
            tmp = Res()
            tmp.r = {ev[0]: ev for ev in evs if ev[0] != eng}
            self.emit(eng, None, writes=[tmp])

    def finalize(self):
        nc = self.nc
        with nc.Block() as block:
            def mk(ename):
                def body(e):
                    for waits, fn, inc, tag in self.q[ename]:
                        for sem, val in waits:
                            e.wait_ge(sem, val)
                        if fn is not None:
                            ins = fn(e).then_inc(inc[0], inc[1])
                            if tag is not None:
                                ins.annotate(tag)
                return body
            block.tensor(mk("pe"))
            block.scalar(mk("act"))
            block.vector(mk("dve"))
            block.gpsimd(mk("pool"))
            block.sync(mk("sp"))


class KB:
    def __init__(self, nc, S, stack):
        self.nc = nc
        self.S = S
        self.st = stack
        self.P = Prog(nc, stack)
        import os as _os
        self.dbg_tags = bool(_os.environ.get("DBG_TAGS"))

    def sb(self, name, shape, dtype, stack=None):
        self.uid = getattr(self, "uid", 0) + 1
        return (stack or self.st).enter_context(self.nc.sbuf_tensor("%s_u%d" % (name, self.uid), list(shape), dtype))

    def ps(self, name, dtype=F32, stack=None):
        n = 512 if dtype == F32 else 1024
        self.uid = getattr(self, "uid", 0) + 1
        return (stack or self.st).enter_context(self.nc.psum_tensor("%s_u%d" % (name, self.uid), [128, n], dtype))

    def mm(self, out, lhsT, rhs, start, stop, R, W, **kw):
        return self.P.emit("pe", lambda e: e.matmul(out, lhsT=lhsT, rhs=rhs, start=start, stop=stop, **kw), R, W)

    def tr(self, out, in_, ident, R, W):
        return self.P.emit("pe", lambda e: e.transpose(out, in_, ident), R, W)

    def act(self, out, in_, func, R, W, eng="act", **kw):
        return self.P.emit(eng, lambda e: e.activation(out=out, in_=in_, func=func, **kw), R, W)

    def copy(self, eng, out, in_, R, W):
        if eng == "act":
            return self.P.emit("act", lambda e: e.copy(out=out, in_=in_), R, W)
        return self.P.emit(eng, lambda e: e.tensor_copy(out=out, in_=in_), R, W)

    def tt(self, eng, out, in0, in1, op, R, W):
        return self.P.emit(eng, lambda e: e.tensor_tensor(out=out, in0=in0, in1=in1, op=op), R, W)

    def ts(self, eng, out, in0, s1, s2, op0, op1, R, W, **kw):
        if s2 is None:
            return self.P.emit(eng, lambda e: e.tensor_scalar(out=out, in0=in0, scalar1=s1, scalar2=None, op0=op0, **kw), R, W)
        return self.P.emit(eng, lambda e: e.tensor_scalar(out=out, in0=in0, scalar1=s1, scalar2=s2, op0=op0, op1=op1, **kw), R, W)

    def stt(self, eng, out, in0, scalar, in1, op0, op1, R, W):
        return self.P.emit(eng, lambda e: e.scalar_tensor_tensor(out=out, in0=in0, scalar=scalar, in1=in1, op0=op0, op1=op1), R, W)

    def memset(self, eng, ap, val, W):
        return self.P.emit(eng, lambda e: e.memset(ap, val), (), W)

    def dma(self, eng, out, in_, sem, R, W, slow=False):
        if slow:
            return self.P.emit(eng, lambda e: e.dma_start(out=out, in_=in_, allow_slow_non_contiguous=True), R, W, dma=sem)
        return self.P.emit(eng, lambda e: e.dma_start(out=out, in_=in_), R, W, dma=sem)

    def setup_consts(self):
        nc = self.nc
        self.ident = self.sb("ident", [128, 128], BF16)
        self.Rconst = Res("const")
        W = [self.Rconst]
        self.eps_rms = self.sb("eps_rms", [128, 4], F32)
        self.memset("pool", self.eps_rms[:, 0:1], RMS_EPS, W)
        self.memset("pool", self.eps_rms[:, 1:2], GN_EPS, W)
        self.memset("pool", self.eps_rms[:, 2:3], 1.0, W)
        self.memset("pool", self.eps_rms[:, 3:4], 0.0, W)
        self.memset("pool", self.ident[:], 1.0, W)
        idt = self.ident
        self.P.emit("pool", lambda e: e.affine_select(out=idt[:], in_=idt[:], pattern=[[-1, 128]],
                                                      compare_op=ALU.is_equal, fill=0.0, base=0,
                                                      channel_multiplier=1), (), W)

    def norm_tile(self, xt, Rxt, gain_b, Rgain, hT, RhT, col0, bufs, i):
        junk, Rjunk = bufs["junk"][i % 2]
        ss, Rss = bufs["ss"][i % 2]
        hb, Rhb = bufs["hb"][i % 2]
        tp, Rtp = bufs["tp"][i % len(bufs["tp"])]
        self.act(junk[:], xt[:], AF.Square, [Rxt], [Rjunk, Rss], accum_out=ss[:, 0:1])
        self.act(ss[:, 1:2], ss[:, 0:1], AF.Ln, [Rss, self.Rconst], [Rss], scale=1.0 / D, bias=self.eps_rms[:, 0:1])
        self.act(ss[:, 2:3], ss[:, 1:2], AF.Exp, [Rss], [Rss], scale=-0.5)
        self.stt("dve", hb[:], xt[:], ss[:, 2:3], gain_b[:], ALU.mult, ALU.mult, [Rxt, Rss, Rgain], [Rhb])
        for c in range(8):
            self.tr(tp[:, c * 128:(c + 1) * 128], hb[:, c * 128:(c + 1) * 128], self.ident[:], [Rhb, self.Rconst], [Rtp])
        self.copy("act", hT[:, :, col0:col0 + 128], tp[:, :].rearrange("p (c t) -> p c t", c=8), [Rtp], [RhT])

    def ffn_phase(self, layer, xin, xout, prm):
        nc, P, S = self.nc, self.P, self.S
        TG = min(1024, S)
        NG = S // TG
        NT = TG // 128
        NH = TG // 512
        with ExitStack() as st:
            gain_b = self.sb("f_gain", [128, D], F32, st)
            hT = self.sb("f_hT", [128, 8, TG], BF16, st)
            actT = self.sb("f_actT", [128, NFC, TG], BF16, st)
            wd = self.sb("f_wd", [128, NFC, D], BF16, st)
            wg = [self.sb("f_wg%d" % i, [128, 8, 256], BF16, st) for i in range(2)]
            wu = [self.sb("f_wu%d" % i, [128, 8, 256], BF16, st) for i in range(2)]
            xts = [self.sb("f_xt%d" % i, [128, D], F32, st) for i in range(3)]
            ots = [self.sb("f_ot%d" % i, [128, 512], F32, st) for i in range(2)]
            sil = [self.sb("f_sil%d" % i, [128, 512], F32, st) for i in range(2)]
            bufs = {
                "junk": [(self.sb("f_junk%d" % i, [128, D], BF16, st), Res()) for i in range(2)],
                "ss": [(self.sb("f_ss%d" % i, [128, 4], F32, st), Res()) for i in range(2)],
                "hb": [(self.sb("f_hb%d" % i, [128, D], BF16, st), Res()) for i in range(2)],
                "tp": [(self.ps("f_tp%d" % i, BF16, st), Res()) for i in range(2)],
            }
            pg = [(self.ps("f_pg%d" % i, F32, st), Res()) for i in range(2)]
            pu = [(self.ps("f_pu%d" % i, F32, st), Res()) for i in range(2)]
            po = [(self.ps("f_po%d" % i, F32, st), Res()) for i in range(2)]
            Rgain = Res(); RhT = [Res() for _ in range(NT)]; Ract = [Res() for _ in range(NFC)]
            Rwd = Res(); Rwg = [Res(), Res()]; Rwu = [Res(), Res()]
            Rxt = [Res() for _ in range(3)]; Rot = [Res(), Res()]; Rsil = [Res(), Res()]
            d_gain = P.dmasem("fgain"); d_x = [P.dmasem("fx%d" % i) for i in range(3)]
            d_wg = [P.dmasem("fwg%d" % i) for i in range(2)]; d_wu = [P.dmasem("fwu%d" % i) for i in range(2)]
            d_wd = P.dmasem("fwd"); d_o = [P.dmasem("fo%d" % i) for i in range(2)]
            Rxdram = self.Rxdram

            self.dma("sp", gain_b[:], prm["ffn_norm"][layer:layer + 1, :].partition_broadcast(128), d_gain, [], [Rgain])
            wgv = prm["ffn_w_gate"][layer].rearrange("(c p) f -> p c f", p=128)
            wuv = prm["ffn_w_up"][layer].rearrange("(c p) f -> p c f", p=128)
            wdv = prm["ffn_w_down"][layer].rearrange("(c p) n -> p c n", p=128)
            xcnt = 0
            ocnt = 0
            step = 0
            for g in range(NG):
                t0 = g * TG
                for c0 in range(0, NFC, 2):
                    self.dma("pool", wd[:, c0:c0 + 2, :], wdv[:, c0:c0 + 2, :], d_wd, [], [Rwd])
                for i in range(NT):
                    b = xcnt % 3
                    tix = (t0 // 128) + i
                    self.dma("sp", xts[b][:], xin[t0 + i * 128:t0 + (i + 1) * 128, :], d_x[b], [Rxdram[tix]], [Rxt[b]])
                    self.norm_tile(xts[b], Rxt[b], gain_b, Rgain, hT, RhT[i], i * 128, bufs, xcnt)
                    xcnt += 1
                for fg in range(NFC // 2):
                    wb = fg % 2
                    self.dma("pool", wg[wb][:], wgv[:, :, fg * 256:(fg + 1) * 256], d_wg[wb], [], [Rwg[wb]])
                    self.dma("pool", wu[wb][:], wuv[:, :, fg * 256:(fg + 1) * 256], d_wu[wb], [], [Rwu[wb]])
                    for fc in range(2):
                        f = fg * 2 + fc
                        for th in range(NH):
                            pb = step % 2
                            pgt, Rpg = pg[pb]
                            put, Rpu = pu[pb]
                            rh = RhT[th * 4:(th + 1) * 4]
                            for c in range(8):
                                self.mm(pgt[:, :], wg[wb][:, c, fc * 128:(fc + 1) * 128], hT[:, c, th * 512:(th + 1) * 512],
                                        c == 0, c == 7, [Rwg[wb]] + rh, [Rpg])
                            for c in range(8):
                                self.mm(put[:, :], wu[wb][:, c, fc * 128:(fc + 1) * 128], hT[:, c, th * 512:(th + 1) * 512],
                                        c == 0, c == 7, [Rwu[wb]] + rh, [Rpu])
                            self.act(sil[pb][:], pgt[:, :], AF.Silu, [Rpg], [Rsil[pb]])
                            self.tt("dve", actT[:, f, th * 512:(th + 1) * 512], sil[pb][:], put[:, :], ALU.mult,
                                    [Rsil[pb], Rpu], [Ract[f]])
                            step += 1
                for i in range(NT):
                    b = xcnt % 3
                    tix = (t0 // 128) + i
                    self.dma("sp", xts[b][:], xin[t0 + i * 128:t0 + (i + 1) * 128, :], d_x[b], [Rxdram[tix]], [Rxt[b]])
                    xcnt += 1
                    for nh in range(2):
                        ob = ocnt % 2
                        pot, Rpo = po[ob]
                        for f in range(NFC):
                            self.mm(pot[:, :], actT[:, f, i * 128:(i + 1) * 128], wd[:, f, nh * 512:(nh + 1) * 512],
                                    f == 0, f == NFC - 1, [Ract[f], Rwd], [Rpo])
                        self.tt("dve", ots[ob][:], pot[:, :], xts[b][:, nh * 512:(nh + 1) * 512], ALU.add,
                                [Rpo, Rxt[b]], [Rot[ob]])
                        self.dma("sp", xout[t0 + i * 128:t0 + (i + 1) * 128, nh * 512:(nh + 1) * 512], ots[ob][:], d_o[ob],
                                 [Rot[ob]], [Rxdram[tix]])
                        ocnt += 1
            P.barrier()


    def hy_consts(self, st):
        c = {}
        W = [self.Rconst]
        sb = lambda n, shp, dt: self.sb(n, shp, dt, st)
        c["negmask"] = sb("c_negmask", [128, 128], BF16)
        c["tri"] = sb("c_tri", [128, 128], F32)
        c["nones"] = sb("c_nones", [128, 128], F32)
        c["identf"] = sb("c_identf", [128, 128], F32)
        c["bd"] = sb("c_bd", [128, 128], BF16)
        c["esel"] = sb("c_esel", [72, 8, 128], BF16)
        c["ones64"] = sb("c_ones64", [128, 64], BF16)
        c["invc"] = sb("c_invc", [128, 4, 16], F32)
        nm, tri, nones, identf, bd, esel, ones64, invc = (c[k] for k in ("negmask", "tri", "nones", "identf", "bd", "esel", "ones64", "invc"))
        self.memset("pool", nm[:], 0.0, W)
        self.P.emit("pool", lambda e: e.affine_select(out=nm[:], in_=nm[:], pattern=[[1, 128]], compare_op=ALU.is_ge,
                                                      fill=-30000.0, base=0, channel_multiplier=-1), (), W)
        self.memset("pool", tri[:], -1.0, W)
        self.P.emit("pool", lambda e: e.affine_select(out=tri[:], in_=tri[:], pattern=[[1, 128]], compare_op=ALU.is_ge,
                                                      fill=0.0, base=0, channel_multiplier=-1), (), W)
        self.memset("pool", nones[:], -1.0, W)
        self.memset("pool", identf[:], 1.0, W)
        self.P.emit("pool", lambda e: e.affine_select(out=identf[:], in_=identf[:], pattern=[[-1, 128]], compare_op=ALU.is_equal,
                                                      fill=0.0, base=0, channel_multiplier=1), (), W)
        self.memset("pool", bd[:], 0.0, W)
        self.memset("pool", bd[0:64, 0:64], 1.0, W)
        self.memset("pool", bd[64:128, 64:128], 1.0, W)
        self.memset("pool", esel[0:8], 8.0, W)
        self.P.emit("pool", lambda e: e.affine_select(out=esel[0:8], in_=esel[0:8], pattern=[[1, 8], [0, 128]], compare_op=ALU.is_equal,
                                                      fill=0.0, base=0, channel_multiplier=-1), (), W)
        d_e = self.P.dmasem("esel")
        self.dma("sp", esel[64:72], esel[0:8], d_e, [self.Rconst], [self.Rconst])
        self.memset("pool", ones64[:], 1.0, W)
        for g, w in enumerate((2, 4, 8, 16)):
            self.memset("pool", invc[:, g, :], 1.0 / w, W)
            for t in range(w - 1):
                self.memset("pool", invc[:, g, t:t + 1], 1.0 / (t + 1), W)
        return c

    def hy_phase(self, e, layer, xin, xout, prm):
        nc, P, S = self.nc, self.P, self.S
        NI = S // 512
        NB = S // 128
        Rc = self.Rconst
        with ExitStack() as st:
            cst = self.hy_consts(st)
            sb = lambda n, shp, dt: self.sb("h_" + n, shp, dt, st)
            w_in = sb("w_in", [128, 8, IN_COLS], BF16)
            w_out = sb("w_out", [128, 8, D], BF16)
            pw = sb("pw", [128, 4, 128], BF16)
            gain_b = sb("gain", [128, D], F32)
            qg = sb("qg", [128, 1], F32); kg = sb("kg", [128, 1], F32)
            pscale = sb("pscale", [128, 4], F32)
            fb = sb("fb", [128, 8], F32)
            kT = sb("kT", [128, 4, S], BF16)
            vc = sb("vc", [128, NB, 512], BF16)
            cumK = sb("cumK", [128, NB, 8], F32)
            kb = sb("kb", [128, NB, 8], F32)
            hc = sb("hc", [128, 8, 512], BF16)
            qT = sb("qT", [128, 4, 512], BF16)
            sgT = sb("sgT", [128, 4, 512], BF16)
            upad = sb("upad", [128, 4, 528], F32)
            tA = sb("tA", [128, 528], F32); tB = sb("tB", [128, 528], F32)
            pooledT = sb("pooledT", [128, 4, 512], BF16)
            xts = [sb("xt%d" % i, [128, D], F32) for i in range(2)]
            ots = [sb("ot%d" % i, [128, 512], F32) for i in range(2)]
            junk = sb("junk", [128, D], BF16)
            kf = sb("kf", [128, 512], F32); sq = sb("sq", [128, 512], BF16); rs = sb("rs", [128, 512], F32)
            pT = [sb("pT%d" % i, [128, 512], BF16) for i in range(4)]
            rden = sb("rden", [128, 512], F32); atmp = sb("atmp", [128, 512], F32)
            carry = [sb("carry%d" % i, [128, 8], F32) for i in range(2)]
            zf = sb("zf", [128, 8], F32); lf = sb("lf", [128, 8], F32)
            qctok = sb("qctok", [128, 4, 8], F32)
            qcT = sb("qcT", [72, 512], BF16)
            t16 = sb("t16", [128, 16], F32)
            bufs = {
                "junk": [(junk, Res()), (junk, Res())],
                "ss": [(sb("ss%d" % i, [128, 4], F32), Res()) for i in range(2)],
                "hb": [(sb("hb%d" % i, [128, D], BF16), Res()) for i in range(1)] * 2,
            }
            gp = [(self.ps("h_gp%d" % i, F32, st), Res()) for i in range(2)]
            bufs["tp"] = [(g_[:, :].bitcast(BF16), Rg_) for g_, Rg_ in gp]
            sbk = [(self.ps("h_s%d" % i, F32, st), Res()) for i in range(4)]
            accN, RaccN = self.ps("h_accN", F32, st), Res()
            accD, RaccD = self.ps("h_accD", F32, st), Res()
            Rw = Res(); Rgain = Res(); Rsmall = Res()
            Rhc = Res(); RqT = Res(); RsgT = Res(); RkT = [Res() for _ in range(NI)]; Rvc = [Res() for _ in range(NB)]
            RcumK = Res(); Rkb = Res(); Rupad = Res(); RtA = Res(); RtB = Res(); Rpooled = Res()
            Rxt = [Res(), Res()]; Rot = [Res(), Res()]; Rkf = Res(); Rsq = Res(); Rrs = Res()
            RpT = [Res() for _ in range(4)]; Rrden = Res(); Ratmp = Res(); Rcarry = [Res(), Res()]
            Rzf = Res(); Rlf = Res(); Rqctok = Res(); RqcT = Res(); Rt16 = Res()
            d_q = P.dmasem("hq"); d_w = P.dmasem("hw"); d_c = P.dmasem("hc"); d_x = [P.dmasem("hx%d" % i) for i in range(2)]
            d_o = [P.dmasem("ho%d" % i) for i in range(2)]
            Rxdram = self.Rxdram
            win_v = prm["hy_w_in"][e].rearrange("(c p) n -> p c n", p=128)
            wout_v = prm["hy_w_out"][e].rearrange("(c p) n -> p c n", p=128)
            for c in range(8):
                self.dma("pool", w_in[:, c, :], win_v[:, c, :], d_w, [], [Rw])
            for c in range(8):
                self.dma("pool", w_out[:, c, :], wout_v[:, c, :], d_w, [], [Rw])
            self.dma("pool", pw[:], prm["hy_pool_w"][e].rearrange("g c d -> c g d"), d_w, [], [Rw])
            self.dma("sp", gain_b[:], prm["mix_norm"][layer:layer + 1, :].partition_broadcast(128), d_c, [], [Rgain])
            qgv = prm["hy_q_gain"][e].rearrange("(d o) -> d o", o=1)
            kgv = prm["hy_k_gain"][e].rearrange("(d o) -> d o", o=1)
            for hp in range(2):
                self.dma("sp", qg[hp * 64:(hp + 1) * 64, :], qgv, d_c, [], [Rsmall])
                self.dma("sp", kg[hp * 64:(hp + 1) * 64, :], kgv, d_c, [], [Rsmall])
            self.dma("sp", pscale[:], prm["hy_pool_scale"][e].rearrange("(g p) -> p g", p=128), d_c, [], [Rsmall], slow=True)
            self.dma("sp", fb[:], prm["hy_f_bias"][e:e + 1, :].partition_broadcast(128), d_c, [], [Rsmall])
            self.memset("pool", carry[0][:], 0.0, [Rcarry[0]])
            self.memset("pool", upad[:, :, 0:16], 0.0, [Rupad])
            xcnt = 0; ocnt = 0; gcnt = 0; scnt = 0; pcnt = 0; ccnt = 0

            def proj_fm(col0, gi):
                pt, Rp = gp[gi % 2]
                for c in range(8):
                    self.mm(pt[:, :], w_in[:, c, col0:col0 + 128], hc[:, c, :], c == 0, c == 7, [Rw, Rhc], [Rp])
                return pt, Rp

            for I in range(NI):
                t0 = I * 512
                for i in range(4):
                    b = xcnt % 2
                    tix = I * 4 + i
                    self.dma("sp", xts[b][:], xin[t0 + i * 128:t0 + (i + 1) * 128, :], d_x[b], [Rxdram[tix]], [Rxt[b]])
                    self.norm_tile(xts[b], Rxt[b], gain_b, Rgain, hc, Rhc, i * 128, bufs, xcnt)
                    xcnt += 1
                for i in range(4):
                    blk = I * 4 + i
                    smp, Rsm = gp[gcnt % 2]; gcnt += 1
                    for c in range(8):
                        self.mm(smp[:, 0:8], hc[:, c, i * 128:(i + 1) * 128], w_in[:, c, 2048:2056], c == 0, c == 7, [Rw, Rhc], [Rsm])
                    self.tt("dve", zf[:], smp[:, 0:8], fb[:], ALU.add, [Rsm, Rsmall], [Rzf])
                    self.act(lf[:], zf[:], AF.Exp, [Rzf], [Rlf], scale=-1.0)
                    self.act(lf[:], lf[:], AF.Ln, [Rlf, Rc], [Rlf], bias=self.eps_rms[:, 2:3])
                    self.mm(smp[:, 8:16], cst["tri"][:], lf[:], True, True, [Rlf, Rc], [Rsm])
                    self.mm(smp[:, 16:24], cst["nones"][:], lf[:], True, True, [Rlf, Rc], [Rsm])
                    cin, cout = carry[ccnt % 2], carry[(ccnt + 1) % 2]
                    Rcin, Rcout = Rcarry[ccnt % 2], Rcarry[(ccnt + 1) % 2]
                    self.tt("dve", cumK[:, blk, :], smp[:, 8:16], cin[:], ALU.add, [Rsm, Rcin], [RcumK])
                    self.tt("dve", cout[:], smp[:, 16:24], cin[:], ALU.add, [Rsm, Rcin], [Rcout])
                    ccnt += 1
                cend, Rcend = carry[ccnt % 2], Rcarry[ccnt % 2]
                nj = 4 * I + 4
                cb4 = cend[:, :].unsqueeze(1).broadcast_to([128, 4, 8])
                self.tt("dve", qctok[:], cumK[:, 4 * I:4 * I + 4, :], cb4, ALU.subtract, [RcumK, Rcend], [Rqctok])
                cbn = cend[:, :].unsqueeze(1).broadcast_to([128, nj, 8])
                self.tt("dve", kb[:, 0:nj, :], cbn, cumK[:, 0:nj, :], ALU.subtract, [RcumK, Rcend], [Rkb])
                ptq, Rpq = gp[gcnt % 2]; gcnt += 1
                for i in range(4):
                    self.tr(ptq[0:8, i * 128:(i + 1) * 128], qctok[:, i, :], cst["identf"][:], [Rqctok, Rc], [Rpq])
                self.copy("dve", qcT[0:8, :], ptq[0:8, 0:512], [Rpq], [RqcT])
                self.dma("sp", qcT[64:72, :], qcT[0:8, :], d_q, [RqcT], [RqcT])
                for which in range(2):
                    for cc in range(4):
                        col0 = (512 if which == 0 else 0) + cc * 128
                        pt, Rp = proj_fm(col0, gcnt); gcnt += 1
                        self.copy("act", kf[:], pt[:, :], [Rp], [Rkf])
                        self.tt("pool", sq[:], kf[:], kf[:], ALU.mult, [Rkf], [Rsq])
                        pt2, Rp2 = gp[gcnt % 2]; gcnt += 1
                        self.mm(pt2[:, :], cst["bd"][:], sq[:], True, True, [Rsq, Rc], [Rp2])
                        self.act(rs[:], pt2[:, :], AF.Ln, [Rp2, Rc], [Rrs], scale=1.0 / 64, bias=self.eps_rms[:, 0:1])
                        self.act(rs[:], rs[:], AF.Exp, [Rrs], [Rrs], scale=-0.5)
                        if which == 0:
                            self.stt("dve", kT[:, cc, t0:t0 + 512], kf[:], kg[:, 0:1], rs[:], ALU.mult, ALU.mult,
                                     [Rkf, Rrs, Rsmall], [RkT[I]])
                        else:
                            self.stt("dve", qT[:, cc, :], kf[:], qg[:, 0:1], rs[:], ALU.mult, ALU.mult,
                                     [Rkf, Rrs, Rsmall], [RqT])
                for i in range(4):
                    blk = I * 4 + i
                    pt, Rp = gp[gcnt % 2]; gcnt += 1
                    for c in range(8):
                        self.mm(pt[:, :], hc[:, c, i * 128:(i + 1) * 128], w_in[:, c, 1024:1536], c == 0, c == 7, [Rw, Rhc], [Rp])
                    self.copy("act", vc[:, blk, :], pt[:, :], [Rp], [Rvc[blk]])
                for cc in range(4):
                    pt, Rp = proj_fm(1536 + cc * 128, gcnt); gcnt += 1
                    self.act(sgT[:, cc, :], pt[:, :], AF.Sigmoid, [Rp], [RsgT])
                for g in range(4):
                    pt, Rp = proj_fm(2056 + g * 128, gcnt); gcnt += 1
                    self.copy("act", upad[:, g, 16:528], pt[:, :], [Rp], [Rupad])
                for g in range(4):
                    u = upad[:, g, :]
                    cur, Rcur = u, Rupad
                    lo = 0
                    tmps = [(tA, RtA), (tB, RtB)]
                    for lvl in range(g + 1):
                        sh = 1 << lvl
                        dst, Rdst = tmps[lvl % 2]
                        nlo = lo + sh
                        self.tt("pool", dst[:, nlo:528], cur[:, nlo:528], cur[:, nlo - sh:528 - sh], ALU.add, [Rcur], [Rdst])
                        cur, Rcur, lo = dst, Rdst, nlo
                    wdt = 2 << g
                    self.stt("dve", pooledT[:, g, :], cur[:, 16:528], 1.0 / wdt, u[:, 16:528], ALU.mult, ALU.subtract,
                             [Rcur, Rupad], [Rpooled])
                    if I == 0:
                        self.tt("pool", t16[:], cur[:, 16:32], cst["invc"][:, g, :], ALU.mult, [Rcur, Rc], [Rt16])
                        self.tt("pool", pooledT[:, g, 0:16], t16[:], u[:, 16:32], ALU.subtract, [Rt16, Rupad], [Rpooled])
                self.copy("pool", upad[:, :, 0:16], upad[:, :, 512:528], [Rupad], [Rupad])
                for g in range(4):
                    pt, Rp = gp[gcnt % 2]; gcnt += 1
                    self.mm(pt[:, :], pw[:, g, :], pooledT[:, g, :], True, True, [Rw, Rpooled], [Rp])
                    self.act(hc[:, 4 + g, :], pt[:, :], AF.Copy, [Rp, Rsmall], [Rhc], scale=pscale[:, g:g + 1])
                steps = [(pr, j) for pr in range(4) for j in range(nj)]

                def qk(step, si):
                    pr, j = step
                    jj = j - 4 * I
                    c0 = 128 * jj if jj > 0 else 0
                    diag = jj >= 0
                    Ij = j // 4
                    for hp in range(2):
                        sbt, Rs = sbk[(si % 2) * 2 + hp]
                        self.mm(sbt[:, c0:512], kT[hp * 64:(hp + 1) * 64, pr, j * 128:(j + 1) * 128],
                                qT[hp * 64:(hp + 1) * 64, pr, c0:512], True, False, [RkT[Ij], RqT], [Rs])
                    for hp in range(2):
                        h = pr * 2 + hp
                        sbt, Rs = sbk[(si % 2) * 2 + hp]
                        self.mm(sbt[:, c0:512], cst["esel"][hp * 64:hp * 64 + 8, h, :], qcT[hp * 64:hp * 64 + 8, c0:512],
                                False, not diag, [RqcT, Rc], [Rs])
                    if diag:
                        for hp in range(2):
                            sbt, Rs = sbk[(si % 2) * 2 + hp]
                            self.mm(sbt[:, c0:c0 + 128], self.ident[:], cst["negmask"][:], False, True, [Rc], [Rs])
                    return c0

                def rest(step, si, c0):
                    pr, j = step
                    first = (j == 0)
                    last = (j == nj - 1)
                    pts = []
                    for hp in range(2):
                        h = pr * 2 + hp
                        sbt, Rs = sbk[(si % 2) * 2 + hp]
                        pt_, Rp_ = pT[(si % 2) * 2 + hp], RpT[(si % 2) * 2 + hp]
                        self.act(pt_[:, c0:512], sbt[:, c0:512], AF.Exp, [Rs, Rkb], [Rp_], scale=0.125, bias=kb[:, j, h:h + 1])
                        pts.append((pt_, Rp_))
                    for hp in range(2):
                        h = pr * 2 + hp
                        pt_, Rp_ = pts[hp]
                        self.mm(accN[hp * 64:(hp + 1) * 64, c0:512], vc[:, j, h * 64:(h + 1) * 64], pt_[:, c0:512],
                                first, last, [Rvc[j], Rp_], [RaccN], tile_position=(0, hp * 64))
                    for hp in range(2):
                        pt_, Rp_ = pts[hp]
                        self.mm(accD[hp * 64:(hp + 1) * 64, c0:512], cst["ones64"][:], pt_[:, c0:512],
                                first, last, [Rc, Rp_], [RaccD], tile_position=(0, hp * 64))
                    if last:
                        self.P.emit("dve", lambda e_: e_.reciprocal(out=rden[:], in_=accD[:, :]), [RaccD], [Rrden])
                        self.tt("dve", atmp[:], accN[:, :], rden[:], ALU.mult, [RaccN, Rrden], [Ratmp])
                        self.tt("pool", hc[:, pr, :], atmp[:], sgT[:, pr, :], ALU.mult, [Ratmp, RsgT], [Rhc])

                pend = []
                for n, stp in enumerate(steps):
                    c0 = qk(stp, scnt + n)
                    pend.append((stp, scnt + n, c0))
                    if len(pend) > 1:
                        rest(*pend.pop(0))
                while pend:
                    rest(*pend.pop(0))
                scnt += len(steps)
                for i in range(4):
                    b = xcnt % 2
                    tix = I * 4 + i
                    self.dma("sp", xts[b][:], xin[t0 + i * 128:t0 + (i + 1) * 128, :], d_x[b], [Rxdram[tix]], [Rxt[b]])
                    xcnt += 1
                    for nh in range(2):
                        ob = ocnt % 2
                        pt, Rp = gp[gcnt % 2]; gcnt += 1
                        for c in range(8):
                            self.mm(pt[:, :], hc[:, c, i * 128:(i + 1) * 128], w_out[:, c, nh * 512:(nh + 1) * 512],
                                    c == 0, c == 7, [Rhc, Rw], [Rp])
                        self.tt("dve", ots[ob][:], pt[:, :], xts[b][:, nh * 512:(nh + 1) * 512], ALU.add, [Rp, Rxt[b]], [Rot[ob]])
                        self.dma("sp", xout[t0 + i * 128:t0 + (i + 1) * 128, nh * 512:(nh + 1) * 512], ots[ob][:], d_o[ob],
                                 [Rot[ob]], [Rxdram[tix]])
                        ocnt += 1
            P.barrier()


    def rw_phase(self, o, layer, xin, xout, prm):
        nc, P, S = self.nc, self.P, self.S
        NCH = S // 128
        Rc = self.Rconst
        first_layer = (o == 0)
        with ExitStack() as st:
            sb = lambda n, shp, dt: self.sb("r_" + n, shp, dt, st)
            Wr = sb("Wr", [128, 8, D], BF16); Wk = sb("Wk", [128, 8, D], BF16)
            Wv = sb("Wv", [128, 8, D], BF16); Wo = sb("Wo", [128, 8, D], BF16)
            l1 = sb("l1", [128, 8, 320], BF16)
            wa2 = sb("wa2", [128, D], BF16)
            g2a = sb("g2a", [128, D], BF16)
            gv2 = sb("gv2", [64, D], BF16)
            bc = sb("bc", [128, 8, D], BF16)
            gain_b = sb("gain", [128, D], F32)
            mu = sb("mu", [128, 6, 8], F32)
            IUf = sb("IUf", [128, 128], F32); SUf = sb("SUf", [128, 128], F32); SLf = sb("SLf", [128, 128], F32)
            onesf = sb("onesf", [128, 128], F32)
            tiny = sb("tiny", [128, 1], F32)
            W = [Rc]
            self.memset("pool", IUf[:], 1.0, W)
            self.P.emit("pool", lambda e: e.affine_select(out=IUf[:], in_=IUf[:], pattern=[[1, 128]], compare_op=ALU.is_ge,
                                                          fill=0.0, base=0, channel_multiplier=-1), (), W)
            self.memset("pool", SUf[:], 1.0, W)
            self.P.emit("pool", lambda e: e.affine_select(out=SUf[:], in_=SUf[:], pattern=[[1, 128]], compare_op=ALU.is_gt,
                                                          fill=0.0, base=0, channel_multiplier=-1), (), W)
            self.memset("pool", SLf[:], 1.0, W)
            self.P.emit("pool", lambda e: e.affine_select(out=SLf[:], in_=SLf[:], pattern=[[-1, 128]], compare_op=ALU.is_gt,
                                                          fill=0.0, base=0, channel_multiplier=1), (), W)
            self.memset("pool", onesf[:], -float(np.exp(-0.5)), W)
            cIU = sb("cIU", [128, 128], F32); cSU = sb("cSU", [128, 128], F32)
            self.ts("pool", cIU[:], IUf[:], -float(np.exp(-0.5)), None, ALU.mult, None, [Rc], W)
            self.ts("pool", cSU[:], SUf[:], -float(np.exp(-0.5)), None, ALU.mult, None, [Rc], W)
            self.memset("pool", tiny[:], 1e-24, W)
            xt = sb("xt", [128, D], F32); hb = sb("hb", [128, D], BF16)
            hTe = sb("hTe", [128, 8, 129], BF16)
            xm = [sb("xm%d" % i, [128, 8, 128], BF16) for i in range(2)]
            xx = sb("xx", [128, 8, 128], BF16)
            sc = [sb("sc%d" % i, [128, D], F32) for i in range(5)]
            r_bf = sb("r_bf", [128, D], BF16); kp_bf = sb("kp_bf", [128, D], BF16); kk_bf = sb("kk_bf", [128, D], BF16)
            b_bf = sb("b_bf", [128, D], BF16); v_bf = sb("v_bf", [128, D], BF16); g_bf = sb("g_bf", [128, D], BF16)
            prod = [sb("prod%d" % i, [128, D], BF16) for i in range(2)]
            Khat = sb("Khat", [128, D], BF16); Bhat = sb("Bhat", [128, D], BF16)
            RtT = sb("RtT", [128, 8, 128], BF16); KtT = sb("KtT", [128, 8, 128], BF16)
            BtT = sb("BtT", [128, 8, 128], BF16); AtT = sb("AtT", [128, 8, 128], BF16)
            lsb = sb("lsb", [128, 512], BF16)
            Mbh = [[sb("Mb%d_%d" % (hf, i), [128, 8, 128], BF16) for i in range(2)] for hf in range(2)]
            Nbh = [[sb("Nb%d_%d" % (hf, i), [128, 8, 128], BF16) for i in range(2)] for hf in range(2)]
            Xbh = [[sb("Xb%d_%d" % (hf, i), [128, 8, 128], BF16) for i in range(2)] for hf in range(2)]
            XT = sb("XT", [128, 16, 128], BF16)
            AakT = sb("AakT", [128, 16, 128], BF16); ArbT = sb("ArbT", [128, 16, 128], BF16); ArkT = sb("ArkT", [128, 16, 128], BF16)
            ST = sb("ST", [128, 8, 64], F32); STb = sb("STb", [128, 8, 64], BF16); STt = sb("STt", [128, 8, 64], F32)
            PCc = sb("PCc", [128, 8], F32)
            st16 = sb("st16", [128, 8, 16], F32)
            yfin = Khat; yT = xm[0]
            ss = sb("ss", [128, 4], F32)
            tp = [(self.ps("r_tp%d" % i, BF16, st), Res()) for i in range(2)]
            gpool = [(self.ps("r_gp%d" % i, F32, st), Res()) for i in range(6)]
            self._gi = 0

            def bank():
                b_ = gpool[self._gi % 6]
                self._gi += 1
                return b_

            self._ti = 0

            def tbank():
                b_ = tp[self._ti % 2]
                self._ti += 1
                return b_

            R = {k: Res(k) for k in ("w", "small", "xt", "hb", "hTe", "xx", "ss", "r_bf", "kp_bf", "kk_bf", "b_bf", "v_bf", "g_bf",
                                     "Khat", "Bhat", "RtT", "KtT", "BtT", "AtT", "lsb", "XT", "AakT", "ArbT", "ArkT", "RHS", "U",
                                     "ST", "STb", "STt", "PCc", "st16", "yfin", "yT", "ot", "gain")}
            Rsc = [Res() for _ in range(5)]; Rxm = [Res(), Res()]; Rprod = [Res(), Res()]
            ot = sc[2]
            R["ot"] = Rsc[2]
            R["yfin"] = R["Khat"]
            R["yT"] = Rxm[0]
            RHS, U = prod[0], prod[1]
            R["RHS"], R["U"] = Rprod[0], Rprod[1]
            RMbh = [[Res(), Res()] for _ in range(2)]; RNbh = [[Res(), Res()] for _ in range(2)]; RXbh = [[Res(), Res()] for _ in range(2)]
            d_x2 = P.dmasem("rx2"); d_w = P.dmasem("rw"); d_c = P.dmasem("rc"); d_x = P.dmasem("rx"); d_o = P.dmasem("ro"); d_v = P.dmasem("rv")
            Rxdram = self.Rxdram
            Rvf = self.Rvf

            def wview(name):
                return prm[name][o].rearrange("(c p) n -> p c n", p=128)
            for Wt, nm in ((Wr, "rw_w_r"), (Wk, "rw_w_k"), (Wv, "rw_w_v"), (Wo, "rw_w_o")):
                v_ = wview(nm)
                for c in range(8):
                    self.dma("pool", Wt[:, c, :], v_[:, c, :], d_w, [], [R["w"]])
            self.dma("pool", l1[:, :, 0:64], wview("rw_w1"), d_w, [], [R["w"]])
            self.dma("pool", l1[:, :, 64:128], wview("rw_a1"), d_w, [], [R["w"]])
            self.dma("pool", l1[:, :, 128:288], wview("rw_g1"), d_w, [], [R["w"]])
            self.dma("pool", wa2[0:64, :], prm["rw_w2"][o], d_w, [], [R["w"]])
            self.dma("pool", wa2[64:128, :], prm["rw_a2"][o], d_w, [], [R["w"]])
            self.dma("pool", g2a[:, :], prm["rw_g2"][o][0:128, :], d_w, [], [R["w"]])
            self.dma("pool", gv2[0:32, :], prm["rw_g2"][o][128:160, :], d_w, [], [R["w"]])
            if not first_layer:
                self.dma("pool", l1[:, :, 288:320], prm["rw_v1"][o - 1].rearrange("(c p) n -> p c n", p=128), d_w, [], [R["w"]])
                self.dma("pool", gv2[32:64, :], prm["rw_v2"][o - 1], d_w, [], [R["w"]])
            rows = [prm["rw_w0"][o:o + 1, :], prm["rw_a0"][o:o + 1, :],
                    (prm["rw_v0"][o - 1:o, :] if not first_layer else prm["rw_w0"][o:o + 1, :]),
                    prm["rw_k_k"][o:o + 1, :], prm["rw_k_a"][o:o + 1, :], prm["rw_ln_w"][o:o + 1, :], prm["rw_ln_b"][o:o + 1, :],
                    prm["rw_r_k"][o:o + 1].rearrange("o h d -> o (h d)")]
            for i, rv in enumerate(rows):
                self.dma("pool", bc[:, i, :], rv.partition_broadcast(128), d_w, [], [R["w"]])
            W0, A0, V0, KK_, KA_, LNW, LNB, RK_ = (bc[:, i, :] for i in range(8))
            self.dma("sp", gain_b[:], prm["mix_norm"][layer:layer + 1, :].partition_broadcast(128), d_c, [], [R["gain"]])
            self.dma("sp", mu[:], prm["rw_mu"][o].rearrange("i (c p) -> p i c", p=128), d_c, [], [R["small"]], slow=True)
            self.memset("pool", ST[:], 0.0, [R["ST"]])
            self.memset("pool", STb[:], 0.0, [R["STb"]])
            self.memset("pool", hTe[:, :, 128:129], 0.0, [R["hTe"]])
            bufs = {"junk": [(b_bf, R["b_bf"])] * 2, "ss": [(ss, R["ss"])] * 2, "hb": [(hb, R["hb"])] * 2, "tp": tp}
            IUb = lambda n_: IUf[:, :].unsqueeze(1).broadcast_to([128, n_, 128])
            SUb = lambda n_: SUf[:, :].unsqueeze(1).broadcast_to([128, n_, 128])
            SLb = lambda n_: SLf[:, :].unsqueeze(1).broadcast_to([128, n_, 128])
            IDb = lambda n_: self.ident[:, :].unsqueeze(1).broadcast_to([128, n_, 128])
            h3 = lambda ap: ap.rearrange("p (h d) -> p h d", d=64)

            def proj_tok(xT, Rx, Wt, lo=0):
                bks = []
                for nh in range(2):
                    pt, Rp = bank()
                    for c in range(8):
                        self.mm(pt[:, :], xT[:, c, :], Wt[:, c, nh * 512:(nh + 1) * 512], c == 0, c == 7, [Rx, R["w"]], [Rp])
                    bks.append((pt, Rp))
                return bks

            def mix(i, j):
                mub = mu[:, i, :].unsqueeze(2).broadcast_to([128, 8, 128])
                self.tt("dve", xm[j][:], xx[:], mub, ALU.mult, [R["xx"], R["small"]], [Rxm[j]])
                self.tt("dve", xm[j][:], xm[j][:], hTe[:, :, 1:129], ALU.add, [Rxm[j], R["hTe"]], [Rxm[j]])
                return xm[j], Rxm[j]

            def evac2(bks, fn):
                for nh, (pt, Rp) in enumerate(bks):
                    fn(nh, pt, Rp, slice(nh * 512, (nh + 1) * 512))

            import os as _os
            stop = int(_os.environ.get("RW_STOP", "99"))

            def early(n_, t0_):
                self.dma("sp", ot[:], xin[t0_:t0_ + 128, :], d_x2, [Rxdram[n_]], [R["ot"]])
                self.dma("sp", xout[t0_:t0_ + 128, :], ot[:], d_o, [R["ot"]], [Rxdram[n_]])

            def stage_r1(n):
                t0 = n * 128
                if self.dbg_tags:
                    P.tag = "c%d.%s" % (n, "R1")
                self.copy("pool", hTe[:, :, 0:1], hTe[:, :, 128:129], [R["hTe"]], [R["hTe"]])
                self.dma("sp", xt[:], xin[t0:t0 + 128, :], d_x, [Rxdram[n]], [R["xt"]])
                self.norm_tile(xt, R["xt"], gain_b, R["gain"], hTe[:, :, 1:129], R["hTe"], 0, bufs, n)
                self.tt("pool", xx[:], hTe[:, :, 0:128], hTe[:, :, 1:129], ALU.subtract, [R["hTe"]], [R["xx"]])

            stage_r1(0)
            for n in range(NCH):
                t0 = n * 128
                if stop <= 1:
                    early(n, t0)
                    continue
                if self.dbg_tags:
                    P.tag = "c%d.%s" % (n, "R2")
                xr, Rxr = mix(0, 0)
                evac2(proj_tok(xr, Rxr, Wr), lambda nh, pt, Rp, sl: self.copy("act", r_bf[:, sl], pt[:, :], [Rp], [R["r_bf"]]))
                xk, Rxk = mix(2, 1)
                evac2(proj_tok(xk, Rxk, Wk), lambda nh, pt, Rp, sl: self.copy("act", sc[0][:, sl], pt[:, :], [Rp], [Rsc[0]]))
                xv, Rxv = mix(3, 0)
                evac2(proj_tok(xv, Rxv, Wv), lambda nh, pt, Rp, sl: self.copy("act", sc[1][:, sl], pt[:, :], [Rp], [Rsc[1]]))
                lp, Rlp = bank()
                if not first_layer:
                    for c in range(8):
                        self.mm(lp[32:64, 256:384], l1[:, c, 288:320], xv[:, c, :], c == 0, c == 7, [Rxv, R["w"]], [Rlp],
                                tile_position=(0, 32))
                xw, Rxw = mix(1, 1)
                for c in range(8):
                    self.mm(lp[0:64, 0:128], l1[:, c, 0:64], xw[:, c, :], c == 0, c == 7, [Rxw, R["w"]], [Rlp])
                xa, Rxa = mix(4, 0)
                for c in range(8):
                    self.mm(lp[64:128, 0:128], l1[:, c, 64:128], xa[:, c, :], c == 0, c == 7, [Rxa, R["w"]], [Rlp],
                            tile_position=(0, 64))
                xg, Rxg = mix(5, 1)
                for c in range(8):
                    self.mm(lp[:, 128:256], l1[:, c, 128:256], xg[:, c, :], c == 0, c == 7, [Rxg, R["w"]], [Rlp])
                for c in range(8):
                    self.mm(lp[0:32, 256:384], l1[:, c, 256:288], xg[:, c, :], c == 0, c == 7, [Rxg, R["w"]], [Rlp])
                self.act(lsb[0:64, 0:128], lp[0:64, 0:128], AF.Tanh, [Rlp], [R["lsb"]])
                self.copy("act", lsb[64:128, 0:128], lp[64:128, 0:128], [Rlp], [R["lsb"]])
                self.act(lsb[:, 128:256], lp[:, 128:256], AF.Sigmoid, [Rlp], [R["lsb"]])
                self.act(lsb[0:32, 256:384], lp[0:32, 256:384], AF.Sigmoid, [Rlp], [R["lsb"]])
                if not first_layer:
                    self.copy("act", lsb[32:64, 256:384], lp[32:64, 256:384], [Rlp], [R["lsb"]])
                for nh in range(2):
                    sl = slice(nh * 512, (nh + 1) * 512)
                    pt, Rp = bank()
                    self.mm(pt[:, :], lsb[64:128, 0:128], wa2[64:128, sl], True, True, [R["lsb"], R["w"]], [Rp])
                    self.tt("dve", sc[3][:, sl], pt[:, :], A0[:, sl], ALU.add, [Rp, R["w"]], [Rsc[3]])
                self.act(sc[3][:], sc[3][:], AF.Sigmoid, [Rsc[3]], [Rsc[3]])
                for nh in range(2):
                    sl = slice(nh * 512, (nh + 1) * 512)
                    pt, Rp = bank()
                    self.mm(pt[:, :], lsb[:, 128:256], g2a[:, sl], True, False, [R["lsb"], R["w"]], [Rp])
                    self.mm(pt[:, :], lsb[0:32, 256:384], gv2[0:32, sl], False, True, [R["lsb"], R["w"]], [Rp])
                    self.copy("act", g_bf[:, sl], pt[:, :], [Rp], [R["g_bf"]])
                if first_layer:
                    self.dma("sp", self.vfirst[t0:t0 + 128, :], sc[1][:], d_v, [Rsc[1]], [Rvf[n]])
                else:
                    for nh in range(2):
                        sl = slice(nh * 512, (nh + 1) * 512)
                        pt, Rp = bank()
                        self.mm(pt[:, :], lsb[32:64, 256:384], gv2[32:64, sl], True, True, [R["lsb"], R["w"]], [Rp])
                        self.tt("dve", sc[4][:, sl], pt[:, :], V0[:, sl], ALU.add, [Rp, R["w"]], [Rsc[4]])
                    self.act(sc[4][:], sc[4][:], AF.Sigmoid, [Rsc[4]], [Rsc[4]])
                    self.dma("sp", ot[:], self.vfirst[t0:t0 + 128, :], d_v, [Rvf[n]], [R["ot"]])
                    self.tt("pool", ot[:], ot[:], sc[1][:], ALU.subtract, [R["ot"], Rsc[1]], [R["ot"]])
                    self.tt("pool", ot[:], ot[:], sc[4][:], ALU.mult, [R["ot"], Rsc[4]], [R["ot"]])
                    self.tt("pool", sc[1][:], sc[1][:], ot[:], ALU.add, [R["ot"], Rsc[1]], [Rsc[1]])
                self.copy("act", v_bf[:], sc[1][:], [Rsc[1]], [R["v_bf"]])
                for nh in range(2):
                    sl = slice(nh * 512, (nh + 1) * 512)
                    pt, Rp = bank()
                    self.mm(pt[:, :], lsb[0:64, 0:128], wa2[0:64, sl], True, True, [R["lsb"], R["w"]], [Rp])
                    self.tt("dve", sc[2][:, sl], pt[:, :], W0[:, sl], ALU.add, [Rp, R["w"]], [Rsc[2]])
                self.act(sc[2][:], sc[2][:], AF.Sigmoid, [Rsc[2]], [Rsc[2]])
                k32, a32 = sc[0], sc[3]
                self.tt("dve", sc[4][:], k32[:], KK_, ALU.mult, [Rsc[0], R["w"]], [Rsc[4]])
                self.act(sc[1][:], sc[4][:], AF.Square, [Rsc[4], R["v_bf"]], [Rsc[1]])
                self.P.emit("dve", lambda e_: e_.tensor_reduce(out=st16[:, 0, :], in_=h3(sc[1][:, :]), axis=AX.X, op=ALU.add),
                            [Rsc[1]], [R["st16"]])
                self.act(st16[:, 1, :], st16[:, 0, :], AF.Ln, [R["st16"], Rc], [R["st16"]], bias=tiny[:, 0:1])
                self.act(st16[:, 1, :], st16[:, 1, :], AF.Exp, [R["st16"]], [R["st16"]], scale=-0.5)
                rnb = st16[:, 1, :].unsqueeze(2).broadcast_to([128, 16, 64])
                self.tt("dve", h3(kk_bf[:, :]), h3(sc[4][:, :]), rnb, ALU.mult, [Rsc[4], R["st16"]], [R["kk_bf"]])
                self.stt("dve", sc[1][:], a32[:], -1.0, KA_, ALU.add, ALU.mult, [Rsc[3], R["w"]], [Rsc[1]])
                self.stt("dve", kp_bf[:], sc[1][:], 1.0, k32[:], ALU.add, ALU.mult, [Rsc[1], Rsc[0]], [R["kp_bf"]])
                self.tt("pool", b_bf[:], kk_bf[:], a32[:], ALU.mult, [R["kk_bf"], Rsc[3]], [R["b_bf"]])
                self.tt("pool", sc[4][:], r_bf[:], kp_bf[:], ALU.mult, [R["r_bf"], R["kp_bf"]], [Rsc[4]])
                self.tt("pool", sc[4][:], sc[4][:], RK_, ALU.mult, [Rsc[4], R["w"]], [Rsc[4]])
                self.P.emit("dve", lambda e_: e_.tensor_reduce(out=st16[:, 2, :], in_=h3(sc[4][:, :]), axis=AX.X, op=ALU.add),
                            [Rsc[4]], [R["st16"]])
                if stop <= 2:
                    early(n, t0)
                    continue
                if self.dbg_tags:
                    P.tag = "c%d.%s" % (n, "R3")
                ld = sc[2]
                for nh in range(2):
                    sl = slice(nh * 512, (nh + 1) * 512)
                    pt, Rp = bank()
                    self.mm(pt[:, :], cIU[:], ld[:, sl], True, True, [Rsc[2], Rc], [Rp])
                    self.act(sc[0][:, sl], pt[:, :], AF.Exp, [Rp], [Rsc[0]])
                    self.act(sc[1][:, sl], pt[:, :], AF.Exp, [Rp], [Rsc[1]], scale=-1.0)
                for nh in range(2):
                    sl = slice(nh * 512, (nh + 1) * 512)
                    pt, Rp = bank()
                    self.mm(pt[:, :], cSU[:], ld[:, sl], True, True, [Rsc[2], Rc], [Rp])
                    self.act(sc[3][:, sl], pt[:, :], AF.Exp, [Rp], [Rsc[3]])
                for nh in range(2):
                    sl = slice(nh * 512, (nh + 1) * 512)
                    pt, Rp = bank()
                    self.mm(pt[:, :], onesf[:], ld[:, sl], True, True, [Rsc[2], Rc], [Rp])
                    self.act(sc[4][:, sl], pt[:, :], AF.Exp, [Rp], [Rsc[4]])
                self.tt("pool", sc[4][:], sc[4][:], sc[1][:], ALU.mult, [Rsc[4], Rsc[1]], [Rsc[4]])
                pt, Rp = bank()
                for c in range(8):
                    self.mm(pt[:, c:c + 1], ld[:, c * 128:(c + 1) * 128], onesf[:, 0:1], True, True, [Rsc[2], Rc], [Rp])
                self.act(PCc[:], pt[:, 0:8], AF.Exp, [Rp], [R["PCc"]])
                if stop <= 3:
                    early(n, t0)
                    continue
                if self.dbg_tags:
                    P.tag = "c%d.%s" % (n, "R4")
                def prod_T(j, eng, in0, Rin0, in1, Rin1, dstT, RdstT, neg=False):
                    if neg:
                        self.stt("dve", prod[j][:], in0, -1.0, in1, ALU.mult, ALU.mult, [Rin0, Rin1], [Rprod[j]])
                    else:
                        self.tt(eng, prod[j][:], in0, in1, ALU.mult, [Rin0, Rin1], [Rprod[j]])
                    tpt, Rtp = tbank()
                    for c in range(8):
                        self.tr(tpt[:, c * 128:(c + 1) * 128], prod[j][:, c * 128:(c + 1) * 128], self.ident[:], [Rprod[j], Rc], [Rtp])
                    self.copy("act", dstT[:, :, :], tpt[:, :].rearrange("p (c t) -> p c t", c=8), [Rtp], [RdstT])
                prod_T(0, "dve", r_bf[:], R["r_bf"], sc[0][:], Rsc[0], RtT, R["RtT"])
                prod_T(1, "pool", kp_bf[:], R["kp_bf"], sc[1][:], Rsc[1], KtT, R["KtT"])
                prod_T(0, "dve", b_bf[:], R["b_bf"], sc[1][:], Rsc[1], BtT, R["BtT"])
                prod_T(1, "dve", kk_bf[:], R["kk_bf"], sc[3][:], Rsc[3], AtT, R["AtT"], neg=True)
                self.tt("pool", Khat[:], kp_bf[:], sc[4][:], ALU.mult, [R["kp_bf"], Rsc[4]], [R["Khat"]])
                self.tt("dve", Bhat[:], b_bf[:], sc[4][:], ALU.mult, [R["b_bf"], Rsc[4]], [R["Bhat"]])
                if stop <= 4:
                    early(n, t0)
                    continue
                if self.dbg_tags:
                    P.tag = "c%d.%s" % (n, "R5")
                def hv(T, h):
                    return T[(h % 2) * 64:(h % 2) * 64 + 64, h // 2, :]
                for half in range(2):
                    Mb, Nb, Xb = Mbh[half], Nbh[half], Xbh[half]
                    RMb, RNb, RXb = RMbh[half], RNbh[half], RXbh[half]
                    hb0 = half * 8
                    v4 = lambda pt_: pt_[:, :].rearrange("p (h t) -> p h t", h=4)

                    def amat(lhs_T, Rl, rhs_T, Rr, dst, dbase, maskb, Rdst):
                        (pe_, Rpe), (po_, Rpo) = bank(), bank()
                        for i in range(4):
                            he, ho = hb0 + 2 * i, hb0 + 2 * i + 1
                            self.mm(pe_[:, i * 128:(i + 1) * 128], hv(lhs_T, he), hv(rhs_T, he), True, True, [Rl, Rr], [Rpe])
                            self.mm(po_[:, i * 128:(i + 1) * 128], hv(lhs_T, ho), hv(rhs_T, ho), True, True, [Rl, Rr], [Rpo])
                        self.tt("dve", dst[:, dbase + 0:dbase + 8:2, :], v4(pe_), maskb, ALU.mult, [Rpe, Rc], [Rdst])
                        self.copy("act", dst[:, dbase + 1:dbase + 8:2, :], v4(po_), [Rpo], [Rdst])
                        self.tt("pool", dst[:, dbase + 1:dbase + 8:2, :], dst[:, dbase + 1:dbase + 8:2, :], maskb, ALU.mult, [Rdst, Rc], [Rdst])

                    amat(BtT, R["BtT"], AtT, R["AtT"], Mb[0], 0, SUb(4), RMb[0])
                    amat(AtT, R["AtT"], BtT, R["BtT"], Nb[0], 0, SLb(4), RNb[0])
                    amat(BtT, R["BtT"], RtT, R["RtT"], ArbT, hb0, IUb(4), R["ArbT"])
                    amat(KtT, R["KtT"], AtT, R["AtT"], AakT, hb0, SUb(4), R["AakT"])
                    amat(KtT, R["KtT"], RtT, R["RtT"], ArkT, hb0, IUb(4), R["ArkT"])
                    self.tt("pool", Xb[0][:], Mb[0][:], IDb(8), ALU.add, [RMb[0], Rc], [RXb[0]])
                if self.dbg_tags:
                    P.tag = "c%d.%s" % (n, "R6")
                cm, cn, cx = 0, 0, 0
                for k in range(1, 7):
                    for half in range(2):
                        Mb, Nb = Mbh[half], Nbh[half]
                        RMb, RNb = RMbh[half], RNbh[half]
                        for q4 in range(2):
                            pt, Rp = bank()
                            for i in range(4):
                                hh = q4 * 4 + i
                                self.mm(pt[:, i * 128:(i + 1) * 128], Mb[cm][:, hh, :], Nb[cn][:, hh, :], True, True, [RMb[cm], RNb[cn]], [Rp])
                            self.copy("act", Nb[1 - cn][:, q4 * 4:q4 * 4 + 4, :], pt[:, :].rearrange("p (h t) -> p h t", h=4), [Rp], [RNb[1 - cn]])
                        if k < 6:
                            for q4 in range(2):
                                pt, Rp = bank()
                                for i in range(4):
                                    hh = q4 * 4 + i
                                    self.mm(pt[:, i * 128:(i + 1) * 128], Nb[cn][:, hh, :], Mb[cm][:, hh, :], True, True, [RMb[cm], RNb[cn]], [Rp])
                                self.copy("act", Mb[1 - cm][:, q4 * 4:q4 * 4 + 4, :], pt[:, :].rearrange("p (h t) -> p h t", h=4), [Rp], [RMb[1 - cm]])
                    cn = 1 - cn
                    if k < 6:
                        cm = 1 - cm
                    for half in range(2):
                        Nb, Xb = Nbh[half], Xbh[half]
                        RNb, RXb = RNbh[half], RXbh[half]
                        for q4 in range(2):
                            pt, Rp = bank()
                            for i in range(4):
                                hh = q4 * 4 + i
                                self.mm(pt[:, i * 128:(i + 1) * 128], Nb[cn][:, hh, :], Xb[cx][:, hh, :], True, True, [RNb[cn], RXb[cx]], [Rp])
                            if k < 6:
                                dst, Rdst = Xb[1 - cx][:, q4 * 4:q4 * 4 + 4, :], RXb[1 - cx]
                            else:
                                dst, Rdst = XT[:, half * 8 + q4 * 4:half * 8 + q4 * 4 + 4, :], R["XT"]
                            self.tt("dve", dst, pt[:, :].rearrange("p (h t) -> p h t", h=4), Xb[cx][:, q4 * 4:q4 * 4 + 4, :], ALU.add,
                                    [Rp, RXb[cx]], [Rdst])
                    cx = 1 - cx
                bb_ = st16[:, 2, :].unsqueeze(2).broadcast_to([128, 16, 64])
                self.tt("dve", h3(sc[3][:, :]), h3(v_bf[:, :]), bb_, ALU.mult, [R["v_bf"], R["st16"]], [Rsc[3]])
                self.tt("pool", sc[3][:], sc[3][:], LNB, ALU.add, [Rsc[3], R["w"]], [Rsc[3]])
                self.tt("pool", sc[3][:], sc[3][:], g_bf[:], ALU.mult, [Rsc[3], R["g_bf"]], [Rsc[3]])
                self.tt("pool", g_bf[:], g_bf[:], LNW, ALU.mult, [R["g_bf"], R["w"]], [R["g_bf"]])
                if n + 1 < NCH:
                    stage_r1(n + 1)
                if stop <= 6:
                    early(n, t0)
                    continue
                if self.dbg_tags:
                    P.tag = "c%d.%s" % (n, "R7")
                sthv = lambda h: STb[(h % 2) * 64:(h % 2) * 64 + 64, h // 2, :]
                hc_ = lambda T, h: T[:, h * 64:(h + 1) * 64]
                bks = [bank(), bank()]
                for h in range(16):
                    pt, Rp = bks[h // 8]
                    o_ = pt[:, (h % 8) * 64:(h % 8) * 64 + 64]
                    self.mm(o_, hv(AtT, h), sthv(h), True, False, [R["AtT"], R["STb"]], [Rp])
                    self.mm(o_, AakT[:, h, :], hc_(v_bf, h), False, True, [R["AakT"], R["v_bf"]], [Rp])
                for nh, (pt, Rp) in enumerate(bks):
                    self.copy("act", RHS[:, nh * 512:(nh + 1) * 512], pt[:, :], [Rp], [R["RHS"]])
                bks = [bank(), bank()]
                for h in range(16):
                    pt, Rp = bks[h // 8]
                    self.mm(pt[:, (h % 8) * 64:(h % 8) * 64 + 64], XT[:, h, :], hc_(RHS, h), True, True, [R["XT"], R["RHS"]], [Rp])
                for nh, (pt, Rp) in enumerate(bks):
                    self.copy("act", U[:, nh * 512:(nh + 1) * 512], pt[:, :], [Rp], [R["U"]])
                bks = [bank(), bank()]
                for h in range(16):
                    pt, Rp = bks[h // 8]
                    o_ = pt[:, (h % 8) * 64:(h % 8) * 64 + 64]
                    self.mm(o_, hv(RtT, h), sthv(h), True, False, [R["RtT"], R["STb"]], [Rp])
                    self.mm(o_, ArbT[:, h, :], hc_(U, h), False, False, [R["ArbT"], R["U"]], [Rp])
                    self.mm(o_, ArkT[:, h, :], hc_(v_bf, h), False, True, [R["ArkT"], R["v_bf"]], [Rp])
                for nh, (pt, Rp) in enumerate(bks):
                    self.copy("act", sc[0][:, nh * 512:(nh + 1) * 512], pt[:, :], [Rp], [Rsc[0]])
                pt, Rp = bank()
                for h in range(16):
                    o_ = pt[(h % 2) * 64:(h % 2) * 64 + 64, (h // 2) * 64:(h // 2) * 64 + 64]
                    self.mm(o_, hc_(Bhat, h), hc_(U, h), True, False, [R["Bhat"], R["U"]], [Rp], tile_position=(0, (h % 2) * 64))
                    self.mm(o_, hc_(Khat, h), hc_(v_bf, h), False, True, [R["Khat"], R["v_bf"]], [Rp], tile_position=(0, (h % 2) * 64))
                pcb = PCc[:, :].unsqueeze(2).broadcast_to([128, 8, 64])
                self.tt("pool", STt[:], ST[:], pcb, ALU.mult, [R["ST"], R["PCc"]], [R["STt"]])
                self.tt("dve", ST[:], STt[:], pt[:, :].rearrange("p (c v) -> p c v", c=8), ALU.add, [R["STt"], Rp], [R["ST"]])
                self.copy("pool", STb[:], ST[:], [R["ST"]], [R["STb"]])
                if stop <= 7:
                    early(n, t0)
                    continue
                if self.dbg_tags:
                    P.tag = "c%d.%s" % (n, "R8")
                y = sc[0]
                self.P.emit("dve", lambda e_: e_.tensor_reduce(out=st16[:, 3, :], in_=h3(y[:, :]), axis=AX.X, op=ALU.add),
                            [Rsc[0]], [R["st16"]])
                self.act(sc[1][:], y[:], AF.Square, [Rsc[0]], [Rsc[1]])
                self.P.emit("dve", lambda e_: e_.tensor_reduce(out=st16[:, 4, :], in_=h3(sc[1][:, :]), axis=AX.X, op=ALU.add),
                            [Rsc[1]], [R["st16"]])
                self.ts("dve", st16[:, 3, :], st16[:, 3, :], 1.0 / 64, None, ALU.mult, None, [R["st16"]], [R["st16"]])
                self.tt("dve", st16[:, 5, :], st16[:, 3, :], st16[:, 3, :], ALU.mult, [R["st16"]], [R["st16"]])
                self.stt("dve", st16[:, 4, :], st16[:, 4, :], 1.0 / 64, st16[:, 5, :], ALU.mult, ALU.subtract, [R["st16"]], [R["st16"]])
                self.act(st16[:, 4, :], st16[:, 4, :], AF.Ln, [R["st16"], Rc], [R["st16"]], bias=self.eps_rms[:, 1:2])
                self.act(st16[:, 4, :], st16[:, 4, :], AF.Exp, [R["st16"]], [R["st16"]], scale=-0.5)
                mb_ = st16[:, 3, :].unsqueeze(2).broadcast_to([128, 16, 64])
                rb_ = st16[:, 4, :].unsqueeze(2).broadcast_to([128, 16, 64])
                self.tt("dve", h3(sc[1][:, :]), h3(y[:, :]), mb_, ALU.subtract, [Rsc[0], R["st16"]], [Rsc[1]])
                self.tt("dve", h3(sc[1][:, :]), h3(sc[1][:, :]), rb_, ALU.mult, [Rsc[1], R["st16"]], [Rsc[1]])
                self.tt("dve", sc[1][:], sc[1][:], g_bf[:], ALU.mult, [Rsc[1], R["g_bf"]], [Rsc[1]])
                self.tt("dve", yfin[:], sc[1][:], sc[3][:], ALU.add, [Rsc[1], Rsc[3]], [R["yfin"]])
                tpt, Rtp = tbank()
                for c in range(8):
                    self.tr(tpt[:, c * 128:(c + 1) * 128], yfin[:, c * 128:(c + 1) * 128], self.ident[:], [R["yfin"], Rc], [Rtp])
                self.copy("act", yT[:, :, :], tpt[:, :].rearrange("p (c t) -> p c t", c=8), [Rtp], [R["yT"]])
                self.dma("sp", ot[:], xin[t0:t0 + 128, :], d_x2, [Rxdram[n]], [R["ot"]])
                for nh, (pt, Rp) in enumerate(proj_tok(yT, R["yT"], Wo)):
                    sl = slice(nh * 512, (nh + 1) * 512)
                    self.tt("dve", ot[:, sl], pt[:, :], ot[:, sl], ALU.add, [Rp, R["ot"]], [R["ot"]])
                self.dma("sp", xout[t0:t0 + 128, :], ot[:], d_o, [R["ot"]], [Rxdram[n]])
            P.barrier()


def build_program(S, sublayers, n_cores=8):
    nc = bass.Bass("TRN2", target_bir_lowering=False)
    specs = param_specs()
    prm = {}
    x = nc.dram_tensor("x", [S, D], F32, kind="ExternalInput").ap()
    for name, shp in specs.items():
        prm[name] = nc.dram_tensor(name, list(shp), F32, kind="ExternalInput").ap()
    out = nc.dram_tensor("out", [S, D], F32, kind="ExternalOutput").ap()
    with ExitStack() as st:
        kb = KB(nc, S, st)
        kb.Rxdram = [Res() for _ in range(S // 128)]
        kb.Rvf = [Res() for _ in range(S // 128)]
        kb.vfirst = nc.dram_tensor("vfirst_scratch", [S, D], F32).ap()
        kb.setup_consts()
        cur = x
        for sl in sublayers:
            if sl[0] == "ffn":
                kb.ffn_phase(sl[1], cur, out, prm)
            elif sl[0] == "hy":
                kb.hy_phase(sl[1], sl[2], cur, out, prm)
            elif sl[0] == "rw":
                kb.rw_phase(sl[1], sl[2], cur, out, prm)
            cur = out
        kb.P.barrier()
        kb.P.finalize()
        kb.stats = (dict(kb.P.n), kb.P.nwaits)
        print("instr counts", kb.P.n, "waits", kb.P.nwaits)
    return nc


def param_specs():
    return {
        "mix_norm": (4, D), "ffn_norm": (4, D),
        "ffn_w_gate": (4, D, DFF), "ffn_w_up": (4, D, DFF), "ffn_w_down": (4, DFF, D),
        "hy_w_in": (2, D, IN_COLS), "hy_f_bias": (2, 8), "hy_q_gain": (2, 64), "hy_k_gain": (2, 64),
        "hy_pool_w": (2, 4, 128, 128), "hy_pool_scale": (2, 512), "hy_w_out": (2, D, D),
        "rw_mu": (2, 6, D), "rw_w_r": (2, D, D), "rw_w_k": (2, D, D), "rw_w_v": (2, D, D),
        "rw_w0": (2, D), "rw_w1": (2, D, 64), "rw_w2": (2, 64, D), "rw_a0": (2, D), "rw_a1": (2, D, 64),
        "rw_a2": (2, 64, D), "rw_g1": (2, D, 160), "rw_g2": (2, 160, D), "rw_k_k": (2, D), "rw_k_a": (2, D),
        "rw_r_k": (2, 16, 64), "rw_ln_w": (2, D), "rw_ln_b": (2, D), "rw_w_o": (2, D, D),
        "rw_v0": (1, D), "rw_v1": (1, D, 32), "rw_v2": (1, 32, D),
    }


FULL = [("hy", 0, 0), ("ffn", 0), ("rw", 0, 1), ("ffn", 1), ("hy", 1, 2), ("ffn", 2), ("rw", 1, 3), ("ffn", 3)]


def run(inputs, S, sublayers, n_cores=8, trace=False):
    nc = build_program(S, sublayers)
    specs = param_specs()
    x = np.ascontiguousarray(np.asarray(inputs["x"], dtype=np.float32))
    shared = {k: np.ascontiguousarray(np.asarray(inputs[k], dtype=np.float32)) for k in specs}
    in_maps = []
    for c in range(n_cores):
        m = dict(shared)
        m["x"] = x[c]
        in_maps.append(m)
    res = run_bass_kernel_spmd(nc, in_maps, core_ids=list(range(n_cores)), trace=trace)
    outs = np.stack([np.asarray(r["out"]) for r in res.results], axis=0)
    return outs, res


def kernel(**inputs):
    outs, _ = run(inputs, 4096, FULL, n_cores=8)
    return outs.astype(np.float32)
```

```python
import numpy as np
from contextlib import ExitStack
import concourse.bass as bass
import concourse.mybir as mybir
from concourse.bass_utils import run_bass_kernel_spmd

F32 = mybir.dt.float32
BF16 = mybir.dt.bfloat16
AF = mybir.ActivationFunctionType
ALU = mybir.AluOpType
AX = mybir.AxisListType

D = 1024
DFF = 2816
NFC = DFF // 128
IN_COLS = 2568
RMS_EPS = 1e-6
GN_EPS = 64e-5
ENGS = ("pe", "act", "dve", "pool", "sp")
CH = 16000


class Res:
    __slots__ = ("name", "w", "r")

    def __init__(self, name=""):
        self.name = name
        self.w = None
        self.r = {}


class DmaSem:
    __slots__ = ("key", "sem", "count")

    def __init__(self, key, sem):
        self.key = key
        self.sem = sem
        self.count = 0


class Prog:
    def __init__(self, nc, stack, same_engine_sync=True):
        self.nc = nc
        self.stack = stack
        self.q = {e: [] for e in ENGS}
        self.n = {e: 0 for e in ENGS}
        self.esem = {}
        self.seen = {e: {} for e in ENGS}
        self.same = same_engine_sync
        self.dsems = []
        self.nwaits = 0
        self.tag = None

    def dmasem(self, name):
        s = self.stack.enter_context(self.nc.semaphore("d%d_%s" % (len(self.dsems), name)))
        d = DmaSem("d%d_%s" % (len(self.dsems), name), s)
        self.dsems.append(d)
        return d

    def _esem(self, e, k):
        if (e, k) not in self.esem:
            self.esem[(e, k)] = self.stack.enter_context(self.nc.semaphore("e_%s_%d" % (e, k)))
        return self.esem[(e, k)]

    def emit(self, eng, fn, reads=(), writes=(), dma=None):
        need = {}

        def want(ev):
            if ev is None:
                return
            key, val = ev[0], ev[1]
            if key == eng and (eng == "pe" or not self.same):
                return
            if ev[2] is not None:
                val = ev[2].count
            if need.get(key, (0,))[0] < val:
                need[key] = (val, ev[2])

        for r in reads:
            want(r.w)
        for w in writes:
            want(w.w)
            for ev in w.r.values():
                want(ev)
        waits = []
        seen = self.seen[eng]
        for key, (val, hinfo) in need.items():
            if seen.get(key, 0) >= val:
                continue
            seen[key] = val
            if key in ENGS:
                k = (val - 1) // CH
                waits.append((self._esem(key, k), val - k * CH))
            else:
                waits.append((hinfo.sem, val))
        self.nwaits += len(waits)
        if fn is None:
            if waits:
                self.q[eng].append((waits, None, None, None))
            return None
        if dma is None:
            self.n[eng] += 1
            idx = self.n[eng]
            k = (idx - 1) // CH
            inc = (self._esem(eng, k), 1)
            ev = (eng, idx, None)
        else:
            dma.count += 16
            inc = (dma.sem, 16)
            ev = (dma.key, dma.count, dma)
        self.q[eng].append((waits, fn, inc, self.tag))
        for r in reads:
            r.r[ev[0]] = ev
        for w in writes:
            w.w = ev
            w.r = {}
        return ev

    def barrier(self):
        evs = [(e, self.n[e], None) for e in ENGS if self.n[e] > 0]
        evs += [(d.key, d.count, d) for d in self.dsems if d.count > 0]
        for eng in ENGS:
            tmp = Res()
            tmp.r = {ev[0]: ev for ev in evs if ev[0] != eng}
            self.emit(eng, None, writes=[tmp])

    def finalize(self):
        nc = self.nc
        with nc.Block() as block:
            def mk(ename):
                def body(e):
                    for waits, fn, inc, tag in self.q[ename]:
                        for sem, val in waits:
                            e.wait_ge(sem, val)
                        if fn is not None:
                            ins = fn(e).then_inc(inc[0], inc[1])
                            if tag is not None:
                                ins.annotate(tag)
                return body
            block.tensor(mk("pe"))
            block.scalar(mk("act"))
            block.vector(mk("dve"))
            block.gpsimd(mk("pool"))
            block.sync(mk("sp"))


class KB:
    def __init__(self, nc, S, stack):
        self.nc = nc
        self.S = S
        self.st = stack
        self.P = Prog(nc, stack)
        import os as _os
        self.dbg_tags = bool(_os.environ.get("DBG_TAGS"))

    def sb(self, name, shape, dtype, stack=None):
        self.uid = getattr(self, "uid", 0) + 1
        return (stack or self.st).enter_context(self.nc.sbuf_tensor("%s_u%d" % (name, self.uid), list(shape), dtype))

    def ps(self, name, dtype=F32, stack=None):
        n = 512 if dtype == F32 else 1024
        self.uid = getattr(self, "uid", 0) + 1
        return (stack or self.st).enter_context(self.nc.psum_tensor("%s_u%d" % (name, self.uid), [128, n], dtype))

    def mm(self, out, lhsT, rhs, start, stop, R, W, **kw):
        return self.P.emit("pe", lambda e: e.matmul(out, lhsT=lhsT, rhs=rhs, start=start, stop=stop, **kw), R, W)

    def tr(self, out, in_, ident, R, W):
        return self.P.emit("pe", lambda e: e.transpose(out, in_, ident), R, W)

    def act(self, out, in_, func, R, W, eng="act", **kw):
        return self.P.emit(eng, lambda e: e.activation(out=out, in_=in_, func=func, **kw), R, W)

    def copy(self, eng, out, in_, R, W):
        if eng == "act":
            return self.P.emit("act", lambda e: e.copy(out=out, in_=in_), R, W)
        return self.P.emit(eng, lambda e: e.tensor_copy(out=out, in_=in_), R, W)

    def tt(self, eng, out, in0, in1, op, R, W):
        return self.P.emit(eng, lambda e: e.tensor_tensor(out=out, in0=in0, in1=in1, op=op), R, W)

    def ts(self, eng, out, in0, s1, s2, op0, op1, R, W, **kw):
        if s2 is None:
            return self.P.emit(eng, lambda e: e.tensor_scalar(out=out, in0=in0, scalar1=s1, scalar2=None, op0=op0, **kw), R, W)
        return self.P.emit(eng, lambda e: e.tensor_scalar(out=out, in0=in0, scalar1=s1, scalar2=s2, op0=op0, op1=op1, **kw), R, W)

    def stt(self, eng, out, in0, scalar, in1, op0, op1, R, W):
        return self.P.emit(eng, lambda e: e.scalar_tensor_tensor(out=out, in0=in0, scalar=scalar, in1=in1, op0=op0, op1=op1), R, W)

    def memset(self, eng, ap, val, W):
        return self.P.emit(eng, lambda e: e.memset(ap, val), (), W)

    def dma(self, eng, out, in_, sem, R, W, slow=False):
        if slow:
            return self.P.emit(eng, lambda e: e.dma_start(out=out, in_=in_, allow_slow_non_contiguous=True), R, W, dma=sem)
        return self.P.emit(eng, lambda e: e.dma_start(out=out, in_=in_), R, W, dma=sem)

    def setup_consts(self):
        nc = self.nc
        self.ident = self.sb("ident", [128, 128], BF16)
        self.Rconst = Res("const")
        W = [self.Rconst]
        self.eps_rms = self.sb("eps_rms", [128, 4], F32)
        self.memset("pool", self.eps_rms[:, 0:1], RMS_EPS, W)
        self.memset("pool", self.eps_rms[:, 1:2], GN_EPS, W)
        self.memset("pool", self.eps_rms[:, 2:3], 1.0, W)
        self.memset("pool", self.eps_rms[:, 3:4], 0.0, W)
        self.memset("pool", self.ident[:], 1.0, W)
        idt = self.ident
        self.P.emit("pool", lambda e: e.affine_select(out=idt[:], in_=idt[:], pattern=[[-1, 128]],
                                                      compare_op=ALU.is_equal, fill=0.0, base=0,
                                                      channel_multiplier=1), (), W)

    def norm_tile(self, xt, Rxt, gain_b, Rgain, hT, RhT, col0, bufs, i):
        junk, Rjunk = bufs["junk"][i % 2]
        ss, Rss = bufs["ss"][i % 2]
        hb, Rhb = bufs["hb"][i % 2]
        tp, Rtp = bufs["tp"][i % len(bufs["tp"])]
        self.act(junk[:], xt[:], AF.Square, [Rxt], [Rjunk, Rss], accum_out=ss[:, 0:1])
        self.act(ss[:, 1:2], ss[:, 0:1], AF.Ln, [Rss, self.Rconst], [Rss], scale=1.0 / D, bias=self.eps_rms[:, 0:1])
        self.act(ss[:, 2:3], ss[:, 1:2], AF.Exp, [Rss], [Rss], scale=-0.5)
        self.stt("dve", hb[:], xt[:], ss[:, 2:3], gain_b[:], ALU.mult, ALU.mult, [Rxt, Rss, Rgain], [Rhb])
        for c in range(8):
            self.tr(tp[:, c * 128:(c + 1) * 128], hb[:, c * 128:(c + 1) * 128], self.ident[:], [Rhb, self.Rconst], [Rtp])
        self.copy("act", hT[:, :, col0:col0 + 128], tp[:, :].rearrange("p (c t) -> p c t", c=8), [Rtp], [RhT])

    def ffn_phase(self, layer, xin, xout, prm):
        nc, P, S = self.nc, self.P, self.S
        TG = min(1024, S)
        NG = S // TG
        NT = TG // 128
        NH = TG // 512
        with ExitStack() as st:
            gain_b = self.sb("f_gain", [128, D], F32, st)
            hT = self.sb("f_hT", [128, 8, TG], BF16, st)
            actT = self.sb("f_actT", [128, NFC, TG], BF16, st)
            wd = self.sb("f_wd", [128, NFC, D], BF16, st)
            wg = [self.sb("f_wg%d" % i, [128, 8, 256], BF16, st) for i in range(2)]
            wu = [self.sb("f_wu%d" % i, [128, 8, 256], BF16, st) for i in range(2)]
            wgs = [self.sb("f_wgs%d" % i, [128, 8, 256], F32, st) for i in range(2)]
            wus = [self.sb("f_wus%d" % i, [128, 8, 256], F32, st) for i in range(2)]
            Rwgs = [Res(), Res()]; Rwus = [Res(), Res()]
            xts = [self.sb("f_xt%d" % i, [128, D], F32, st) for i in range(3)]
            ots = [self.sb("f_ot%d" % i, [128, 512], F32, st) for i in range(2)]
            sil = [self.sb("f_sil%d" % i, [128, 512], F32, st) for i in range(2)]
            bufs = {
                "junk": [(self.sb("f_junk%d" % i, [128, D], BF16, st), Res()) for i in range(2)],
                "ss": [(self.sb("f_ss%d" % i, [128, 4], F32, st), Res()) for i in range(2)],
                "hb": [(self.sb("f_hb%d" % i, [128, D], BF16, st), Res()) for i in range(2)],
                "tp": [(self.ps("f_tp%d" % i, BF16, st), Res()) for i in range(2)],
            }
            pg = [(self.ps("f_pg%d" % i, F32, st), Res()) for i in range(2)]
            pu = [(self.ps("f_pu%d" % i, F32, st), Res()) for i in range(2)]
            po = [(self.ps("f_po%d" % i, F32, st), Res()) for i in range(2)]
            Rgain = Res(); RhT = [Res() for _ in range(NT)]; Ract = [Res() for _ in range(NFC)]
            Rwd = Res(); Rwg = [Res(), Res()]; Rwu = [Res(), Res()]
            Rxt = [Res() for _ in range(3)]; Rot = [Res(), Res()]; Rsil = [Res(), Res()]
            d_gain = P.dmasem("fgain"); d_x = [P.dmasem("fx%d" % i) for i in range(3)]
            d_wg = [P.dmasem("fwg%d" % i) for i in range(2)]; d_wu = [P.dmasem("fwu%d" % i) for i in range(2)]
            d_wd = P.dmasem("fwd"); d_o = [P.dmasem("fo%d" % i) for i in range(2)]
            Rxdram = self.Rxdram

            self.dma("sp", gain_b[:], prm["ffn_norm"][layer:layer + 1, :].partition_broadcast(128), d_gain, [], [Rgain])
            wgv = prm["ffn_w_gate"][layer].rearrange("(c p) f -> p c f", p=128)
            wuv = prm["ffn_w_up"][layer].rearrange("(c p) f -> p c f", p=128)
            wdv = prm["ffn_w_down"][layer].rearrange("(c p) n -> p c n", p=128)
            xcnt = 0
            ocnt = 0
            step = 0
            for g in range(NG):
                t0 = g * TG
                for c0 in range(0, NFC, 2):
                    self.dma("pool", wd[:, c0:c0 + 2, :], wdv[:, c0:c0 + 2, :], d_wd, [], [Rwd])
                for i in range(NT):
                    b = xcnt % 3
                    tix = (t0 // 128) + i
                    self.dma("sp", xts[b][:], xin[t0 + i * 128:t0 + (i + 1) * 128, :], d_x[b], [Rxdram[tix]], [Rxt[b]])
                    self.norm_tile(xts[b], Rxt[b], gain_b, Rgain, hT, RhT[i], i * 128, bufs, xcnt)
                    xcnt += 1
                for fg in range(NFC // 2):
                    wb = fg % 2
                    self.dma("sp", wgs[wb][:], wgv[:, :, fg * 256:(fg + 1) * 256], d_wg[wb], [], [Rwgs[wb]])
                    self.dma("sp", wus[wb][:], wuv[:, :, fg * 256:(fg + 1) * 256], d_wu[wb], [], [Rwus[wb]])
                    self.copy("act", wg[wb][:], wgs[wb][:], [Rwgs[wb]], [Rwg[wb]])
                    self.copy("dve", wu[wb][:], wus[wb][:], [Rwus[wb]], [Rwu[wb]])
                    for fc in range(2):
                        f = fg * 2 + fc
                        for th in range(NH):
                            pb = step % 2
                            pgt, Rpg = pg[pb]
                            put, Rpu = pu[pb]
                            rh = RhT[th * 4:(th + 1) * 4]
                            for c in range(8):
                                self.mm(pgt[:, :], wg[wb][:, c, fc * 128:(fc + 1) * 128], hT[:, c, th * 512:(th + 1) * 512],
                                        c == 0, c == 7, [Rwg[wb]] + rh, [Rpg])
                            for c in range(8):
                                self.mm(put[:, :], wu[wb][:, c, fc * 128:(fc + 1) * 128], hT[:, c, th * 512:(th + 1) * 512],
                                        c == 0, c == 7, [Rwu[wb]] + rh, [Rpu])
                            self.act(sil[pb][:], pgt[:, :], AF.Silu, [Rpg], [Rsil[pb]])
                            self.tt("dve", actT[:, f, th * 512:(th + 1) * 512], sil[pb][:], put[:, :], ALU.mult,
                                    [Rsil[pb], Rpu], [Ract[f]])
                            step += 1
                for i in range(NT):
                    b = xcnt % 3
                    tix = (t0 // 128) + i
                    self.dma("sp", xts[b][:], xin[t0 + i * 128:t0 + (i + 1) * 128, :], d_x[b], [Rxdram[tix]], [Rxt[b]])
                    xcnt += 1
                    for nh in range(2):
                        ob = ocnt % 2
                        pot, Rpo = po[ob]
                        for f in range(NFC):
                            self.mm(pot[:, :], actT[:, f, i * 128:(i + 1) * 128], wd[:, f, nh * 512:(nh + 1) * 512],
                                    f == 0, f == NFC - 1, [Ract[f], Rwd], [Rpo])
                        self.tt("dve", ots[ob][:], pot[:, :], xts[b][:, nh * 512:(nh + 1) * 512], ALU.add,
                                [Rpo, Rxt[b]], [Rot[ob]])
                        self.dma("sp", xout[t0 + i * 128:t0 + (i + 1) * 128, nh * 512:(nh + 1) * 512], ots[ob][:], d_o[ob],
                                 [Rot[ob]], [Rxdram[tix]])
                        ocnt += 1
            P.barrier()


    def hy_consts(self, st):
        c = {}
        W = [self.Rconst]
        sb = lambda n, shp, dt: self.sb(n, shp, dt, st)
        c["negmask"] = sb("c_negmask", [128, 128], BF16)
        c["tri"] = sb("c_tri", [128, 128], F32)
        c["nones"] = sb("c_nones", [128, 128], F32)
        c["identf"] = sb("c_identf", [128, 128], F32)
        c["bd"] = sb("c_bd", [128, 128], BF16)
        c["esel"] = sb("c_esel", [72, 8, 128], BF16)
        c["ones64"] = sb("c_ones64", [128, 64], BF16)
        c["invc"] = sb("c_invc", [128, 4, 16], F32)
        nm, tri, nones, identf, bd, esel, ones64, invc = (c[k] for k in ("negmask", "tri", "nones", "identf", "bd", "esel", "ones64", "invc"))
        self.memset("pool", nm[:], 0.0, W)
        self.P.emit("pool", lambda e: e.affine_select(out=nm[:], in_=nm[:], pattern=[[1, 128]], compare_op=ALU.is_ge,
                                                      fill=-30000.0, base=0, channel_multiplier=-1), (), W)
        self.memset("pool", tri[:], -1.0, W)
        self.P.emit("pool", lambda e: e.affine_select(out=tri[:], in_=tri[:], pattern=[[1, 128]], compare_op=ALU.is_ge,
                                                      fill=0.0, base=0, channel_multiplier=-1), (), W)
        self.memset("pool", nones[:], -1.0, W)
        self.memset("pool", identf[:], 1.0, W)
        self.P.emit("pool", lambda e: e.affine_select(out=identf[:], in_=identf[:], pattern=[[-1, 128]], compare_op=ALU.is_equal,
                                                      fill=0.0, base=0, channel_multiplier=1), (), W)
        self.memset("pool", bd[:], 0.0, W)
        self.memset("pool", bd[0:64, 0:64], 1.0, W)
        self.memset("pool", bd[64:128, 64:128], 1.0, W)
        self.memset("pool", esel[0:8], 8.0, W)
        self.P.emit("pool", lambda e: e.affine_select(out=esel[0:8], in_=esel[0:8], pattern=[[1, 8], [0, 128]], compare_op=ALU.is_equal,
                                                      fill=0.0, base=0, channel_multiplier=-1), (), W)
        d_e = self.P.dmasem("esel")
        self.dma("sp", esel[64:72], esel[0:8], d_e, [self.Rconst], [self.Rconst])
        self.memset("pool", ones64[:], 1.0, W)
        for g, w in enumerate((2, 4, 8, 16)):
            self.memset("pool", invc[:, g, :], 1.0 / w, W)
            for t in range(w - 1):
                self.memset("pool", invc[:, g, t:t + 1], 1.0 / (t + 1), W)
        return c

    def hy_phase(self, e, layer, xin, xout, prm):
        nc, P, S = self.nc, self.P, self.S
        NI = S // 512
        NB = S // 128
        Rc = self.Rconst
        with ExitStack() as st:
            cst = self.hy_consts(st)
            sb = lambda n, shp, dt: self.sb("h_" + n, shp, dt, st)
            w_in = sb("w_in", [128, 8, IN_COLS], BF16)
            w_out = sb("w_out", [128, 8, D], BF16)
            pw = sb("pw", [128, 4, 128], BF16)
            gain_b = sb("gain", [128, D], F32)
            qg = sb("qg", [128, 1], F32); kg = sb("kg", [128, 1], F32)
            pscale = sb("pscale", [128, 4], F32)
            fb = sb("fb", [128, 8], F32)
            kT = sb("kT", [128, 4, S], BF16)
            vc = sb("vc", [128, NB, 512], BF16)
            cumK = sb("cumK", [128, NB, 8], F32)
            kb = sb("kb", [128, NB, 8], F32)
            hc = sb("hc", [128, 8, 512], BF16)
            qT = sb("qT", [128, 4, 512], BF16)
            sgT = sb("sgT", [128, 4, 512], BF16)
            upad = sb("upad", [128, 4, 528], F32)
            tA = sb("tA", [128, 528], F32); tB = sb("tB", [128, 528], F32)
            pooledT = sb("pooledT", [128, 4, 512], BF16)
            xts = [sb("xt%d" % i, [128, D], F32) for i in range(2)]
            ots = [sb("ot%d" % i, [128, 512], F32) for i in range(2)]
            junk = sb("junk", [128, D], BF16)
            kf = sb("kf", [128, 512], F32); sq = sb("sq", [128, 512], BF16); rs = sb("rs", [128, 512], F32)
            pT = [sb("pT%d" % i, [128, 512], BF16) for i in range(4)]
            rden = sb("rden", [128, 512], F32); atmp = sb("atmp", [128, 512], F32)
            carry = [sb("carry%d" % i, [128, 8], F32) for i in range(2)]
            zf = sb("zf", [128, 8], F32); lf = sb("lf", [128, 8], F32)
            qctok = sb("qctok", [128, 4, 8], F32)
            qcT = sb("qcT", [72, 512], BF16)
            t16 = sb("t16", [128, 16], F32)
            bufs = {
                "junk": [(junk, Res()), (junk, Res())],
                "ss": [(sb("ss%d" % i, [128, 4], F32), Res()) for i in range(2)],
                "hb": [(sb("hb%d" % i, [128, D], BF16), Res()) for i in range(1)] * 2,
            }
            gp = [(self.ps("h_gp%d" % i, F32, st), Res()) for i in range(2)]
            bufs["tp"] = [(g_[:, :].bitcast(BF16), Rg_) for g_, Rg_ in gp]
            sbk = [(self.ps("h_s%d" % i, F32, st), Res()) for i in range(4)]
            accN, RaccN = self.ps("h_accN", F32, st), Res()
            accD, RaccD = self.ps("h_accD", F32, st), Res()
            Rw = Res(); Rgain = Res(); Rsmall = Res()
            Rhc = Res(); RqT = Res(); RsgT = Res(); RkT = [Res() for _ in range(NI)]; Rvc = [Res() for _ in range(NB)]
            RcumK = Res(); Rkb = Res(); Rupad = Res(); RtA = Res(); RtB = Res(); Rpooled = Res()
            Rxt = [Res(), Res()]; Rot = [Res(), Res()]; Rkf = Res(); Rsq = Res(); Rrs = Res()
            RpT = [Res() for _ in range(4)]; Rrden = Res(); Ratmp = Res(); Rcarry = [Res(), Res()]
            Rzf = Res(); Rlf = Res(); Rqctok = Res(); RqcT = Res(); Rt16 = Res()
            d_q = P.dmasem("hq"); d_w = P.dmasem("hw"); d_c = P.dmasem("hc"); d_x = [P.dmasem("hx%d" % i) for i in range(2)]
            d_o = [P.dmasem("ho%d" % i) for i in range(2)]
            Rxdram = self.Rxdram
            win_v = prm["hy_w_in"][e].rearrange("(c p) n -> p c n", p=128)
            wout_v = prm["hy_w_out"][e].rearrange("(c p) n -> p c n", p=128)
            for c in range(8):
                self.dma("pool", w_in[:, c, :], win_v[:, c, :], d_w, [], [Rw])
            for c in range(8):
                self.dma("pool", w_out[:, c, :], wout_v[:, c, :], d_w, [], [Rw])
            self.dma("pool", pw[:], prm["hy_pool_w"][e].rearrange("g c d -> c g d"), d_w, [], [Rw])
            self.dma("sp", gain_b[:], prm["mix_norm"][layer:layer + 1, :].partition_broadcast(128), d_c, [], [Rgain])
            qgv = prm["hy_q_gain"][e].rearrange("(d o) -> d o", o=1)
            kgv = prm["hy_k_gain"][e].rearrange("(d o) -> d o", o=1)
            for hp in range(2):
                self.dma("sp", qg[hp * 64:(hp + 1) * 64, :], qgv, d_c, [], [Rsmall])
                self.dma("sp", kg[hp * 64:(hp + 1) * 64, :], kgv, d_c, [], [Rsmall])
            self.dma("sp", pscale[:], prm["hy_pool_scale"][e].rearrange("(g p) -> p g", p=128), d_c, [], [Rsmall], slow=True)
            self.dma("sp", fb[:], prm["hy_f_bias"][e:e + 1, :].partition_broadcast(128), d_c, [], [Rsmall])
            self.memset("pool", carry[0][:], 0.0, [Rcarry[0]])
            self.memset("pool", upad[:, :, 0:16], 0.0, [Rupad])
            xcnt = 0; ocnt = 0; gcnt = 0; scnt = 0; pcnt = 0; ccnt = 0

            def proj_fm(col0, gi):
                pt, Rp = gp[gi % 2]
                for c in range(8):
                    self.mm(pt[:, :], w_in[:, c, col0:col0 + 128], hc[:, c, :], c == 0, c == 7, [Rw, Rhc], [Rp])
                return pt, Rp

            for I in range(NI):
                t0 = I * 512
                for i in range(4):
                    b = xcnt % 2
                    tix = I * 4 + i
                    self.dma("sp", xts[b][:], xin[t0 + i * 128:t0 + (i + 1) * 128, :], d_x[b], [Rxdram[tix]], [Rxt[b]])
                    self.norm_tile(xts[b], Rxt[b], gain_b, Rgain, hc, Rhc, i * 128, bufs, xcnt)
                    xcnt += 1
                for i in range(4):
                    blk = I * 4 + i
                    smp, Rsm = gp[gcnt % 2]; gcnt += 1
                    for c in range(8):
                        self.mm(smp[:, 0:8], hc[:, c, i * 128:(i + 1) * 128], w_in[:, c, 2048:2056], c == 0, c == 7, [Rw, Rhc], [Rsm])
                    self.tt("dve", zf[:], smp[:, 0:8], fb[:], ALU.add, [Rsm, Rsmall], [Rzf])
                    self.act(lf[:], zf[:], AF.Exp, [Rzf], [Rlf], scale=-1.0)
                    self.act(lf[:], lf[:], AF.Ln, [Rlf, Rc], [Rlf], bias=self.eps_rms[:, 2:3])
                    self.mm(smp[:, 8:16], cst["tri"][:], lf[:], True, True, [Rlf, Rc], [Rsm])
                    self.mm(smp[:, 16:24], cst["nones"][:], lf[:], True, True, [Rlf, Rc], [Rsm])
                    cin, cout = carry[ccnt % 2], carry[(ccnt + 1) % 2]
                    Rcin, Rcout = Rcarry[ccnt % 2], Rcarry[(ccnt + 1) % 2]
                    self.tt("dve", cumK[:, blk, :], smp[:, 8:16], cin[:], ALU.add, [Rsm, Rcin], [RcumK])
                    self.tt("dve", cout[:], smp[:, 16:24], cin[:], ALU.add, [Rsm, Rcin], [Rcout])
                    ccnt += 1
                cend, Rcend = carry[ccnt % 2], Rcarry[ccnt % 2]
                nj = 4 * I + 4
                cb4 = cend[:, :].unsqueeze(1).broadcast_to([128, 4, 8])
                self.tt("dve", qctok[:], cumK[:, 4 * I:4 * I + 4, :], cb4, ALU.subtract, [RcumK, Rcend], [Rqctok])
                cbn = cend[:, :].unsqueeze(1).broadcast_to([128, nj, 8])
                self.tt("dve", kb[:, 0:nj, :], cbn, cumK[:, 0:nj, :], ALU.subtract, [RcumK, Rcend], [Rkb])
                ptq, Rpq = gp[gcnt % 2]; gcnt += 1
                for i in range(4):
                    self.tr(ptq[0:8, i * 128:(i + 1) * 128], qctok[:, i, :], cst["identf"][:], [Rqctok, Rc], [Rpq])
                self.copy("dve", qcT[0:8, :], ptq[0:8, 0:512], [Rpq], [RqcT])
                self.dma("sp", qcT[64:72, :], qcT[0:8, :], d_q, [RqcT], [RqcT])
                for which in range(2):
                    for cc in range(4):
                        col0 = (512 if which == 0 else 0) + cc * 128
                        pt, Rp = proj_fm(col0, gcnt); gcnt += 1
                        self.copy("act", kf[:], pt[:, :], [Rp], [Rkf])
                        self.tt("pool", sq[:], kf[:], kf[:], ALU.mult, [Rkf], [Rsq])
                        pt2, Rp2 = gp[gcnt % 2]; gcnt += 1
                        self.mm(pt2[:, :], cst["bd"][:], sq[:], True, True, [Rsq, Rc], [Rp2])
                        self.act(rs[:], pt2[:, :], AF.Ln, [Rp2, Rc], [Rrs], scale=1.0 / 64, bias=self.eps_rms[:, 0:1])
                        self.act(rs[:], rs[:], AF.Exp, [Rrs], [Rrs], scale=-0.5)
                        if which == 0:
                            self.stt("dve", kT[:, cc, t0:t0 + 512], kf[:], kg[:, 0:1], rs[:], ALU.mult, ALU.mult,
                                     [Rkf, Rrs, Rsmall], [RkT[I]])
                        else:
                            self.stt("dve", qT[:, cc, :], kf[:], qg[:, 0:1], rs[:], ALU.mult, ALU.mult,
                                     [Rkf, Rrs, Rsmall], [RqT])
                for i in range(4):
                    blk = I * 4 + i
                    pt, Rp = gp[gcnt % 2]; gcnt += 1
                    for c in range(8):
                        self.mm(pt[:, :], hc[:, c, i * 128:(i + 1) * 128], w_in[:, c, 1024:1536], c == 0, c == 7, [Rw, Rhc], [Rp])
                    self.copy("act", vc[:, blk, :], pt[:, :], [Rp], [Rvc[blk]])
                for cc in range(4):
                    pt, Rp = proj_fm(1536 + cc * 128, gcnt); gcnt += 1
                    self.act(sgT[:, cc, :], pt[:, :], AF.Sigmoid, [Rp], [RsgT])
                for g in range(4):
                    pt, Rp = proj_fm(2056 + g * 128, gcnt); gcnt += 1
                    self.copy("act", upad[:, g, 16:528], pt[:, :], [Rp], [Rupad])
                for g in range(4):
                    u = upad[:, g, :]
                    cur, Rcur = u, Rupad
                    lo = 0
                    tmps = [(tA, RtA), (tB, RtB)]
                    for lvl in range(g + 1):
                        sh = 1 << lvl
                        dst, Rdst = tmps[lvl % 2]
                        nlo = lo + sh
                        self.tt("pool", dst[:, nlo:528], cur[:, nlo:528], cur[:, nlo - sh:528 - sh], ALU.add, [Rcur], [Rdst])
                        cur, Rcur, lo = dst, Rdst, nlo
                    wdt = 2 << g
                    self.stt("dve", pooledT[:, g, :], cur[:, 16:528], 1.0 / wdt, u[:, 16:528], ALU.mult, ALU.subtract,
                             [Rcur, Rupad], [Rpooled])
                    if I == 0:
                        self.tt("pool", t16[:], cur[:, 16:32], cst["invc"][:, g, :], ALU.mult, [Rcur, Rc], [Rt16])
                        self.tt("pool", pooledT[:, g, 0:16], t16[:], u[:, 16:32], ALU.subtract, [Rt16, Rupad], [Rpooled])
                self.copy("pool", upad[:, :, 0:16], upad[:, :, 512:528], [Rupad], [Rupad])
                for g in range(4):
                    pt, Rp = gp[gcnt % 2]; gcnt += 1
                    self.mm(pt[:, :], pw[:, g, :], pooledT[:, g, :], True, True, [Rw, Rpooled], [Rp])
                    self.act(hc[:, 4 + g, :], pt[:, :], AF.Copy, [Rp, Rsmall], [Rhc], scale=pscale[:, g:g + 1])
                steps = [(pr, j) for pr in range(4) for j in range(nj)]

                def qk(step, si):
                    pr, j = step
                    jj = j - 4 * I
                    c0 = 128 * jj if jj > 0 else 0
                    diag = jj >= 0
                    Ij = j // 4
                    for hp in range(2):
                        sbt, Rs = sbk[(si % 2) * 2 + hp]
                        self.mm(sbt[:, c0:512], kT[hp * 64:(hp + 1) * 64, pr, j * 128:(j + 1) * 128],
                                qT[hp * 64:(hp + 1) * 64, pr, c0:512], True, False, [RkT[Ij], RqT], [Rs])
                    for hp in range(2):
                        h = pr * 2 + hp
                        sbt, Rs = sbk[(si % 2) * 2 + hp]
                        self.mm(sbt[:, c0:512], cst["esel"][hp * 64:hp * 64 + 8, h, :], qcT[hp * 64:hp * 64 + 8, c0:512],
                                False, not diag, [RqcT, Rc], [Rs])
                    if diag:
                        for hp in range(2):
                            sbt, Rs = sbk[(si % 2) * 2 + hp]
                            self.mm(sbt[:, c0:c0 + 128], self.ident[:], cst["negmask"][:], False, True, [Rc], [Rs])
                    return c0

                def rest(step, si, c0):
                    pr, j = step
                    first = (j == 0)
                    last = (j == nj - 1)
                    pts = []
                    for hp in range(2):
                        h = pr * 2 + hp
                        sbt, Rs = sbk[(si % 2) * 2 + hp]
                        pt_, Rp_ = pT[(si % 2) * 2 + hp], RpT[(si % 2) * 2 + hp]
                        self.act(pt_[:, c0:512], sbt[:, c0:512], AF.Exp, [Rs, Rkb], [Rp_], scale=0.125, bias=kb[:, j, h:h + 1])
                        pts.append((pt_, Rp_))
                    for hp in range(2):
                        h = pr * 2 + hp
                        pt_, Rp_ = pts[hp]
                        self.mm(accN[hp * 64:(hp + 1) * 64, c0:512], vc[:, j, h * 64:(h + 1) * 64], pt_[:, c0:512],
                                first, last, [Rvc[j], Rp_], [RaccN], tile_position=(0, hp * 64))
                    for hp in range(2):
                        pt_, Rp_ = pts[hp]
                        self.mm(accD[hp * 64:(hp + 1) * 64, c0:512], cst["ones64"][:], pt_[:, c0:512],
                                first, last, [Rc, Rp_], [RaccD], tile_position=(0, hp * 64))
                    if last:
                        self.P.emit("dve", lambda e_: e_.reciprocal(out=rden[:], in_=accD[:, :]), [RaccD], [Rrden])
                        self.tt("dve", atmp[:], accN[:, :], rden[:], ALU.mult, [RaccN, Rrden], [Ratmp])
                        self.tt("pool", hc[:, pr, :], atmp[:], sgT[:, pr, :], ALU.mult, [Ratmp, RsgT], [Rhc])

                pend = []
                for n, stp in enumerate(steps):
                    c0 = qk(stp, scnt + n)
                    pend.append((stp, scnt + n, c0))
                    if len(pend) > 1:
                        rest(*pend.pop(0))
                while pend:
                    rest(*pend.pop(0))
                scnt += len(steps)
                for i in range(4):
                    b = xcnt % 2
                    tix = I * 4 + i
                    self.dma("sp", xts[b][:], xin[t0 + i * 128:t0 + (i + 1) * 128, :], d_x[b], [Rxdram[tix]], [Rxt[b]])
                    xcnt += 1
                    for nh in range(2):
                        ob = ocnt % 2
                        pt, Rp = gp[gcnt % 2]; gcnt += 1
                        for c in range(8):
                            self.mm(pt[:, :], hc[:, c, i * 128:(i + 1) * 128], w_out[:, c, nh * 512:(nh + 1) * 512],
                                    c == 0, c == 7, [Rhc, Rw], [Rp])
                        self.tt("dve", ots[ob][:], pt[:, :], xts[b][:, nh * 512:(nh + 1) * 512], ALU.add, [Rp, Rxt[b]], [Rot[ob]])
                        self.dma("sp", xout[t0 + i * 128:t0 + (i + 1) * 128, nh * 512:(nh + 1) * 512], ots[ob][:], d_o[ob],
                                 [Rot[ob]], [Rxdram[tix]])
                        ocnt += 1
            P.barrier()


    def rw_phase(self, o, layer, xin, xout, prm):
        nc, P, S = self.nc, self.P, self.S
        NCH = S // 128
        Rc = self.Rconst
        first_layer = (o == 0)
        with ExitStack() as st:
            sb = lambda n, shp, dt: self.sb("r_" + n, shp, dt, st)
            Wr = sb("Wr", [128, 8, D], BF16); Wk = sb("Wk", [128, 8, D], BF16)
            Wv = sb("Wv", [128, 8, D], BF16); Wo = sb("Wo", [128, 8, D], BF16)
            l1 = sb("l1", [128, 8, 320], BF16)
            wa2 = sb("wa2", [128, D], BF16)
            g2a = sb("g2a", [128, D], BF16)
            gv2 = sb("gv2", [64, D], BF16)
            bc = sb("bc", [128, 8, D], BF16)
            gain_b = sb("gain", [128, D], F32)
            mu = sb("mu", [128, 6, 8], F32)
            IUf = sb("IUf", [128, 128], F32); SUf = sb("SUf", [128, 128], F32); SLf = sb("SLf", [128, 128], F32)
            onesf = sb("onesf", [128, 128], F32)
            tiny = sb("tiny", [128, 1], F32)
            W = [Rc]
            self.memset("pool", IUf[:], 1.0, W)
            self.P.emit("pool", lambda e: e.affine_select(out=IUf[:], in_=IUf[:], pattern=[[1, 128]], compare_op=ALU.is_ge,
                                                          fill=0.0, base=0, channel_multiplier=-1), (), W)
            self.memset("pool", SUf[:], 1.0, W)
            self.P.emit("pool", lambda e: e.affine_select(out=SUf[:], in_=SUf[:], pattern=[[1, 128]], compare_op=ALU.is_gt,
                                                          fill=0.0, base=0, channel_multiplier=-1), (), W)
            self.memset("pool", SLf[:], 1.0, W)
            self.P.emit("pool", lambda e: e.affine_select(out=SLf[:], in_=SLf[:], pattern=[[-1, 128]], compare_op=ALU.is_gt,
                                                          fill=0.0, base=0, channel_multiplier=1), (), W)
            self.memset("pool", onesf[:], -float(np.exp(-0.5)), W)
            cIU = sb("cIU", [128, 128], F32); cSU = sb("cSU", [128, 128], F32)
            self.ts("pool", cIU[:], IUf[:], -float(np.exp(-0.5)), None, ALU.mult, None, [Rc], W)
            self.ts("pool", cSU[:], SUf[:], -float(np.exp(-0.5)), None, ALU.mult, None, [Rc], W)
            self.memset("pool", tiny[:], 1e-24, W)
            xt = sb("xt", [128, D], F32); hb = sb("hb", [128, D], BF16)
            hTe = sb("hTe", [128, 8, 129], BF16)
            xm = [sb("xm%d" % i, [128, 8, 128], BF16) for i in range(2)]
            xx = sb("xx", [128, 8, 128], BF16)
            sc = [sb("sc%d" % i, [128, D], F32) for i in range(5)]
            r_bf = sb("r_bf", [128, D], BF16); kp_bf = sb("kp_bf", [128, D], BF16); kk_bf = sb("kk_bf", [128, D], BF16)
            b_bf = sb("b_bf", [128, D], BF16); v_bf = sb("v_bf", [128, D], BF16); g_bf = sb("g_bf", [128, D], BF16)
            prod = [sb("prod%d" % i, [128, D], BF16) for i in range(2)]
            Khat = sb("Khat", [128, D], BF16); Bhat = sb("Bhat", [128, D], BF16)
            RtT = sb("RtT", [128, 8, 128], BF16); KtT = sb("KtT", [128, 8, 128], BF16)
            BtT = sb("BtT", [128, 8, 128], BF16); AtT = sb("AtT", [128, 8, 128], BF16)
            lsb = sb("lsb", [128, 512], BF16)
            Mbh = [[sb("Mb%d_%d" % (hf, i), [128, 8, 128], BF16) for i in range(2)] for hf in range(2)]
            Nbh = [[sb("Nb%d_%d" % (hf, i), [128, 8, 128], BF16) for i in range(2)] for hf in range(2)]
            Xbh = [[sb("Xb%d_%d" % (hf, i), [128, 8, 128], BF16) for i in range(2)] for hf in range(2)]
            XT = sb("XT", [128, 16, 128], BF16)
            AakT = sb("AakT", [128, 16, 128], BF16); ArbT = sb("ArbT", [128, 16, 128], BF16); ArkT = sb("ArkT", [128, 16, 128], BF16)
            ST = sb("ST", [128, 8, 64], F32); STb = sb("STb", [128, 8, 64], BF16); STt = sb("STt", [128, 8, 64], F32)
            PCc = sb("PCc", [128, 8], F32)
            st16 = sb("st16", [128, 8, 16], F32)
            yfin = Khat; yT = xm[0]
            ss = sb("ss", [128, 4], F32)
            tp = [(self.ps("r_tp%d" % i, BF16, st), Res()) for i in range(2)]
            gpool = [(self.ps("r_gp%d" % i, F32, st), Res()) for i in range(6)]
            self._gi = 0

            def bank():
                b_ = gpool[self._gi % 6]
                self._gi += 1
                return b_

            self._ti = 0

            def tbank():
                b_ = tp[self._ti % 2]
                self._ti += 1
                return b_

            R = {k: Res(k) for k in ("w", "small", "xt", "hb", "hTe", "xx", "ss", "r_bf", "kp_bf", "kk_bf", "b_bf", "v_bf", "g_bf",
                                     "Khat", "Bhat", "RtT", "KtT", "BtT", "AtT", "lsb", "XT", "AakT", "ArbT", "ArkT", "RHS", "U",
                                     "ST", "STb", "STt", "PCc", "st16", "yfin", "yT", "ot", "gain")}
            Rsc = [Res() for _ in range(5)]; Rxm = [Res(), Res()]; Rprod = [Res(), Res()]
            ot = sc[2]
            R["ot"] = Rsc[2]
            R["yfin"] = R["Khat"]
            R["yT"] = Rxm[0]
            RHS, U = prod[0], prod[1]
            R["RHS"], R["U"] = Rprod[0], Rprod[1]
            RMbh = [[Res(), Res()] for _ in range(2)]; RNbh = [[Res(), Res()] for _ in range(2)]; RXbh = [[Res(), Res()] for _ in range(2)]
            d_x2 = P.dmasem("rx2"); d_w = P.dmasem("rw"); d_c = P.dmasem("rc"); d_x = P.dmasem("rx"); d_o = P.dmasem("ro"); d_v = P.dmasem("rv")
            Rxdram = self.Rxdram
            Rvf = self.Rvf

            def wview(name):
                return prm[name][o].rearrange("(c p) n -> p c n", p=128)
            for Wt, nm in ((Wr, "rw_w_r"), (Wk, "rw_w_k"), (Wv, "rw_w_v"), (Wo, "rw_w_o")):
                v_ = wview(nm)
                for c in range(8):
                    self.dma("pool", Wt[:, c, :], v_[:, c, :], d_w, [], [R["w"]])
            self.dma("pool", l1[:, :, 0:64], wview("rw_w1"), d_w, [], [R["w"]])
            self.dma("pool", l1[:, :, 64:128], wview("rw_a1"), d_w, [], [R["w"]])
            self.dma("pool", l1[:, :, 128:288], wview("rw_g1"), d_w, [], [R["w"]])
            self.dma("pool", wa2[0:64, :], prm["rw_w2"][o], d_w, [], [R["w"]])
            self.dma("pool", wa2[64:128, :], prm["rw_a2"][o], d_w, [], [R["w"]])
            self.dma("pool", g2a[:, :], prm["rw_g2"][o][0:128, :], d_w, [], [R["w"]])
            self.dma("pool", gv2[0:32, :], prm["rw_g2"][o][128:160, :], d_w, [], [R["w"]])
            if not first_layer:
                self.dma("pool", l1[:, :, 288:320], prm["rw_v1"][o - 1].rearrange("(c p) n -> p c n", p=128), d_w, [], [R["w"]])
                self.dma("pool", gv2[32:64, :], prm["rw_v2"][o - 1], d_w, [], [R["w"]])
            rows = [prm["rw_w0"][o:o + 1, :], prm["rw_a0"][o:o + 1, :],
                    (prm["rw_v0"][o - 1:o, :] if not first_layer else prm["rw_w0"][o:o + 1, :]),
                    prm["rw_k_k"][o:o + 1, :], prm["rw_k_a"][o:o + 1, :], prm["rw_ln_w"][o:o + 1, :], prm["rw_ln_b"][o:o + 1, :],
                    prm["rw_r_k"][o:o + 1].rearrange("o h d -> o (h d)")]
            for i, rv in enumerate(rows):
                self.dma("pool", bc[:, i, :], rv.partition_broadcast(128), d_w, [], [R["w"]])
            W0, A0, V0, KK_, KA_, LNW, LNB, RK_ = (bc[:, i, :] for i in range(8))
            self.dma("sp", gain_b[:], prm["mix_norm"][layer:layer + 1, :].partition_broadcast(128), d_c, [], [R["gain"]])
            self.dma("sp", mu[:], prm["rw_mu"][o].rearrange("i (c p) -> p i c", p=128), d_c, [], [R["small"]], slow=True)
            self.memset("pool", ST[:], 0.0, [R["ST"]])
            self.memset("pool", STb[:], 0.0, [R["STb"]])
            self.memset("pool", hTe[:, :, 128:129], 0.0, [R["hTe"]])
            bufs = {"junk": [(b_bf, R["b_bf"])] * 2, "ss": [(ss, R["ss"])] * 2, "hb": [(hb, R["hb"])] * 2, "tp": tp}
            IUb = lambda n_: IUf[:, :].unsqueeze(1).broadcast_to([128, n_, 128])
            SUb = lambda n_: SUf[:, :].unsqueeze(1).broadcast_to([128, n_, 128])
            SLb = lambda n_: SLf[:, :].unsqueeze(1).broadcast_to([128, n_, 128])
            IDb = lambda n_: self.ident[:, :].unsqueeze(1).broadcast_to([128, n_, 128])
            h3 = lambda ap: ap.rearrange("p (h d) -> p h d", d=64)

            def proj_tok(xT, Rx, Wt, lo=0):
                bks = []
                for nh in range(2):
                    pt, Rp = bank()
                    for c in range(8):
                        self.mm(pt[:, :], xT[:, c, :], Wt[:, c, nh * 512:(nh + 1) * 512], c == 0, c == 7, [Rx, R["w"]], [Rp])
                    bks.append((pt, Rp))
                return bks

            def mix(i, j):
                mub = mu[:, i, :].unsqueeze(2).broadcast_to([128, 8, 128])
                self.tt("dve", xm[j][:], xx[:], mub, ALU.mult, [R["xx"], R["small"]], [Rxm[j]])
                self.tt("dve", xm[j][:], xm[j][:], hTe[:, :, 1:129], ALU.add, [Rxm[j], R["hTe"]], [Rxm[j]])
                return xm[j], Rxm[j]

            def evac2(bks, fn):
                for nh, (pt, Rp) in enumerate(bks):
                    fn(nh, pt, Rp, slice(nh * 512, (nh + 1) * 512))

            import os as _os
            stop = int(_os.environ.get("RW_STOP", "99"))

            def early(n_, t0_):
                self.dma("sp", ot[:], xin[t0_:t0_ + 128, :], d_x2, [Rxdram[n_]], [R["ot"]])
                self.dma("sp", xout[t0_:t0_ + 128, :], ot[:], d_o, [R["ot"]], [Rxdram[n_]])

            def stage_r1(n):
                t0 = n * 128
                if self.dbg_tags:
                    P.tag = "c%d.%s" % (n, "R1")
                self.copy("pool", hTe[:, :, 0:1], hTe[:, :, 128:129], [R["hTe"]], [R["hTe"]])
                self.dma("sp", xt[:], xin[t0:t0 + 128, :], d_x, [Rxdram[n]], [R["xt"]])
                self.norm_tile(xt, R["xt"], gain_b, R["gain"], hTe[:, :, 1:129], R["hTe"], 0, bufs, n)
                self.tt("pool", xx[:], hTe[:, :, 0:128], hTe[:, :, 1:129], ALU.subtract, [R["hTe"]], [R["xx"]])

            stage_r1(0)
            for n in range(NCH):
                t0 = n * 128
                if stop <= 1:
                    early(n, t0)
                    continue
                if self.dbg_tags:
                    P.tag = "c%d.%s" % (n, "R2")
                xr, Rxr = mix(0, 0)
                evac2(proj_tok(xr, Rxr, Wr), lambda nh, pt, Rp, sl: self.copy("act", r_bf[:, sl], pt[:, :], [Rp], [R["r_bf"]]))
                xk, Rxk = mix(2, 1)
                evac2(proj_tok(xk, Rxk, Wk), lambda nh, pt, Rp, sl: self.copy("act", sc[0][:, sl], pt[:, :], [Rp], [Rsc[0]]))
                xv, Rxv = mix(3, 0)
                evac2(proj_tok(xv, Rxv, Wv), lambda nh, pt, Rp, sl: self.copy("act", sc[1][:, sl], pt[:, :], [Rp], [Rsc[1]]))
                lp, Rlp = bank()
                if not first_layer:
                    for c in range(8):
                        self.mm(lp[32:64, 256:384], l1[:, c, 288:320], xv[:, c, :], c == 0, c == 7, [Rxv, R["w"]], [Rlp],
                                tile_position=(0, 32))
                xw, Rxw = mix(1, 1)
                for c in range(8):
                    self.mm(lp[0:64, 0:128], l1[:, c, 0:64], xw[:, c, :], c == 0, c == 7, [Rxw, R["w"]], [Rlp])
                xa, Rxa = mix(4, 0)
                for c in range(8):
                    self.mm(lp[64:128, 0:128], l1[:, c, 64:128], xa[:, c, :], c == 0, c == 7, [Rxa, R["w"]], [Rlp],
                            tile_position=(0, 64))
                xg, Rxg = mix(5, 1)
                for c in range(8):
                    self.mm(lp[:, 128:256], l1[:, c, 128:256], xg[:, c, :], c == 0, c == 7, [Rxg, R["w"]], [Rlp])
                for c in range(8):
                    self.mm(lp[0:32, 256:384], l1[:, c, 256:288], xg[:, c, :], c == 0, c == 7, [Rxg, R["w"]], [Rlp])
                self.act(lsb[0:64, 0:128], lp[0:64, 0:128], AF.Tanh, [Rlp], [R["lsb"]])
                self.copy("act", lsb[64:128, 0:128], lp[64:128, 0:128], [Rlp], [R["lsb"]])
                self.act(lsb[:, 128:256], lp[:, 128:256], AF.Sigmoid, [Rlp], [R["lsb"]])
                self.act(lsb[0:32, 256:384], lp[0:32, 256:384], AF.Sigmoid, [Rlp], [R["lsb"]])
                if not first_layer:
                    self.copy("act", lsb[32:64, 256:384], lp[32:64, 256:384], [Rlp], [R["lsb"]])
                for nh in range(2):
                    sl = slice(nh * 512, (nh + 1) * 512)
                    pt, Rp = bank()
                    self.mm(pt[:, :], lsb[64:128, 0:128], wa2[64:128, sl], True, True, [R["lsb"], R["w"]], [Rp])
                    self.tt("dve", sc[3][:, sl], pt[:, :], A0[:, sl], ALU.add, [Rp, R["w"]], [Rsc[3]])
                self.act(sc[3][:], sc[3][:], AF.Sigmoid, [Rsc[3]], [Rsc[3]])
                for nh in range(2):
                    sl = slice(nh * 512, (nh + 1) * 512)
                    pt, Rp = bank()
                    self.mm(pt[:, :], lsb[:, 128:256], g2a[:, sl], True, False, [R["lsb"], R["w"]], [Rp])
                    self.mm(pt[:, :], lsb[0:32, 256:384], gv2[0:32, sl], False, True, [R["lsb"], R["w"]], [Rp])
                    self.copy("act", g_bf[:, sl], pt[:, :], [Rp], [R["g_bf"]])
                if first_layer:
                    self.dma("sp", self.vfirst[t0:t0 + 128, :], sc[1][:], d_v, [Rsc[1]], [Rvf[n]])
                else:
                    for nh in range(2):
                        sl = slice(nh * 512, (nh + 1) * 512)
                        pt, Rp = bank()
                        self.mm(pt[:, :], lsb[32:64, 256:384], gv2[32:64, sl], True, True, [R["lsb"], R["w"]], [Rp])
                        self.tt("dve", sc[4][:, sl], pt[:, :], V0[:, sl], ALU.add, [Rp, R["w"]], [Rsc[4]])
                    self.act(sc[4][:], sc[4][:], AF.Sigmoid, [Rsc[4]], [Rsc[4]])
                    self.dma("sp", ot[:], self.vfirst[t0:t0 + 128, :], d_v, [Rvf[n]], [R["ot"]])
                    self.tt("pool", ot[:], ot[:], sc[1][:], ALU.subtract, [R["ot"], Rsc[1]], [R["ot"]])
                    self.tt("pool", ot[:], ot[:], sc[4][:], ALU.mult, [R["ot"], Rsc[4]], [R["ot"]])
                    self.tt("pool", sc[1][:], sc[1][:], ot[:], ALU.add, [R["ot"], Rsc[1]], [Rsc[1]])
                self.copy("act", v_bf[:], sc[1][:], [Rsc[1]], [R["v_bf"]])
                for nh in range(2):
                    sl = slice(nh * 512, (nh + 1) * 512)
                    pt, Rp = bank()
                    self.mm(pt[:, :], lsb[0:64, 0:128], wa2[0:64, sl], True, True, [R["lsb"], R["w"]], [Rp])
                    self.tt("dve", sc[2][:, sl], pt[:, :], W0[:, sl], ALU.add, [Rp, R["w"]], [Rsc[2]])
                self.act(sc[2][:], sc[2][:], AF.Sigmoid, [Rsc[2]], [Rsc[2]])
                k32, a32 = sc[0], sc[3]
                self.tt("dve", sc[4][:], k32[:], KK_, ALU.mult, [Rsc[0], R["w"]], [Rsc[4]])
                self.act(sc[1][:], sc[4][:], AF.Square, [Rsc[4], R["v_bf"]], [Rsc[1]])
                self.P.emit("dve", lambda e_: e_.tensor_reduce(out=st16[:, 0, :], in_=h3(sc[1][:, :]), axis=AX.X, op=ALU.add),
                            [Rsc[1]], [R["st16"]])
                self.act(st16[:, 1, :], st16[:, 0, :], AF.Ln, [R["st16"], Rc], [R["st16"]], bias=tiny[:, 0:1])
                self.act(st16[:, 1, :], st16[:, 1, :], AF.Exp, [R["st16"]], [R["st16"]], scale=-0.5)
                rnb = st16[:, 1, :].unsqueeze(2).broadcast_to([128, 16, 64])
                self.tt("dve", h3(kk_bf[:, :]), h3(sc[4][:, :]), rnb, ALU.mult, [Rsc[4], R["st16"]], [R["kk_bf"]])
                self.stt("dve", sc[1][:], a32[:], -1.0, KA_, ALU.add, ALU.mult, [Rsc[3], R["w"]], [Rsc[1]])
                self.stt("dve", kp_bf[:], sc[1][:], 1.0, k32[:], ALU.add, ALU.mult, [Rsc[1], Rsc[0]], [R["kp_bf"]])
                self.tt("pool", b_bf[:], kk_bf[:], a32[:], ALU.mult, [R["kk_bf"], Rsc[3]], [R["b_bf"]])
                self.tt("pool", sc[4][:], r_bf[:], kp_bf[:], ALU.mult, [R["r_bf"], R["kp_bf"]], [Rsc[4]])
                self.tt("pool", sc[4][:], sc[4][:], RK_, ALU.mult, [Rsc[4], R["w"]], [Rsc[4]])
                self.P.emit("dve", lambda e_: e_.tensor_reduce(out=st16[:, 2, :], in_=h3(sc[4][:, :]), axis=AX.X, op=ALU.add),
                            [Rsc[4]], [R["st16"]])
                if stop <= 2:
                    early(n, t0)
                    continue
                if self.dbg_tags:
                    P.tag = "c%d.%s" % (n, "R3")
                ld = sc[2]
                for nh in range(2):
                    sl = slice(nh * 512, (nh + 1) * 512)
                    pt, Rp = bank()
                    self.mm(pt[:, :], cIU[:], ld[:, sl], True, True, [Rsc[2], Rc], [Rp])
                    self.act(sc[0][:, sl], pt[:, :], AF.Exp, [Rp], [Rsc[0]])
                    self.act(sc[1][:, sl], pt[:, :], AF.Exp, [Rp], [Rsc[1]], scale=-1.0)
                for nh in range(2):
                    sl = slice(nh * 512, (nh + 1) * 512)
                    pt, Rp = bank()
                    self.mm(pt[:, :], cSU[:], ld[:, sl], True, True, [Rsc[2], Rc], [Rp])
                    self.act(sc[3][:, sl], pt[:, :], AF.Exp, [Rp], [Rsc[3]])
                for nh in range(2):
                    sl = slice(nh * 512, (nh + 1) * 512)
                    pt, Rp = bank()
                    self.mm(pt[:, :], onesf[:], ld[:, sl], True, True, [Rsc[2], Rc], [Rp])
                    self.act(sc[4][:, sl], pt[:, :], AF.Exp, [Rp], [Rsc[4]])
                self.tt("pool", sc[4][:], sc[4][:], sc[1][:], ALU.mult, [Rsc[4], Rsc[1]], [Rsc[4]])
                pt, Rp = bank()
                for c in range(8):
                    self.mm(pt[:, c:c + 1], ld[:, c * 128:(c + 1) * 128], onesf[:, 0:1], True, True, [Rsc[2], Rc], [Rp])
                self.act(PCc[:], pt[:, 0:8], AF.Exp, [Rp], [R["PCc"]])
                if stop <= 3:
                    early(n, t0)
                    continue
                if self.dbg_tags:
                    P.tag = "c%d.%s" % (n, "R4")
                def prod_T(j, eng, in0, Rin0, in1, Rin1, dstT, RdstT, neg=False):
                    if neg:
                        self.stt("dve", prod[j][:], in0, -1.0, in1, ALU.mult, ALU.mult, [Rin0, Rin1], [Rprod[j]])
                    else:
                        self.tt(eng, prod[j][:], in0, in1, ALU.mult, [Rin0, Rin1], [Rprod[j]])
                    tpt, Rtp = tbank()
                    for c in range(8):
                        self.tr(tpt[:, c * 128:(c + 1) * 128], prod[j][:, c * 128:(c + 1) * 128], self.ident[:], [Rprod[j], Rc], [Rtp])
                    self.copy("act", dstT[:, :, :], tpt[:, :].rearrange("p (c t) -> p c t", c=8), [Rtp], [RdstT])
                prod_T(0, "dve", r_bf[:], R["r_bf"], sc[0][:], Rsc[0], RtT, R["RtT"])
                prod_T(1, "pool", kp_bf[:], R["kp_bf"], sc[1][:], Rsc[1], KtT, R["KtT"])
                prod_T(0, "dve", b_bf[:], R["b_bf"], sc[1][:], Rsc[1], BtT, R["BtT"])
                prod_T(1, "dve", kk_bf[:], R["kk_bf"], sc[3][:], Rsc[3], AtT, R["AtT"], neg=True)
                self.tt("pool", Khat[:], kp_bf[:], sc[4][:], ALU.mult, [R["kp_bf"], Rsc[4]], [R["Khat"]])
                self.tt("dve", Bhat[:], b_bf[:], sc[4][:], ALU.mult, [R["b_bf"], Rsc[4]], [R["Bhat"]])
                if stop <= 4:
                    early(n, t0)
                    continue
                if self.dbg_tags:
                    P.tag = "c%d.%s" % (n, "R5")
                def hv(T, h):
                    return T[(h % 2) * 64:(h % 2) * 64 + 64, h // 2, :]
                for half in range(2):
                    Mb, Nb, Xb = Mbh[half], Nbh[half], Xbh[half]
                    RMb, RNb, RXb = RMbh[half], RNbh[half], RXbh[half]
                    hb0 = half * 8
                    v4 = lambda pt_: pt_[:, :].rearrange("p (h t) -> p h t", h=4)

                    def amat(lhs_T, Rl, rhs_T, Rr, dst, dbase, maskb, Rdst, fast=False):
                        (pe_, Rpe), (po_, Rpo) = bank(), bank()
                        for i in range(4):
                            he, ho = hb0 + 2 * i, hb0 + 2 * i + 1
                            self.mm(pe_[:, i * 128:(i + 1) * 128], hv(lhs_T, he), hv(rhs_T, he), True, True, [Rl, Rr], [Rpe])
                            self.mm(po_[:, i * 128:(i + 1) * 128], hv(lhs_T, ho), hv(rhs_T, ho), True, True, [Rl, Rr], [Rpo])
                        self.tt("dve", dst[:, dbase + 0:dbase + 8:2, :], v4(pe_), maskb, ALU.mult, [Rpe, Rc], [Rdst])
                        if fast:
                            self.tt("dve", dst[:, dbase + 1:dbase + 8:2, :], v4(po_), maskb, ALU.mult, [Rpo, Rc], [Rdst])
                        else:
                            self.copy("act", dst[:, dbase + 1:dbase + 8:2, :], v4(po_), [Rpo], [Rdst])
                            self.tt("pool", dst[:, dbase + 1:dbase + 8:2, :], dst[:, dbase + 1:dbase + 8:2, :], maskb, ALU.mult, [Rdst, Rc], [Rdst])

                    amat(BtT, R["BtT"], AtT, R["AtT"], Mb[0], 0, SUb(4), RMb[0], fast=True)
                    amat(AtT, R["AtT"], BtT, R["BtT"], Nb[0], 0, SLb(4), RNb[0], fast=True)
                    amat(BtT, R["BtT"], RtT, R["RtT"], ArbT, hb0, IUb(4), R["ArbT"])
                    amat(KtT, R["KtT"], AtT, R["AtT"], AakT, hb0, SUb(4), R["AakT"])
                    amat(KtT, R["KtT"], RtT, R["RtT"], ArkT, hb0, IUb(4), R["ArkT"])
                    self.tt("pool", Xb[0][:], Mb[0][:], IDb(8), ALU.add, [RMb[0], Rc], [RXb[0]])
                if self.dbg_tags:
                    P.tag = "c%d.%s" % (n, "R6")
                cm, cn, cx = 0, 0, 0
                for k in range(1, 7):
                    for half in range(2):
                        Mb, Nb = Mbh[half], Nbh[half]
                        RMb, RNb = RMbh[half], RNbh[half]
                        for q4 in range(2):
                            pt, Rp = bank()
                            for i in range(4):
                                hh = q4 * 4 + i
                                self.mm(pt[:, i * 128:(i + 1) * 128], Mb[cm][:, hh, :], Nb[cn][:, hh, :], True, True, [RMb[cm], RNb[cn]], [Rp])
                            self.copy("act", Nb[1 - cn][:, q4 * 4:q4 * 4 + 4, :], pt[:, :].rearrange("p (h t) -> p h t", h=4), [Rp], [RNb[1 - cn]])
                        if k < 6:
                            for q4 in range(2):
                                pt, Rp = bank()
                                for i in range(4):
                                    hh = q4 * 4 + i
                                    self.mm(pt[:, i * 128:(i + 1) * 128], Nb[cn][:, hh, :], Mb[cm][:, hh, :], True, True, [RMb[cm], RNb[cn]], [Rp])
                                self.copy("act", Mb[1 - cm][:, q4 * 4:q4 * 4 + 4, :], pt[:, :].rearrange("p (h t) -> p h t", h=4), [Rp], [RMb[1 - cm]])
                    cn = 1 - cn
                    if k < 6:
                        cm = 1 - cm
                    for half in range(2):
                        Nb, Xb = Nbh[half], Xbh[half]
                        RNb, RXb = RNbh[half], RXbh[half]
                        for q4 in range(2):
                            pt, Rp = bank()
                            for i in range(4):
                                hh = q4 * 4 + i
                                self.mm(pt[:, i * 128:(i + 1) * 128], Nb[cn][:, hh, :], Xb[cx][:, hh, :], True, True, [RNb[cn], RXb[cx]], [Rp])
                            if k < 6:
                                dst, Rdst = Xb[1 - cx][:, q4 * 4:q4 * 4 + 4, :], RXb[1 - cx]
                            else:
                                dst, Rdst = XT[:, half * 8 + q4 * 4:half * 8 + q4 * 4 + 4, :], R["XT"]
                            self.tt("dve", dst, pt[:, :].rearrange("p (h t) -> p h t", h=4), Xb[cx][:, q4 * 4:q4 * 4 + 4, :], ALU.add,
                                    [Rp, RXb[cx]], [Rdst])
                    cx = 1 - cx
                bb_ = st16[:, 2, :].unsqueeze(2).broadcast_to([128, 16, 64])
                self.tt("dve", h3(sc[3][:, :]), h3(v_bf[:, :]), bb_, ALU.mult, [R["v_bf"], R["st16"]], [Rsc[3]])
                self.tt("pool", sc[3][:], sc[3][:], LNB, ALU.add, [Rsc[3], R["w"]], [Rsc[3]])
                self.tt("pool", sc[3][:], sc[3][:], g_bf[:], ALU.mult, [Rsc[3], R["g_bf"]], [Rsc[3]])
                self.tt("pool", g_bf[:], g_bf[:], LNW, ALU.mult, [R["g_bf"], R["w"]], [R["g_bf"]])
                if n + 1 < NCH:
                    stage_r1(n + 1)
                if stop <= 6:
                    early(n, t0)
                    continue
                if self.dbg_tags:
                    P.tag = "c%d.%s" % (n, "R7")
                sthv = lambda h: STb[(h % 2) * 64:(h % 2) * 64 + 64, h // 2, :]
                hc_ = lambda T, h: T[:, h * 64:(h + 1) * 64]
                bks = [bank(), bank()]
                for h in range(16):
                    pt, Rp = bks[h // 8]
                    o_ = pt[:, (h % 8) * 64:(h % 8) * 64 + 64]
                    self.mm(o_, hv(AtT, h), sthv(h), True, False, [R["AtT"], R["STb"]], [Rp])
                    self.mm(o_, AakT[:, h, :], hc_(v_bf, h), False, True, [R["AakT"], R["v_bf"]], [Rp])
                for nh, (pt, Rp) in enumerate(bks):
                    self.copy("act", RHS[:, nh * 512:(nh + 1) * 512], pt[:, :], [Rp], [R["RHS"]])
                bks = [bank(), bank()]
                for h in range(16):
                    pt, Rp = bks[h // 8]
                    self.mm(pt[:, (h % 8) * 64:(h % 8) * 64 + 64], XT[:, h, :], hc_(RHS, h), True, True, [R["XT"], R["RHS"]], [Rp])
                for nh, (pt, Rp) in enumerate(bks):
                    self.copy("act", U[:, nh * 512:(nh + 1) * 512], pt[:, :], [Rp], [R["U"]])
                bks = [bank(), bank()]
                for h in range(16):
                    pt, Rp = bks[h // 8]
                    o_ = pt[:, (h % 8) * 64:(h % 8) * 64 + 64]
                    self.mm(o_, hv(RtT, h), sthv(h), True, False, [R["RtT"], R["STb"]], [Rp])
                    self.mm(o_, ArbT[:, h, :], hc_(U, h), False, False, [R["ArbT"], R["U"]], [Rp])
                    self.mm(o_, ArkT[:, h, :], hc_(v_bf, h), False, True, [R["ArkT"], R["v_bf"]], [Rp])
                for nh, (pt, Rp) in enumerate(bks):
                    self.copy("act", sc[0][:, nh * 512:(nh + 1) * 512], pt[:, :], [Rp], [Rsc[0]])
                pt, Rp = bank()
                for h in range(16):
                    o_ = pt[(h % 2) * 64:(h % 2) * 64 + 64, (h // 2) * 64:(h // 2) * 64 + 64]
                    self.mm(o_, hc_(Bhat, h), hc_(U, h), True, False, [R["Bhat"], R["U"]], [Rp], tile_position=(0, (h % 2) * 64))
                    self.mm(o_, hc_(Khat, h), hc_(v_bf, h), False, True, [R["Khat"], R["v_bf"]], [Rp], tile_position=(0, (h % 2) * 64))
                pcb = PCc[:, :].unsqueeze(2).broadcast_to([128, 8, 64])
                self.tt("pool", STt[:], ST[:], pcb, ALU.mult, [R["ST"], R["PCc"]], [R["STt"]])
                self.tt("dve", ST[:], STt[:], pt[:, :].rearrange("p (c v) -> p c v", c=8), ALU.add, [R["STt"], Rp], [R["ST"]])
                self.copy("pool", STb[:], ST[:], [R["ST"]], [R["STb"]])
                if stop <= 7:
                    early(n, t0)
                    continue
                if self.dbg_tags:
                    P.tag = "c%d.%s" % (n, "R8")
                y = sc[0]
                self.P.emit("dve", lambda e_: e_.tensor_reduce(out=st16[:, 3, :], in_=h3(y[:, :]), axis=AX.X, op=ALU.add),
                            [Rsc[0]], [R["st16"]])
                self.act(sc[1][:], y[:], AF.Square, [Rsc[0]], [Rsc[1]])
                self.P.emit("dve", lambda e_: e_.tensor_reduce(out=st16[:, 4, :], in_=h3(sc[1][:, :]), axis=AX.X, op=ALU.add),
                            [Rsc[1]], [R["st16"]])
                self.ts("dve", st16[:, 3, :], st16[:, 3, :], 1.0 / 64, None, ALU.mult, None, [R["st16"]], [R["st16"]])
                self.tt("dve", st16[:, 5, :], st16[:, 3, :], st16[:, 3, :], ALU.mult, [R["st16"]], [R["st16"]])
                self.stt("dve", st16[:, 4, :], st16[:, 4, :], 1.0 / 64, st16[:, 5, :], ALU.mult, ALU.subtract, [R["st16"]], [R["st16"]])
                self.act(st16[:, 4, :], st16[:, 4, :], AF.Ln, [R["st16"], Rc], [R["st16"]], bias=self.eps_rms[:, 1:2])
                self.act(st16[:, 4, :], st16[:, 4, :], AF.Exp, [R["st16"]], [R["st16"]], scale=-0.5)
                mb_ = st16[:, 3, :].unsqueeze(2).broadcast_to([128, 16, 64])
                rb_ = st16[:, 4, :].unsqueeze(2).broadcast_to([128, 16, 64])
                self.tt("dve", h3(sc[1][:, :]), h3(y[:, :]), mb_, ALU.subtract, [Rsc[0], R["st16"]], [Rsc[1]])
                self.tt("dve", h3(sc[1][:, :]), h3(sc[1][:, :]), rb_, ALU.mult, [Rsc[1], R["st16"]], [Rsc[1]])
                self.tt("dve", sc[1][:], sc[1][:], g_bf[:], ALU.mult, [Rsc[1], R["g_bf"]], [Rsc[1]])
                self.tt("dve", yfin[:], sc[1][:], sc[3][:], ALU.add, [Rsc[1], Rsc[3]], [R["yfin"]])
                tpt, Rtp = tbank()
                for c in range(8):
                    self.tr(tpt[:, c * 128:(c + 1) * 128], yfin[:, c * 128:(c + 1) * 128], self.ident[:], [R["yfin"], Rc], [Rtp])
                self.copy("act", yT[:, :, :], tpt[:, :].rearrange("p (c t) -> p c t", c=8), [Rtp], [R["yT"]])
                self.dma("sp", ot[:], xin[t0:t0 + 128, :], d_x2, [Rxdram[n]], [R["ot"]])
                for nh, (pt, Rp) in enumerate(proj_tok(yT, R["yT"], Wo)):
                    sl = slice(nh * 512, (nh + 1) * 512)
                    self.tt("dve", ot[:, sl], pt[:, :], ot[:, sl], ALU.add, [Rp, R["ot"]], [R["ot"]])
                self.dma("sp", xout[t0:t0 + 128, :], ot[:], d_o, [R["ot"]], [Rxdram[n]])
            P.barrier()


def build_program(S, sublayers, n_cores=8):
    nc = bass.Bass("TRN2", target_bir_lowering=False)
    specs = param_specs()
    prm = {}
    x = nc.dram_tensor("x", [S, D], F32, kind="ExternalInput").ap()
    for name, shp in specs.items():
        prm[name] = nc.dram_tensor(name, list(shp), F32, kind="ExternalInput").ap()
    out = nc.dram_tensor("out", [S, D], F32, kind="ExternalOutput").ap()
    with ExitStack() as st:
        kb = KB(nc, S, st)
        kb.Rxdram = [Res() for _ in range(S // 128)]
        kb.Rvf = [Res() for _ in range(S // 128)]
        kb.vfirst = nc.dram_tensor("vfirst_scratch", [S, D], F32).ap()
        kb.setup_consts()
        cur = x
        for sl in sublayers:
            if sl[0] == "ffn":
                kb.ffn_phase(sl[1], cur, out, prm)
            elif sl[0] == "hy":
                kb.hy_phase(sl[1], sl[2], cur, out, prm)
            elif sl[0] == "rw":
                kb.rw_phase(sl[1], sl[2], cur, out, prm)
            cur = out
        kb.P.barrier()
        kb.P.finalize()
        kb.stats = (dict(kb.P.n), kb.P.nwaits)
        print("instr counts", kb.P.n, "waits", kb.P.nwaits)
    return nc


def param_specs():
    return {
        "mix_norm": (4, D), "ffn_norm": (4, D),
        "ffn_w_gate": (4, D, DFF), "ffn_w_up": (4, D, DFF), "ffn_w_down": (4, DFF, D),
        "hy_w_in": (2, D, IN_COLS), "hy_f_bias": (2, 8), "hy_q_gain": (2, 64), "hy_k_gain": (2, 64),
        "hy_pool_w": (2, 4, 128, 128), "hy_pool_scale": (2, 512), "hy_w_out": (2, D, D),
        "rw_mu": (2, 6, D), "rw_w_r": (2, D, D), "rw_w_k": (2, D, D), "rw_w_v": (2, D, D),
        "rw_w0": (2, D), "rw_w1": (2, D, 64), "rw_w2": (2, 64, D), "rw_a0": (2, D), "rw_a1": (2, D, 64),
        "rw_a2": (2, 64, D), "rw_g1": (2, D, 160), "rw_g2": (2, 160, D), "rw_k_k": (2, D), "rw_k_a": (2, D),
        "rw_r_k": (2, 16, 64), "rw_ln_w": (2, D), "rw_ln_b": (2, D), "rw_w_o": (2, D, D),
        "rw_v0": (1, D), "rw_v1": (1, D, 32), "rw_v2": (1, 32, D),
    }


FULL = [("hy", 0, 0), ("ffn", 0), ("rw", 0, 1), ("ffn", 1), ("hy", 1, 2), ("ffn", 2), ("rw", 1, 3), ("ffn", 3)]


def run(inputs, S, sublayers, n_cores=8, trace=False):
    nc = build_program(S, sublayers)
    specs = param_specs()
    x = np.ascontiguousarray(np.asarray(inputs["x"], dtype=np.float32))
    shared = {k: np.ascontiguousarray(np.asarray(inputs[k], dtype=np.float32)) for k in specs}
    in_maps = []
    for c in range(n_cores):
        m = dict(shared)
        m["x"] = x[c]
        in_maps.append(m)
    res = run_bass_kernel_spmd(nc, in_maps, core_ids=list(range(n_cores)), trace=trace)
    outs = np.stack([np.asarray(r["out"]) for r in res.results], axis=0)
    return outs, res


def kernel(**inputs):
    outs, _ = run(inputs, 4096, FULL, n_cores=8)
    return outs.astype(np.float32)
```

```python
import numpy as np
from contextlib import ExitStack
import concourse.bass as bass
import concourse.mybir as mybir
from concourse.bass_utils import run_bass_kernel_spmd

F32 = mybir.dt.float32
BF16 = mybir.dt.bfloat16
AF = mybir.ActivationFunctionType
ALU = mybir.AluOpType
AX = mybir.AxisListType

D = 1024
DFF = 2816
NFC = DFF // 128
IN_COLS = 2568
RMS_EPS = 1e-6
GN_EPS = 64e-5
ENGS = ("pe", "act", "dve", "pool", "sp")
CH = 16000


class Res:
    __slots__ = ("name", "w", "r")

    def __init__(self, name=""):
        self.name = name
        self.w = None
        self.r = {}


class DmaSem:
    __slots__ = ("key", "sem", "count")

    def __init__(self, key, sem):
        self.key = key
        self.sem = sem
        self.count = 0


class Prog:
    def __init__(self, nc, stack, same_engine_sync=True):
        self.nc = nc
        self.stack = stack
        self.q = {e: [] for e in ENGS}
        self.n = {e: 0 for e in ENGS}
        self.esem = {}
        self.seen = {e: {} for e in ENGS}
        self.same = same_engine_sync
        self.dsems = []
        self.nwaits = 0
        self.tag = None

    def dmasem(self, name):
        s = self.stack.enter_context(self.nc.semaphore("d%d_%s" % (len(self.dsems), name)))
        d = DmaSem("d%d_%s" % (len(self.dsems), name), s)
        self.dsems.append(d)
        return d

    def _esem(self, e, k):
        if (e, k) not in self.esem:
            self.esem[(e, k)] = self.stack.enter_context(self.nc.semaphore("e_%s_%d" % (e, k)))
        return self.esem[(e, k)]

    def emit(self, eng, fn, reads=(), writes=(), dma=None):
        need = {}

        def want(ev):
            if ev is None:
                return
            key, val = ev[0], ev[1]
            if key == eng and (eng == "pe" or not self.same):
                return
            if ev[2] is not None:
                val = ev[2].count
            if need.get(key, (0,))[0] < val:
                need[key] = (val, ev[2])

        for r in reads:
            want(r.w)
        for w in writes:
            want(w.w)
            for ev in w.r.values():
                want(ev)
        waits = []
        seen = self.seen[eng]
        for key, (val, hinfo) in need.items():
            if seen.get(key, 0) >= val:
                continue
            seen[key] = val
            if key in ENGS:
                k = (val - 1) // CH
                waits.append((self._esem(key, k), val - k * CH))
            else:
                waits.append((hinfo.sem, val))
        self.nwaits += len(waits)
        if fn is None:
            if waits:
                self.q[eng].append((waits, None, None, None))
            return None
        if dma is None:
            self.n[eng] += 1
            idx = self.n[eng]
            k = (idx - 1) // CH
            inc = (self._esem(eng, k), 1)
            ev = (eng, idx, None)
        else:
            dma.count += 16
            inc = (dma.sem, 16)
            ev = (dma.key, dma.count, dma)
        self.q[eng].append((waits, fn, inc, self.tag))
        for r in reads:
            r.r[ev[0]] = ev
        for w in writes:
            w.w = ev
            w.r = {}
        return ev

    def barrier(self):
        evs = [(e, self.n[e], None) for e in ENGS if self.n[e] > 0]
        evs += [(d.key, d.count, d) for d in self.dsems if d.count > 0]
        for eng in ENGS:
            tmp = Res()
            tmp.r = {ev[0]: ev for ev in evs if ev[0] != eng}
            self.emit(eng, None, writes=[tmp])

    def finalize(self):
        nc = self.nc
        with nc.Block() as block:
            def mk(ename):
                def body(e):
                    for waits, fn, inc, tag in self.q[ename]:
                        for sem, val in waits:
                            e.wait_ge(sem, val)
                        if fn is not None:
                            ins = fn(e).then_inc(inc[0], inc[1])
                            if tag is not None:
                                ins.annotate(tag)
                return body
            block.tensor(mk("pe"))
            block.scalar(mk("act"))
            block.vector(mk("dve"))
            block.gpsimd(mk("pool"))
            block.sync(mk("sp"))


class KB:
    def __init__(self, nc, S, stack):
        self.nc = nc
        self.S = S
        self.st = stack
        self.P = Prog(nc, stack)
        import os as _os
        self.dbg_tags = bool(_os.environ.get("DBG_TAGS"))

    def sb(self, name, shape, dtype, stack=None):
        self.uid = getattr(self, "uid", 0) + 1
        return (stack or self.st).enter_context(self.nc.sbuf_tensor("%s_u%d" % (name, self.uid), list(shape), dtype))

    def ps(self, name, dtype=F32, stack=None):
        n = 512 if dtype == F32 else 1024
        self.uid = getattr(self, "uid", 0) + 1
        return (stack or self.st).enter_context(self.nc.psum_tensor("%s_u%d" % (name, self.uid), [128, n], dtype))

    def mm(self, out, lhsT, rhs, start, stop, R, W, **kw):
        return self.P.emit("pe", lambda e: e.matmul(out, lhsT=lhsT, rhs=rhs, start=start, stop=stop, **kw), R, W)

    def tr(self, out, in_, ident, R, W):
        return self.P.emit("pe", lambda e: e.transpose(out, in_, ident), R, W)

    def act(self, out, in_, func, R, W, eng="act", **kw):
        return self.P.emit(eng, lambda e: e.activation(out=out, in_=in_, func=func, **kw), R, W)

    def copy(self, eng, out, in_, R, W):
        if eng == "act":
            return self.P.emit("act", lambda e: e.copy(out=out, in_=in_), R, W)
        return self.P.emit(eng, lambda e: e.tensor_copy(out=out, in_=in_), R, W)

    def tt(self, eng, out, in0, in1, op, R, W):
        return self.P.emit(eng, lambda e: e.tensor_tensor(out=out, in0=in0, in1=in1, op=op), R, W)

    def ts(self, eng, out, in0, s1, s2, op0, op1, R, W, **kw):
        if s2 is None:
            return self.P.emit(eng, lambda e: e.tensor_scalar(out=out, in0=in0, scalar1=s1, scalar2=None, op0=op0, **kw), R, W)
        return self.P.emit(eng, lambda e: e.tensor_scalar(out=out, in0=in0, scalar1=s1, scalar2=s2, op0=op0, op1=op1, **kw), R, W)

    def stt(self, eng, out, in0, scalar, in1, op0, op1, R, W):
        return self.P.emit(eng, lambda e: e.scalar_tensor_tensor(out=out, in0=in0, scalar=scalar, in1=in1, op0=op0, op1=op1), R, W)

    def memset(self, eng, ap, val, W):
        return self.P.emit(eng, lambda e: e.memset(ap, val), (), W)

    def dma(self, eng, out, in_, sem, R, W, slow=False):
        if slow:
            return self.P.emit(eng, lambda e: e.dma_start(out=out, in_=in_, allow_slow_non_contiguous=True), R, W, dma=sem)
        return self.P.emit(eng, lambda e: e.dma_start(out=out, in_=in_), R, W, dma=sem)

    def setup_consts(self):
        nc = self.nc
        self.ident = self.sb("ident", [128, 128], BF16)
        self.Rconst = Res("const")
        W = [self.Rconst]
        self.eps_rms = self.sb("eps_rms", [128, 4], F32)
        self.memset("pool", self.eps_rms[:, 0:1], RMS_EPS, W)
        self.memset("pool", self.eps_rms[:, 1:2], GN_EPS, W)
        self.memset("pool", self.eps_rms[:, 2:3], 1.0, W)
        self.memset("pool", self.eps_rms[:, 3:4], 0.0, W)
        self.memset("pool", self.ident[:], 1.0, W)
        idt = self.ident
        self.P.emit("pool", lambda e: e.affine_select(out=idt[:], in_=idt[:], pattern=[[-1, 128]],
                                                      compare_op=ALU.is_equal, fill=0.0, base=0,
                                                      channel_multiplier=1), (), W)

    def norm_tile(self, xt, Rxt, gain_b, Rgain, hT, RhT, col0, bufs, i):
        junk, Rjunk = bufs["junk"][i % 2]
        ss, Rss = bufs["ss"][i % 2]
        hb, Rhb = bufs["hb"][i % 2]
        tp, Rtp = bufs["tp"][i % len(bufs["tp"])]
        self.act(junk[:], xt[:], AF.Square, [Rxt], [Rjunk, Rss], accum_out=ss[:, 0:1])
        self.act(ss[:, 1:2], ss[:, 0:1], AF.Ln, [Rss, self.Rconst], [Rss], scale=1.0 / D, bias=self.eps_rms[:, 0:1])
        self.act(ss[:, 2:3], ss[:, 1:2], AF.Exp, [Rss], [Rss], scale=-0.5)
        self.stt("dve", hb[:], xt[:], ss[:, 2:3], gain_b[:], ALU.mult, ALU.mult, [Rxt, Rss, Rgain], [Rhb])
        for c in range(8):
            self.tr(tp[:, c * 128:(c + 1) * 128], hb[:, c * 128:(c + 1) * 128], self.ident[:], [Rhb, self.Rconst], [Rtp])
        self.copy("act", hT[:, :, col0:col0 + 128], tp[:, :].rearrange("p (c t) -> p c t", c=8), [Rtp], [RhT])

    def ffn_phase(self, layer, xin, xout, prm):
        nc, P, S = self.nc, self.P, self.S
        TG = min(1024, S)
        NG = S // TG
        NT = TG // 128
        NH = TG // 512
        with ExitStack() as st:
            gain_b = self.sb("f_gain", [128, D], F32, st)
            hT = self.sb("f_hT", [128, 8, TG], BF16, st)
            actT = self.sb("f_actT", [128, NFC, TG], BF16, st)
            wd = self.sb("f_wd", [128, NFC, D], BF16, st)
            wg = [self.sb("f_wg%d" % i, [128, 8, 256], BF16, st) for i in range(2)]
            wu = [self.sb("f_wu%d" % i, [128, 8, 256], BF16, st) for i in range(2)]
            wgs = [self.sb("f_wgs%d" % i, [128, 8, 256], F32, st) for i in range(2)]
            wus = [self.sb("f_wus%d" % i, [128, 8, 256], F32, st) for i in range(2)]
            Rwgs = [Res(), Res()]; Rwus = [Res(), Res()]
            xts = [self.sb("f_xt%d" % i, [128, D], F32, st) for i in range(3)]
            ots = [self.sb("f_ot%d" % i, [128, 512], F32, st) for i in range(2)]
            sil = [self.sb("f_sil%d" % i, [128, 512], F32, st) for i in range(2)]
            bufs = {
                "junk": [(self.sb("f_junk%d" % i, [128, D], BF16, st), Res()) for i in range(2)],
                "ss": [(self.sb("f_ss%d" % i, [128, 4], F32, st), Res()) for i in range(2)],
                "hb": [(self.sb("f_hb%d" % i, [128, D], BF16, st), Res()) for i in range(2)],
                "tp": [(self.ps("f_tp%d" % i, BF16, st), Res()) for i in range(2)],
            }
            pg = [(self.ps("f_pg%d" % i, F32, st), Res()) for i in range(2)]
            pu = [(self.ps("f_pu%d" % i, F32, st), Res()) for i in range(2)]
            po = [(self.ps("f_po%d" % i, F32, st), Res()) for i in range(2)]
            Rgain = Res(); RhT = [Res() for _ in range(NT)]; Ract = [Res() for _ in range(NFC)]
            Rwd = Res(); Rwg = [Res(), Res()]; Rwu = [Res(), Res()]
            Rxt = [Res() for _ in range(3)]; Rot = [Res(), Res()]; Rsil = [Res(), Res()]
            d_gain = P.dmasem("fgain"); d_x = [P.dmasem("fx%d" % i) for i in range(3)]
            d_wg = [P.dmasem("fwg%d" % i) for i in range(2)]; d_wu = [P.dmasem("fwu%d" % i) for i in range(2)]
            d_wd = P.dmasem("fwd"); d_o = [P.dmasem("fo%d" % i) for i in range(2)]
            Rxdram = self.Rxdram

            self.dma("sp", gain_b[:], prm["ffn_norm"][layer:layer + 1, :].partition_broadcast(128), d_gain, [], [Rgain])
            wgv = prm["ffn_w_gate"][layer].rearrange("(c p) f -> p c f", p=128)
            wuv = prm["ffn_w_up"][layer].rearrange("(c p) f -> p c f", p=128)
            wdv = prm["ffn_w_down"][layer].rearrange("(c p) n -> p c n", p=128)
            xcnt = 0
            ocnt = 0
            step = 0
            for g in range(NG):
                t0 = g * TG
                if g == 0:
                    for c0 in range(0, NFC, 2):
                        self.dma("pool", wd[:, c0:c0 + 2, :], wdv[:, c0:c0 + 2, :], d_wd, [], [Rwd])
                for i in range(NT):
                    b = xcnt % 3
                    tix = (t0 // 128) + i
                    self.dma("sp", xts[b][:], xin[t0 + i * 128:t0 + (i + 1) * 128, :], d_x[b], [Rxdram[tix]], [Rxt[b]])
                    self.norm_tile(xts[b], Rxt[b], gain_b, Rgain, hT, RhT[i], i * 128, bufs, xcnt)
                    xcnt += 1
                for fg in range(NFC // 2):
                    wb = fg % 2
                    self.dma("sp", wgs[wb][:], wgv[:, :, fg * 256:(fg + 1) * 256], d_wg[wb], [], [Rwgs[wb]])
                    self.dma("sp", wus[wb][:], wuv[:, :, fg * 256:(fg + 1) * 256], d_wu[wb], [], [Rwus[wb]])
                    self.copy("act", wg[wb][:], wgs[wb][:], [Rwgs[wb]], [Rwg[wb]])
                    self.copy("dve", wu[wb][:], wus[wb][:], [Rwus[wb]], [Rwu[wb]])
                    for fc in range(2):
                        f = fg * 2 + fc
                        for th in range(NH):
                            pb = step % 2
                            pgt, Rpg = pg[pb]
                            put, Rpu = pu[pb]
                            rh = RhT[th * 4:(th + 1) * 4]
                            for c in range(8):
                                self.mm(pgt[:, :], wg[wb][:, c, fc * 128:(fc + 1) * 128], hT[:, c, th * 512:(th + 1) * 512],
                                        c == 0, c == 7, [Rwg[wb]] + rh, [Rpg])
                            for c in range(8):
                                self.mm(put[:, :], wu[wb][:, c, fc * 128:(fc + 1) * 128], hT[:, c, th * 512:(th + 1) * 512],
                                        c == 0, c == 7, [Rwu[wb]] + rh, [Rpu])
                            self.act(sil[pb][:], pgt[:, :], AF.Silu, [Rpg], [Rsil[pb]])
                            self.tt("dve", actT[:, f, th * 512:(th + 1) * 512], sil[pb][:], put[:, :], ALU.mult,
                                    [Rsil[pb], Rpu], [Ract[f]])
                            step += 1
                for i in range(NT):
                    b = xcnt % 3
                    tix = (t0 // 128) + i
                    self.dma("sp", xts[b][:], xin[t0 + i * 128:t0 + (i + 1) * 128, :], d_x[b], [Rxdram[tix]], [Rxt[b]])
                    xcnt += 1
                    for nh in range(2):
                        ob = ocnt % 2
                        pot, Rpo = po[ob]
                        for f in range(NFC):
                            self.mm(pot[:, :], actT[:, f, i * 128:(i + 1) * 128], wd[:, f, nh * 512:(nh + 1) * 512],
                                    f == 0, f == NFC - 1, [Ract[f], Rwd], [Rpo])
                        self.tt("dve", ots[ob][:], pot[:, :], xts[b][:, nh * 512:(nh + 1) * 512], ALU.add,
                                [Rpo, Rxt[b]], [Rot[ob]])
                        self.dma("sp", xout[t0 + i * 128:t0 + (i + 1) * 128, nh * 512:(nh + 1) * 512], ots[ob][:], d_o[ob],
                                 [Rot[ob]], [Rxdram[tix]])
                        ocnt += 1
            P.barrier()


    def hy_consts(self, st):
        c = {}
        W = [self.Rconst]
        sb = lambda n, shp, dt: self.sb(n, shp, dt, st)
        c["negmask"] = sb("c_negmask", [128, 128], BF16)
        c["tri"] = sb("c_tri", [128, 128], F32)
        c["nones"] = sb("c_nones", [128, 128], F32)
        c["identf"] = sb("c_identf", [128, 128], F32)
        c["bd"] = sb("c_bd", [128, 128], BF16)
        c["esel"] = sb("c_esel", [72, 8, 128], BF16)
        c["ones64"] = sb("c_ones64", [128, 64], BF16)
        c["invc"] = sb("c_invc", [128, 4, 16], F32)
        nm, tri, nones, identf, bd, esel, ones64, invc = (c[k] for k in ("negmask", "tri", "nones", "identf", "bd", "esel", "ones64", "invc"))
        self.memset("pool", nm[:], 0.0, W)
        self.P.emit("pool", lambda e: e.affine_select(out=nm[:], in_=nm[:], pattern=[[1, 128]], compare_op=ALU.is_ge,
                                                      fill=-30000.0, base=0, channel_multiplier=-1), (), W)
        self.memset("pool", tri[:], -1.0, W)
        self.P.emit("pool", lambda e: e.affine_select(out=tri[:], in_=tri[:], pattern=[[1, 128]], compare_op=ALU.is_ge,
                                                      fill=0.0, base=0, channel_multiplier=-1), (), W)
        self.memset("pool", nones[:], -1.0, W)
        self.memset("pool", identf[:], 1.0, W)
        self.P.emit("pool", lambda e: e.affine_select(out=identf[:], in_=identf[:], pattern=[[-1, 128]], compare_op=ALU.is_equal,
                                                      fill=0.0, base=0, channel_multiplier=1), (), W)
        self.memset("pool", bd[:], 0.0, W)
        self.memset("pool", bd[0:64, 0:64], 1.0, W)
        self.memset("pool", bd[64:128, 64:128], 1.0, W)
        self.memset("pool", esel[0:8], 8.0, W)
        self.P.emit("pool", lambda e: e.affine_select(out=esel[0:8], in_=esel[0:8], pattern=[[1, 8], [0, 128]], compare_op=ALU.is_equal,
                                                      fill=0.0, base=0, channel_multiplier=-1), (), W)
        d_e = self.P.dmasem("esel")
        self.dma("sp", esel[64:72], esel[0:8], d_e, [self.Rconst], [self.Rconst])
        self.memset("pool", ones64[:], 1.0, W)
        for g, w in enumerate((2, 4, 8, 16)):
            self.memset("pool", invc[:, g, :], 1.0 / w, W)
            for t in range(w - 1):
                self.memset("pool", invc[:, g, t:t + 1], 1.0 / (t + 1), W)
        return c

    def hy_phase(self, e, layer, xin, xout, prm):
        nc, P, S = self.nc, self.P, self.S
        NI = S // 512
        NB = S // 128
        Rc = self.Rconst
        with ExitStack() as st:
            cst = self.hy_consts(st)
            sb = lambda n, shp, dt: self.sb("h_" + n, shp, dt, st)
            w_in = sb("w_in", [128, 8, IN_COLS], BF16)
            w_out = sb("w_out", [128, 8, D], BF16)
            pw = sb("pw", [128, 4, 128], BF16)
            gain_b = sb("gain", [128, D], F32)
            qg = sb("qg", [128, 1], F32); kg = sb("kg", [128, 1], F32)
            pscale = sb("pscale", [128, 4], F32)
            fb = sb("fb", [128, 8], F32)
            kT = sb("kT", [128, 4, S], BF16)
            vc = sb("vc", [128, NB, 512], BF16)
            cumK = sb("cumK", [128, NB, 8], F32)
            kb = sb("kb", [128, NB, 8], F32)
            hc = sb("hc", [128, 8, 512], BF16)
            qT = sb("qT", [128, 4, 512], BF16)
            sgT = sb("sgT", [128, 4, 512], BF16)
            upad = sb("upad", [128, 4, 528], F32)
            tA = sb("tA", [128, 528], F32); tB = sb("tB", [128, 528], F32)
            pooledT = sb("pooledT", [128, 4, 512], BF16)
            xts = [sb("xt%d" % i, [128, D], F32) for i in range(2)]
            ots = [sb("ot%d" % i, [128, 512], F32) for i in range(2)]
            junk = sb("junk", [128, D], BF16)
            kf = sb("kf", [128, 512], F32); sq = sb("sq", [128, 512], BF16); rs = sb("rs", [128, 512], F32)
            pT = [sb("pT%d" % i, [128, 512], BF16) for i in range(4)]
            rden = sb("rden", [128, 512], F32); atmp = sb("atmp", [128, 512], F32)
            carry = [sb("carry%d" % i, [128, 8], F32) for i in range(2)]
            zf = sb("zf", [128, 8], F32); lf = sb("lf", [128, 8], F32)
            qctok = sb("qctok", [128, 4, 8], F32)
            qcT = sb("qcT", [72, 512], BF16)
            t16 = sb("t16", [128, 16], F32)
            bufs = {
                "junk": [(junk, Res()), (junk, Res())],
                "ss": [(sb("ss%d" % i, [128, 4], F32), Res()) for i in range(2)],
                "hb": [(sb("hb%d" % i, [128, D], BF16), Res()) for i in range(1)] * 2,
            }
            gp = [(self.ps("h_gp%d" % i, F32, st), Res()) for i in range(2)]
            bufs["tp"] = [(g_[:, :].bitcast(BF16), Rg_) for g_, Rg_ in gp]
            sbk = [(self.ps("h_s%d" % i, F32, st), Res()) for i in range(4)]
            accN, RaccN = self.ps("h_accN", F32, st), Res()
            accD, RaccD = self.ps("h_accD", F32, st), Res()
            Rw = Res(); Rgain = Res(); Rsmall = Res()
            Rhc = Res(); RqT = Res(); RsgT = Res(); RkT = [Res() for _ in range(NI)]; Rvc = [Res() for _ in range(NB)]
            RcumK = Res(); Rkb = Res(); Rupad = Res(); RtA = Res(); RtB = Res(); Rpooled = Res()
            Rxt = [Res(), Res()]; Rot = [Res(), Res()]; Rkf = Res(); Rsq = Res(); Rrs = Res()
            RpT = [Res() for _ in range(4)]; Rrden = Res(); Ratmp = Res(); Rcarry = [Res(), Res()]
            Rzf = Res(); Rlf = Res(); Rqctok = Res(); RqcT = Res(); Rt16 = Res()
            d_q = P.dmasem("hq"); d_w = P.dmasem("hw"); d_c = P.dmasem("hc"); d_x = [P.dmasem("hx%d" % i) for i in range(2)]
            d_o = [P.dmasem("ho%d" % i) for i in range(2)]
            Rxdram = self.Rxdram
            win_v = prm["hy_w_in"][e].rearrange("(c p) n -> p c n", p=128)
            wout_v = prm["hy_w_out"][e].rearrange("(c p) n -> p c n", p=128)
            for c in range(8):
                self.dma("pool", w_in[:, c, :], win_v[:, c, :], d_w, [], [Rw])
            for c in range(8):
                self.dma("pool", w_out[:, c, :], wout_v[:, c, :], d_w, [], [Rw])
            self.dma("pool", pw[:], prm["hy_pool_w"][e].rearrange("g c d -> c g d"), d_w, [], [Rw])
            self.dma("sp", gain_b[:], prm["mix_norm"][layer:layer + 1, :].partition_broadcast(128), d_c, [], [Rgain])
            qgv = prm["hy_q_gain"][e].rearrange("(d o) -> d o", o=1)
            kgv = prm["hy_k_gain"][e].rearrange("(d o) -> d o", o=1)
            for hp in range(2):
                self.dma("sp", qg[hp * 64:(hp + 1) * 64, :], qgv, d_c, [], [Rsmall])
                self.dma("sp", kg[hp * 64:(hp + 1) * 64, :], kgv, d_c, [], [Rsmall])
            self.dma("sp", pscale[:], prm["hy_pool_scale"][e].rearrange("(g p) -> p g", p=128), d_c, [], [Rsmall], slow=True)
            self.dma("sp", fb[:], prm["hy_f_bias"][e:e + 1, :].partition_broadcast(128), d_c, [], [Rsmall])
            self.memset("pool", carry[0][:], 0.0, [Rcarry[0]])
            self.memset("pool", upad[:, :, 0:16], 0.0, [Rupad])
            xcnt = 0; ocnt = 0; gcnt = 0; scnt = 0; pcnt = 0; ccnt = 0

            def proj_fm(col0, gi):
                pt, Rp = gp[gi % 2]
                for c in range(8):
                    self.mm(pt[:, :], w_in[:, c, col0:col0 + 128], hc[:, c, :], c == 0, c == 7, [Rw, Rhc], [Rp])
                return pt, Rp

            for I in range(NI):
                t0 = I * 512
                for i in range(4):
                    b = xcnt % 2
                    tix = I * 4 + i
                    self.dma("sp", xts[b][:], xin[t0 + i * 128:t0 + (i + 1) * 128, :], d_x[b], [Rxdram[tix]], [Rxt[b]])
                    self.norm_tile(xts[b], Rxt[b], gain_b, Rgain, hc, Rhc, i * 128, bufs, xcnt)
                    xcnt += 1
                for i in range(4):
                    blk = I * 4 + i
                    smp, Rsm = gp[gcnt % 2]; gcnt += 1
                    for c in range(8):
                        self.mm(smp[:, 0:8], hc[:, c, i * 128:(i + 1) * 128], w_in[:, c, 2048:2056], c == 0, c == 7, [Rw, Rhc], [Rsm])
                    self.tt("dve", zf[:], smp[:, 0:8], fb[:], ALU.add, [Rsm, Rsmall], [Rzf])
                    self.act(lf[:], zf[:], AF.Exp, [Rzf], [Rlf], scale=-1.0)
                    self.act(lf[:], lf[:], AF.Ln, [Rlf, Rc], [Rlf], bias=self.eps_rms[:, 2:3])
                    self.mm(smp[:, 8:16], cst["tri"][:], lf[:], True, True, [Rlf, Rc], [Rsm])
                    self.mm(smp[:, 16:24], cst["nones"][:], lf[:], True, True, [Rlf, Rc], [Rsm])
                    cin, cout = carry[ccnt % 2], carry[(ccnt + 1) % 2]
                    Rcin, Rcout = Rcarry[ccnt % 2], Rcarry[(ccnt + 1) % 2]
                    self.tt("dve", cumK[:, blk, :], smp[:, 8:16], cin[:], ALU.add, [Rsm, Rcin], [RcumK])
                    self.tt("dve", cout[:], smp[:, 16:24], cin[:], ALU.add, [Rsm, Rcin], [Rcout])
                    ccnt += 1
                cend, Rcend = carry[ccnt % 2], Rcarry[ccnt % 2]
                nj = 4 * I + 4
                cb4 = cend[:, :].unsqueeze(1).broadcast_to([128, 4, 8])
                self.tt("dve", qctok[:], cumK[:, 4 * I:4 * I + 4, :], cb4, ALU.subtract, [RcumK, Rcend], [Rqctok])
                cbn = cend[:, :].unsqueeze(1).broadcast_to([128, nj, 8])
                self.tt("dve", kb[:, 0:nj, :], cbn, cumK[:, 0:nj, :], ALU.subtract, [RcumK, Rcend], [Rkb])
                ptq, Rpq = gp[gcnt % 2]; gcnt += 1
                for i in range(4):
                    self.tr(ptq[0:8, i * 128:(i + 1) * 128], qctok[:, i, :], cst["identf"][:], [Rqctok, Rc], [Rpq])
                self.copy("dve", qcT[0:8, :], ptq[0:8, 0:512], [Rpq], [RqcT])
                self.dma("sp", qcT[64:72, :], qcT[0:8, :], d_q, [RqcT], [RqcT])
                for which in range(2):
                    for cc in range(4):
                        col0 = (512 if which == 0 else 0) + cc * 128
                        pt, Rp = proj_fm(col0, gcnt); gcnt += 1
                        self.copy("act", kf[:], pt[:, :], [Rp], [Rkf])
                        self.tt("pool", sq[:], kf[:], kf[:], ALU.mult, [Rkf], [Rsq])
                        pt2, Rp2 = gp[gcnt % 2]; gcnt += 1
                        self.mm(pt2[:, :], cst["bd"][:], sq[:], True, True, [Rsq, Rc], [Rp2])
                        self.act(rs[:], pt2[:, :], AF.Ln, [Rp2, Rc], [Rrs], scale=1.0 / 64, bias=self.eps_rms[:, 0:1])
                        self.act(rs[:], rs[:], AF.Exp, [Rrs], [Rrs], scale=-0.5)
                        if which == 0:
                            self.stt("dve", kT[:, cc, t0:t0 + 512], kf[:], kg[:, 0:1], rs[:], ALU.mult, ALU.mult,
                                     [Rkf, Rrs, Rsmall], [RkT[I]])
                        else:
                            self.stt("dve", qT[:, cc, :], kf[:], qg[:, 0:1], rs[:], ALU.mult, ALU.mult,
                                     [Rkf, Rrs, Rsmall], [RqT])
                for i in range(4):
                    blk = I * 4 + i
                    pt, Rp = gp[gcnt % 2]; gcnt += 1
                    for c in range(8):
                        self.mm(pt[:, :], hc[:, c, i * 128:(i + 1) * 128], w_in[:, c, 1024:1536], c == 0, c == 7, [Rw, Rhc], [Rp])
                    self.copy("act", vc[:, blk, :], pt[:, :], [Rp], [Rvc[blk]])
                for cc in range(4):
                    pt, Rp = proj_fm(1536 + cc * 128, gcnt); gcnt += 1
                    self.act(sgT[:, cc, :], pt[:, :], AF.Sigmoid, [Rp], [RsgT])
                for g in range(4):
                    pt, Rp = proj_fm(2056 + g * 128, gcnt); gcnt += 1
                    self.copy("act", upad[:, g, 16:528], pt[:, :], [Rp], [Rupad])
                for g in range(4):
                    u = upad[:, g, :]
                    cur, Rcur = u, Rupad
                    lo = 0
                    tmps = [(tA, RtA), (tB, RtB)]
                    for lvl in range(g + 1):
                        sh = 1 << lvl
                        dst, Rdst = tmps[lvl % 2]
                        nlo = lo + sh
                        self.tt("pool", dst[:, nlo:528], cur[:, nlo:528], cur[:, nlo - sh:528 - sh], ALU.add, [Rcur], [Rdst])
                        cur, Rcur, lo = dst, Rdst, nlo
                    wdt = 2 << g
                    self.stt("dve", pooledT[:, g, :], cur[:, 16:528], 1.0 / wdt, u[:, 16:528], ALU.mult, ALU.subtract,
                             [Rcur, Rupad], [Rpooled])
                    if I == 0:
                        self.tt("pool", t16[:], cur[:, 16:32], cst["invc"][:, g, :], ALU.mult, [Rcur, Rc], [Rt16])
                        self.tt("pool", pooledT[:, g, 0:16], t16[:], u[:, 16:32], ALU.subtract, [Rt16, Rupad], [Rpooled])
                self.copy("pool", upad[:, :, 0:16], upad[:, :, 512:528], [Rupad], [Rupad])
                for g in range(4):
                    pt, Rp = gp[gcnt % 2]; gcnt += 1
                    self.mm(pt[:, :], pw[:, g, :], pooledT[:, g, :], True, True, [Rw, Rpooled], [Rp])
                    self.act(hc[:, 4 + g, :], pt[:, :], AF.Copy, [Rp, Rsmall], [Rhc], scale=pscale[:, g:g + 1])
                steps = [(pr, j) for pr in range(4) for j in range(nj)]

                def qk(step, si):
                    pr, j = step
                    jj = j - 4 * I
                    c0 = 128 * jj if jj > 0 else 0
                    diag = jj >= 0
                    Ij = j // 4
                    for hp in range(2):
                        sbt, Rs = sbk[(si % 2) * 2 + hp]
                        self.mm(sbt[:, c0:512], kT[hp * 64:(hp + 1) * 64, pr, j * 128:(j + 1) * 128],
                                qT[hp * 64:(hp + 1) * 64, pr, c0:512], True, False, [RkT[Ij], RqT], [Rs])
                    for hp in range(2):
                        h = pr * 2 + hp
                        sbt, Rs = sbk[(si % 2) * 2 + hp]
                        self.mm(sbt[:, c0:512], cst["esel"][hp * 64:hp * 64 + 8, h, :], qcT[hp * 64:hp * 64 + 8, c0:512],
                                False, not diag, [RqcT, Rc], [Rs])
                    if diag:
                        for hp in range(2):
                            sbt, Rs = sbk[(si % 2) * 2 + hp]
                            self.mm(sbt[:, c0:c0 + 128], self.ident[:], cst["negmask"][:], False, True, [Rc], [Rs])
                    return c0

                def rest(step, si, c0):
                    pr, j = step
                    first = (j == 0)
                    last = (j == nj - 1)
                    pts = []
                    for hp in range(2):
                        h = pr * 2 + hp
                        sbt, Rs = sbk[(si % 2) * 2 + hp]
                        pt_, Rp_ = pT[(si % 2) * 2 + hp], RpT[(si % 2) * 2 + hp]
                        self.act(pt_[:, c0:512], sbt[:, c0:512], AF.Exp, [Rs, Rkb], [Rp_], scale=0.125, bias=kb[:, j, h:h + 1])
                        pts.append((pt_, Rp_))
                    for hp in range(2):
                        h = pr * 2 + hp
                        pt_, Rp_ = pts[hp]
                        self.mm(accN[hp * 64:(hp + 1) * 64, c0:512], vc[:, j, h * 64:(h + 1) * 64], pt_[:, c0:512],
                                first, last, [Rvc[j], Rp_], [RaccN], tile_position=(0, hp * 64))
                    for hp in range(2):
                        pt_, Rp_ = pts[hp]
                        self.mm(accD[hp * 64:(hp + 1) * 64, c0:512], cst["ones64"][:], pt_[:, c0:512],
                                first, last, [Rc, Rp_], [RaccD], tile_position=(0, hp * 64))
                    if last:
                        self.P.emit("dve", lambda e_: e_.reciprocal(out=rden[:], in_=accD[:, :]), [RaccD], [Rrden])
                        self.tt("dve", atmp[:], accN[:, :], rden[:], ALU.mult, [RaccN, Rrden], [Ratmp])
                        self.tt("pool", hc[:, pr, :], atmp[:], sgT[:, pr, :], ALU.mult, [Ratmp, RsgT], [Rhc])

                pend = []
                for n, stp in enumerate(steps):
                    c0 = qk(stp, scnt + n)
                    pend.append((stp, scnt + n, c0))
                    if len(pend) > 1:
                        rest(*pend.pop(0))
                while pend:
                    rest(*pend.pop(0))
                scnt += len(steps)
                for i in range(4):
                    b = xcnt % 2
                    tix = I * 4 + i
                    self.dma("sp", xts[b][:], xin[t0 + i * 128:t0 + (i + 1) * 128, :], d_x[b], [Rxdram[tix]], [Rxt[b]])
                    xcnt += 1
                    for nh in range(2):
                        ob = ocnt % 2
                        pt, Rp = gp[gcnt % 2]; gcnt += 1
                        for c in range(8):
                            self.mm(pt[:, :], hc[:, c, i * 128:(i + 1) * 128], w_out[:, c, nh * 512:(nh + 1) * 512],
                                    c == 0, c == 7, [Rhc, Rw], [Rp])
                        self.tt("dve", ots[ob][:], pt[:, :], xts[b][:, nh * 512:(nh + 1) * 512], ALU.add, [Rp, Rxt[b]], [Rot[ob]])
                        self.dma("sp", xout[t0 + i * 128:t0 + (i + 1) * 128, nh * 512:(nh + 1) * 512], ots[ob][:], d_o[ob],
                                 [Rot[ob]], [Rxdram[tix]])
                        ocnt += 1
            P.barrier()


    def rw_phase(self, o, layer, xin, xout, prm):
        nc, P, S = self.nc, self.P, self.S
        NCH = S // 128
        Rc = self.Rconst
        first_layer = (o == 0)
        with ExitStack() as st:
            sb = lambda n, shp, dt: self.sb("r_" + n, shp, dt, st)
            Wr = sb("Wr", [128, 8, D], BF16); Wk = sb("Wk", [128, 8, D], BF16)
            Wv = sb("Wv", [128, 8, D], BF16); Wo = sb("Wo", [128, 8, D], BF16)
            l1 = sb("l1", [128, 8, 320], BF16)
            wa2 = sb("wa2", [128, D], BF16)
            g2a = sb("g2a", [128, D], BF16)
            gv2 = sb("gv2", [64, D], BF16)
            bc = sb("bc", [128, 8, D], BF16)
            gain_b = sb("gain", [128, D], F32)
            mu = sb("mu", [128, 6, 8], F32)
            IUf = sb("IUf", [128, 128], F32); SUf = sb("SUf", [128, 128], F32); SLf = sb("SLf", [128, 128], F32)
            onesf = sb("onesf", [128, 128], F32)
            tiny = sb("tiny", [128, 1], F32)
            W = [Rc]
            self.memset("pool", IUf[:], 1.0, W)
            self.P.emit("pool", lambda e: e.affine_select(out=IUf[:], in_=IUf[:], pattern=[[1, 128]], compare_op=ALU.is_ge,
                                                          fill=0.0, base=0, channel_multiplier=-1), (), W)
            self.memset("pool", SUf[:], 1.0, W)
            self.P.emit("pool", lambda e: e.affine_select(out=SUf[:], in_=SUf[:], pattern=[[1, 128]], compare_op=ALU.is_gt,
                                                          fill=0.0, base=0, channel_multiplier=-1), (), W)
            self.memset("pool", SLf[:], 1.0, W)
            self.P.emit("pool", lambda e: e.affine_select(out=SLf[:], in_=SLf[:], pattern=[[-1, 128]], compare_op=ALU.is_gt,
                                                          fill=0.0, base=0, channel_multiplier=1), (), W)
            self.memset("pool", onesf[:], -float(np.exp(-0.5)), W)
            cIU = sb("cIU", [128, 128], F32); cSU = sb("cSU", [128, 128], F32)
            self.ts("pool", cIU[:], IUf[:], -float(np.exp(-0.5)), None, ALU.mult, None, [Rc], W)
            self.ts("pool", cSU[:], SUf[:], -float(np.exp(-0.5)), None, ALU.mult, None, [Rc], W)
            self.memset("pool", tiny[:], 1e-24, W)
            xt = sb("xt", [128, D], F32); hb = sb("hb", [128, D], BF16)
            hTe = sb("hTe", [128, 8, 129], BF16)
            xm = [sb("xm%d" % i, [128, 8, 128], BF16) for i in range(2)]
            xx = sb("xx", [128, 8, 128], BF16)
            sc = [sb("sc%d" % i, [128, D], F32) for i in range(5)]
            r_bf = sb("r_bf", [128, D], BF16); kp_bf = sb("kp_bf", [128, D], BF16); kk_bf = sb("kk_bf", [128, D], BF16)
            b_bf = sb("b_bf", [128, D], BF16); v_bf = sb("v_bf", [128, D], BF16); g_bf = sb("g_bf", [128, D], BF16)
            prod = [sb("prod%d" % i, [128, D], BF16) for i in range(2)]
            Khat = sb("Khat", [128, D], BF16); Bhat = sb("Bhat", [128, D], BF16)
            RtT = sb("RtT", [128, 8, 128], BF16); KtT = sb("KtT", [128, 8, 128], BF16)
            BtT = sb("BtT", [128, 8, 128], BF16); AtT = sb("AtT", [128, 8, 128], BF16)
            lsb = sb("lsb", [128, 512], BF16)
            Mbh = [[sb("Mb%d_%d" % (hf, i), [128, 8, 128], BF16) for i in range(2)] for hf in range(2)]
            Nbh = [[sb("Nb%d_%d" % (hf, i), [128, 8, 128], BF16) for i in range(2)] for hf in range(2)]
            Xbh = [[sb("Xb%d_%d" % (hf, i), [128, 8, 128], BF16) for i in range(2)] for hf in range(2)]
            XT = sb("XT", [128, 16, 128], BF16)
            AakT = sb("AakT", [128, 16, 128], BF16); ArbT = sb("ArbT", [128, 16, 128], BF16); ArkT = sb("ArkT", [128, 16, 128], BF16)
            ST = sb("ST", [128, 8, 64], F32); STb = sb("STb", [128, 8, 64], BF16); STt = sb("STt", [128, 8, 64], F32)
            PCc = sb("PCc", [128, 8], F32)
            st16 = sb("st16", [128, 8, 16], F32)
            yfin = Khat; yT = xm[0]
            ss = sb("ss", [128, 4], F32)
            tp = [(self.ps("r_tp%d" % i, BF16, st), Res()) for i in range(2)]
            gpool = [(self.ps("r_gp%d" % i, F32, st), Res()) for i in range(6)]
            self._gi = 0

            def bank():
                b_ = gpool[self._gi % 6]
                self._gi += 1
                return b_

            self._ti = 0

            def tbank():
                b_ = tp[self._ti % 2]
                self._ti += 1
                return b_

            R = {k: Res(k) for k in ("w", "small", "xt", "hb", "hTe", "xx", "ss", "r_bf", "kp_bf", "kk_bf", "b_bf", "v_bf", "g_bf",
                                     "Khat", "Bhat", "RtT", "KtT", "BtT", "AtT", "lsb", "XT", "AakT", "ArbT", "ArkT", "RHS", "U",
                                     "ST", "STb", "STt", "PCc", "st16", "yfin", "yT", "ot", "gain")}
            Rsc = [Res() for _ in range(5)]; Rxm = [Res(), Res()]; Rprod = [Res(), Res()]
            ot = sc[2]
            R["ot"] = Rsc[2]
            R["yfin"] = R["Khat"]
            R["yT"] = Rxm[0]
            RHS, U = prod[0], prod[1]
            R["RHS"], R["U"] = Rprod[0], Rprod[1]
            RMbh = [[Res(), Res()] for _ in range(2)]; RNbh = [[Res(), Res()] for _ in range(2)]; RXbh = [[Res(), Res()] for _ in range(2)]
            d_x2 = P.dmasem("rx2"); d_w = P.dmasem("rw"); d_c = P.dmasem("rc"); d_x = P.dmasem("rx"); d_o = P.dmasem("ro"); d_v = P.dmasem("rv")
            Rxdram = self.Rxdram
            Rvf = self.Rvf

            def wview(name):
                return prm[name][o].rearrange("(c p) n -> p c n", p=128)
            for Wt, nm in ((Wr, "rw_w_r"), (Wk, "rw_w_k"), (Wv, "rw_w_v"), (Wo, "rw_w_o")):
                v_ = wview(nm)
                for c in range(8):
                    self.dma("pool", Wt[:, c, :], v_[:, c, :], d_w, [], [R["w"]])
            self.dma("pool", l1[:, :, 0:64], wview("rw_w1"), d_w, [], [R["w"]])
            self.dma("pool", l1[:, :, 64:128], wview("rw_a1"), d_w, [], [R["w"]])
            self.dma("pool", l1[:, :, 128:288], wview("rw_g1"), d_w, [], [R["w"]])
            self.dma("pool", wa2[0:64, :], prm["rw_w2"][o], d_w, [], [R["w"]])
            self.dma("pool", wa2[64:128, :], prm["rw_a2"][o], d_w, [], [R["w"]])
            self.dma("pool", g2a[:, :], prm["rw_g2"][o][0:128, :], d_w, [], [R["w"]])
            self.dma("pool", gv2[0:32, :], prm["rw_g2"][o][128:160, :], d_w, [], [R["w"]])
            if not first_layer:
                self.dma("pool", l1[:, :, 288:320], prm["rw_v1"][o - 1].rearrange("(c p) n -> p c n", p=128), d_w, [], [R["w"]])
                self.dma("pool", gv2[32:64, :], prm["rw_v2"][o - 1], d_w, [], [R["w"]])
            rows = [prm["rw_w0"][o:o + 1, :], prm["rw_a0"][o:o + 1, :],
                    (prm["rw_v0"][o - 1:o, :] if not first_layer else prm["rw_w0"][o:o + 1, :]),
                    prm["rw_k_k"][o:o + 1, :], prm["rw_k_a"][o:o + 1, :], prm["rw_ln_w"][o:o + 1, :], prm["rw_ln_b"][o:o + 1, :],
                    prm["rw_r_k"][o:o + 1].rearrange("o h d -> o (h d)")]
            for i, rv in enumerate(rows):
                self.dma("pool", bc[:, i, :], rv.partition_broadcast(128), d_w, [], [R["w"]])
            W0, A0, V0, KK_, KA_, LNW, LNB, RK_ = (bc[:, i, :] for i in range(8))
            self.dma("sp", gain_b[:], prm["mix_norm"][layer:layer + 1, :].partition_broadcast(128), d_c, [], [R["gain"]])
            self.dma("sp", mu[:], prm["rw_mu"][o].rearrange("i (c p) -> p i c", p=128), d_c, [], [R["small"]], slow=True)
            self.memset("pool", ST[:], 0.0, [R["ST"]])
            self.memset("pool", STb[:], 0.0, [R["STb"]])
            self.memset("pool", hTe[:, :, 128:129], 0.0, [R["hTe"]])
            bufs = {"junk": [(b_bf, R["b_bf"])] * 2, "ss": [(ss, R["ss"])] * 2, "hb": [(hb, R["hb"])] * 2, "tp": tp}
            IUb = lambda n_: IUf[:, :].unsqueeze(1).broadcast_to([128, n_, 128])
            SUb = lambda n_: SUf[:, :].unsqueeze(1).broadcast_to([128, n_, 128])
            SLb = lambda n_: SLf[:, :].unsqueeze(1).broadcast_to([128, n_, 128])
            IDb = lambda n_: self.ident[:, :].unsqueeze(1).broadcast_to([128, n_, 128])
            h3 = lambda ap: ap.rearrange("p (h d) -> p h d", d=64)

            def proj_tok(xT, Rx, Wt, lo=0):
                bks = []
                for nh in range(2):
                    pt, Rp = bank()
                    for c in range(8):
                        self.mm(pt[:, :], xT[:, c, :], Wt[:, c, nh * 512:(nh + 1) * 512], c == 0, c == 7, [Rx, R["w"]], [Rp])
                    bks.append((pt, Rp))
                return bks

            def mix(i, j):
                mub = mu[:, i, :].unsqueeze(2).broadcast_to([128, 8, 128])
                self.tt("dve", xm[j][:], xx[:], mub, ALU.mult, [R["xx"], R["small"]], [Rxm[j]])
                self.tt("dve", xm[j][:], xm[j][:], hTe[:, :, 1:129], ALU.add, [Rxm[j], R["hTe"]], [Rxm[j]])
                return xm[j], Rxm[j]

            def evac2(bks, fn):
                for nh, (pt, Rp) in enumerate(bks):
                    fn(nh, pt, Rp, slice(nh * 512, (nh + 1) * 512))

            import os as _os
            stop = int(_os.environ.get("RW_STOP", "99"))

            def early(n_, t0_):
                self.dma("sp", ot[:], xin[t0_:t0_ + 128, :], d_x2, [Rxdram[n_]], [R["ot"]])
                self.dma("sp", xout[t0_:t0_ + 128, :], ot[:], d_o, [R["ot"]], [Rxdram[n_]])

            def stage_r1(n):
                t0 = n * 128
                if self.dbg_tags:
                    P.tag = "c%d.%s" % (n, "R1")
                self.copy("pool", hTe[:, :, 0:1], hTe[:, :, 128:129], [R["hTe"]], [R["hTe"]])
                self.dma("sp", xt[:], xin[t0:t0 + 128, :], d_x, [Rxdram[n]], [R["xt"]])
                self.norm_tile(xt, R["xt"], gain_b, R["gain"], hTe[:, :, 1:129], R["hTe"], 0, bufs, n)
                self.tt("pool", xx[:], hTe[:, :, 0:128], hTe[:, :, 1:129], ALU.subtract, [R["hTe"]], [R["xx"]])

            stage_r1(0)
            for n in range(NCH):
                t0 = n * 128
                if stop <= 1:
                    early(n, t0)
                    continue
                if self.dbg_tags:
                    P.tag = "c%d.%s" % (n, "R2")
                xr, Rxr = mix(0, 0)
                evac2(proj_tok(xr, Rxr, Wr), lambda nh, pt, Rp, sl: self.copy("act", r_bf[:, sl], pt[:, :], [Rp], [R["r_bf"]]))
                xk, Rxk = mix(2, 1)
                evac2(proj_tok(xk, Rxk, Wk), lambda nh, pt, Rp, sl: self.copy("act", sc[0][:, sl], pt[:, :], [Rp], [Rsc[0]]))
                xv, Rxv = mix(3, 0)
                evac2(proj_tok(xv, Rxv, Wv), lambda nh, pt, Rp, sl: self.copy("act", sc[1][:, sl], pt[:, :], [Rp], [Rsc[1]]))
                lp, Rlp = bank()
                if not first_layer:
                    for c in range(8):
                        self.mm(lp[32:64, 256:384], l1[:, c, 288:320], xv[:, c, :], c == 0, c == 7, [Rxv, R["w"]], [Rlp],
                                tile_position=(0, 32))
                xw, Rxw = mix(1, 1)
                for c in range(8):
                    self.mm(lp[0:64, 0:128], l1[:, c, 0:64], xw[:, c, :], c == 0, c == 7, [Rxw, R["w"]], [Rlp])
                xa, Rxa = mix(4, 0)
                for c in range(8):
                    self.mm(lp[64:128, 0:128], l1[:, c, 64:128], xa[:, c, :], c == 0, c == 7, [Rxa, R["w"]], [Rlp],
                            tile_position=(0, 64))
                xg, Rxg = mix(5, 1)
                for c in range(8):
                    self.mm(lp[:, 128:256], l1[:, c, 128:256], xg[:, c, :], c == 0, c == 7, [Rxg, R["w"]], [Rlp])
                for c in range(8):
                    self.mm(lp[0:32, 256:384], l1[:, c, 256:288], xg[:, c, :], c == 0, c == 7, [Rxg, R["w"]], [Rlp])
                self.act(lsb[0:64, 0:128], lp[0:64, 0:128], AF.Tanh, [Rlp], [R["lsb"]])
                self.copy("act", lsb[64:128, 0:128], lp[64:128, 0:128], [Rlp], [R["lsb"]])
                self.act(lsb[:, 128:256], lp[:, 128:256], AF.Sigmoid, [Rlp], [R["lsb"]])
                self.act(lsb[0:32, 256:384], lp[0:32, 256:384], AF.Sigmoid, [Rlp], [R["lsb"]])
                if not first_layer:
                    self.copy("act", lsb[32:64, 256:384], lp[32:64, 256:384], [Rlp], [R["lsb"]])
                for nh in range(2):
                    sl = slice(nh * 512, (nh + 1) * 512)
                    pt, Rp = bank()
                    self.mm(pt[:, :], lsb[64:128, 0:128], wa2[64:128, sl], True, True, [R["lsb"], R["w"]], [Rp])
                    self.tt("dve", sc[3][:, sl], pt[:, :], A0[:, sl], ALU.add, [Rp, R["w"]], [Rsc[3]])
                self.act(sc[3][:], sc[3][:], AF.Sigmoid, [Rsc[3]], [Rsc[3]])
                for nh in range(2):
                    sl = slice(nh * 512, (nh + 1) * 512)
                    pt, Rp = bank()
                    self.mm(pt[:, :], lsb[:, 128:256], g2a[:, sl], True, False, [R["lsb"], R["w"]], [Rp])
                    self.mm(pt[:, :], lsb[0:32, 256:384], gv2[0:32, sl], False, True, [R["lsb"], R["w"]], [Rp])
                    self.copy("act", g_bf[:, sl], pt[:, :], [Rp], [R["g_bf"]])
                if first_layer:
                    self.dma("sp", self.vfirst[t0:t0 + 128, :], sc[1][:], d_v, [Rsc[1]], [Rvf[n]])
                else:
                    for nh in range(2):
                        sl = slice(nh * 512, (nh + 1) * 512)
                        pt, Rp = bank()
                        self.mm(pt[:, :], lsb[32:64, 256:384], gv2[32:64, sl], True, True, [R["lsb"], R["w"]], [Rp])
                        self.tt("dve", sc[4][:, sl], pt[:, :], V0[:, sl], ALU.add, [Rp, R["w"]], [Rsc[4]])
                    self.act(sc[4][:], sc[4][:], AF.Sigmoid, [Rsc[4]], [Rsc[4]])
                    self.dma("sp", ot[:], self.vfirst[t0:t0 + 128, :], d_v, [Rvf[n]], [R["ot"]])
                    self.tt("pool", ot[:], ot[:], sc[1][:], ALU.subtract, [R["ot"], Rsc[1]], [R["ot"]])
                    self.tt("pool", ot[:], ot[:], sc[4][:], ALU.mult, [R["ot"], Rsc[4]], [R["ot"]])
                    self.tt("pool", sc[1][:], sc[1][:], ot[:], ALU.add, [R["ot"], Rsc[1]], [Rsc[1]])
                self.copy("act", v_bf[:], sc[1][:], [Rsc[1]], [R["v_bf"]])
                for nh in range(2):
                    sl = slice(nh * 512, (nh + 1) * 512)
                    pt, Rp = bank()
                    self.mm(pt[:, :], lsb[0:64, 0:128], wa2[0:64, sl], True, True, [R["lsb"], R["w"]], [Rp])
                    self.tt("dve", sc[2][:, sl], pt[:, :], W0[:, sl], ALU.add, [Rp, R["w"]], [Rsc[2]])
                self.act(sc[2][:], sc[2][:], AF.Sigmoid, [Rsc[2]], [Rsc[2]])
                k32, a32 = sc[0], sc[3]
                self.tt("dve", sc[4][:], k32[:], KK_, ALU.mult, [Rsc[0], R["w"]], [Rsc[4]])
                self.act(sc[1][:], sc[4][:], AF.Square, [Rsc[4], R["v_bf"]], [Rsc[1]])
                self.P.emit("dve", lambda e_: e_.tensor_reduce(out=st16[:, 0, :], in_=h3(sc[1][:, :]), axis=AX.X, op=ALU.add),
                            [Rsc[1]], [R["st16"]])
                self.act(st16[:, 1, :], st16[:, 0, :], AF.Ln, [R["st16"], Rc], [R["st16"]], bias=tiny[:, 0:1])
                self.act(st16[:, 1, :], st16[:, 1, :], AF.Exp, [R["st16"]], [R["st16"]], scale=-0.5)
                rnb = st16[:, 1, :].unsqueeze(2).broadcast_to([128, 16, 64])
                self.tt("dve", h3(kk_bf[:, :]), h3(sc[4][:, :]), rnb, ALU.mult, [Rsc[4], R["st16"]], [R["kk_bf"]])
                self.stt("dve", sc[1][:], a32[:], -1.0, KA_, ALU.add, ALU.mult, [Rsc[3], R["w"]], [Rsc[1]])
                self.stt("dve", kp_bf[:], sc[1][:], 1.0, k32[:], ALU.add, ALU.mult, [Rsc[1], Rsc[0]], [R["kp_bf"]])
                self.tt("pool", b_bf[:], kk_bf[:], a32[:], ALU.mult, [R["kk_bf"], Rsc[3]], [R["b_bf"]])
                self.tt("pool", sc[4][:], r_bf[:], kp_bf[:], ALU.mult, [R["r_bf"], R["kp_bf"]], [Rsc[4]])
                self.tt("pool", sc[4][:], sc[4][:], RK_, ALU.mult, [Rsc[4], R["w"]], [Rsc[4]])
                self.P.emit("dve", lambda e_: e_.tensor_reduce(out=st16[:, 2, :], in_=h3(sc[4][:, :]), axis=AX.X, op=ALU.add),
                            [Rsc[4]], [R["st16"]])
                if stop <= 2:
                    early(n, t0)
                    continue
                if self.dbg_tags:
                    P.tag = "c%d.%s" % (n, "R3")
                ld = sc[2]
                for nh in range(2):
                    sl = slice(nh * 512, (nh + 1) * 512)
                    pt, Rp = bank()
                    self.mm(pt[:, :], cIU[:], ld[:, sl], True, True, [Rsc[2], Rc], [Rp])
                    self.act(sc[0][:, sl], pt[:, :], AF.Exp, [Rp], [Rsc[0]])
                    self.act(sc[1][:, sl], pt[:, :], AF.Exp, [Rp], [Rsc[1]], scale=-1.0)
                for nh in range(2):
                    sl = slice(nh * 512, (nh + 1) * 512)
                    pt, Rp = bank()
                    self.mm(pt[:, :], cSU[:], ld[:, sl], True, True, [Rsc[2], Rc], [Rp])
                    self.act(sc[3][:, sl], pt[:, :], AF.Exp, [Rp], [Rsc[3]])
                for nh in range(2):
                    sl = slice(nh * 512, (nh + 1) * 512)
                    pt, Rp = bank()
                    self.mm(pt[:, :], onesf[:], ld[:, sl], True, True, [Rsc[2], Rc], [Rp])
                    self.act(sc[4][:, sl], pt[:, :], AF.Exp, [Rp], [Rsc[4]])
                self.tt("pool", sc[4][:], sc[4][:], sc[1][:], ALU.mult, [Rsc[4], Rsc[1]], [Rsc[4]])
                pt, Rp = bank()
                for c in range(8):
                    self.mm(pt[:, c:c + 1], ld[:, c * 128:(c + 1) * 128], onesf[:, 0:1], True, True, [Rsc[2], Rc], [Rp])
                self.act(PCc[:], pt[:, 0:8], AF.Exp, [Rp], [R["PCc"]])
                if stop <= 3:
                    early(n, t0)
                    continue
                if self.dbg_tags:
                    P.tag = "c%d.%s" % (n, "R4")
                def prod_T(j, eng, in0, Rin0, in1, Rin1, dstT, RdstT, neg=False):
                    if neg:
                        self.stt("dve", prod[j][:], in0, -1.0, in1, ALU.mult, ALU.mult, [Rin0, Rin1], [Rprod[j]])
                    else:
                        self.tt(eng, prod[j][:], in0, in1, ALU.mult, [Rin0, Rin1], [Rprod[j]])
                    tpt, Rtp = tbank()
                    for c in range(8):
                        self.tr(tpt[:, c * 128:(c + 1) * 128], prod[j][:, c * 128:(c + 1) * 128], self.ident[:], [Rprod[j], Rc], [Rtp])
                    self.copy("act", dstT[:, :, :], tpt[:, :].rearrange("p (c t) -> p c t", c=8), [Rtp], [RdstT])
                prod_T(0, "dve", r_bf[:], R["r_bf"], sc[0][:], Rsc[0], RtT, R["RtT"])
                prod_T(1, "pool", kp_bf[:], R["kp_bf"], sc[1][:], Rsc[1], KtT, R["KtT"])
                prod_T(0, "dve", b_bf[:], R["b_bf"], sc[1][:], Rsc[1], BtT, R["BtT"])
                prod_T(1, "dve", kk_bf[:], R["kk_bf"], sc[3][:], Rsc[3], AtT, R["AtT"], neg=True)
                self.tt("pool", Khat[:], kp_bf[:], sc[4][:], ALU.mult, [R["kp_bf"], Rsc[4]], [R["Khat"]])
                self.tt("dve", Bhat[:], b_bf[:], sc[4][:], ALU.mult, [R["b_bf"], Rsc[4]], [R["Bhat"]])
                if stop <= 4:
                    early(n, t0)
                    continue
                if self.dbg_tags:
                    P.tag = "c%d.%s" % (n, "R5")
                def hv(T, h):
                    return T[(h % 2) * 64:(h % 2) * 64 + 64, h // 2, :]
                for half in range(2):
                    Mb, Nb, Xb = Mbh[half], Nbh[half], Xbh[half]
                    RMb, RNb, RXb = RMbh[half], RNbh[half], RXbh[half]
                    hb0 = half * 8
                    v4 = lambda pt_: pt_[:, :].rearrange("p (h t) -> p h t", h=4)

                    def amat(lhs_T, Rl, rhs_T, Rr, dst, dbase, maskb, Rdst, fast=False):
                        (pe_, Rpe), (po_, Rpo) = bank(), bank()
                        for i in range(4):
                            he, ho = hb0 + 2 * i, hb0 + 2 * i + 1
                            self.mm(pe_[:, i * 128:(i + 1) * 128], hv(lhs_T, he), hv(rhs_T, he), True, True, [Rl, Rr], [Rpe])
                            self.mm(po_[:, i * 128:(i + 1) * 128], hv(lhs_T, ho), hv(rhs_T, ho), True, True, [Rl, Rr], [Rpo])
                        self.tt("dve", dst[:, dbase + 0:dbase + 8:2, :], v4(pe_), maskb, ALU.mult, [Rpe, Rc], [Rdst])
                        if fast:
                            self.tt("dve", dst[:, dbase + 1:dbase + 8:2, :], v4(po_), maskb, ALU.mult, [Rpo, Rc], [Rdst])
                        else:
                            self.copy("act", dst[:, dbase + 1:dbase + 8:2, :], v4(po_), [Rpo], [Rdst])
                            self.tt("pool", dst[:, dbase + 1:dbase + 8:2, :], dst[:, dbase + 1:dbase + 8:2, :], maskb, ALU.mult, [Rdst, Rc], [Rdst])

                    amat(BtT, R["BtT"], AtT, R["AtT"], Mb[0], 0, SUb(4), RMb[0], fast=True)
                    amat(AtT, R["AtT"], BtT, R["BtT"], Nb[0], 0, SLb(4), RNb[0], fast=True)
                    amat(BtT, R["BtT"], RtT, R["RtT"], ArbT, hb0, IUb(4), R["ArbT"])
                    amat(KtT, R["KtT"], AtT, R["AtT"], AakT, hb0, SUb(4), R["AakT"])
                    amat(KtT, R["KtT"], RtT, R["RtT"], ArkT, hb0, IUb(4), R["ArkT"])
                    self.tt("pool", Xb[0][:], Mb[0][:], IDb(8), ALU.add, [RMb[0], Rc], [RXb[0]])
                if self.dbg_tags:
                    P.tag = "c%d.%s" % (n, "R6")
                cm, cn, cx = 0, 0, 0
                for k in range(1, 7):
                    for half in range(2):
                        Mb, Nb = Mbh[half], Nbh[half]
                        RMb, RNb = RMbh[half], RNbh[half]
                        for q4 in range(2):
                            pt, Rp = bank()
                            for i in range(4):
                                hh = q4 * 4 + i
                                self.mm(pt[:, i * 128:(i + 1) * 128], Mb[cm][:, hh, :], Nb[cn][:, hh, :], True, True, [RMb[cm], RNb[cn]], [Rp])
                            self.copy("act", Nb[1 - cn][:, q4 * 4:q4 * 4 + 4, :], pt[:, :].rearrange("p (h t) -> p h t", h=4), [Rp], [RNb[1 - cn]])
                        if k < 6:
                            for q4 in range(2):
                                pt, Rp = bank()
                                for i in range(4):
                                    hh = q4 * 4 + i
                                    self.mm(pt[:, i * 128:(i + 1) * 128], Nb[cn][:, hh, :], Mb[cm][:, hh, :], True, True, [RMb[cm], RNb[cn]], [Rp])
                                self.copy("act", Mb[1 - cm][:, q4 * 4:q4 * 4 + 4, :], pt[:, :].rearrange("p (h t) -> p h t", h=4), [Rp], [RMb[1 - cm]])
                    cn = 1 - cn
                    if k < 6:
                        cm = 1 - cm
                    for half in range(2):
                        Nb, Xb = Nbh[half], Xbh[half]
                        RNb, RXb = RNbh[half], RXbh[half]
                        for q4 in range(2):
                            pt, Rp = bank()
                            for i in range(4):
                                hh = q4 * 4 + i
                                self.mm(pt[:, i * 128:(i + 1) * 128], Nb[cn][:, hh, :], Xb[cx][:, hh, :], True, True, [RNb[cn], RXb[cx]], [Rp])
                            if k < 6:
                                dst, Rdst = Xb[1 - cx][:, q4 * 4:q4 * 4 + 4, :], RXb[1 - cx]
                            else:
                                dst, Rdst = XT[:, half * 8 + q4 * 4:half * 8 + q4 * 4 + 4, :], R["XT"]
                            self.tt("dve", dst, pt[:, :].rearrange("p (h t) -> p h t", h=4), Xb[cx][:, q4 * 4:q4 * 4 + 4, :], ALU.add,
                                    [Rp, RXb[cx]], [Rdst])
                    cx = 1 - cx
                bb_ = st16[:, 2, :].unsqueeze(2).broadcast_to([128, 16, 64])
                self.tt("dve", h3(sc[3][:, :]), h3(v_bf[:, :]), bb_, ALU.mult, [R["v_bf"], R["st16"]], [Rsc[3]])
                self.tt("pool", sc[3][:], sc[3][:], LNB, ALU.add, [Rsc[3], R["w"]], [Rsc[3]])
                self.tt("pool", sc[3][:], sc[3][:], g_bf[:], ALU.mult, [Rsc[3], R["g_bf"]], [Rsc[3]])
                self.tt("pool", g_bf[:], g_bf[:], LNW, ALU.mult, [R["g_bf"], R["w"]], [R["g_bf"]])
                if n + 1 < NCH:
                    stage_r1(n + 1)
                if stop <= 6:
                    early(n, t0)
                    continue
                if self.dbg_tags:
                    P.tag = "c%d.%s" % (n, "R7")
                sthv = lambda h: STb[(h % 2) * 64:(h % 2) * 64 + 64, h // 2, :]
                hc_ = lambda T, h: T[:, h * 64:(h + 1) * 64]
                bks = [bank(), bank()]
                for h in range(16):
                    pt, Rp = bks[h // 8]
                    o_ = pt[:, (h % 8) * 64:(h % 8) * 64 + 64]
                    self.mm(o_, hv(AtT, h), sthv(h), True, False, [R["AtT"], R["STb"]], [Rp])
                    self.mm(o_, AakT[:, h, :], hc_(v_bf, h), False, True, [R["AakT"], R["v_bf"]], [Rp])
                for nh, (pt, Rp) in enumerate(bks):
                    self.copy("act", RHS[:, nh * 512:(nh + 1) * 512], pt[:, :], [Rp], [R["RHS"]])
                bks = [bank(), bank()]
                for h in range(16):
                    pt, Rp = bks[h // 8]
                    self.mm(pt[:, (h % 8) * 64:(h % 8) * 64 + 64], XT[:, h, :], hc_(RHS, h), True, True, [R["XT"], R["RHS"]], [Rp])
                for nh, (pt, Rp) in enumerate(bks):
                    self.copy("act", U[:, nh * 512:(nh + 1) * 512], pt[:, :], [Rp], [R["U"]])
                bks = [bank(), bank()]
                for h in range(16):
                    pt, Rp = bks[h // 8]
                    o_ = pt[:, (h % 8) * 64:(h % 8) * 64 + 64]
                    self.mm(o_, hv(RtT, h), sthv(h), True, False, [R["RtT"], R["STb"]], [Rp])
                    self.mm(o_, ArbT[:, h, :], hc_(U, h), False, False, [R["ArbT"], R["U"]], [Rp])
                    self.mm(o_, ArkT[:, h, :], hc_(v_bf, h), False, True, [R["ArkT"], R["v_bf"]], [Rp])
                for nh, (pt, Rp) in enumerate(bks):
                    self.copy("act", sc[0][:, nh * 512:(nh + 1) * 512], pt[:, :], [Rp], [Rsc[0]])
                pt, Rp = bank()
                for h in range(16):
                    o_ = pt[(h % 2) * 64:(h % 2) * 64 + 64, (h // 2) * 64:(h // 2) * 64 + 64]
                    self.mm(o_, hc_(Bhat, h), hc_(U, h), True, False, [R["Bhat"], R["U"]], [Rp], tile_position=(0, (h % 2) * 64))
                    self.mm(o_, hc_(Khat, h), hc_(v_bf, h), False, True, [R["Khat"], R["v_bf"]], [Rp], tile_position=(0, (h % 2) * 64))
                pcb = PCc[:, :].unsqueeze(2).broadcast_to([128, 8, 64])
                self.tt("pool", STt[:], ST[:], pcb, ALU.mult, [R["ST"], R["PCc"]], [R["STt"]])
                self.tt("dve", ST[:], STt[:], pt[:, :].rearrange("p (c v) -> p c v", c=8), ALU.add, [R["STt"], Rp], [R["ST"]])
                self.copy("pool", STb[:], ST[:], [R["ST"]], [R["STb"]])
                if stop <= 7:
                    early(n, t0)
                    continue
                if self.dbg_tags:
                    P.tag = "c%d.%s" % (n, "R8")
                y = sc[0]
                self.P.emit("dve", lambda e_: e_.tensor_reduce(out=st16[:, 3, :], in_=h3(y[:, :]), axis=AX.X, op=ALU.add),
                            [Rsc[0]], [R["st16"]])
                self.act(sc[1][:], y[:], AF.Square, [Rsc[0]], [Rsc[1]])
                self.P.emit("dve", lambda e_: e_.tensor_reduce(out=st16[:, 4, :], in_=h3(sc[1][:, :]), axis=AX.X, op=ALU.add),
                            [Rsc[1]], [R["st16"]])
                self.ts("dve", st16[:, 3, :], st16[:, 3, :], 1.0 / 64, None, ALU.mult, None, [R["st16"]], [R["st16"]])
                self.tt("dve", st16[:, 5, :], st16[:, 3, :], st16[:, 3, :], ALU.mult, [R["st16"]], [R["st16"]])
                self.stt("dve", st16[:, 4, :], st16[:, 4, :], 1.0 / 64, st16[:, 5, :], ALU.mult, ALU.subtract, [R["st16"]], [R["st16"]])
                self.act(st16[:, 4, :], st16[:, 4, :], AF.Ln, [R["st16"], Rc], [R["st16"]], bias=self.eps_rms[:, 1:2])
                self.act(st16[:, 4, :], st16[:, 4, :], AF.Exp, [R["st16"]], [R["st16"]], scale=-0.5)
                mb_ = st16[:, 3, :].unsqueeze(2).broadcast_to([128, 16, 64])
                rb_ = st16[:, 4, :].unsqueeze(2).broadcast_to([128, 16, 64])
                self.tt("dve", h3(sc[1][:, :]), h3(y[:, :]), mb_, ALU.subtract, [Rsc[0], R["st16"]], [Rsc[1]])
                self.tt("dve", h3(sc[1][:, :]), h3(sc[1][:, :]), rb_, ALU.mult, [Rsc[1], R["st16"]], [Rsc[1]])
                self.tt("dve", sc[1][:], sc[1][:], g_bf[:], ALU.mult, [Rsc[1], R["g_bf"]], [Rsc[1]])
                self.tt("dve", yfin[:], sc[1][:], sc[3][:], ALU.add, [Rsc[1], Rsc[3]], [R["yfin"]])
                tpt, Rtp = tbank()
                for c in range(8):
                    self.tr(tpt[:, c * 128:(c + 1) * 128], yfin[:, c * 128:(c + 1) * 128], self.ident[:], [R["yfin"], Rc], [Rtp])
                self.copy("act", yT[:, :, :], tpt[:, :].rearrange("p (c t) -> p c t", c=8), [Rtp], [R["yT"]])
                self.dma("sp", ot[:], xin[t0:t0 + 128, :], d_x2, [Rxdram[n]], [R["ot"]])
                for nh, (pt, Rp) in enumerate(proj_tok(yT, R["yT"], Wo)):
                    sl = slice(nh * 512, (nh + 1) * 512)
                    self.tt("dve", ot[:, sl], pt[:, :], ot[:, sl], ALU.add, [Rp, R["ot"]], [R["ot"]])
                self.dma("sp", xout[t0:t0 + 128, :], ot[:], d_o, [R["ot"]], [Rxdram[n]])
            P.barrier()


def build_program(S, sublayers, n_cores=8):
    nc = bass.Bass("TRN2", target_bir_lowering=False)
    specs = param_specs()
    prm = {}
    x = nc.dram_tensor("x", [S, D], F32, kind="ExternalInput").ap()
    for name, shp in specs.items():
        prm[name] = nc.dram_tensor(name, list(shp), F32, kind="ExternalInput").ap()
    out = nc.dram_tensor("out", [S, D], F32, kind="ExternalOutput").ap()
    with ExitStack() as st:
        kb = KB(nc, S, st)
        kb.Rxdram = [Res() for _ in range(S // 128)]
        kb.Rvf = [Res() for _ in range(S // 128)]
        kb.vfirst = nc.dram_tensor("vfirst_scratch", [S, D], F32).ap()
        kb.setup_consts()
        cur = x
        for sl in sublayers:
            if sl[0] == "ffn":
                kb.ffn_phase(sl[1], cur, out, prm)
            elif sl[0] == "hy":
                kb.hy_phase(sl[1], sl[2], cur, out, prm)
            elif sl[0] == "rw":
                kb.rw_phase(sl[1], sl[2], cur, out, prm)
            cur = out
        kb.P.barrier()
        kb.P.finalize()
        kb.stats = (dict(kb.P.n), kb.P.nwaits)
        print("instr counts", kb.P.n, "waits", kb.P.nwaits)
    return nc


def param_specs():
    return {
        "mix_norm": (4, D), "ffn_norm": (4, D),
        "ffn_w_gate": (4, D, DFF), "ffn_w_up": (4, D, DFF), "ffn_w_down": (4, DFF, D),
        "hy_w_in": (2, D, IN_COLS), "hy_f_bias": (2, 8), "hy_q_gain": (2, 64), "hy_k_gain": (2, 64),
        "hy_pool_w": (2, 4, 128, 128), "hy_pool_scale": (2, 512), "hy_w_out": (2, D, D),
        "rw_mu": (2, 6, D), "rw_w_r": (2, D, D), "rw_w_k": (2, D, D), "rw_w_v": (2, D, D),
        "rw_w0": (2, D), "rw_w1": (2, D, 64), "rw_w2": (2, 64, D), "rw_a0": (2, D), "rw_a1": (2, D, 64),
        "rw_a2": (2, 64, D), "rw_g1": (2, D, 160), "rw_g2": (2, 160, D), "rw_k_k": (2, D), "rw_k_a": (2, D),
        "rw_r_k": (2, 16, 64), "rw_ln_w": (2, D), "rw_ln_b": (2, D), "rw_w_o": (2, D, D),
        "rw_v0": (1, D), "rw_v1": (1, D, 32), "rw_v2": (1, 32, D),
    }


FULL = [("hy", 0, 0), ("ffn", 0), ("rw", 0, 1), ("ffn", 1), ("hy", 1, 2), ("ffn", 2), ("rw", 1, 3), ("ffn", 3)]


def run(inputs, S, sublayers, n_cores=8, trace=False):
    nc = build_program(S, sublayers)
    specs = param_specs()
    x = np.ascontiguousarray(np.asarray(inputs["x"], dtype=np.float32))
    shared = {k: np.ascontiguousarray(np.asarray(inputs[k], dtype=np.float32)) for k in specs}
    in_maps = []
    for c in range(n_cores):
        m = dict(shared)
        m["x"] = x[c]
        in_maps.append(m)
    res = run_bass_kernel_spmd(nc, in_maps, core_ids=list(range(n_cores)), trace=trace)
    outs = np.stack([np.asarray(r["out"]) for r in res.results], axis=0)
    return outs, res


def kernel(**inputs):
    outs, _ = run(inputs, 4096, FULL, n_cores=8)
    return outs.astype(np.float32)
```
